# Optimizing a Trainium2 kernel written in Bass

```python
import math
import jax, jax.numpy as jnp
from jax import lax
import numpy as np

D_MODEL = 1024
BATCH = 8
SEQ = 4096
DEPTH = 4

N_MIXERS = 3
NORM_EPS = 1e-6

RW_HEAD = 64
RW_HEADS = D_MODEL // RW_HEAD
RW_WIDTH = RW_HEADS * RW_HEAD
RW_DECAY_RANK = 64
RW_AAA_RANK = 64
RW_VRES_RANK = 32
RW_GN_EPS = 64e-5

SSD_EXPAND = 2
SSD_WIDTH = SSD_EXPAND * D_MODEL
SSD_HEAD = 64
SSD_HEADS = SSD_WIDTH // SSD_HEAD
SSD_GROUPS = 8
SSD_STATE = 128
SSD_CONV = 4
SSD_CHUNK = 128
SSD_CONV_DIM = SSD_WIDTH + 2 * SSD_GROUPS * SSD_STATE

SB_HEAD = 64
SB_HEADS = D_MODEL // SB_HEAD
SB_WIDTH = SB_HEADS * SB_HEAD
SB_BLOCK = 128
SB_SCALE = 1.0 / math.sqrt(SB_HEAD)

kernel_name = "hybrid_rwkv7_ssd_stickbreak_trunk"


def rms_norm(u, g, eps=NORM_EPS):
    u32 = u.astype(jnp.float32)
    y = u32 * lax.rsqrt(jnp.mean(u32 * u32, axis=-1, keepdims=True) + eps)
    return (y * g.astype(jnp.float32)).astype(u.dtype)


def token_shift(h):
    return jnp.pad(h, ((0, 0), (1, 0), (0, 0)))[:, :-1]


def rwkv7_mixer(h, p, v_first):
    mu, w_in, w_up, w0, a_up, a0, k_k, k_a, r_k, gn_w, gn_b, w_out = p[:12]
    Bsz, T, _ = h.shape
    W, H, N = RW_WIDTH, RW_HEADS, RW_HEAD
    dx = token_shift(h) - h
    x_r, x_w, x_k, x_v, x_a, x_g = (h + dx * mu[i] for i in range(6))
    c = np.cumsum([0, W, W, W, W, RW_DECAY_RANK, RW_AAA_RANK])
    r = x_r @ w_in[:, c[0]:c[1]]
    k = x_k @ w_in[:, c[1]:c[2]]
    v = x_v @ w_in[:, c[2]:c[3]]
    g = jax.nn.silu(x_g @ w_in[:, c[3]:c[4]])
    w_lo = jnp.tanh(x_w @ w_in[:, c[4]:c[5]])
    a_lo = x_a @ w_in[:, c[5]:c[6]]
    f32 = jnp.float32
    w_raw = (w0 + w_lo @ w_up).astype(f32)
    log_decay = -jnp.exp(-jax.nn.softplus(-w_raw) - 0.5)
    a = jax.nn.sigmoid((a0 + a_lo @ a_up).astype(f32))
    if v_first is None:
        v_first = v
    else:
        v_up, v0 = p[12:]
        v_lo = x_v @ w_in[:, c[6]:]
        v = v + (v_first - v) * jax.nn.sigmoid(v0 + v_lo @ v_up)

    def heads(t):
        return t.astype(f32).reshape(Bsz, T, H, N)

    kk = heads(k * k_k)
    kk = kk / jnp.maximum(jnp.sqrt(jnp.sum(kk * kk, axis=-1, keepdims=True)), 1e-12)
    k = k.astype(f32) * (1.0 + (a - 1.0) * k_a.astype(f32))
    r_h, k_h, v_h, a_h = heads(r), heads(k), heads(v), heads(a)
    decay_h = jnp.exp(heads(log_decay))

    def step(S, inp):
        r_t, w_t, k_t, v_t, kk_t, a_t = inp
        sa = jnp.einsum('bhvk,bhk->bhv', S, -kk_t)
        S = (S * w_t[:, :, None, :]
             + sa[..., None] * (kk_t * a_t)[:, :, None, :]
             + v_t[..., None] * k_t[:, :, None, :])
        y_t = jnp.einsum('bhvk,bhk->bhv', S, r_t)
        return S, y_t

    tm = lambda t: jnp.moveaxis(t, 1, 0)
    S0 = jnp.zeros((Bsz, H, N, N), f32)
    _, y = lax.scan(step, S0, (tm(r_h), tm(decay_h), tm(k_h), tm(v_h), tm(kk), tm(a_h)))
    y = jnp.moveaxis(y, 0, 1)
    mean = jnp.mean(y, axis=-1, keepdims=True)
    var = jnp.mean(jnp.square(y - mean), axis=-1, keepdims=True)
    y = (y - mean) * lax.rsqrt(var + RW_GN_EPS)
    y = y * gn_w.astype(f32).reshape(H, N) + gn_b.astype(f32).reshape(H, N)
    y = y + jnp.sum(r_h * k_h * r_k.astype(f32), axis=-1, keepdims=True) * v_h
    y = y.reshape(Bsz, T, W).astype(h.dtype) * g
    return y @ w_out, v_first


def causal_depthwise_conv(u, w, b):
    out = lax.conv_general_dilated(
        u, w[:, None, :].astype(u.dtype), window_strides=(1,),
        padding=[(w.shape[0] - 1, 0)], dimension_numbers=('NWC', 'WIO', 'NWC'),
        feature_group_count=u.shape[-1])
    return out + b


def ssd_chunked(x, a, Bm, Cm):
    Bsz, T, H, P = x.shape
    G, R, l = SSD_GROUPS, H // SSD_GROUPS, SSD_CHUNK
    nc = T // l
    x = x.reshape(Bsz, nc, l, G, R, P)
    Bm = Bm.reshape(Bsz, nc, l, G, SSD_STATE)
    Cm = Cm.reshape(Bsz, nc, l, G, SSD_STATE)
    ac = jnp.cumsum(jnp.moveaxis(a.reshape(Bsz, nc, l, G, R), 2, -1), axis=-1)
    causal = jnp.tril(jnp.ones((l, l), bool))
    seg = ac[..., :, None] - ac[..., None, :]
    decay_in = jnp.exp(jnp.where(causal, seg, -jnp.inf))
    cb = jnp.einsum('bclgn,bcsgn->bcgls', Cm, Bm)
    y_diag = jnp.einsum('bcgrls,bcsgrp->bclgrp', cb[:, :, :, None] * decay_in, x)
    decay_states = jnp.moveaxis(jnp.exp(ac[..., -1:] - ac), -1, 2)
    states = jnp.einsum('bclgn,bclgrp->bcgrpn', Bm, x * decay_states[..., None])
    chunk_decay = ac[..., -1]

    def chunk_step(h_prev, inp):
        st, dec = inp
        return h_prev * jnp.exp(dec)[..., None, None] + st, h_prev

    h0 = jnp.zeros((Bsz, G, R, P, SSD_STATE), x.dtype)
    _, h_in = lax.scan(chunk_step, h0, (jnp.moveaxis(states, 1, 0), jnp.moveaxis(chunk_decay, 1, 0)))
    h_in = jnp.moveaxis(h_in, 0, 1)
    decay_out = jnp.moveaxis(jnp.exp(ac), -1, 2)
    y_off = jnp.einsum('bclgn,bcgrpn->bclgrp', Cm, h_in) * decay_out[..., None]
    return (y_diag + y_off).reshape(Bsz, T, H, P)


def mamba2_mixer(h, p):
    w_in, conv_w, conv_b, dt_bias, a_log, d_skip, gnorm_w, w_out = p
    Bsz, T, _ = h.shape
    f32 = jnp.float32
    proj = h @ w_in
    z = proj[..., :SSD_WIDTH]
    xbc = proj[..., SSD_WIDTH:SSD_WIDTH + SSD_CONV_DIM]
    dt_raw = proj[..., SSD_WIDTH + SSD_CONV_DIM:]
    xbc = jax.nn.silu(causal_depthwise_conv(xbc, conv_w, conv_b)).astype(f32)
    gn = SSD_GROUPS * SSD_STATE
    xs = xbc[..., :SSD_WIDTH].reshape(Bsz, T, SSD_HEADS, SSD_HEAD)
    Bm = xbc[..., SSD_WIDTH:SSD_WIDTH + gn].reshape(Bsz, T, SSD_GROUPS, SSD_STATE)
    Cm = xbc[..., SSD_WIDTH + gn:].reshape(Bsz, T, SSD_GROUPS, SSD_STATE)
    dt = jax.nn.softplus((dt_raw + dt_bias).astype(f32))
    A = -jnp.exp(a_log.astype(f32))
    y = ssd_chunked(xs * dt[..., None], dt * A, Bm, Cm)
    y = y + xs * d_skip.astype(f32)[:, None]
    y = y.reshape(Bsz, T, SSD_WIDTH) * jax.nn.silu(z.astype(f32))
    y = y.reshape(Bsz, T, SSD_GROUPS, SSD_WIDTH // SSD_GROUPS)
    y = y * lax.rsqrt(jnp.mean(y * y, axis=-1, keepdims=True) + NORM_EPS)
    y = y.reshape(Bsz, T, SSD_WIDTH) * gnorm_w.astype(f32)
    return y.astype(h.dtype) @ w_out


def stick_breaking_mixer(h, p):
    w_in, q_norm, k_norm, w_out = p
    Bsz, T, _ = h.shape
    f32 = jnp.float32
    proj = h @ w_in
    W = SB_WIDTH
    q = rms_norm(proj[..., :W].reshape(Bsz, T, SB_HEADS, SB_HEAD), q_norm).astype(f32)
    k = rms_norm(proj[..., W:2 * W].reshape(Bsz, T, SB_HEADS, SB_HEAD), k_norm).astype(f32)
    v = proj[..., 2 * W:3 * W].reshape(Bsz, T, SB_HEADS, SB_HEAD).astype(f32)
    g = jax.nn.silu(proj[..., 3 * W:])
    outs = []
    for blk in range(T // SB_BLOCK):
        start, end = blk * SB_BLOCK, (blk + 1) * SB_BLOCK
        qb, kb, vb = q[:, start:end], k[:, :end], v[:, :end]
        z = jnp.einsum('bthd,bshd->bhts', qb, kb) * SB_SCALE
        t_idx = start + jnp.arange(SB_BLOCK)
        s_idx = jnp.arange(end)
        strict = s_idx[None, :] < t_idx[:, None]
        log_beta = jax.nn.log_sigmoid(z)
        log_1m_beta = jnp.where(strict, log_beta - z, 0.0)
        tail = lax.cumsum(log_1m_beta, axis=3, reverse=True) - log_1m_beta
        att = jnp.where(strict, jnp.exp(log_beta + tail), 0.0)
        outs.append(jnp.einsum('bhts,bshd->bthd', att, vb))
    o = jnp.concatenate(outs, axis=1).reshape(Bsz, T, W).astype(h.dtype)
    return (o * g) @ w_out


def _normal(key, shape, scale):
    return scale * jax.random.normal(key, shape, jnp.float32)


def _rwkv_params(key, prefix, value_residual):
    ks = jax.random.split(key, 16)
    W = RW_WIDTH
    n_cols = 4 * W + RW_DECAY_RANK + RW_AAA_RANK + (RW_VRES_RANK if value_residual else 0)
    p = {
        prefix + "norm": 1.0 + _normal(ks[0], (D_MODEL,), 0.05),
        prefix + "mu": jax.random.uniform(ks[1], (6, D_MODEL), jnp.float32),
        prefix + "w_in": _normal(ks[2], (D_MODEL, n_cols), D_MODEL ** -0.5),
        prefix + "w_up": _normal(ks[3], (RW_DECAY_RANK, W), 0.5 * RW_DECAY_RANK ** -0.5),
        prefix + "w0": jax.random.uniform(ks[4], (W,), jnp.float32, -6.0, 1.0),
        prefix + "a_up": _normal(ks[5], (RW_AAA_RANK, W), 0.5 * RW_AAA_RANK ** -0.5),
        prefix + "a0": _normal(ks[6], (W,), 0.5),
        prefix + "k_k": 0.85 + _normal(ks[7], (W,), 0.05),
        prefix + "k_a": 1.0 + _normal(ks[8], (W,), 0.05),
        prefix + "r_k": _normal(ks[9], (RW_HEADS, RW_HEAD), 0.1),
        prefix + "gn_w": 1.0 + _normal(ks[10], (W,), 0.05),
        prefix + "gn_b": _normal(ks[11], (W,), 0.01),
        prefix + "w_out": _normal(ks[12], (W, D_MODEL), W ** -0.5),
    }
    if value_residual:
        p[prefix + "v_up"] = _normal(ks[13], (RW_VRES_RANK, W), 0.5 * RW_VRES_RANK ** -0.5)
        p[prefix + "v0"] = _normal(ks[14], (W,), 0.5)
    return p


def _mamba2_params(key, prefix):
    ks = jax.random.split(key, 10)
    n_cols = SSD_WIDTH + SSD_CONV_DIM + SSD_HEADS
    dt = jnp.exp(jax.random.uniform(ks[4], (SSD_HEADS,), jnp.float32, math.log(1e-3), math.log(1e-1)))
    return {
        prefix + "norm": 1.0 + _normal(ks[0], (D_MODEL,), 0.05),
        prefix + "w_in": _normal(ks[1], (D_MODEL, n_cols), D_MODEL ** -0.5),
        prefix + "conv_w": _normal(ks[2], (SSD_CONV, SSD_CONV_DIM), SSD_CONV ** -0.5),
        prefix + "conv_b": _normal(ks[3], (SSD_CONV_DIM,), 0.01),
        prefix + "dt_bias": dt + jnp.log(-jnp.expm1(-dt)),
        prefix + "a_log": jnp.log(jax.random.uniform(ks[5], (SSD_HEADS,), jnp.float32, 1.0, 16.0)),
        prefix + "d_skip": 1.0 + _normal(ks[6], (SSD_HEADS,), 0.05),
        prefix + "gnorm_w": 1.0 + _normal(ks[7], (SSD_WIDTH,), 0.05),
        prefix + "w_out": _normal(ks[8], (SSD_WIDTH, D_MODEL), SSD_WIDTH ** -0.5),
    }


def _stickbreak_params(key, prefix):
    ks = jax.random.split(key, 5)
    return {
        prefix + "norm": 1.0 + _normal(ks[0], (D_MODEL,), 0.05),
        prefix + "w_in": _normal(ks[1], (D_MODEL, 4 * SB_WIDTH), D_MODEL ** -0.5),
        prefix + "q_norm": 1.0 + _normal(ks[2], (SB_HEAD,), 0.05),
        prefix + "k_norm": 1.0 + _normal(ks[3], (SB_HEAD,), 0.05),
        prefix + "w_out": _normal(ks[4], (SB_WIDTH, D_MODEL), SB_WIDTH ** -0.5),
    }


def setup_inputs(seed: int = 0) -> dict:
    key = jax.random.key(seed)
    kx, k0, k1, k2, k3 = jax.random.split(key, 5)
    inputs = {"x": jax.random.normal(kx, (BATCH, SEQ, D_MODEL), jnp.float32)}
    inputs.update(_rwkv_params(k0, "l0_", value_residual=False))
    inputs.update(_mamba2_params(k1, "l1_"))
    inputs.update(_stickbreak_params(k2, "l2_"))
    inputs.update(_rwkv_params(k3, "l3_", value_residual=True))
    return inputs


def reference(x,
              l0_norm, l0_mu, l0_w_in, l0_w_up, l0_w0, l0_a_up, l0_a0, l0_k_k, l0_k_a, l0_r_k,
              l0_gn_w, l0_gn_b, l0_w_out,
              l1_norm, l1_w_in, l1_conv_w, l1_conv_b, l1_dt_bias, l1_a_log, l1_d_skip, l1_gnorm_w,
              l1_w_out,
              l2_norm, l2_w_in, l2_q_norm, l2_k_norm, l2_w_out,
              l3_norm, l3_mu, l3_w_in, l3_w_up, l3_w0, l3_a_up, l3_a0, l3_k_k, l3_k_a, l3_r_k,
              l3_gn_w, l3_gn_b, l3_w_out, l3_v_up, l3_v0):
    layer_params = [
        (l0_norm, (l0_mu, l0_w_in, l0_w_up, l0_w0, l0_a_up, l0_a0, l0_k_k, l0_k_a, l0_r_k,
                   l0_gn_w, l0_gn_b, l0_w_out)),
        (l1_norm, (l1_w_in, l1_conv_w, l1_conv_b, l1_dt_bias, l1_a_log, l1_d_skip, l1_gnorm_w,
                   l1_w_out)),
        (l2_norm, (l2_w_in, l2_q_norm, l2_k_norm, l2_w_out)),
        (l3_norm, (l3_mu, l3_w_in, l3_w_up, l3_w0, l3_a_up, l3_a0, l3_k_k, l3_k_a, l3_r_k,
                   l3_gn_w, l3_gn_b, l3_w_out, l3_v_up, l3_v0)),
    ]
    v_first = None
    for i in range(DEPTH):
        norm_g, p = layer_params[i]
        h = rms_norm(x, norm_g)
        kind = i % N_MIXERS
        if kind == 0:
            y, v_first = rwkv7_mixer(h, p, v_first)
        elif kind == 1:
            y = mamba2_mixer(h, p)
        else:
            y = stick_breaking_mixer(h, p)
        x = x + y.astype(x.dtype)
    return x
```

```python
import contextlib
import math
import numpy as np
import concourse.bass as bass
import concourse.mybir as mybir
from concourse.bass_utils import run_bass_kernel_spmd

F32 = mybir.dt.float32
BF16 = mybir.dt.bfloat16
AF = mybir.ActivationFunctionType
ALU = mybir.AluOpType
AX = mybir.AxisListType

D = 1024
NORM_EPS = 1e-6
N_DMA_SEMS = 24


class Prog:
    ENGS = ("pe", "act", "dve", "pool", "sp")

    def __init__(self, nc):
        self.nc = nc
        self.root = contextlib.ExitStack()
        self.stack = self.root
        self.ops = {e: [] for e in self.ENGS}
        self.cnt = {e: 0 for e in self.ENGS}
        self.esem = {e: self.root.enter_context(nc.semaphore("s_" + e)) for e in self.ENGS}
        self.dsem = [self.root.enter_context(nc.semaphore("d%d" % i)) for i in range(N_DMA_SEMS)]
        self.dcnt = [0] * N_DMA_SEMS
        self.dlast = [None] * N_DMA_SEMS
        self.ndma = 0
        self.lw = {}
        self.rd = {}
        self.waited = {e: {} for e in self.ENGS}
        self.sem_by_id = {}
        self.final = []
        self.n_inst = 0
        self.uid = 0

    def _sid(self, sem):
        self.sem_by_id[id(sem)] = sem
        return id(sem)

    def sb(self, name, shape, dt=F32, root=False):
        self.uid += 1
        return (self.root if root else self.stack).enter_context(self.nc.sbuf_tensor("%s_%d" % (name, self.uid), list(shape), dt))

    def ps(self, name, shape, dt=F32):
        self.uid += 1
        return self.stack.enter_context(self.nc.psum_tensor("%s_%d" % (name, self.uid), list(shape), dt))

    def _deps(self, e, reads, writes):
        deps = []
        for k in reads:
            t = self.lw.get(k)
            if t is not None:
                deps.append(t)
        for k in writes:
            t = self.lw.get(k)
            if t is not None:
                deps.append(t)
            deps.extend(self.rd.get(k, ()))
        waits = {}
        for (sid, val, src) in deps:
            if src == e and e == "pe":
                continue
            if self.waited[e].get(sid, 0) >= val:
                continue
            if waits.get(sid, 0) < val:
                waits[sid] = val
        for sid, val in waits.items():
            self.waited[e][sid] = val
        return [(self.sem_by_id[sid], val) for sid, val in waits.items()]

    def _record(self, tok, reads, writes):
        for k in reads:
            lst = self.rd.setdefault(k, [])
            lst[:] = [t for t in lst if t[0] != tok[0]]
            lst.append(tok)
        for k in writes:
            self.lw[k] = tok
            self.rd[k] = []

    def op(self, e, fn, reads=(), writes=()):
        waits = self._deps(e, reads, writes)
        self.cnt[e] += 1
        sem = self.esem[e]
        tok = (self._sid(sem), self.cnt[e], e)
        self.ops[e].append((waits, fn, (sem, 1)))
        self._record(tok, reads, writes)
        self.n_inst += 1
        return tok

    def dma(self, out, in_, reads=(), writes=(), e="sp", final=False):
        i = self.ndma % N_DMA_SEMS
        self.ndma += 1
        sem = self.dsem[i]
        waits = self._deps(e, reads, writes)
        prev = self.dlast[i]
        if prev is not None and self.waited[e].get(prev[0], 0) < prev[1]:
            waits.append((sem, prev[1]))
            self.waited[e][prev[0]] = prev[1]
        self.dcnt[i] += 16
        tok = (self._sid(sem), self.dcnt[i], "dma")
        self.dlast[i] = tok
        self.ops[e].append((waits, lambda eng: eng.dma_start(out=out, in_=in_), (sem, 16)))
        self._record(tok, reads, writes)
        if final:
            self.final.append(tok)
        self.n_inst += 1
        return tok

    @contextlib.contextmanager
    def phase(self, last=False):
        outer = self.stack
        with contextlib.ExitStack() as st:
            self.stack = st
            yield
            self._emit(last)
        self.stack = outer

    def _emit(self, last):
        nc = self.nc
        fin = [(self.dsem[i], self.dcnt[i]) for i in range(N_DMA_SEMS) if self.dcnt[i] > 0]
        for i in range(N_DMA_SEMS):
            if self.dcnt[i] > 0:
                self.waited["sp"][id(self.dsem[i])] = self.dcnt[i]
        ops = self.ops
        with nc.Block() as block:
            def body(ename):
                def f(eng):
                    for waits, fn, (sem, inc) in ops[ename]:
                        for (s, v) in waits:
                            eng.wait_ge(s, v)
                        fn(eng).then_inc(sem, inc)
                    if ename == "sp":
                        for (s, v) in fin:
                            eng.wait_ge(s, v)
                return f
            block.sync(body("sp"))
            block.scalar(body("act"))
            block.vector(body("dve"))
            block.gpsimd(body("pool"))
            block.tensor(body("pe"))
        self.ops = {e: [] for e in self.ENGS}

    def close(self):
        self.root.close()


class Ring:
    def __init__(self, P, name, shape, dt=F32, n=2, psum=False):
        mk = P.ps if psum else P.sb
        self.t = [mk("%s%d" % (name, i), shape, dt) for i in range(n)]
        self.k = ["%s#%d#%d" % (name, P.uid, i) for i in range(n)]
        self.i = -1
        self.n = n

    def next(self):
        self.i = (self.i + 1) % self.n
        return self.t[self.i], self.k[self.i]


class Ctx:
    pass


def load_const(P, key, ap_dram, shape, cast=None):
    if cast is None:
        t = P.sb(key, shape, F32, root=True)
        P.dma(t[:], ap_dram, writes=[key])
        return t
    t = P.sb(key + "_f", shape, F32)
    P.dma(t[:], ap_dram, writes=[key + "_f"])
    tb = P.sb(key, shape, cast, root=True)
    P.op("pool", lambda e: e.tensor_copy(out=tb[:], in_=t[:]), reads=[key + "_f"], writes=[key])
    return tb


def load_weight_bf16(P, name, w_dram, kdim, ncols, stage):
    nck = kdim // 128
    wb = P.sb(name, [128, nck, ncols], BF16)
    wv = w_dram.rearrange("(c p) n -> p c n", p=128)
    CW = stage.t[0].shape[1]
    for c in range(nck):
        for c0 in range(0, ncols, CW):
            cw = min(CW, ncols - c0)
            st, sk = stage.next()
            P.dma(st[:, 0:cw], wv[:, c, c0:c0 + cw], writes=[sk])
            P.op("pool", lambda e, st=st, c=c, c0=c0, cw=cw: e.tensor_copy(out=wb[:, c, c0:c0 + cw], in_=st[:, 0:cw]),
                 reads=[sk], writes=[name])
    return wb


def make_hT(P, C, j, src, TT, rings, gcol, eps=NORM_EPS):
    hT, hk = rings["hT"].next()
    for s in range(TT // 128):
        r0 = j * TT + s * 128
        xt, xk = rings["xt"].next()
        P.dma(xt[:], src[r0:r0 + 128, :], reads=[("x", r0 // 128)], writes=[xk])
        sq, sqk = rings["sq"].next()
        ss, ssk = rings["ss"].next()
        P.op("act", lambda e, xt=xt, sq=sq, ss=ss: e.activation(out=sq[:], in_=xt[:], func=AF.Square, accum_out=ss[:, 0:1]),
             reads=[xk], writes=[sqk, ssk])
        P.op("act", lambda e, ss=ss: e.activation(out=ss[:, 1:2], in_=ss[:, 0:1], func=AF.Sqrt, scale=1.0 / D, bias=C.epsc[:, 0:1]),
             reads=[ssk], writes=[ssk])
        P.op("dve", lambda e, ss=ss: e.reciprocal(out=ss[:, 2:3], in_=ss[:, 1:2]), reads=[ssk], writes=[ssk])
        P.op("dve", lambda e, xt=xt, sq=sq, ss=ss: e.tensor_scalar(out=sq[:], in0=xt[:], scalar1=ss[:, 2:3], scalar2=None, op0=ALU.mult),
             reads=[xk, ssk], writes=[sqk])
        pT, pk = rings["pT"].next()
        for c in range(8):
            P.op("pe", lambda e, pT=pT, sq=sq, c=c: e.transpose(pT[:, c * 128:(c + 1) * 128], sq[:, c * 128:(c + 1) * 128], C.ident[:]),
                 reads=[sqk, "ident"], writes=[pk])
        for c in range(8):
            P.op("act", lambda e, pT=pT, hT=hT, c=c, s=s: e.activation(
                out=hT[:, c, 1 + s * 128:1 + (s + 1) * 128], in_=pT[:, c * 128:(c + 1) * 128], func=AF.Copy, scale=gcol[:, c:c + 1]),
                reads=[pk, "gcol"], writes=[hk])
    return hT, hk


def out_proj(P, C, L, T, ygT, kdim, w_out, cur, y_out, last):
    nck = kdim // 128
    with P.phase(last=last):
        stage = Ring(P, "wst", [128, 2048], F32, 2)
        wb = load_weight_bf16(P, "wout", w_out, kdim, D, stage)
        ygr = Ring(P, "ygr", [128, nck, 128], BF16, 3)
        xr = Ring(P, "xr", [128, D], F32, 3)
        pr = Ring(P, "po", [128, 512], F32, 4, psum=True)
        ygv = ygT.rearrange("(c p) t -> p c t", p=128)
        for s in range(T // 128):
            yg, ygk = ygr.next()
            P.dma(yg[:], ygv[:, :, s * 128:(s + 1) * 128], reads=[("yg", L, s // 4)], writes=[ygk])
            xt, xk = xr.next()
            P.dma(xt[:], cur[s * 128:(s + 1) * 128, :], reads=[("x", s)], writes=[xk], e="act")
            for hf in range(2):
                po, pk = pr.next()
                for c in range(nck):
                    P.op("pe", lambda e, po=po, yg=yg, c=c, hf=hf: e.matmul(
                        po[:], lhsT=yg[:, c, :], rhs=wb[:, c, hf * 512:(hf + 1) * 512], start=(c == 0), stop=(c == nck - 1)),
                        reads=[ygk, "wout"], writes=[pk])
                P.op("dve", lambda e, po=po, xt=xt, hf=hf: e.tensor_tensor(
                    out=xt[:, hf * 512:(hf + 1) * 512], in0=po[:], in1=xt[:, hf * 512:(hf + 1) * 512], op=ALU.add),
                    reads=[pk, xk], writes=[xk])
            P.dma(y_out[s * 128:(s + 1) * 128, :], xt[:], reads=[xk], writes=[("x", s)], final=last)


def in_rings(P, TT, nh=2, nx=3, nq=2):
    return {
        "hT": Ring(P, "hT", [128, 8, 1 + TT], F32, nh),
        "xt": Ring(P, "xt", [128, D], F32, nx),
        "sq": Ring(P, "sq", [128, D], F32, nq),
        "ss": Ring(P, "ss", [128, 4], F32, 4),
        "pT": Ring(P, "pT", [128, D], F32, 1, psum=True),
    }


def layer_sb(P, C, L, T, W, cur, y_out, last):
    nc = P.nc
    TT = 512
    NT = T // TT
    H = 16
    qT = nc.dram_tensor("sb_qT", [D, T], BF16, kind="Internal").ap()
    kT = nc.dram_tensor("sb_kT", [D, T], BF16, kind="Internal").ap()
    gT = nc.dram_tensor("sb_gT", [D, T], BF16, kind="Internal").ap()
    vTM = nc.dram_tensor("sb_v", [T, D], BF16, kind="Internal").ap()
    ygT = nc.dram_tensor("sb_ygT", [D, T], BF16, kind="Internal").ap()

    with P.phase():
        stage = Ring(P, "wst", [128, 2048], F32, 2)
        wb = load_weight_bf16(P, "win", W["w_in"], D, 4 * D, stage)
        gcol = P.sb("gcol", [128, 8], F32)
        P.dma(gcol[:], W["norm"], writes=["gcol"])
        qn = P.sb("qn", [128, 2], F32)
        P.dma(qn[:, 0:1], W["q_norm"], writes=["qn"])
        P.dma(qn[:, 1:2], W["k_norm"], writes=["qn"])
        P.op("dve", lambda e: e.tensor_scalar(out=qn[:, 0:1], in0=qn[:, 0:1], scalar1=0.125, scalar2=None, op0=ALU.mult),
             reads=["qn"], writes=["qn"])
        rings = in_rings(P, TT)
        hbr = Ring(P, "hb", [128, 8, TT], BF16, 2)
        pp = Ring(P, "pp", [128, 512], F32, 3, psum=True)
        pm = Ring(P, "pm", [128, 512], F32, 2, psum=True)
        sqb = Ring(P, "sqb", [128, 512], BF16, 2)
        rs = Ring(P, "rs", [128, 512], F32, 2)
        ob = Ring(P, "ob", [128, 512], BF16, 4)
        for j in range(NT):
            hT, hk = make_hT(P, C, j, cur, TT, rings, gcol)
            hb, hbk = hbr.next()
            for c in range(8):
                P.op("pool", lambda e, hb=hb, hT=hT, c=c: e.tensor_copy(out=hb[:, c, :], in_=hT[:, c, 1:1 + TT]),
                     reads=[hk], writes=[hbk])
            tsl = slice(j * TT, (j + 1) * TT)
            for which in range(3):
                cbase = {0: 0, 1: D, 2: 3 * D}[which]
                for fc in range(8):
                    p, pk = pp.next()
                    for c in range(8):
                        P.op("pe", lambda e, p=p, c=c, hb=hb, col=cbase + fc * 128: e.matmul(
                            p[:], lhsT=wb[:, c, col:col + 128], rhs=hb[:, c, :], start=(c == 0), stop=(c == 7)),
                            reads=["win", hbk], writes=[pk])
                    o, ok = ob.next()
                    if which == 2:
                        P.op("act", lambda e, p=p, o=o: e.activation(out=o[:], in_=p[:], func=AF.Silu), reads=[pk], writes=[ok])
                        P.dma(gT[fc * 128:(fc + 1) * 128, tsl], o[:], reads=[ok], writes=[("sbg", j)])
                    else:
                        s2, s2k = sqb.next()
                        P.op("act", lambda e, p=p, s2=s2: e.activation(out=s2[:], in_=p[:], func=AF.Square), reads=[pk], writes=[s2k])
                        m, mk = pm.next()
                        P.op("pe", lambda e, m=m, s2=s2: e.matmul(m[:], lhsT=C.bonesb[:], rhs=s2[:], start=True, stop=True),
                             reads=["bonesb", s2k], writes=[mk])
                        r, rk = rs.next()
                        P.op("act", lambda e, m=m, r=r: e.activation(out=r[:], in_=m[:], func=AF.Sqrt, scale=1.0 / 64, bias=C.epsc[:, 0:1]),
                             reads=[mk], writes=[rk])
                        P.op("dve", lambda e, r=r: e.reciprocal(out=r[:], in_=r[:]), reads=[rk], writes=[rk])
                        P.op("dve", lambda e, p=p, r=r, o=o, which=which: e.scalar_tensor_tensor(
                            out=o[:], in0=p[:], scalar=qn[:, which:which + 1], in1=r[:], op0=ALU.mult, op1=ALU.mult),
                            reads=[pk, rk, "qn"], writes=[ok])
                        dst = qT if which == 0 else kT
                        P.dma(dst[fc * 128:(fc + 1) * 128, tsl], o[:], reads=[ok], writes=[("sbq" if which == 0 else "sbk", j)])
            for s in range(TT // 128):
                for cb in range(2):
                    p, pk = pp.next()
                    for c in range(8):
                        P.op("pe", lambda e, p=p, c=c, hb=hb, s=s, cb=cb: e.matmul(
                            p[:], lhsT=hb[:, c, s * 128:(s + 1) * 128], rhs=wb[:, c, 2 * D + cb * 512:2 * D + (cb + 1) * 512],
                            start=(c == 0), stop=(c == 7)), reads=["win", hbk], writes=[pk])
                    o, ok = ob.next()
                    P.op("act", lambda e, p=p, o=o: e.copy(out=o[:], in_=p[:]), reads=[pk], writes=[ok])
                    r0 = j * TT + s * 128
                    P.dma(vTM[r0:r0 + 128, cb * 512:(cb + 1) * 512], o[:], reads=[ok], writes=[("sbv", j)])

    with P.phase():
        NB = T // 128
        kh_r = Ring(P, "kh", [64, T], BF16, 2)
        vh_r = Ring(P, "vh", [128, NB, 64], BF16, 2)
        qh_r = Ring(P, "qh", [64, TT], BF16, 2)
        gh_r = Ring(P, "gh", [64, TT], BF16, 2)
        pz = Ring(P, "pz", [128, 512], F32, 2, psum=True)
        pb = Ring(P, "pb", [128, 512], F32, 2, psum=True)
        pc = Ring(P, "pc", [64, 512], F32, 2, psum=True)
        po = Ring(P, "pov", [64, 512], F32, 2, psum=True)
        Er = Ring(P, "E", [128, 512], F32, 2)
        spr = Ring(P, "sp", [128, 512], BF16, 3)
        Pr = Ring(P, "Pm", [128, 512], BF16, 3)
        fr = Ring(P, "f", [64, 512], F32, 2)
        accr = Ring(P, "acc", [64, 512], F32, 2)
        ygr = Ring(P, "yg", [64, 512], BF16, 2)
        vview = vTM.rearrange("(b p) d -> p b d", p=128)
        qh_r = Ring(P, "qh3", [64, TT], BF16, 3)
        gh_r = Ring(P, "gh3", [64, TT], BF16, 3)
        spr = Ring(P, "sp4", [128, 512], BF16, 4)
        Pr = Ring(P, "Pm4", [128, 512], BF16, 4)
        units = [(h, tq) for h in range(H) for tq in range(NT)]
        ures = {}

        def load_unit(u):
            h, tq = units[u]
            r = {}
            if tq == 0:
                kh, khk = kh_r.next()
                P.dma(kh[:], kT[h * 64:(h + 1) * 64, :], reads=[("sbk", j) for j in range(NT)], writes=[khk])
                vh, vhk = vh_r.next()
                P.dma(vh[:], vview[:, :, h * 64:(h + 1) * 64], reads=[("sbv", j) for j in range(NT)], writes=[vhk], e="act")
                ures[("kv", h)] = (kh, khk, vh, vhk)
            tsl = slice(tq * TT, (tq + 1) * TT)
            r["qh"], r["qhk"] = qh_r.next()
            P.dma(r["qh"][:], qT[h * 64:(h + 1) * 64, tsl], reads=[("sbq", tq)], writes=[r["qhk"]])
            r["gh"], r["ghk"] = gh_r.next()
            P.dma(r["gh"][:], gT[h * 64:(h + 1) * 64, tsl], reads=[("sbg", tq)], writes=[r["ghk"]])
            r["acc"], r["acck"] = accr.next()
            ures[u] = r

        blocks = []
        for u, (h, tq) in enumerate(units):
            nkb = 4 * tq + 4
            for b_ in range(nkb):
                blocks.append((u, h, tq, b_, b_ - 4 * tq, b_ == nkb - 1))
        NBK = len(blocks)
        bres = {}
        load_unit(0)

        def S12(i):
            u, h, tq, b_, r, lastb = blocks[i]
            if b_ == 0 and u + 1 < len(units):
                load_unit(u + 1)
            U_ = ures[u]
            kh, khk, vh, vhk = ures[("kv", h)]
            R = {}
            z, zk = pz.next()
            mm(P, z[:], kh[:, b_ * 128:(b_ + 1) * 128], U_["qh"][:], True, True, [khk, U_["qhk"]], [zk])
            E, Ek = Er.next()
            actf(P, E[:], z[:], AF.Exp, [zk], [Ek])
            sp, spk = spr.next()
            actf(P, sp[:], E[:], AF.Ln, [Ek, "onec"], [spk], bias=C.onec[:, 0:1])
            if r >= 0:
                tt(P, "pool", sp[:], sp[:], C.maskb[:, r, :], ALU.mult, [spk, "maskb"], [spk])
            R["sp"], R["spk"] = sp, spk
            bres[i] = R

        def S34(i):
            u, h, tq, b_, r, lastb = blocks[i]
            U_ = ures[u]
            kh, khk, vh, vhk = ures[("kv", h)]
            R = bres[i]
            sp, spk = R["sp"], R["spk"]
            bb, bk = pb.next()
            mm(P, bb[:], C.ntril[:], sp[:], True, False, ["ntril", spk], [bk])
            mm(P, bb[:], kh[:, b_ * 128:(b_ + 1) * 128], U_["qh"][:], False, True, [khk, U_["qhk"]], [bk])
            cc, ck = pc.next()
            if b_ > 0:
                mm(P, cc[:], C.onesb[:, 0:64], sp[:], True, True, ["onesb", spk], [ck])
            Pm, Pk = Pr.next()
            actf(P, Pm[:], bb[:], AF.Exp, [bk], [Pk])
            if r >= 0:
                tt(P, "pool", Pm[:], Pm[:], C.maskb[:, r, :], ALU.mult, [Pk, "maskb"], [Pk])
            R["Pm"], R["Pk"] = Pm, Pk
            if b_ > 0:
                f, fk = fr.next()
                actf(P, f[:], cc[:], AF.Exp, [ck], [fk], scale=-1.0)
                R["f"], R["fk"] = f, fk

        def S56(i):
            u, h, tq, b_, r, lastb = blocks[i]
            U_ = ures[u]
            kh, khk, vh, vhk = ures[("kv", h)]
            R = bres.pop(i)
            acc, acck = U_["acc"], U_["acck"]
            ov, ovk = po.next()
            mm(P, ov[:], vh[:, b_, :], R["Pm"][:], True, True, [vhk, R["Pk"]], [ovk])
            if b_ == 0:
                P.op("dve", lambda e, acc=acc, ov=ov: e.tensor_copy(out=acc[:], in_=ov[:]), reads=[ovk], writes=[acck])
            else:
                tt(P, "dve", acc[:], acc[:], R["f"][:], ALU.mult, [acck, R["fk"]], [acck])
                tt(P, "dve", acc[:], ov[:], acc[:], ALU.add, [acck, ovk], [acck])
            if lastb:
                tsl = slice(tq * TT, (tq + 1) * TT)
                yg, ygk = ygr.next()
                tt(P, "dve", yg[:], acc[:], U_["gh"][:], ALU.mult, [acck, U_["ghk"]], [ygk])
                P.dma(ygT[h * 64:(h + 1) * 64, tsl], yg[:], reads=[ygk], writes=[("yg", L, tq)])
                del ures[u]

        for i in range(NBK + 2):
            if i < NBK:
                S12(i)
            if 0 <= i - 1 < NBK:
                S34(i - 1)
            if 0 <= i - 2 < NBK:
                S56(i - 2)

    out_proj(P, C, L, T, ygT, D, W["w_out"], cur, y_out, last)


def layer_ssd(P, C, L, T, W, cur, y_out, last):
    nc = P.nc
    TT = 512
    NT = T // TT
    NCH = T // 128
    xbcT = nc.dram_tensor("ssd_xbcT", [4096, T], BF16, kind="Internal").ap()
    zT = nc.dram_tensor("ssd_zT", [2048, T], BF16, kind="Internal").ap()
    dtD = nc.dram_tensor("ssd_dt", [T, 64], F32, kind="Internal").ap()
    ygT = nc.dram_tensor("ssd_ygT", [2048, T], BF16, kind="Internal").ap()

    with P.phase():
        stage = Ring(P, "wst", [128, 2048], F32, 2)
        wb = load_weight_bf16(P, "win", W["w_in"], D, 6176, stage)
        gcol = P.sb("gcol", [128, 8], F32)
        P.dma(gcol[:], W["norm"], writes=["gcol"])
        cw = P.sb("cw", [128, 32, 4], F32)
        P.dma(cw[:], W["conv_w"], writes=["cw"])
        cbias = P.sb("cbias", [128, 32], F32)
        P.dma(cbias[:], W["conv_b"], writes=["cbias"])
        dtb = P.sb("dtb", [128, 32], F32)
        P.dma(dtb[:], W["dt_bias"], writes=["dtb"])
        Arep = P.sb("Arep", [128, 32], F32)
        P.dma(Arep[:], W["a_log"], writes=["Arep"])
        P.op("act", lambda e: e.activation(out=Arep[:], in_=Arep[:], func=AF.Exp), reads=["Arep"], writes=["Arep"])
        P.op("dve", lambda e: e.tensor_scalar(out=Arep[:], in0=Arep[:], scalar1=-1.0, scalar2=None, op0=ALU.mult),
             reads=["Arep"], writes=["Arep"])
        hist = P.sb("hist", [128, 32, 3], F32)
        P.op("pool", lambda e: e.memset(hist[:], 0.0), writes=["hist"])
        rings = in_rings(P, TT, nh=1, nx=2)
        hbr = Ring(P, "hb", [128, 8, TT], BF16, 2)
        pp = Ring(P, "pp", [128, 512], F32, 3, psum=True)
        pd = Ring(P, "pd", [128, 64], F32, 2, psum=True)
        xcr = Ring(P, "xc", [128, 3 + TT], F32, 2)
        cvr = Ring(P, "cv", [128, TT], F32, 2)
        ob = Ring(P, "ob", [128, 512], BF16, 4)
        dtr = Ring(P, "dtr", [128, 64], F32, 2)
        for j in range(NT):
            hT, hk = make_hT(P, C, j, cur, TT, rings, gcol)
            hb, hbk = hbr.next()
            for c in range(8):
                P.op("pool", lambda e, hb=hb, hT=hT, c=c: e.tensor_copy(out=hb[:, c, :], in_=hT[:, c, 1:1 + TT]),
                     reads=[hk], writes=[hbk])
            tsl = slice(j * TT, (j + 1) * TT)
            for fc in range(48):
                p, pk = pp.next()
                for c in range(8):
                    P.op("pe", lambda e, p=p, c=c, hb=hb, col=fc * 128: e.matmul(
                        p[:], lhsT=wb[:, c, col:col + 128], rhs=hb[:, c, :], start=(c == 0), stop=(c == 7)),
                        reads=["win", hbk], writes=[pk])
                o, ok = ob.next()
                if fc < 16:
                    P.op("act", lambda e, p=p, o=o: e.activation(out=o[:], in_=p[:], func=AF.Silu), reads=[pk], writes=[ok])
                    P.dma(zT[fc * 128:(fc + 1) * 128, tsl], o[:], reads=[ok], writes=[("ssdz", j)])
                else:
                    cc = fc - 16
                    xc, xck = xcr.next()
                    P.op("pool", lambda e, xc=xc, cc=cc: e.tensor_copy(out=xc[:, 0:3], in_=hist[:, cc, :]), reads=["hist"], writes=[xck])
                    P.op("act", lambda e, xc=xc, p=p: e.copy(out=xc[:, 3:3 + TT], in_=p[:]), reads=[pk], writes=[xck])
                    P.op("pool", lambda e, xc=xc, cc=cc: e.tensor_copy(out=hist[:, cc, :], in_=xc[:, TT:TT + 3]), reads=[xck], writes=["hist"])
                    cv, cvk = cvr.next()
                    P.op("dve", lambda e, cv=cv, xc=xc, cc=cc: e.tensor_scalar(
                        out=cv[:], in0=xc[:, 0:TT], scalar1=cw[:, cc, 0:1], scalar2=None, op0=ALU.mult), reads=[xck, "cw"], writes=[cvk])
                    for kk in range(1, 4):
                        P.op("dve", lambda e, cv=cv, xc=xc, cc=cc, kk=kk: e.scalar_tensor_tensor(
                            out=cv[:], in0=xc[:, kk:kk + TT], scalar=cw[:, cc, kk:kk + 1], in1=cv[:], op0=ALU.mult, op1=ALU.add),
                            reads=[xck, "cw", cvk], writes=[cvk])
                    P.op("act", lambda e, cv=cv, o=o, cc=cc: e.activation(out=o[:], in_=cv[:], func=AF.Silu, bias=cbias[:, cc:cc + 1]),
                         reads=[cvk, "cbias"], writes=[ok])
                    P.dma(xbcT[cc * 128:(cc + 1) * 128, tsl], o[:], reads=[ok], writes=[("ssdx", j)])
            for s in range(TT // 128):
                p, pk = pd.next()
                for c in range(8):
                    P.op("pe", lambda e, p=p, c=c, hb=hb, s=s: e.matmul(
                        p[:, 0:32], lhsT=hb[:, c, s * 128:(s + 1) * 128], rhs=wb[:, c, 6144:6176], start=(c == 0), stop=(c == 7)),
                        reads=["win", hbk], writes=[pk])
                d, dk = dtr.next()
                P.op("dve", lambda e, d=d, p=p: e.tensor_tensor(out=d[:, 0:32], in0=p[:, 0:32], in1=dtb[:], op=ALU.add),
                     reads=[pk, "dtb"], writes=[dk])
                P.op("act", lambda e, d=d: e.activation(out=d[:, 0:32], in_=d[:, 0:32], func=AF.Exp), reads=[dk], writes=[dk])
                P.op("act", lambda e, d=d: e.activation(out=d[:, 0:32], in_=d[:, 0:32], func=AF.Ln, bias=C.onec[:, 0:1]),
                     reads=[dk], writes=[dk])
                P.op("dve", lambda e, d=d: e.tensor_tensor(out=d[:, 32:64], in0=d[:, 0:32], in1=Arep[:], op=ALU.mult),
                     reads=[dk, "Arep"], writes=[dk])
                r0 = j * TT + s * 128
                P.dma(dtD[r0:r0 + 128, :], d[:], reads=[dk], writes=[("ssdd", j)])

    with P.phase():
        dsk = P.sb("dsk", [128, 16], F32)
        P.dma(dsk[:], W["d_skip"], writes=["dsk"])
        gnw = P.sb("gnw", [128, 16], F32)
        P.dma(gnw[:], W["gnorm_w"], writes=["gnw"])
        hsf = P.sb("hsf", [128, 32, 64], F32)
        hsb = P.sb("hsb", [128, 32, 64], BF16)
        for g in range(8):
            P.op("pool", lambda e, g=g: e.memset(hsf[:, 4 * g:4 * g + 4, :], 0.0), writes=[("hsf", g)])
            P.op("pool", lambda e, g=g: e.memset(hsb[:, 4 * g:4 * g + 4, :], 0.0), writes=[("hsb", g)])
        xbr = Ring(P, "xb", [128, 32, 128], BF16, 2)
        zr = Ring(P, "zs", [128, 16, 128], BF16, 2)
        dr = Ring(P, "dta", [128, 64], F32, 2)
        p_cb = Ring(P, "pcb", [128, 512], F32, 1, psum=True)
        p_rb = Ring(P, "prb", [128, 512], F32, 2, psum=True)
        p_y = Ring(P, "py", [128, 512], F32, 1, psum=True)
        p_ms = Ring(P, "pms", [128, 512], F32, 1, psum=True)
        p_st = Ring(P, "pst", [128, 512], F32, 1, psum=True)
        p_tr = Ring(P, "ptr", [128, 1024], BF16, 1, psum=True)
        p_ac = Ring(P, "pac", [128, 512], F32, 1, psum=True)
        nacr = Ring(P, "nac", [128, 32], F32, 2)
        cdr = Ring(P, "cd", [128, 32], F32, 2)
        dsr = Ring(P, "ds", [128, 32], F32, 2)
        xdtr = Ring(P, "xdt", [128, 32, 64], BF16, 2)
        xsr = Ring(P, "xs", [128, 32, 64], BF16, 2)
        bmr = Ring(P, "bm", [128, 1024], BF16, 2)
        cbmr = Ring(P, "cbm", [128, 128], F32, 2)
        dmr = Ring(P, "dm", [128, 128], F32, 3)
        Er = Ring(P, "E", [128, 128], F32, 3)
        ELr = Ring(P, "EL", [128, 128], F32, 3)
        MTr = Ring(P, "MT", [128, 128], BF16, 3)
        CSr = Ring(P, "CS", [128, 128], BF16, 3)
        y1r = Ring(P, "y1", [128, 128], F32, 2)
        Yr = Ring(P, "Y", [128, 16, 128], F32, 2)
        sqr = Ring(P, "sqy", [128, 128], BF16, 2)
        rrr = Ring(P, "rr", [128, 128], F32, 2)
        YGr = Ring(P, "YG", [128, 16, 128], BF16, 2)
        xv = xbcT.rearrange("(c p) t -> p c t", p=128)
        zv = zT.rearrange("(c p) t -> p c t", p=128)
        ygv = ygT.rearrange("(c p) t -> p c t", p=128)
        for ch in range(NCH):
            csl = slice(ch * 128, (ch + 1) * 128)
            jt = ch // 4
            xb, xbk = xbr.next()
            P.dma(xb[:], xv[:, :, csl], reads=[("ssdx", jt)], writes=[xbk])
            zs, zsk = zr.next()
            P.dma(zs[:], zv[:, :, csl], reads=[("ssdz", jt)], writes=[zsk], e="act")
            da, dak = dr.next()
            P.dma(da[:], dtD[csl, :], reads=[("ssdd", jt)], writes=[dak])
            ac, ack = p_ac.next()
            P.op("pe", lambda e, ac=ac, da=da: e.matmul(ac[:, 0:32], lhsT=C.triu[:], rhs=da[:, 32:64], start=True, stop=True),
                 reads=["triu", dak], writes=[ack])
            P.op("pe", lambda e, ac=ac, da=da: e.matmul(ac[:, 32:64], lhsT=C.onesf[:], rhs=da[:, 32:64], start=True, stop=True),
                 reads=["onesf", dak], writes=[ack])
            nac, nack = nacr.next()
            P.op("dve", lambda e, nac=nac, ac=ac: e.tensor_scalar(out=nac[:], in0=ac[:, 0:32], scalar1=-1.0, scalar2=None, op0=ALU.mult),
                 reads=[ack], writes=[nack])
            cd, cdk = cdr.next()
            P.op("act", lambda e, cd=cd, ac=ac: e.activation(out=cd[:], in_=ac[:, 32:64], func=AF.Exp), reads=[ack], writes=[cdk])
            ds, dsk_ = dsr.next()
            P.op("dve", lambda e, ds=ds, ac=ac, nac=nac: e.tensor_tensor(out=ds[:], in0=ac[:, 32:64], in1=nac[:], op=ALU.add),
                 reads=[ack, nack], writes=[dsk_])
            P.op("act", lambda e, ds=ds: e.activation(out=ds[:], in_=ds[:], func=AF.Exp), reads=[dsk_], writes=[dsk_])
            xdt, xdtk = xdtr.next()
            xs, xsk = xsr.next()
            for hf in range(2):
                tr, trk = p_tr.next()
                for q in range(8):
                    P.op("pe", lambda e, tr=tr, xb=xb, q=q, hf=hf: e.transpose(tr[:, q * 128:(q + 1) * 128], xb[:, hf * 8 + q, :], C.identb[:]),
                         reads=[xbk, "identb"], writes=[trk])
                P.op("dve", lambda e, tr=tr, xdt=xdt, da=da, hf=hf: e.tensor_tensor(
                    out=xdt[:, hf * 16:(hf + 1) * 16, :], in0=tr[:].rearrange("p (h d) -> p h d", d=64),
                    in1=da[:, hf * 16:(hf + 1) * 16].unsqueeze(2).broadcast_to([128, 16, 64]), op=ALU.mult),
                    reads=[trk, dak], writes=[xdtk])
            P.op("pool", lambda e, xs=xs, xdt=xdt, ds=ds: e.tensor_tensor(
                out=xs[:], in0=xdt[:], in1=ds[:].unsqueeze(2).broadcast_to([128, 32, 64]), op=ALU.mult),
                reads=[xdtk, dsk_], writes=[xsk])
            tr, trk = p_tr.next()
            for q in range(8):
                P.op("pe", lambda e, tr=tr, xb=xb, q=q: e.transpose(tr[:, q * 128:(q + 1) * 128], xb[:, 16 + q, :], C.identb[:]),
                     reads=[xbk, "identb"], writes=[trk])
            bm, bmk = bmr.next()
            P.op("act", lambda e, bm=bm, tr=tr: e.copy(out=bm[:], in_=tr[:]), reads=[trk], writes=[bmk])
            Y, Yk = Yr.next()
            YG, YGk = YGr.next()
            for g in range(8):
                cb, cbk = p_cb.next()
                P.op("pe", lambda e, cb=cb, xb=xb, g=g: e.matmul(cb[:, 0:128], lhsT=xb[:, 16 + g, :], rhs=xb[:, 24 + g, :], start=True, stop=True),
                     reads=[xbk], writes=[cbk])
                cbm, cbmk = cbmr.next()
                P.op("dve", lambda e, cbm=cbm, cb=cb: e.tensor_tensor(out=cbm[:], in0=cb[:, 0:128], in1=C.triu[:], op=ALU.mult),
                     reads=[cbk, "triu"], writes=[cbmk])
                ms, msk = p_ms.next()
                for r in range(4):
                    h = 4 * g + r
                    pq = h // 2
                    rb, rbk = p_rb.next()
                    P.op("pe", lambda e, rb=rb, da=da, h=h: e.matmul(
                        rb[:, 0:128], lhsT=da[:, 32 + h:33 + h].broadcast_to([128, 128]), rhs=C.triu[:], start=True, stop=True),
                        reads=[dak, "triu"], writes=[rbk])
                    dm, dmk = dmr.next()
                    P.op("dve", lambda e, dm=dm, rb=rb, nac=nac, h=h: e.tensor_scalar(
                        out=dm[:], in0=rb[:, 0:128], scalar1=nac[:, h:h + 1], scalar2=0.0, op0=ALU.add, op1=ALU.min),
                        reads=[rbk, nack], writes=[dmk])
                    E, Ek = Er.next()
                    P.op("act", lambda e, E=E, dm=dm: e.activation(out=E[:], in_=dm[:], func=AF.Exp), reads=[dmk], writes=[Ek])
                    MT, MTk = MTr.next()
                    P.op("pool", lambda e, MT=MT, E=E, cbm=cbm: e.tensor_tensor(out=MT[:], in0=E[:], in1=cbm[:], op=ALU.mult),
                         reads=[Ek, cbmk], writes=[MTk])
                    if r % 2 == 0:
                        py, pyk = p_y.next()
                    osl = slice((r % 2) * 64, (r % 2) * 64 + 64)
                    if ch > 0:
                        EL, ELk = ELr.next()
                        P.op("act", lambda e, EL=EL, rb=rb: e.activation(out=EL[:], in_=rb[:, 0:128], func=AF.Exp), reads=[rbk], writes=[ELk])
                        CS, CSk = CSr.next()
                        P.op("pool", lambda e, CS=CS, EL=EL, xb=xb, g=g: e.tensor_tensor(out=CS[:], in0=EL[:], in1=xb[:, 24 + g, :], op=ALU.mult),
                             reads=[ELk, xbk], writes=[CSk])
                    P.op("pe", lambda e, py=py, xdt=xdt, MT=MT, h=h, osl=osl, ch=ch: e.matmul(
                        py[osl, 0:128], lhsT=xdt[:, h, :], rhs=MT[:], start=True, stop=(ch == 0)),
                        reads=[xdtk, MTk], writes=[pyk])
                    if ch > 0:
                        P.op("pe", lambda e, py=py, CS=CS, h=h, osl=osl: e.matmul(
                            py[osl, 0:128], lhsT=hsb[:, h, :], rhs=CS[:], start=False, stop=True),
                            reads=[("hsb", g), CSk], writes=[pyk])
                    if r % 2 == 1:
                        y1, y1k = y1r.next()
                        P.op("dve", lambda e, y1=y1, xb=xb, pq=pq, py=py: e.scalar_tensor_tensor(
                            out=y1[:], in0=xb[:, pq, :], scalar=dsk[:, pq:pq + 1], in1=py[:, 0:128], op0=ALU.mult, op1=ALU.add),
                            reads=[xbk, "dsk", pyk], writes=[y1k])
                        P.op("pool", lambda e, Y=Y, y1=y1, zs=zs, pq=pq: e.tensor_tensor(out=Y[:, pq, :], in0=y1[:], in1=zs[:, pq, :], op=ALU.mult),
                             reads=[y1k, zsk], writes=[(Yk, pq)])
                        sq, sqk = sqr.next()
                        P.op("act", lambda e, sq=sq, Y=Y, pq=pq: e.activation(out=sq[:], in_=Y[:, pq, :], func=AF.Square),
                             reads=[(Yk, pq)], writes=[sqk])
                        P.op("pe", lambda e, ms=ms, sq=sq, r=r: e.matmul(ms[:, 0:128], lhsT=C.onesb[:], rhs=sq[:], start=(r == 1), stop=(r == 3)),
                             reads=["onesb", sqk], writes=[msk])
                rr, rrk = rrr.next()
                P.op("act", lambda e, rr=rr, ms=ms: e.activation(out=rr[:], in_=ms[:, 0:128], func=AF.Sqrt, scale=1.0 / 256, bias=C.epsc[:, 0:1]),
                     reads=[msk], writes=[rrk])
                P.op("dve", lambda e, rr=rr: e.reciprocal(out=rr[:], in_=rr[:]), reads=[rrk], writes=[rrk])
                for pq in (2 * g, 2 * g + 1):
                    P.op("dve", lambda e, YG=YG, Y=Y, rr=rr, pq=pq: e.scalar_tensor_tensor(
                        out=YG[:, pq, :], in0=Y[:, pq, :], scalar=gnw[:, pq:pq + 1], in1=rr[:], op0=ALU.mult, op1=ALU.mult),
                        reads=[(Yk, pq), rrk, "gnw"], writes=[YGk])
                if ch < NCH - 1:
                    st, stk = p_st.next()
                    P.op("pe", lambda e, st=st, bm=bm, xs=xs, g=g: e.matmul(
                        st[:, 0:256], lhsT=bm[:, g * 128:(g + 1) * 128], rhs=xs[:, 4 * g:4 * g + 4, :].rearrange("p h d -> p (h d)"),
                        start=True, stop=True), reads=[bmk, xsk], writes=[stk])
                    P.op("dve", lambda e, cd=cd, g=g: e.tensor_tensor(
                        out=hsf[:, 4 * g:4 * g + 4, :], in0=hsf[:, 4 * g:4 * g + 4, :],
                        in1=cd[:, 4 * g:4 * g + 4].unsqueeze(2).broadcast_to([128, 4, 64]), op=ALU.mult),
                        reads=[("hsf", g), cdk], writes=[("hsf", g)])
                    P.op("dve", lambda e, st=st, g=g: e.tensor_tensor(
                        out=hsf[:, 4 * g:4 * g + 4, :], in0=st[:, 0:256].rearrange("p (h d) -> p h d", d=64),
                        in1=hsf[:, 4 * g:4 * g + 4, :], op=ALU.add), reads=[("hsf", g), stk], writes=[("hsf", g)])
                    P.op("act", lambda e, g=g: e.copy(out=hsb[:, 4 * g:4 * g + 4, :], in_=hsf[:, 4 * g:4 * g + 4, :]),
                         reads=[("hsf", g)], writes=[("hsb", g)])
            P.dma(ygv[:, :, csl], YG[:], reads=[YGk], writes=[("yg", L, jt)])

    out_proj(P, C, L, T, ygT, 2048, W["w_out"], cur, y_out, last)


def tt(P, eng, out, in0, in1, op, r, w):
    return P.op(eng, lambda e: e.tensor_tensor(out=out, in0=in0, in1=in1, op=op), reads=r, writes=w)


def ts(P, eng, out, in0, s1, s2, op0, op1, r, w):
    if s2 is None:
        return P.op(eng, lambda e: e.tensor_scalar(out=out, in0=in0, scalar1=s1, scalar2=None, op0=op0), reads=r, writes=w)
    return P.op(eng, lambda e: e.tensor_scalar(out=out, in0=in0, scalar1=s1, scalar2=s2, op0=op0, op1=op1), reads=r, writes=w)


def stt(P, out, in0, scalar, in1, op0, op1, r, w):
    return P.op("dve", lambda e: e.scalar_tensor_tensor(out=out, in0=in0, scalar=scalar, in1=in1, op0=op0, op1=op1), reads=r, writes=w)


def actf(P, out, in_, func, r, w, bias=None, scale=None):
    kw = {}
    if bias is not None:
        kw["bias"] = bias
    if scale is not None:
        kw["scale"] = scale
    return P.op("act", lambda e: e.activation(out=out, in_=in_, func=func, **kw), reads=r, writes=w)


def mm(P, out, lhsT, rhs, start, stop, r, w):
    return P.op("pe", lambda e: e.matmul(out, lhsT=lhsT, rhs=rhs, start=start, stop=stop), reads=r, writes=w)


def trp(P, out, in_, ident, r, w):
    return P.op("pe", lambda e: e.transpose(out, in_, ident), reads=r, writes=w)


RW_GN_EPS = 64e-5
NEG_EXP_HALF = -math.exp(-0.5)
CDT = F32


def layer_rwkv(P, C, L, T, W, cur, y_out, last):
    nc = P.nc
    vres = (L == 3)
    TT = 256
    NT = T // TT
    NCK = T // 64
    ncols = 4 * D + 128 + (32 if vres else 0)
    pre = "rw%d_" % L
    names = ["At", "Bt", "Kt", "Rt", "Bg", "Kg", "V", "BON"]
    Dm = {n: nc.dram_tensor(pre + n, [D, T], (F32 if n == "BON" else BF16), kind="Internal").ap() for n in names}
    gCd = nc.dram_tensor(pre + "gC", [D, NCK], F32, kind="Internal").ap()
    gTd = nc.dram_tensor(pre + "gT", [D, T], BF16, kind="Internal").ap()
    yTd = nc.dram_tensor(pre + "yT", [D, T], F32, kind="Internal").ap()
    if L == 0:
        C.vfirst = nc.dram_tensor("rw_vfirst", [D, T], F32, kind="Internal").ap()
    vfd = C.vfirst

    def colp(name, n=8):
        t = P.sb(name, [128, n], F32)
        P.dma(t[:], W[name], writes=[name])
        return t

    with P.phase():
        stage = Ring(P, "wst", [128, 1024], F32, 2)
        wb = load_weight_bf16(P, "win", W["w_in"], D, ncols, stage)
        gcol = P.sb("gcol", [128, 8], F32)
        P.dma(gcol[:], W["norm"], writes=["gcol"])
        mu = P.sb("mu", [128, 6, 8], F32)
        P.dma(mu[:], W["mu"], writes=["mu"])
        w0 = colp("w0"); a0 = colp("a0"); k_k = colp("k_k"); k_a = colp("k_a"); gdum = None
        r_k = P.sb("r_k", [128, 8], F32)
        P.dma(r_k[:], W["r_k"], writes=["r_k"])
        omka = P.sb("omka", [128, 8], F32)
        ts(P, "dve", omka[:], k_a[:], -1.0, 1.0, ALU.mult, ALU.add, ["k_a"], ["omka"])
        def lowrank(nm, rows):
            t = P.sb(nm, [rows, D], BF16)
            st, sk = stage.next()
            P.dma(st[0:rows, :], W[nm], writes=[sk])
            P.op("pool", lambda e: e.tensor_copy(out=t[:], in_=st[0:rows, :]), reads=[sk], writes=[nm])
            return t
        wup = lowrank("w_up", 64)
        aup = lowrank("a_up", 64)
        if vres:
            v0 = colp("v0")
            vup = lowrank("v_up", 32)
        rings = in_rings(P, TT, nh=1, nx=2, nq=1)
        carry = P.sb("carry", [128, 8, 1], F32)
        P.op("pool", lambda e: e.memset(carry[:], 0.0), writes=["carry"])
        dxT = P.sb("dxT", [128, 8, TT], F32)
        xmix = {i: P.sb("xm%d" % i, [128, 8, TT], BF16) for i in (0, 2, 3)}
        xsm = Ring(P, "xsm", [128, 8, TT], BF16, 2)
        wl = P.sb("wl", [64, TT], BF16); al = P.sb("al", [64, TT], BF16); vl = P.sb("vl", [32, TT], BF16)
        pq = Ring(P, "pq", [128, 512], F32, 5, psum=True)
        F = lambda nm, n=2, dt=F32: Ring(P, nm, [128, TT], dt, n)
        R_ = {nm: F(nm, 1) for nm in ["sg", "lw", "cum", "gi", "ge", "gv", "gd", "aa", "kkr", "nrm", "kk", "t1", "kf", "d1", "ka", "sv", "vf"]}
        R_.update({nm: F(nm, 2) for nm in ["vv", "BON"]})
        R_.update({nm: F(nm, 2, BF16) for nm in ["At", "Bt", "Bg", "Kt", "Kg", "Rt", "vb"]})
        R_["sqb"] = F("sqb", 2, BF16); R_["rkb"] = F("rkb", 2, BF16); R_["go"] = F("go", 2, BF16)
        gCr = Ring(P, "gCt", [128, TT // 64], F32, 2)
        for j in range(NT):
            hT, hk = make_hT(P, C, j, cur, TT, rings, gcol)
            P.op("pool", lambda e, hT=hT: e.tensor_copy(out=hT[:, :, 0:1], in_=carry[:]), reads=["carry"], writes=[hk])
            P.op("pool", lambda e, hT=hT: e.tensor_copy(out=carry[:], in_=hT[:, :, TT:TT + 1]), reads=[hk], writes=["carry"])
            tt(P, "pool", dxT[:], hT[:, :, 0:TT], hT[:, :, 1:TT + 1], ALU.subtract, [hk], ["dxT"])
            tsl = slice(j * TT, (j + 1) * TT)

            def mix(i, dst, dk):
                for c in range(8):
                    stt(P, dst[:, c, :], dxT[:, c, :], mu[:, i, c:c + 1], hT[:, c, 1:TT + 1], ALU.mult, ALU.add,
                        ["dxT", "mu", hk], [dk])

            xw, xwk = xsm.next(); mix(1, xw, xwk)
            p, pk = pq.next()
            for c in range(8):
                mm(P, p[0:64, 0:TT], wb[:, c, 4096:4160], xw[:, c, :], c == 0, c == 7, ["win", xwk], [pk])
            actf(P, wl[:], p[0:64, 0:TT], AF.Tanh, [pk], ["wl"])
            xa, xak = xsm.next(); mix(4, xa, xak)
            p, pk = pq.next()
            for c in range(8):
                mm(P, p[0:64, 0:TT], wb[:, c, 4160:4224], xa[:, c, :], c == 0, c == 7, ["win", xak], [pk])
            actf(P, al[:], p[0:64, 0:TT], AF.Copy, [pk], ["al"])
            xg, xgk = xsm.next(); mix(5, xg, xgk)
            for fc in range(8):
                p, pk = pq.next()
                for c in range(8):
                    mm(P, p[:, 0:TT], wb[:, c, 3072 + fc * 128:3072 + (fc + 1) * 128], xg[:, c, :], c == 0, c == 7, ["win", xgk], [pk])
                go, gok = R_["go"].next()
                actf(P, go[:], p[:, 0:TT], AF.Silu, [pk], [gok])
                P.dma(gTd[fc * 128:(fc + 1) * 128, tsl], go[:], reads=[gok], writes=[(pre + "g", j)])
            for i in (0, 2, 3):
                mix(i, xmix[i], "xm%d" % i)
            if vres:
                p, pk = pq.next()
                for c in range(8):
                    mm(P, p[0:32, 0:TT], wb[:, c, 4224:4256], xmix[3][:, c, :], c == 0, c == 7, ["win", "xm3"], [pk])
                actf(P, vl[:], p[0:32, 0:TT], AF.Copy, [pk], ["vl"])
            for fc in range(8):
                fs = slice(fc * 128, (fc + 1) * 128)
                col = lambda t_: t_[:, fc:fc + 1]
                pr_, prk = pq.next(); pk_, pkk = pq.next(); pv_, pvk = pq.next(); pw_, pwk = pq.next()
                for c in range(8):
                    mm(P, pr_[:, 0:TT], wb[:, c, fc * 128:(fc + 1) * 128], xmix[0][:, c, :], c == 0, c == 7, ["win", "xm0"], [prk])
                for c in range(8):
                    mm(P, pk_[:, 0:TT], wb[:, c, D + fc * 128:D + (fc + 1) * 128], xmix[2][:, c, :], c == 0, c == 7, ["win", "xm2"], [pkk])
                for c in range(8):
                    mm(P, pv_[:, 0:TT], wb[:, c, 2 * D + fc * 128:2 * D + (fc + 1) * 128], xmix[3][:, c, :], c == 0, c == 7, ["win", "xm3"], [pvk])
                mm(P, pw_[:, 0:TT], wup[:, fs], wl[:], True, True, ["w_up", "wl"], [pwk])
                mm(P, pw_[:, TT:2 * TT], aup[:, fs], al[:], True, True, ["a_up", "al"], [pwk])
                pr = pr_[:, 0:TT]; pkp = pk_[:, 0:TT]; pv = pv_[:, 0:TT]; pw2 = pw_[:, 0:TT]; pa2 = pw_[:, TT:2 * TT]
                sg, sgk = R_["sg"].next()
                actf(P, sg[:], pw2, AF.Sigmoid, [pwk, "w0"], [sgk], bias=col(w0))
                lw, lwk = R_["lw"].next()
                ts(P, "dve", lw[:], sg[:], NEG_EXP_HALF, None, ALU.mult, None, [sgk], [lwk])
                cum, cumk = R_["cum"].next()
                P.op("dve", lambda e, cum=cum, lw=lw: e.tensor_tensor_scan(
                    out=cum[:], data0=C.mask0[:, 0:TT], data1=lw[:], initial=0.0, op0=ALU.mult, op1=ALU.add),
                    reads=[lwk, "mask0"], writes=[cumk])
                gi, gik = R_["gi"].next()
                actf(P, gi[:], cum[:], AF.Exp, [cumk], [gik])
                ge, gek = R_["ge"].next()
                tt(P, "pool", ge[:], cum[:], lw[:], ALU.subtract, [cumk, lwk], [gek])
                actf(P, ge[:], ge[:], AF.Exp, [gek], [gek])
                gv, gvk = R_["gv"].next()
                actf(P, gv[:], cum[:], AF.Exp, [cumk], [gvk], scale=-1.0)
                gd, gdk = R_["gd"].next()
                cum3 = cum[:].rearrange("p (c t) -> p c t", t=64)
                tt(P, "pool", gd[:].rearrange("p (c t) -> p c t", t=64), cum3[:, :, 63:64].broadcast_to([128, TT // 64, 64]), cum3,
                   ALU.subtract, [cumk], [gdk])
                actf(P, gd[:], gd[:], AF.Exp, [gdk], [gdk])
                gCt, gCk = gCr.next()
                actf(P, gCt[:].unsqueeze(2), cum3[:, :, 63:64], AF.Exp, [cumk], [gCk])
                P.dma(gCd[fs, j * (TT // 64):(j + 1) * (TT // 64)], gCt[:], reads=[gCk], writes=[(pre + "gC", j)])
                aa, aak = R_["aa"].next()
                actf(P, aa[:], pa2, AF.Sigmoid, [pwk, "a0"], [aak], bias=col(a0))
                kkr, kkrk = R_["kkr"].next()
                ts(P, "dve", kkr[:], pkp, col(k_k), None, ALU.mult, None, [pkk, "k_k"], [kkrk])
                sqb, sqbk = R_["sqb"].next()
                actf(P, sqb[:], kkr[:], AF.Square, [kkrk], [sqbk])
                pn, pnk = pq.next()
                mm(P, pn[:, 0:TT], C.bonesb[:], sqb[:], True, True, ["bonesb", sqbk], [pnk])
                nrm, nrmk = R_["nrm"].next()
                actf(P, nrm[:], pn[:, 0:TT], AF.Sqrt, [pnk], [nrmk])
                ts(P, "dve", nrm[:], nrm[:], 1e-12, None, ALU.max, None, [nrmk], [nrmk])
                P.op("dve", lambda e, nrm=nrm: e.reciprocal(out=nrm[:], in_=nrm[:]), reads=[nrmk], writes=[nrmk])
                kk, kkk = R_["kk"].next()
                tt(P, "dve", kk[:], kkr[:], nrm[:], ALU.mult, [kkrk, nrmk], [kkk])
                t1, t1k = R_["t1"].next()
                ts(P, "dve", t1[:], aa[:], col(k_a), col(omka), ALU.mult, ALU.add, [aak, "k_a", "omka"], [t1k])
                kf, kfk = R_["kf"].next()
                tt(P, "dve", kf[:], pkp, t1[:], ALU.mult, [pkk, t1k], [kfk])
                vv, vvk = R_["vv"].next()
                if not vres:
                    actf(P, vv[:], pv, AF.Copy, [pvk], [vvk])
                    if L == 0:
                        P.dma(vfd[fs, tsl], vv[:], reads=[vvk], writes=[("vfirst", j)])
                else:
                    vf, vfk = R_["vf"].next()
                    P.dma(vf[:], vfd[fs, tsl], reads=[("vfirst", j)], writes=[vfk])
                    mm(P, pn[:, TT:2 * TT], vup[:, fs], vl[:], True, True, ["v_up", "vl"], [pnk])
                    sv, svk = R_["sv"].next()
                    actf(P, sv[:], pn[:, TT:2 * TT], AF.Sigmoid, [pnk, "v0"], [svk], bias=col(v0))
                    d1, d1k = R_["d1"].next()
                    tt(P, "dve", d1[:], vf[:], pv, ALU.subtract, [vfk, pvk], [d1k])
                    tt(P, "pool", d1[:], d1[:], sv[:], ALU.mult, [d1k, svk], [d1k])
                    tt(P, "dve", vv[:], d1[:], pv, ALU.add, [d1k, pvk], [vvk])
                vb, vbk = R_["vb"].next()
                P.op("pool", lambda e, vb=vb, vv=vv: e.tensor_copy(out=vb[:], in_=vv[:]), reads=[vvk], writes=[vbk])
                P.dma(Dm["V"][fs, tsl], vb[:], reads=[vbk], writes=[(pre + "V", j)])
                At, Atk = R_["At"].next()
                stt(P, At[:], kk[:], -1.0, ge[:], ALU.mult, ALU.mult, [kkk, gek], [Atk])
                P.dma(Dm["At"][fs, tsl], At[:], reads=[Atk], writes=[(pre + "At", j)])
                ka, kak = R_["ka"].next()
                tt(P, "pool", ka[:], kk[:], aa[:], ALU.mult, [kkk, aak], [kak])
                Bt, Btk = R_["Bt"].next()
                tt(P, "dve", Bt[:], ka[:], gv[:], ALU.mult, [kak, gvk], [Btk])
                P.dma(Dm["Bt"][fs, tsl], Bt[:], reads=[Btk], writes=[(pre + "Bt", j)])
                Bg, Bgk = R_["Bg"].next()
                tt(P, "pool", Bg[:], ka[:], gd[:], ALU.mult, [kak, gdk], [Bgk])
                P.dma(Dm["Bg"][fs, tsl], Bg[:], reads=[Bgk], writes=[(pre + "Bg", j)])
                Kt, Ktk = R_["Kt"].next()
                tt(P, "pool", Kt[:], kf[:], gv[:], ALU.mult, [kfk, gvk], [Ktk])
                P.dma(Dm["Kt"][fs, tsl], Kt[:], reads=[Ktk], writes=[(pre + "Kt", j)])
                Kg, Kgk = R_["Kg"].next()
                tt(P, "pool", Kg[:], kf[:], gd[:], ALU.mult, [kfk, gdk], [Kgk])
                P.dma(Dm["Kg"][fs, tsl], Kg[:], reads=[Kgk], writes=[(pre + "Kg", j)])
                Rt, Rtk = R_["Rt"].next()
                tt(P, "dve", Rt[:], pr, gi[:], ALU.mult, [prk, gik], [Rtk])
                P.dma(Dm["Rt"][fs, tsl], Rt[:], reads=[Rtk], writes=[(pre + "Rt", j)])
                rkb, rkbk = R_["rkb"].next()
                stt(P, rkb[:], pr, col(r_k), kf[:], ALU.mult, ALU.mult, [prk, "r_k", kfk], [rkbk])
                mm(P, pn[:, 0:TT], C.bonesb[:], rkb[:], True, True, ["bonesb", rkbk], [pnk])
                BON, BONk = R_["BON"].next()
                tt(P, "dve", BON[:], pn[:, 0:TT], vv[:], ALU.mult, [pnk, vvk], [BONk])
                P.dma(Dm["BON"][fs, tsl], BON[:], reads=[BONk], writes=[(pre + "BON", j)])

    import os
    if os.environ.get("RW_STOP") == "A":
        return
    with P.phase():
        TB = 128
        NCB = TB // 64
        opn = ["At", "Bt", "Kt", "Rt", "Bg", "Kg", "V"]
        opr = {n: Ring(P, "o" + n, [128, 8, TB], BF16, 2) for n in opn}
        gCr2 = Ring(P, "gC2", [128, 8, NCB], F32, 2)
        ST = P.sb("ST", [128, 8, 64], F32)
        STb = P.sb("STb", [128, 8, 64], BF16)
        P.op("pool", lambda e: e.memset(ST[:], 0.0), writes=["ST"])
        P.op("pool", lambda e: e.memset(STb[:], 0.0), writes=["STb"])
        YT = Ring(P, "YT", [128, 8, TB], F32, 2)
        bk = [P.ps("rwbk%d" % i, [128, 512], F32) for i in range(7)]
        bkk = ["rwbank%d_%d" % (L, i) for i in range(7)]

        class RingOf:
            def __init__(self, ids):
                self.ids = ids
                self.i = -1

            def next(self):
                self.i = (self.i + 1) % len(self.ids)
                j = self.ids[self.i]
                return bk[j], bkk[j]

        pA5 = RingOf([0, 1, 2])
        pZ = RingOf([0, 1, 2])
        pW = RingOf([2, 3])
        pU = RingOf([3]); pUO = RingOf([4]); pY2 = RingOf([5]); pS = RingOf([6])
        pT = Ring(P, "pT", [128, 1024], BF16, 1, psum=True)
        tmr = {n: Ring(P, "tm" + n, [64, D], BF16, 2) for n in ["At", "V", "Bg", "Kg"]}
        A5r = Ring(P, "A5s", [64, 192], BF16, 20)
        Zall = [P.sb("Zall%d" % i, [64, 16, 256], BF16) for i in range(2)]
        ZcAr = Ring(P, "ZcA", [64, 16, 64], BF16, 2)
        ZcPr = Ring(P, "ZcP", [64, 16, 64], F32, 2)
        ApT = Ring(P, "ApT", [128, 8, 64], BF16, 2)
        Ur = Ring(P, "U", [64, 16, 64], BF16, 2)
        Y1r = Ring(P, "Y1s", [64, 512], F32, 2)
        Ysr = Ring(P, "Ys", [64, D], BF16, 2)
        tmpS = Ring(P, "tmpS", [128, 4, 64], F32, 2)
        for jb in range(T // TB):
            tsl = slice(jb * TB, (jb + 1) * TB)
            jA = (jb * TB) // TT
            ot = {}
            for n in opn:
                t_, k_ = opr[n].next()
                P.dma(t_[:], Dm[n].rearrange("(c p) t -> p c t", p=128)[:, :, tsl], reads=[(pre + n, jA)], writes=[k_],
                      e=("act" if n in ("Bg", "Kg", "V") else "sp"))
                ot[n] = (t_, k_)
            gC2, gC2k = gCr2.next()
            P.dma(gC2[:], gCd.rearrange("(c p) k -> p c k", p=128)[:, :, jb * NCB:(jb + 1) * NCB], reads=[(pre + "gC", jA)], writes=[gC2k])
            yt, ytk = YT.next()
            for cq in range(NCB):
                cs = slice(cq * 64, (cq + 1) * 64)
                tm = {}
                for n in ["At", "V", "Bg", "Kg"]:
                    dst, dk = tmr[n].next()
                    src, sk = ot[n]
                    for hf in range(2):
                        p, pk = pT.next()
                        for q in range(4):
                            fc = hf * 4 + q
                            trp(P, p[0:64, q * 128:(q + 1) * 128], src[:, fc, cs], C.identb[:], [sk, "identb"], [pk])
                        actf(P, dst[:, hf * 512:(hf + 1) * 512], p[0:64, 0:512], AF.Copy, [pk], [(dk, hf)])
                    tm[n] = (dst, dk)
                ZcA, Zck = ZcAr.next()
                ZcP, _zp = ZcPr.next()
                apt, aptk = ApT.next()
                A5h = {}
                Z0 = Zall[0]
                zkey = lambda par, fc, part: ("Zall", L, par, fc, part)
                for h in range(16):
                    fc, e_ = h // 2, h % 2
                    rows = slice(e_ * 64, e_ * 64 + 64)
                    a5, a5k = pA5.next()
                    Kt_ = ot["Kt"][0][rows, fc, cs]; Bt_ = ot["Bt"][0][rows, fc, cs]
                    At_ = ot["At"][0][rows, fc, cs]; Rt_ = ot["Rt"][0][rows, fc, cs]
                    rk5 = [ot["Kt"][1], ot["Bt"][1], ot["At"][1], ot["Rt"][1]]
                    mm(P, a5[0:64, 0:64], Kt_, At_, True, True, rk5, [a5k])
                    mm(P, a5[0:64, 64:128], Kt_, Rt_, True, True, rk5, [a5k])
                    mm(P, a5[0:64, 128:192], Bt_, Rt_, True, True, rk5, [a5k])
                    mm(P, a5[0:64, 192:256], At_, Bt_, True, True, rk5, [a5k])
                    mm(P, a5[0:64, 256:320], Bt_, At_, True, True, rk5, [a5k])
                    a5s, a5sk = A5r.next()
                    tt(P, "dve", a5s[:], a5[0:64, 0:192], C.mask5[:, 0:192], ALU.mult, [a5k, "mask5"], [a5sk])
                    tt(P, "dve", Z0[:, h, 128:256], a5[0:64, 192:320], C.mask5[:, 192:320], ALU.mult, [a5k, "mask5"], [zkey(0, fc, "L")])
                    A5h[h] = (a5s, a5sk)
                P.op("pool", lambda e, Z0=Z0, src=tm["At"][0]: e.tensor_copy(out=Z0[:, :, 0:64], in_=src[:].rearrange("p (h d) -> p h d", d=64)),
                     reads=[(tm["At"][1], 0), (tm["At"][1], 1)], writes=[zkey(0, fc, "X0") for fc in range(8)])
                for hf in range(2):
                    pw_, pwk = pW.next()
                    for hh in range(8):
                        h = hf * 8 + hh
                        a5s, a5sk = A5h[h]
                        mm(P, pw_[0:64, hh * 64:(hh + 1) * 64], a5s[:, 0:64], tm["V"][0][:, h * 64:(h + 1) * 64], True, True,
                           [a5sk, (tm["V"][1], hf)], [pwk])
                    actf(P, Z0[:, hf * 8:(hf + 1) * 8, 64:128], pw_[0:64, :].rearrange("p (h v) -> p h v", v=64), AF.Copy, [pwk],
                         [zkey(0, fc, "X1") for fc in range(hf * 4, hf * 4 + 4)])
                for lev in range(6):
                    par = lev % 2
                    Zc_, Zn_ = Zall[par], Zall[1 - par]
                    for fc in range(8):
                        pz, pzk = pZ.next()
                        rdk = [zkey(par, fc, "X0"), zkey(par, fc, "X1"), zkey(par, fc, "L")]
                        for e_ in range(2):
                            h = 2 * fc + e_
                            lt_ap = Zc_[:, h, 192:256]
                            for c0 in range(0, 192 if lev < 5 else 128, 64):
                                mm(P, pz[0:64, e_ * 256 + c0:e_ * 256 + c0 + 64], lt_ap, Zc_[:, h, c0:c0 + 64], True, True, rdk, [pzk])
                            if lev < 5:
                                mm(P, pz[0:64, e_ * 256 + 192:e_ * 256 + 256], Zc_[:, h, 128:192], lt_ap, True, True, rdk, [pzk])
                        pz3 = pz[0:64, :].rearrange("p (e c) -> p e c", c=256)
                        hs = slice(2 * fc, 2 * fc + 2)
                        if lev < 5:
                            tt(P, "dve", Zn_[:, hs, 0:128], pz3[:, :, 0:128], Zc_[:, hs, 0:128], ALU.add, [pzk] + rdk,
                               [zkey(1 - par, fc, "X0"), zkey(1 - par, fc, "X1")])
                            P.op("dve", lambda e, Zn_=Zn_, pz3=pz3, hs=hs: e.tensor_copy(out=Zn_[:, hs, 128:256], in_=pz3[:, :, 128:256]),
                                 reads=[pzk], writes=[zkey(1 - par, fc, "L")])
                        else:
                            tt(P, "dve", ZcA[:, hs, :], pz3[:, :, 0:64], Zc_[:, hs, 0:64], ALU.add, [pzk] + rdk, [(Zck, fc)])
                            tt(P, "dve", ZcP[:, hs, :], pz3[:, :, 64:128], Zc_[:, hs, 64:128], ALU.add, [pzk] + rdk, [(Zck, fc)])
                p, pk = pT.next()
                for fc in range(8):
                    trp(P, p[:, fc * 64:(fc + 1) * 64], ZcA[:, 2 * fc:2 * fc + 2, :].rearrange("p e d -> p (e d)"), C.identb[0:64, 0:64],
                        [(Zck, fc), "identb"], [pk])
                actf(P, apt[:].rearrange("p c t -> p (c t)"), p[:, 0:512], AF.Copy, [pk], [aptk])
                ys, ysk = Ysr.next()
                U, Uk = Ur.next()
                for hf in range(2):
                    bE, bEk = pU.next()
                    bO, bOk = pUO.next()
                    banks = [(bE, bEk), (bO, bOk)]
                    for hh in range(8):
                        h = hf * 8 + hh
                        fc, e_ = h // 2, h % 2
                        q = hh // 2
                        rows = slice(e_ * 64, e_ * 64 + 64)
                        bank, bk_ = banks[e_]
                        mm(P, bank[0:64, q * 64:(q + 1) * 64], apt[rows, fc, :], STb[rows, fc, :], True, True, [aptk, "STb"], [bk_])
                    for e_ in range(2):
                        bank, bk_ = banks[e_]
                        Uv = U[:, hf * 8:(hf + 1) * 8, :].rearrange("p (q e) v -> p q e v", e=2)[:, :, e_, :]
                        Pv = ZcP[:, hf * 8:(hf + 1) * 8, :].rearrange("p (q e) v -> p q e v", e=2)[:, :, e_, :]
                        tt(P, "dve", Uv, bank[0:64, 0:256].rearrange("p (q v) -> p q v", v=64), Pv, ALU.add,
                           [bk_] + [(Zck, fc) for fc in range(hf * 4, hf * 4 + 4)], [(Uk, hf)])
                    py2, py2k = pY2.next()
                    for hh in range(8):
                        h = hf * 8 + hh
                        fc, e_ = h // 2, h % 2
                        q = hh // 2
                        rows = slice(e_ * 64, e_ * 64 + 64)
                        bank, bk_ = banks[e_]
                        mm(P, bank[0:64, 256 + q * 64:256 + (q + 1) * 64], ot["Rt"][0][rows, fc, cs], STb[rows, fc, :], True, True,
                           [ot["Rt"][1], "STb"], [bk_])
                        a5s, a5sk = A5h[h]
                        mm(P, py2[0:64, hh * 64:(hh + 1) * 64], a5s[:, 128:192], U[:, h, :], True, False, [a5sk, (Uk, hf)], [py2k])
                        mm(P, py2[0:64, hh * 64:(hh + 1) * 64], a5s[:, 64:128], tm["V"][0][:, h * 64:(h + 1) * 64], False, True,
                           [a5sk, (tm["V"][1], hf)], [py2k])
                    y1s, y1sk = Y1r.next()
                    for e_ in range(2):
                        bank, bk_ = banks[e_]
                        actf(P, y1s[:].rearrange("p (q e v) -> p q e v", e=2, v=64)[:, :, e_, :],
                             bank[0:64, 256:512].rearrange("p (q v) -> p q v", v=64), AF.Copy, [bk_], [y1sk])
                    tt(P, "dve", ys[:, hf * 512:(hf + 1) * 512], py2[0:64, :], y1s[:], ALU.add, [py2k, y1sk], [(ysk, hf)])
                    ps_, psk = pS.next()
                    for hh in range(8):
                        h = hf * 8 + hh
                        fc, e_ = h // 2, h % 2
                        orow = slice(e_ * 64, e_ * 64 + 64)
                        ocol = slice((fc % 4) * 64, (fc % 4) * 64 + 64)
                        mm(P, ps_[orow, ocol], tm["Bg"][0][:, h * 64:(h + 1) * 64], U[:, h, :], True, False,
                           [(tm["Bg"][1], hf), (Uk, hf)], [psk])
                        mm(P, ps_[orow, ocol], tm["Kg"][0][:, h * 64:(h + 1) * 64], tm["V"][0][:, h * 64:(h + 1) * 64], False, True,
                           [(tm["Kg"][1], hf), (tm["V"][1], hf)], [psk])
                    tS, tSk = tmpS.next()
                    fsl = slice(hf * 4, hf * 4 + 4)
                    tt(P, "dve", tS[:], ST[:, fsl, :], gC2[:, fsl, cq:cq + 1].broadcast_to([128, 4, 64]), ALU.mult,
                       ["ST", gC2k], [tSk])
                    tt(P, "dve", ST[:, fsl, :], ps_[:, 0:256].rearrange("p (c v) -> p c v", v=64), tS[:], ALU.add, [psk, tSk], ["ST"])
                    actf(P, STb[:, fsl, :], ST[:, fsl, :], AF.Copy, ["ST"], ["STb"])
                for hf in range(2):
                    p, pk = pT.next()
                    for q in range(4):
                        fc = hf * 4 + q
                        trp(P, p[:, q * 64:(q + 1) * 64], ys[:, fc * 128:(fc + 1) * 128], C.identb[0:64, 0:64], [(ysk, hf), "identb"], [pk])
                    P.op("dve", lambda e, yt=yt, p=p, hf=hf, cs=cs: e.tensor_copy(
                        out=yt[:, hf * 4:(hf + 1) * 4, cs], in_=p[:, 0:256].rearrange("p (c t) -> p c t", t=64)), reads=[pk], writes=[ytk])
            P.dma(yTd.rearrange("(c p) t -> p c t", p=128)[:, :, tsl], yt[:], reads=[ytk], writes=[(pre + "yT", jb)])

    if os.environ.get("RW_STOP") == "B":
        return
    with P.phase(last=last):
        TC = 512
        stage = Ring(P, "wst", [128, 2048], F32, 2)
        wob = load_weight_bf16(P, "wout", W["w_out"], D, D, stage)
        gnw = colp("gn_w"); gnb = colp("gn_b")
        yr = Ring(P, "yin", [128, 8, TC], F32, 2)
        br = Ring(P, "bin", [128, 8, TC], F32, 2)
        gr = Ring(P, "gin", [128, 8, TC], BF16, 2)
        ygr = Ring(P, "ygo", [128, 8, TC], BF16, 2)
        pm = Ring(P, "pm", [128, 512], F32, 2, psum=True)
        pv_ = Ring(P, "pv", [128, 512], F32, 2, psum=True)
        po = Ring(P, "po", [128, 512], F32, 3, psum=True)
        ycr = Ring(P, "yc", [128, TC], F32, 2)
        sqr = Ring(P, "sq2", [128, TC], BF16, 2)
        rsr = Ring(P, "rs2", [128, TC], F32, 2)
        xr = Ring(P, "xr", [128, D], F32, 3)
        gneps = P.sb("gneps", [128, 1], F32)
        P.op("pool", lambda e: e.memset(gneps[:], RW_GN_EPS), writes=["gneps"])
        for j in range(T // TC):
            tsl = slice(j * TC, (j + 1) * TC)
            yi, yik = yr.next(); bi, bik = br.next(); gi_, gik_ = gr.next()
            P.dma(yi[:], yTd.rearrange("(c p) t -> p c t", p=128)[:, :, tsl], reads=[(pre + "yT", jj) for jj in range(4 * j, 4 * j + 4)], writes=[yik])
            P.dma(bi[:], Dm["BON"].rearrange("(c p) t -> p c t", p=128)[:, :, tsl], reads=[(pre + "BON", jj) for jj in (2 * j, 2 * j + 1)], writes=[bik], e="act")
            P.dma(gi_[:], gTd.rearrange("(c p) t -> p c t", p=128)[:, :, tsl], reads=[(pre + "g", jj) for jj in (2 * j, 2 * j + 1)], writes=[gik_])
            yg, ygk = ygr.next()
            for fc in range(8):
                m, mk = pm.next()
                for c0 in range(0, TC, 128):
                    mm(P, m[:, c0:c0 + 128], C.bonesf[:], yi[:, fc, c0:c0 + 128], True, True, ["bonesf", yik], [mk])
                yc, yck = ycr.next()
                stt(P, yc[:], m[:], -1.0 / 64, yi[:, fc, :], ALU.mult, ALU.add, [mk, yik], [yck])
                sq, sqk = sqr.next()
                actf(P, sq[:], yc[:], AF.Square, [yck], [sqk])
                v_, vk = pv_.next()
                mm(P, v_[:], C.bonesb[:], sq[:], True, True, ["bonesb", sqk], [vk])
                rs, rsk = rsr.next()
                actf(P, rs[:], v_[:], AF.Sqrt, [vk, "gneps"], [rsk], bias=gneps[:, 0:1], scale=1.0 / 64)
                P.op("dve", lambda e, rs=rs: e.reciprocal(out=rs[:], in_=rs[:]), reads=[rsk], writes=[rsk])
                tt(P, "pool", yc[:], yc[:], rs[:], ALU.mult, [yck, rsk], [yck])
                ts(P, "dve", yc[:], yc[:], gnw[:, fc:fc + 1], gnb[:, fc:fc + 1], ALU.mult, ALU.add, [yck, "gn_w", "gn_b"], [yck])
                tt(P, "pool", yc[:], yc[:], bi[:, fc, :], ALU.add, [yck, bik], [yck])
                tt(P, "dve", yg[:, fc, :], yc[:], gi_[:, fc, :], ALU.mult, [yck, gik_], [(ygk, fc)])
            for s in range(TC // 128):
                r0 = j * TC + s * 128
                xt, xk = xr.next()
                P.dma(xt[:], cur[r0:r0 + 128, :], reads=[("x", r0 // 128)], writes=[xk], e="act")
                for hf in range(2):
                    o, ok = po.next()
                    for c in range(8):
                        mm(P, o[:], yg[:, c, s * 128:(s + 1) * 128], wob[:, c, hf * 512:(hf + 1) * 512], c == 0, c == 7,
                           [(ygk, c), "wout"], [ok])
                    tt(P, "dve", xt[:, hf * 512:(hf + 1) * 512], o[:], xt[:, hf * 512:(hf + 1) * 512], ALU.add, [ok, xk], [xk])
                P.dma(y_out[r0:r0 + 128, :], xt[:], reads=[xk], writes=[("x", r0 // 128)], final=last)


LAYER_PARAMS = {
    0: ["norm", "mu", "w_in", "w_up", "w0", "a_up", "a0", "k_k", "k_a", "r_k", "gn_w", "gn_b", "w_out"],
    1: ["norm", "w_in", "conv_w", "conv_b", "dt_bias", "a_log", "d_skip", "gnorm_w", "w_out"],
    2: ["norm", "w_in", "q_norm", "k_norm", "w_out"],
    3: ["norm", "mu", "w_in", "w_up", "w0", "a_up", "a0", "k_k", "k_a", "r_k", "gn_w", "gn_b", "w_out", "v_up", "v0"],
}


def host_consts():
    c = {}
    c["c_ident"] = np.eye(128, dtype=np.float32)
    i = np.arange(128)
    c["c_bones"] = (i[:, None] // 64 == i[None, :] // 64).astype(np.float32)
    tq = np.arange(256)
    c["c_mask0"] = np.tile(((tq % 64) != 0).astype(np.float32)[None, :], (128, 1))
    ss_, tt_ = np.arange(64)[:, None], np.arange(64)[None, :]
    strict = (ss_ < tt_).astype(np.float32); incl = (ss_ <= tt_).astype(np.float32)
    c["c_mask5"] = np.concatenate([strict, incl, incl, strict.T, strict], axis=1)
    c["c_triu"] = (i[:, None] <= i[None, :]).astype(np.float32)
    c["c_ntril"] = -(i[:, None] >= i[None, :]).astype(np.float32)
    t = np.arange(512)
    c["c_mask"] = np.stack([((128 * r + i)[:, None] < t[None, :]).astype(np.float32) for r in range(4)], axis=1)
    return c


def host_layout(name, arr):
    a = np.asarray(arr, dtype=np.float32)
    short = name.split("_", 1)[1]
    if short == "norm":
        return np.ascontiguousarray(a.reshape(8, 128).T)
    if short in ("q_norm", "k_norm"):
        return np.ascontiguousarray(np.tile(a, 2).reshape(128, 1))
    if short in ("w0", "a0", "k_k", "k_a", "gn_w", "gn_b", "v0", "r_k"):
        return np.ascontiguousarray(a.reshape(8, 128).T)
    if short == "mu":
        return np.ascontiguousarray(a.reshape(6, 8, 128).transpose(2, 0, 1))
    if name == "l1_conv_w":
        return np.ascontiguousarray(a.T.reshape(32, 128, 4).transpose(1, 0, 2))
    if name == "l1_conv_b":
        return np.ascontiguousarray(a.reshape(32, 128).T)
    if name in ("l1_dt_bias", "l1_a_log"):
        return np.ascontiguousarray(np.tile(a[None, :], (128, 1)))
    if name == "l1_d_skip":
        return np.ascontiguousarray(np.repeat(a, 64).reshape(16, 128).T)
    if name == "l1_gnorm_w":
        return np.ascontiguousarray(a.reshape(16, 128).T)
    return np.ascontiguousarray(a)


def build_program(T, layers):
    nc = bass.Bass("TRN2", target_bir_lowering=False)
    x_in = nc.dram_tensor("x", [T, D], F32, kind="ExternalInput").ap()
    y_out = nc.dram_tensor("y", [T, D], F32, kind="ExternalOutput").ap()
    hc = host_consts()
    cap = {k: nc.dram_tensor(k, list(v.shape), F32, kind="ExternalInput").ap() for k, v in hc.items()}
    Wd = {}
    shapes = param_shapes()
    for L in layers:
        Wd[L] = {}
        for pn in LAYER_PARAMS[L]:
            full = "l%d_%s" % (L, pn)
            Wd[L][pn] = nc.dram_tensor(full, list(shapes[full]), F32, kind="ExternalInput").ap()
    P = Prog(nc)
    C = Ctx()
    specs = [("ident", "c_ident", [128, 128], None), ("bonesb", "c_bones", [128, 128], BF16),
             ("ntril", "c_ntril", [128, 128], BF16), ("maskb", "c_mask", [128, 4, 512], BF16),
             ("triu", "c_triu", [128, 128], None), ("identb", "c_ident", [128, 128], BF16),
             ("mask0", "c_mask0", [128, 256], None), ("mask5", "c_mask5", [64, 320], None),
             ("bonesf", "c_bones", [128, 128], None)]
    tiles = {}
    for key, cn, shape, cast in specs:
        tiles[key] = P.sb(key, shape, cast or F32)
        setattr(C, key, tiles[key])
    C.onesf = P.sb("onesf", [128, 128], F32)
    C.epsc = P.sb("epsc", [128, 1], F32)
    C.onec = P.sb("onec", [128, 1], F32)
    C.onesb = P.sb("onesb", [128, 128], BF16)
    with P.phase():
        for key, cn, shape, cast in specs:
            t = tiles[key]
            if cast is None:
                P.dma(t[:], cap[cn], writes=[key])
            else:
                st = P.sb(key + "_f", shape, F32)
                P.dma(st[:], cap[cn], writes=[key + "_f"])
                P.op("pool", lambda e, t=t, st=st: e.tensor_copy(out=t[:], in_=st[:]), reads=[key + "_f"], writes=[key])
        P.op("pool", lambda e: e.memset(C.onesf[:], 1.0), writes=["onesf"])
        P.op("pool", lambda e: e.memset(C.epsc[:], NORM_EPS), writes=["epsc"])
        P.op("pool", lambda e: e.memset(C.onec[:], 1.0), writes=["onec"])
        P.op("pool", lambda e: e.memset(C.onesb[:], 1.0), writes=["onesb"])
    cur = x_in
    fns = {0: None, 1: None, 2: layer_sb, 3: None}
    for i, L in enumerate(layers):
        last = (i == len(layers) - 1)
        FNS[L](P, C, L, T, Wd[L], cur, y_out, last)
        cur = y_out
    P.close()
    return nc


def param_shapes():
    s = {}
    for L, vres in ((0, False), (3, True)):
        p = "l%d_" % L
        ncols = 4 * 1024 + 64 + 64 + (32 if vres else 0)
        s.update({p + "norm": (128, 8), p + "mu": (128, 6, 8), p + "w_in": (1024, ncols), p + "w_up": (64, 1024),
                  p + "w0": (128, 8), p + "a_up": (64, 1024), p + "a0": (128, 8), p + "k_k": (128, 8), p + "k_a": (128, 8),
                  p + "r_k": (128, 8), p + "gn_w": (128, 8), p + "gn_b": (128, 8), p + "w_out": (1024, 1024)})
        if vres:
            s.update({p + "v_up": (32, 1024), p + "v0": (128, 8)})
    s.update({"l1_norm": (128, 8), "l1_w_in": (1024, 6176), "l1_conv_w": (128, 32, 4), "l1_conv_b": (128, 32), "l1_dt_bias": (128, 32),
              "l1_a_log": (128, 32), "l1_d_skip": (128, 16), "l1_gnorm_w": (128, 16), "l1_w_out": (2048, 1024)})
    s.update({"l2_norm": (128, 8), "l2_w_in": (1024, 4096), "l2_q_norm": (128, 1), "l2_k_norm": (128, 1), "l2_w_out": (1024, 1024)})
    return s


FNS = {0: layer_rwkv, 1: layer_ssd, 2: layer_sb, 3: layer_rwkv}


def make_in_maps(inputs, layers, ncores, T):
    hc = host_consts()
    shapes = param_shapes()
    shared = dict(hc)
    for L in layers:
        for pn in LAYER_PARAMS[L]:
            full = "l%d_%s" % (L, pn)
            a = host_layout(full, inputs[full])
            assert tuple(a.shape) == tuple(shapes[full]), (full, a.shape, shapes[full])
            shared[full] = a
    x = np.asarray(inputs["x"], dtype=np.float32)
    maps = []
    for b in range(ncores):
        m = dict(shared)
        m["x"] = np.ascontiguousarray(x[b, :T])
        maps.append(m)
    return maps


ALL_INPUT_NAMES = (
    "x",
    "l0_norm",
    "l0_mu",
    "l0_w_in",
    "l0_w_up",
    "l0_w0",
    "l0_a_up",
    "l0_a0",
    "l0_k_k",
    "l0_k_a",
    "l0_r_k",
    "l0_gn_w",
    "l0_gn_b",
    "l0_w_out",
    "l1_norm",
    "l1_w_in",
    "l1_conv_w",
    "l1_conv_b",
    "l1_dt_bias",
    "l1_a_log",
    "l1_d_skip",
    "l1_gnorm_w",
    "l1_w_out",
    "l2_norm",
    "l2_w_in",
    "l2_q_norm",
    "l2_k_norm",
    "l2_w_out",
    "l3_norm",
    "l3_mu",
    "l3_w_in",
    "l3_w_up",
    "l3_w0",
    "l3_a_up",
    "l3_a0",
    "l3_k_k",
    "l3_k_a",
    "l3_r_k",
    "l3_gn_w",
    "l3_gn_b",
    "l3_w_out",
    "l3_v_up",
    "l3_v0",
)


def kernel(**inputs):
    missing = [n for n in ALL_INPUT_NAMES if n not in inputs]
    assert not missing, missing
    T = 4096
    layers = [0, 1, 2, 3]
    nc = build_program(T, layers)
    maps = make_in_maps(inputs, layers, 8, T)
    res = run_bass_kernel_spmd(nc, maps, core_ids=list(range(8)))
    return np.stack([r["y"] for r in res.results], axis=0).astype(np.float32)
```

```python
import contextlib
import math
import numpy as np
import concourse.bass as bass
import concourse.mybir as mybir
from concourse.bass_utils import run_bass_kernel_spmd

F32 = mybir.dt.float32
BF16 = mybir.dt.bfloat16
AF = mybir.ActivationFunctionType
ALU = mybir.AluOpType
AX = mybir.AxisListType

D = 1024
NORM_EPS = 1e-6
N_DMA_SEMS = 24


class Prog:
    ENGS = ("pe", "act", "dve", "pool", "sp")

    def __init__(self, nc):
        self.nc = nc
        self.root = contextlib.ExitStack()
        self.stack = self.root
        self.ops = {e: [] for e in self.ENGS}
        self.cnt = {e: 0 for e in self.ENGS}
        self.esem = {e: self.root.enter_context(nc.semaphore("s_" + e)) for e in self.ENGS}
        self.dsem = [self.root.enter_context(nc.semaphore("d%d" % i)) for i in range(N_DMA_SEMS)]
        self.dcnt = [0] * N_DMA_SEMS
        self.dlast = [None] * N_DMA_SEMS
        self.ndma = 0
        self.lw = {}
        self.rd = {}
        self.waited = {e: {} for e in self.ENGS}
        self.sem_by_id = {}
        self.final = []
        self.n_inst = 0
        self.uid = 0

    def _sid(self, sem):
        self.sem_by_id[id(sem)] = sem
        return id(sem)

    def sb(self, name, shape, dt=F32, root=False):
        self.uid += 1
        return (self.root if root else self.stack).enter_context(self.nc.sbuf_tensor("%s_%d" % (name, self.uid), list(shape), dt))

    def ps(self, name, shape, dt=F32):
        self.uid += 1
        return self.stack.enter_context(self.nc.psum_tensor("%s_%d" % (name, self.uid), list(shape), dt))

    def _deps(self, e, reads, writes):
        deps = []
        for k in reads:
            t = self.lw.get(k)
            if t is not None:
                deps.append(t)
        for k in writes:
            t = self.lw.get(k)
            if t is not None:
                deps.append(t)
            deps.extend(self.rd.get(k, ()))
        waits = {}
        for (sid, val, src) in deps:
            if src == e and e == "pe":
                continue
            if self.waited[e].get(sid, 0) >= val:
                continue
            if waits.get(sid, 0) < val:
                waits[sid] = val
        for sid, val in waits.items():
            self.waited[e][sid] = val
        return [(self.sem_by_id[sid], val) for sid, val in waits.items()]

    def _record(self, tok, reads, writes):
        for k in reads:
            lst = self.rd.setdefault(k, [])
            lst[:] = [t for t in lst if t[0] != tok[0]]
            lst.append(tok)
        for k in writes:
            self.lw[k] = tok
            self.rd[k] = []

    def op(self, e, fn, reads=(), writes=()):
        waits = self._deps(e, reads, writes)
        self.cnt[e] += 1
        sem = self.esem[e]
        tok = (self._sid(sem), self.cnt[e], e)
        self.ops[e].append((waits, fn, (sem, 1)))
        self._record(tok, reads, writes)
        self.n_inst += 1
        return tok

    def dma(self, out, in_, reads=(), writes=(), e="sp", final=False):
        i = self.ndma % N_DMA_SEMS
        self.ndma += 1
        sem = self.dsem[i]
        waits = self._deps(e, reads, writes)
        prev = self.dlast[i]
        if prev is not None and self.waited[e].get(prev[0], 0) < prev[1]:
            waits.append((sem, prev[1]))
            self.waited[e][prev[0]] = prev[1]
        self.dcnt[i] += 16
        tok = (self._sid(sem), self.dcnt[i], "dma")
        self.dlast[i] = tok
        self.ops[e].append((waits, lambda eng: eng.dma_start(out=out, in_=in_), (sem, 16)))
        self._record(tok, reads, writes)
        if final:
            self.final.append(tok)
        self.n_inst += 1
        return tok

    @contextlib.contextmanager
    def phase(self, last=False):
        outer = self.stack
        with contextlib.ExitStack() as st:
            self.stack = st
            yield
            self._emit(last)
        self.stack = outer

    def _emit(self, last):
        nc = self.nc
        fin = [(self.dsem[i], self.dcnt[i]) for i in range(N_DMA_SEMS) if self.dcnt[i] > 0]
        for i in range(N_DMA_SEMS):
            if self.dcnt[i] > 0:
                self.waited["sp"][id(self.dsem[i])] = self.dcnt[i]
        ops = self.ops
        with nc.Block() as block:
            def body(ename):
                def f(eng):
                    for waits, fn, (sem, inc) in ops[ename]:
                        for (s, v) in waits:
                            eng.wait_ge(s, v)
                        fn(eng).then_inc(sem, inc)
                    if ename == "sp":
                        for (s, v) in fin:
                            eng.wait_ge(s, v)
                return f
            block.sync(body("sp"))
            block.scalar(body("act"))
            block.vector(body("dve"))
            block.gpsimd(body("pool"))
            block.tensor(body("pe"))
        self.ops = {e: [] for e in self.ENGS}

    def close(self):
        self.root.close()


class Ring:
    def __init__(self, P, name, shape, dt=F32, n=2, psum=False):
        mk = P.ps if psum else P.sb
        self.t = [mk("%s%d" % (name, i), shape, dt) for i in range(n)]
        self.k = ["%s#%d#%d" % (name, P.uid, i) for i in range(n)]
        self.i = -1
        self.n = n

    def next(self):
        self.i = (self.i + 1) % self.n
        return self.t[self.i], self.k[self.i]


class Ctx:
    pass


def load_const(P, key, ap_dram, shape, cast=None):
    if cast is None:
        t = P.sb(key, shape, F32, root=True)
        P.dma(t[:], ap_dram, writes=[key])
        return t
    t = P.sb(key + "_f", shape, F32)
    P.dma(t[:], ap_dram, writes=[key + "_f"])
    tb = P.sb(key, shape, cast, root=True)
    P.op("pool", lambda e: e.tensor_copy(out=tb[:], in_=t[:]), reads=[key + "_f"], writes=[key])
    return tb


def load_weight_bf16(P, name, w_dram, kdim, ncols, stage):
    nck = kdim // 128
    wb = P.sb(name, [128, nck, ncols], BF16)
    wv = w_dram.rearrange("(c p) n -> p c n", p=128)
    CW = stage.t[0].shape[1]
    for c in range(nck):
        for c0 in range(0, ncols, CW):
            cw = min(CW, ncols - c0)
            st, sk = stage.next()
            P.dma(st[:, 0:cw], wv[:, c, c0:c0 + cw], writes=[sk])
            P.op("pool", lambda e, st=st, c=c, c0=c0, cw=cw: e.tensor_copy(out=wb[:, c, c0:c0 + cw], in_=st[:, 0:cw]),
                 reads=[sk], writes=[name])
    return wb


def make_hT(P, C, j, src, TT, rings, gcol, eps=NORM_EPS):
    hT, hk = rings["hT"].next()
    for s in range(TT // 128):
        r0 = j * TT + s * 128
        xt, xk = rings["xt"].next()
        P.dma(xt[:], src[r0:r0 + 128, :], reads=[("x", r0 // 128)], writes=[xk])
        sq, sqk = rings["sq"].next()
        ss, ssk = rings["ss"].next()
        P.op("act", lambda e, xt=xt, sq=sq, ss=ss: e.activation(out=sq[:], in_=xt[:], func=AF.Square, accum_out=ss[:, 0:1]),
             reads=[xk], writes=[sqk, ssk])
        P.op("act", lambda e, ss=ss: e.activation(out=ss[:, 1:2], in_=ss[:, 0:1], func=AF.Sqrt, scale=1.0 / D, bias=C.epsc[:, 0:1]),
             reads=[ssk], writes=[ssk])
        P.op("dve", lambda e, ss=ss: e.reciprocal(out=ss[:, 2:3], in_=ss[:, 1:2]), reads=[ssk], writes=[ssk])
        P.op("dve", lambda e, xt=xt, sq=sq, ss=ss: e.tensor_scalar(out=sq[:], in0=xt[:], scalar1=ss[:, 2:3], scalar2=None, op0=ALU.mult),
             reads=[xk, ssk], writes=[sqk])
        for hf in range(2):
            pT, pk = rings["pT"].next()
            for c4 in range(4):
                c = hf * 4 + c4
                P.op("pe", lambda e, pT=pT, sq=sq, c=c, c4=c4: e.transpose(pT[:, c4 * 128:(c4 + 1) * 128], sq[:, c * 128:(c + 1) * 128], C.ident[:]),
                     reads=[sqk, "ident"], writes=[pk])
            for c4 in range(4):
                c = hf * 4 + c4
                P.op("act", lambda e, pT=pT, hT=hT, c=c, c4=c4, s=s: e.activation(
                    out=hT[:, c, 1 + s * 128:1 + (s + 1) * 128], in_=pT[:, c4 * 128:(c4 + 1) * 128], func=AF.Copy, scale=gcol[:, c:c + 1]),
                    reads=[pk, "gcol"], writes=[hk])
    return hT, hk


def out_proj(P, C, L, T, ygT, kdim, w_out, cur, y_out, last):
    nck = kdim // 128
    with P.phase(last=last):
        stage = Ring(P, "wst", [128, 2048], F32, 2)
        wb = load_weight_bf16(P, "wout", w_out, kdim, D, stage)
        ygr = Ring(P, "ygr", [128, nck, 128], BF16, 3)
        xr = Ring(P, "xr", [128, D], F32, 3)
        pr = Ring(P, "po", [128, 512], F32, 4, psum=True)
        ygv = ygT.rearrange("(c p) t -> p c t", p=128)
        for s in range(T // 128):
            yg, ygk = ygr.next()
            P.dma(yg[:], ygv[:, :, s * 128:(s + 1) * 128], reads=[("yg", L, s // 4)], writes=[ygk])
            xt, xk = xr.next()
            P.dma(xt[:], cur[s * 128:(s + 1) * 128, :], reads=[("x", s)], writes=[xk], e="act")
            for hf in range(2):
                po, pk = pr.next()
                for c in range(nck):
                    P.op("pe", lambda e, po=po, yg=yg, c=c, hf=hf: e.matmul(
                        po[:], lhsT=yg[:, c, :], rhs=wb[:, c, hf * 512:(hf + 1) * 512], start=(c == 0), stop=(c == nck - 1)),
                        reads=[ygk, "wout"], writes=[pk])
                P.op("dve", lambda e, po=po, xt=xt, hf=hf: e.tensor_tensor(
                    out=xt[:, hf * 512:(hf + 1) * 512], in0=po[:], in1=xt[:, hf * 512:(hf + 1) * 512], op=ALU.add),
                    reads=[pk, xk], writes=[xk])
            P.dma(y_out[s * 128:(s + 1) * 128, :], xt[:], reads=[xk], writes=[("x", s)], final=last)


def in_rings(P, TT, nh=2, nx=3, nq=2, npt=2):
    return {
        "hT": Ring(P, "hT", [128, 8, 1 + TT], F32, nh),
        "xt": Ring(P, "xt", [128, D], F32, nx),
        "sq": Ring(P, "sq", [128, D], F32, nq),
        "ss": Ring(P, "ss", [128, 4], F32, 4),
        "pT": Ring(P, "pT", [128, 512], F32, npt, psum=True),
    }


def layer_sb(P, C, L, T, W, cur, y_out, last):
    nc = P.nc
    TT = 512
    NT = T // TT
    H = 16
    qT = nc.dram_tensor("sb_qT", [D, T], BF16, kind="Internal").ap()
    kT = nc.dram_tensor("sb_kT", [D, T], BF16, kind="Internal").ap()
    gT = nc.dram_tensor("sb_gT", [D, T], BF16, kind="Internal").ap()
    vTM = nc.dram_tensor("sb_v", [T, D], BF16, kind="Internal").ap()
    ygT = nc.dram_tensor("sb_ygT", [D, T], BF16, kind="Internal").ap()

    with P.phase():
        stage = Ring(P, "wst", [128, 2048], F32, 2)
        wb = load_weight_bf16(P, "win", W["w_in"], D, 4 * D, stage)
        gcol = P.sb("gcol", [128, 8], F32)
        P.dma(gcol[:], W["norm"], writes=["gcol"])
        qn = P.sb("qn", [128, 2], F32)
        P.dma(qn[:, 0:1], W["q_norm"], writes=["qn"])
        P.dma(qn[:, 1:2], W["k_norm"], writes=["qn"])
        P.op("dve", lambda e: e.tensor_scalar(out=qn[:, 0:1], in0=qn[:, 0:1], scalar1=0.125, scalar2=None, op0=ALU.mult),
             reads=["qn"], writes=["qn"])
        rings = in_rings(P, TT)
        hbr = Ring(P, "hb", [128, 8, TT], BF16, 2)
        pp = Ring(P, "pp", [128, 512], F32, 3, psum=True)
        pm = Ring(P, "pm", [128, 512], F32, 2, psum=True)
        sqb = Ring(P, "sqb", [128, 512], BF16, 2)
        rs = Ring(P, "rs", [128, 512], F32, 2)
        ob = Ring(P, "ob", [128, 512], BF16, 4)
        for j in range(NT):
            hT, hk = make_hT(P, C, j, cur, TT, rings, gcol)
            hb, hbk = hbr.next()
            for c in range(8):
                P.op("pool", lambda e, hb=hb, hT=hT, c=c: e.tensor_copy(out=hb[:, c, :], in_=hT[:, c, 1:1 + TT]),
                     reads=[hk], writes=[hbk])
            tsl = slice(j * TT, (j + 1) * TT)
            for which in range(3):
                cbase = {0: 0, 1: D, 2: 3 * D}[which]
                for fc in range(8):
                    p, pk = pp.next()
                    for c in range(8):
                        P.op("pe", lambda e, p=p, c=c, hb=hb, col=cbase + fc * 128: e.matmul(
                            p[:], lhsT=wb[:, c, col:col + 128], rhs=hb[:, c, :], start=(c == 0), stop=(c == 7)),
                            reads=["win", hbk], writes=[pk])
                    o, ok = ob.next()
                    if which == 2:
                        P.op("act", lambda e, p=p, o=o: e.activation(out=o[:], in_=p[:], func=AF.Silu), reads=[pk], writes=[ok])
                        P.dma(gT[fc * 128:(fc + 1) * 128, tsl], o[:], reads=[ok], writes=[("sbg", j)])
                    else:
                        s2, s2k = sqb.next()
                        P.op("act", lambda e, p=p, s2=s2: e.activation(out=s2[:], in_=p[:], func=AF.Square), reads=[pk], writes=[s2k])
                        m, mk = pm.next()
                        P.op("pe", lambda e, m=m, s2=s2: e.matmul(m[:], lhsT=C.bonesb[:], rhs=s2[:], start=True, stop=True),
                             reads=["bonesb", s2k], writes=[mk])
                        r, rk = rs.next()
                        P.op("act", lambda e, m=m, r=r: e.activation(out=r[:], in_=m[:], func=AF.Sqrt, scale=1.0 / 64, bias=C.epsc[:, 0:1]),
                             reads=[mk], writes=[rk])
                        P.op("dve", lambda e, r=r: e.reciprocal(out=r[:], in_=r[:]), reads=[rk], writes=[rk])
                        P.op("dve", lambda e, p=p, r=r, o=o, which=which: e.scalar_tensor_tensor(
                            out=o[:], in0=p[:], scalar=qn[:, which:which + 1], in1=r[:], op0=ALU.mult, op1=ALU.mult),
                            reads=[pk, rk, "qn"], writes=[ok])
                        dst = qT if which == 0 else kT
                        P.dma(dst[fc * 128:(fc + 1) * 128, tsl], o[:], reads=[ok], writes=[("sbq" if which == 0 else "sbk", j)])
            for s in range(TT // 128):
                for cb in range(2):
                    p, pk = pp.next()
                    for c in range(8):
                        P.op("pe", lambda e, p=p, c=c, hb=hb, s=s, cb=cb: e.matmul(
                            p[:], lhsT=hb[:, c, s * 128:(s + 1) * 128], rhs=wb[:, c, 2 * D + cb * 512:2 * D + (cb + 1) * 512],
                            start=(c == 0), stop=(c == 7)), reads=["win", hbk], writes=[pk])
                    o, ok = ob.next()
                    P.op("act", lambda e, p=p, o=o: e.copy(out=o[:], in_=p[:]), reads=[pk], writes=[ok])
                    r0 = j * TT + s * 128
                    P.dma(vTM[r0:r0 + 128, cb * 512:(cb + 1) * 512], o[:], reads=[ok], writes=[("sbv", j)])

    with P.phase():
        NB = T // 128
        kh_r = Ring(P, "kh", [64, T], BF16, 2)
        vh_r = Ring(P, "vh", [128, NB, 64], BF16, 2)
        qh_r = Ring(P, "qh", [64, TT], BF16, 2)
        gh_r = Ring(P, "gh", [64, TT], BF16, 2)
        pz = Ring(P, "pz", [128, 512], F32, 2, psum=True)
        pb = Ring(P, "pb", [128, 512], F32, 2, psum=True)
        pc = Ring(P, "pc", [64, 512], F32, 2, psum=True)
        po = Ring(P, "pov", [64, 512], F32, 2, psum=True)
        Er = Ring(P, "E", [128, 512], F32, 2)
        spr = Ring(P, "sp", [128, 512], BF16, 3)
        Pr = Ring(P, "Pm", [128, 512], BF16, 3)
        fr = Ring(P, "f", [64, 512], F32, 2)
        accr = Ring(P, "acc", [64, 512], F32, 2)
        ygr = Ring(P, "yg", [64, 512], BF16, 2)
        vview = vTM.rearrange("(b p) d -> p b d", p=128)
        qh_r = Ring(P, "qh3", [64, TT], BF16, 3)
        gh_r = Ring(P, "gh3", [64, TT], BF16, 3)
        spr = Ring(P, "sp4", [128, 512], BF16, 4)
        Pr = Ring(P, "Pm4", [128, 512], BF16, 4)
        units = [(h, tq) for h in range(H) for tq in range(NT)]
        ures = {}

        def load_unit(u):
            h, tq = units[u]
            r = {}
            if tq == 0:
                kh, khk = kh_r.next()
                P.dma(kh[:], kT[h * 64:(h + 1) * 64, :], reads=[("sbk", j) for j in range(NT)], writes=[khk])
                vh, vhk = vh_r.next()
                P.dma(vh[:], vview[:, :, h * 64:(h + 1) * 64], reads=[("sbv", j) for j in range(NT)], writes=[vhk], e="act")
                ures[("kv", h)] = (kh, khk, vh, vhk)
            tsl = slice(tq * TT, (tq + 1) * TT)
            r["qh"], r["qhk"] = qh_r.next()
            P.dma(r["qh"][:], qT[h * 64:(h + 1) * 64, tsl], reads=[("sbq", tq)], writes=[r["qhk"]])
            r["gh"], r["ghk"] = gh_r.next()
            P.dma(r["gh"][:], gT[h * 64:(h + 1) * 64, tsl], reads=[("sbg", tq)], writes=[r["ghk"]])
            r["acc"], r["acck"] = accr.next()
            ures[u] = r

        blocks = []
        for u, (h, tq) in enumerate(units):
            nkb = 4 * tq + 4
            for b_ in range(nkb):
                blocks.append((u, h, tq, b_, b_ - 4 * tq, b_ == nkb - 1))
        NBK = len(blocks)
        bres = {}
        load_unit(0)

        def S12(i):
            u, h, tq, b_, r, lastb = blocks[i]
            if b_ == 0 and u + 1 < len(units):
                load_unit(u + 1)
            U_ = ures[u]
            kh, khk, vh, vhk = ures[("kv", h)]
            R = {}
            z, zk = pz.next()
            mm(P, z[:], kh[:, b_ * 128:(b_ + 1) * 128], U_["qh"][:], True, True, [khk, U_["qhk"]], [zk])
            E, Ek = Er.next()
            actf(P, E[:], z[:], AF.Exp, [zk], [Ek])
            sp, spk = spr.next()
            actf(P, sp[:], E[:], AF.Ln, [Ek, "onec"], [spk], bias=C.onec[:, 0:1])
            if r >= 0:
                tt(P, "pool", sp[:], sp[:], C.maskb[:, r, :], ALU.mult, [spk, "maskb"], [spk])
            R["sp"], R["spk"] = sp, spk
            bres[i] = R

        def S34(i):
            u, h, tq, b_, r, lastb = blocks[i]
            U_ = ures[u]
            kh, khk, vh, vhk = ures[("kv", h)]
            R = bres[i]
            sp, spk = R["sp"], R["spk"]
            bb, bk = pb.next()
            mm(P, bb[:], C.ntril[:], sp[:], True, False, ["ntril", spk], [bk])
            mm(P, bb[:], kh[:, b_ * 128:(b_ + 1) * 128], U_["qh"][:], False, True, [khk, U_["qhk"]], [bk])
            cc, ck = pc.next()
            if b_ > 0:
                mm(P, cc[:], C.onesb[:, 0:64], sp[:], True, True, ["onesb", spk], [ck])
            Pm, Pk = Pr.next()
            actf(P, Pm[:], bb[:], AF.Exp, [bk], [Pk])
            if r >= 0:
                tt(P, "pool", Pm[:], Pm[:], C.maskb[:, r, :], ALU.mult, [Pk, "maskb"], [Pk])
            R["Pm"], R["Pk"] = Pm, Pk
            if b_ > 0:
                f, fk = fr.next()
                actf(P, f[:], cc[:], AF.Exp, [ck], [fk], scale=-1.0)
                R["f"], R["fk"] = f, fk

        def S56(i):
            u, h, tq, b_, r, lastb = blocks[i]
            U_ = ures[u]
            kh, khk, vh, vhk = ures[("kv", h)]
            R = bres.pop(i)
            acc, acck = U_["acc"], U_["acck"]
            ov, ovk = po.next()
            mm(P, ov[:], vh[:, b_, :], R["Pm"][:], True, True, [vhk, R["Pk"]], [ovk])
            if b_ == 0:
                P.op("dve", lambda e, acc=acc, ov=ov: e.tensor_copy(out=acc[:], in_=ov[:]), reads=[ovk], writes=[acck])
            else:
                tt(P, "dve", acc[:], acc[:], R["f"][:], ALU.mult, [acck, R["fk"]], [acck])
                tt(P, "dve", acc[:], ov[:], acc[:], ALU.add, [acck, ovk], [acck])
            if lastb:
                tsl = slice(tq * TT, (tq + 1) * TT)
                yg, ygk = ygr.next()
                tt(P, "dve", yg[:], acc[:], U_["gh"][:], ALU.mult, [acck, U_["ghk"]], [ygk])
                P.dma(ygT[h * 64:(h + 1) * 64, tsl], yg[:], reads=[ygk], writes=[("yg", L, tq)])
                del ures[u]

        for i in range(NBK + 2):
            if i < NBK:
                S12(i)
            if 0 <= i - 1 < NBK:
                S34(i - 1)
            if 0 <= i - 2 < NBK:
                S56(i - 2)

    out_proj(P, C, L, T, ygT, D, W["w_out"], cur, y_out, last)


def layer_ssd(P, C, L, T, W, cur, y_out, last):
    nc = P.nc
    TT = 512
    NT = T // TT
    NCH = T // 128
    xbcT = nc.dram_tensor("ssd_xbcT", [4096, T], BF16, kind="Internal").ap()
    zT = nc.dram_tensor("ssd_zT", [2048, T], BF16, kind="Internal").ap()
    dtD = nc.dram_tensor("ssd_dt", [T, 64], F32, kind="Internal").ap()
    ygT = nc.dram_tensor("ssd_ygT", [2048, T], BF16, kind="Internal").ap()

    with P.phase():
        stage = Ring(P, "wst", [128, 2048], F32, 2)
        wb = load_weight_bf16(P, "win", W["w_in"], D, 6176, stage)
        gcol = P.sb("gcol", [128, 8], F32)
        P.dma(gcol[:], W["norm"], writes=["gcol"])
        cw = P.sb("cw", [128, 32, 4], F32)
        P.dma(cw[:], W["conv_w"], writes=["cw"])
        cbias = P.sb("cbias", [128, 32], F32)
        P.dma(cbias[:], W["conv_b"], writes=["cbias"])
        dtb = P.sb("dtb", [128, 32], F32)
        P.dma(dtb[:], W["dt_bias"], writes=["dtb"])
        Arep = P.sb("Arep", [128, 32], F32)
        P.dma(Arep[:], W["a_log"], writes=["Arep"])
        P.op("act", lambda e: e.activation(out=Arep[:], in_=Arep[:], func=AF.Exp), reads=["Arep"], writes=["Arep"])
        P.op("dve", lambda e: e.tensor_scalar(out=Arep[:], in0=Arep[:], scalar1=-1.0, scalar2=None, op0=ALU.mult),
             reads=["Arep"], writes=["Arep"])
        hist = P.sb("hist", [128, 32, 3], F32)
        P.op("pool", lambda e: e.memset(hist[:], 0.0), writes=["hist"])
        rings = in_rings(P, TT, nh=1, nx=2)
        hbr = Ring(P, "hb", [128, 8, TT], BF16, 2)
        pp = Ring(P, "pp", [128, 512], F32, 3, psum=True)
        pd = Ring(P, "pd", [128, 64], F32, 2, psum=True)
        xcr = Ring(P, "xc", [128, 3 + TT], F32, 2)
        cvr = Ring(P, "cv", [128, TT], F32, 2)
        ob = Ring(P, "ob", [128, 512], BF16, 4)
        dtr = Ring(P, "dtr", [128, 64], F32, 2)
        for j in range(NT):
            hT, hk = make_hT(P, C, j, cur, TT, rings, gcol)
            hb, hbk = hbr.next()
            for c in range(8):
                P.op("pool", lambda e, hb=hb, hT=hT, c=c: e.tensor_copy(out=hb[:, c, :], in_=hT[:, c, 1:1 + TT]),
                     reads=[hk], writes=[hbk])
            tsl = slice(j * TT, (j + 1) * TT)
            for fc in range(48):
                p, pk = pp.next()
                for c in range(8):
                    P.op("pe", lambda e, p=p, c=c, hb=hb, col=fc * 128: e.matmul(
                        p[:], lhsT=wb[:, c, col:col + 128], rhs=hb[:, c, :], start=(c == 0), stop=(c == 7)),
                        reads=["win", hbk], writes=[pk])
                o, ok = ob.next()
                if fc < 16:
                    P.op("act", lambda e, p=p, o=o: e.activation(out=o[:], in_=p[:], func=AF.Silu), reads=[pk], writes=[ok])
                    P.dma(zT[fc * 128:(fc + 1) * 128, tsl], o[:], reads=[ok], writes=[("ssdz", j)])
                else:
                    cc = fc - 16
                    xc, xck = xcr.next()
                    P.op("pool", lambda e, xc=xc, cc=cc: e.tensor_copy(out=xc[:, 0:3], in_=hist[:, cc, :]), reads=["hist"], writes=[xck])
                    P.op("act", lambda e, xc=xc, p=p: e.copy(out=xc[:, 3:3 + TT], in_=p[:]), reads=[pk], writes=[xck])
                    P.op("pool", lambda e, xc=xc, cc=cc: e.tensor_copy(out=hist[:, cc, :], in_=xc[:, TT:TT + 3]), reads=[xck], writes=["hist"])
                    cv, cvk = cvr.next()
                    P.op("dve", lambda e, cv=cv, xc=xc, cc=cc: e.tensor_scalar(
                        out=cv[:], in0=xc[:, 0:TT], scalar1=cw[:, cc, 0:1], scalar2=None, op0=ALU.mult), reads=[xck, "cw"], writes=[cvk])
                    for kk in range(1, 4):
                        P.op("dve", lambda e, cv=cv, xc=xc, cc=cc, kk=kk: e.scalar_tensor_tensor(
                            out=cv[:], in0=xc[:, kk:kk + TT], scalar=cw[:, cc, kk:kk + 1], in1=cv[:], op0=ALU.mult, op1=ALU.add),
                            reads=[xck, "cw", cvk], writes=[cvk])
                    P.op("act", lambda e, cv=cv, o=o, cc=cc: e.activation(out=o[:], in_=cv[:], func=AF.Silu, bias=cbias[:, cc:cc + 1]),
                         reads=[cvk, "cbias"], writes=[ok])
                    P.dma(xbcT[cc * 128:(cc + 1) * 128, tsl], o[:], reads=[ok], writes=[("ssdx", j)])
            for s in range(TT // 128):
                p, pk = pd.next()
                for c in range(8):
                    P.op("pe", lambda e, p=p, c=c, hb=hb, s=s: e.matmul(
                        p[:, 0:32], lhsT=hb[:, c, s * 128:(s + 1) * 128], rhs=wb[:, c, 6144:6176], start=(c == 0), stop=(c == 7)),
                        reads=["win", hbk], writes=[pk])
                d, dk = dtr.next()
                P.op("dve", lambda e, d=d, p=p: e.tensor_tensor(out=d[:, 0:32], in0=p[:, 0:32], in1=dtb[:], op=ALU.add),
                     reads=[pk, "dtb"], writes=[dk])
                P.op("act", lambda e, d=d: e.activation(out=d[:, 0:32], in_=d[:, 0:32], func=AF.Exp), reads=[dk], writes=[dk])
                P.op("act", lambda e, d=d: e.activation(out=d[:, 0:32], in_=d[:, 0:32], func=AF.Ln, bias=C.onec[:, 0:1]),
                     reads=[dk], writes=[dk])
                P.op("dve", lambda e, d=d: e.tensor_tensor(out=d[:, 32:64], in0=d[:, 0:32], in1=Arep[:], op=ALU.mult),
                     reads=[dk, "Arep"], writes=[dk])
                r0 = j * TT + s * 128
                P.dma(dtD[r0:r0 + 128, :], d[:], reads=[dk], writes=[("ssdd", j)])

    with P.phase():
        dsk = P.sb("dsk", [128, 16], F32)
        P.dma(dsk[:], W["d_skip"], writes=["dsk"])
        gnw = P.sb("gnw", [128, 16], F32)
        P.dma(gnw[:], W["gnorm_w"], writes=["gnw"])
        hsf = P.sb("hsf", [128, 32, 64], F32)
        hsb = P.sb("hsb", [128, 32, 64], BF16)
        for g in range(8):
            P.op("pool", lambda e, g=g: e.memset(hsf[:, 4 * g:4 * g + 4, :], 0.0), writes=[("hsf", g)])
            P.op("pool", lambda e, g=g: e.memset(hsb[:, 4 * g:4 * g + 4, :], 0.0), writes=[("hsb", g)])
        xbr = Ring(P, "xb", [128, 32, 128], BF16, 2)
        zr = Ring(P, "zs", [128, 16, 128], BF16, 2)
        dr = Ring(P, "dta", [128, 64], F32, 2)
        p_cb = Ring(P, "pcb", [128, 512], F32, 1, psum=True)
        p_rb = Ring(P, "prb", [128, 512], F32, 2, psum=True)
        p_y = Ring(P, "py", [128, 512], F32, 1, psum=True)
        p_ms = Ring(P, "pms", [128, 512], F32, 1, psum=True)
        p_st = Ring(P, "pst", [128, 512], F32, 1, psum=True)
        p_tr = Ring(P, "ptr", [128, 1024], BF16, 1, psum=True)
        p_ac = Ring(P, "pac", [128, 512], F32, 1, psum=True)
        nacr = Ring(P, "nac", [128, 32], F32, 2)
        cdr = Ring(P, "cd", [128, 32], F32, 2)
        dsr = Ring(P, "ds", [128, 32], F32, 2)
        xdtr = Ring(P, "xdt", [128, 32, 64], BF16, 2)
        xsr = Ring(P, "xs", [128, 32, 64], BF16, 2)
        bmr = Ring(P, "bm", [128, 1024], BF16, 2)
        cbmr = Ring(P, "cbm", [128, 128], F32, 2)
        dmr = Ring(P, "dm", [128, 128], F32, 3)
        Er = Ring(P, "E", [128, 128], F32, 3)
        ELr = Ring(P, "EL", [128, 128], F32, 3)
        MTr = Ring(P, "MT", [128, 128], BF16, 3)
        CSr = Ring(P, "CS", [128, 128], BF16, 3)
        y1r = Ring(P, "y1", [128, 128], F32, 2)
        Yr = Ring(P, "Y", [128, 16, 128], F32, 2)
        sqr = Ring(P, "sqy", [128, 128], BF16, 2)
        rrr = Ring(P, "rr", [128, 128], F32, 2)
        YGr = Ring(P, "YG", [128, 16, 128], BF16, 2)
        xv = xbcT.rearrange("(c p) t -> p c t", p=128)
        zv = zT.rearrange("(c p) t -> p c t", p=128)
        ygv = ygT.rearrange("(c p) t -> p c t", p=128)
        for ch in range(NCH):
            csl = slice(ch * 128, (ch + 1) * 128)
            jt = ch // 4
            xb, xbk = xbr.next()
            P.dma(xb[:], xv[:, :, csl], reads=[("ssdx", jt)], writes=[xbk])
            zs, zsk = zr.next()
            P.dma(zs[:], zv[:, :, csl], reads=[("ssdz", jt)], writes=[zsk], e="act")
            da, dak = dr.next()
            P.dma(da[:], dtD[csl, :], reads=[("ssdd", jt)], writes=[dak])
            ac, ack = p_ac.next()
            P.op("pe", lambda e, ac=ac, da=da: e.matmul(ac[:, 0:32], lhsT=C.triu[:], rhs=da[:, 32:64], start=True, stop=True),
                 reads=["triu", dak], writes=[ack])
            P.op("pe", lambda e, ac=ac, da=da: e.matmul(ac[:, 32:64], lhsT=C.onesf[:], rhs=da[:, 32:64], start=True, stop=True),
                 reads=["onesf", dak], writes=[ack])
            nac, nack = nacr.next()
            P.op("dve", lambda e, nac=nac, ac=ac: e.tensor_scalar(out=nac[:], in0=ac[:, 0:32], scalar1=-1.0, scalar2=None, op0=ALU.mult),
                 reads=[ack], writes=[nack])
            cd, cdk = cdr.next()
            P.op("act", lambda e, cd=cd, ac=ac: e.activation(out=cd[:], in_=ac[:, 32:64], func=AF.Exp), reads=[ack], writes=[cdk])
            ds, dsk_ = dsr.next()
            P.op("dve", lambda e, ds=ds, ac=ac, nac=nac: e.tensor_tensor(out=ds[:], in0=ac[:, 32:64], in1=nac[:], op=ALU.add),
                 reads=[ack, nack], writes=[dsk_])
            P.op("act", lambda e, ds=ds: e.activation(out=ds[:], in_=ds[:], func=AF.Exp), reads=[dsk_], writes=[dsk_])
            xdt, xdtk = xdtr.next()
            xs, xsk = xsr.next()
            for hf in range(2):
                tr, trk = p_tr.next()
                for q in range(8):
                    P.op("pe", lambda e, tr=tr, xb=xb, q=q, hf=hf: e.transpose(tr[:, q * 128:(q + 1) * 128], xb[:, hf * 8 + q, :], C.identb[:]),
                         reads=[xbk, "identb"], writes=[trk])
                P.op("dve", lambda e, tr=tr, xdt=xdt, da=da, hf=hf: e.tensor_tensor(
                    out=xdt[:, hf * 16:(hf + 1) * 16, :], in0=tr[:].rearrange("p (h d) -> p h d", d=64),
                    in1=da[:, hf * 16:(hf + 1) * 16].unsqueeze(2).broadcast_to([128, 16, 64]), op=ALU.mult),
                    reads=[trk, dak], writes=[xdtk])
            P.op("pool", lambda e, xs=xs, xdt=xdt, ds=ds: e.tensor_tensor(
                out=xs[:], in0=xdt[:], in1=ds[:].unsqueeze(2).broadcast_to([128, 32, 64]), op=ALU.mult),
                reads=[xdtk, dsk_], writes=[xsk])
            tr, trk = p_tr.next()
            for q in range(8):
                P.op("pe", lambda e, tr=tr, xb=xb, q=q: e.transpose(tr[:, q * 128:(q + 1) * 128], xb[:, 16 + q, :], C.identb[:]),
                     reads=[xbk, "identb"], writes=[trk])
            bm, bmk = bmr.next()
            P.op("act", lambda e, bm=bm, tr=tr: e.copy(out=bm[:], in_=tr[:]), reads=[trk], writes=[bmk])
            Y, Yk = Yr.next()
            YG, YGk = YGr.next()
            for g in range(8):
                cb, cbk = p_cb.next()
                P.op("pe", lambda e, cb=cb, xb=xb, g=g: e.matmul(cb[:, 0:128], lhsT=xb[:, 16 + g, :], rhs=xb[:, 24 + g, :], start=True, stop=True),
                     reads=[xbk], writes=[cbk])
                cbm, cbmk = cbmr.next()
                P.op("dve", lambda e, cbm=cbm, cb=cb: e.tensor_tensor(out=cbm[:], in0=cb[:, 0:128], in1=C.triu[:], op=ALU.mult),
                     reads=[cbk, "triu"], writes=[cbmk])
                ms, msk = p_ms.next()
                for r in range(4):
                    h = 4 * g + r
                    pq = h // 2
                    rb, rbk = p_rb.next()
                    P.op("pe", lambda e, rb=rb, da=da, h=h: e.matmul(
                        rb[:, 0:128], lhsT=da[:, 32 + h:33 + h].broadcast_to([128, 128]), rhs=C.triu[:], start=True, stop=True),
                        reads=[dak, "triu"], writes=[rbk])
                    dm, dmk = dmr.next()
                    P.op("dve", lambda e, dm=dm, rb=rb, nac=nac, h=h: e.tensor_scalar(
                        out=dm[:], in0=rb[:, 0:128], scalar1=nac[:, h:h + 1], scalar2=0.0, op0=ALU.add, op1=ALU.min),
                        reads=[rbk, nack], writes=[dmk])
                    E, Ek = Er.next()
                    P.op("act", lambda e, E=E, dm=dm: e.activation(out=E[:], in_=dm[:], func=AF.Exp), reads=[dmk], writes=[Ek])
                    MT, MTk = MTr.next()
                    P.op("pool", lambda e, MT=MT, E=E, cbm=cbm: e.tensor_tensor(out=MT[:], in0=E[:], in1=cbm[:], op=ALU.mult),
                         reads=[Ek, cbmk], writes=[MTk])
                    if r % 2 == 0:
                        py, pyk = p_y.next()
                    osl = slice((r % 2) * 64, (r % 2) * 64 + 64)
                    if ch > 0:
                        EL, ELk = ELr.next()
                        P.op("act", lambda e, EL=EL, rb=rb: e.activation(out=EL[:], in_=rb[:, 0:128], func=AF.Exp), reads=[rbk], writes=[ELk])
                        CS, CSk = CSr.next()
                        P.op("pool", lambda e, CS=CS, EL=EL, xb=xb, g=g: e.tensor_tensor(out=CS[:], in0=EL[:], in1=xb[:, 24 + g, :], op=ALU.mult),
                             reads=[ELk, xbk], writes=[CSk])
                    P.op("pe", lambda e, py=py, xdt=xdt, MT=MT, h=h, osl=osl, ch=ch: e.matmul(
                        py[osl, 0:128], lhsT=xdt[:, h, :], rhs=MT[:], start=True, stop=(ch == 0)),
                        reads=[xdtk, MTk], writes=[pyk])
                    if ch > 0:
                        P.op("pe", lambda e, py=py, CS=CS, h=h, osl=osl: e.matmul(
                            py[osl, 0:128], lhsT=hsb[:, h, :], rhs=CS[:], start=False, stop=True),
                            reads=[("hsb", g), CSk], writes=[pyk])
                    if r % 2 == 1:
                        y1, y1k = y1r.next()
                        P.op("dve", lambda e, y1=y1, xb=xb, pq=pq, py=py: e.scalar_tensor_tensor(
                            out=y1[:], in0=xb[:, pq, :], scalar=dsk[:, pq:pq + 1], in1=py[:, 0:128], op0=ALU.mult, op1=ALU.add),
                            reads=[xbk, "dsk", pyk], writes=[y1k])
                        P.op("pool", lambda e, Y=Y, y1=y1, zs=zs, pq=pq: e.tensor_tensor(out=Y[:, pq, :], in0=y1[:], in1=zs[:, pq, :], op=ALU.mult),
                             reads=[y1k, zsk], writes=[(Yk, pq)])
                        sq, sqk = sqr.next()
                        P.op("act", lambda e, sq=sq, Y=Y, pq=pq: e.activation(out=sq[:], in_=Y[:, pq, :], func=AF.Square),
                             reads=[(Yk, pq)], writes=[sqk])
                        P.op("pe", lambda e, ms=ms, sq=sq, r=r: e.matmul(ms[:, 0:128], lhsT=C.onesb[:], rhs=sq[:], start=(r == 1), stop=(r == 3)),
                             reads=["onesb", sqk], writes=[msk])
                rr, rrk = rrr.next()
                P.op("act", lambda e, rr=rr, ms=ms: e.activation(out=rr[:], in_=ms[:, 0:128], func=AF.Sqrt, scale=1.0 / 256, bias=C.epsc[:, 0:1]),
                     reads=[msk], writes=[rrk])
                P.op("dve", lambda e, rr=rr: e.reciprocal(out=rr[:], in_=rr[:]), reads=[rrk], writes=[rrk])
                for pq in (2 * g, 2 * g + 1):
                    P.op("dve", lambda e, YG=YG, Y=Y, rr=rr, pq=pq: e.scalar_tensor_tensor(
                        out=YG[:, pq, :], in0=Y[:, pq, :], scalar=gnw[:, pq:pq + 1], in1=rr[:], op0=ALU.mult, op1=ALU.mult),
                        reads=[(Yk, pq), rrk, "gnw"], writes=[YGk])
                if ch < NCH - 1:
                    st, stk = p_st.next()
                    P.op("pe", lambda e, st=st, bm=bm, xs=xs, g=g: e.matmul(
                        st[:, 0:256], lhsT=bm[:, g * 128:(g + 1) * 128], rhs=xs[:, 4 * g:4 * g + 4, :].rearrange("p h d -> p (h d)"),
                        start=True, stop=True), reads=[bmk, xsk], writes=[stk])
                    P.op("dve", lambda e, cd=cd, g=g: e.tensor_tensor(
                        out=hsf[:, 4 * g:4 * g + 4, :], in0=hsf[:, 4 * g:4 * g + 4, :],
                        in1=cd[:, 4 * g:4 * g + 4].unsqueeze(2).broadcast_to([128, 4, 64]), op=ALU.mult),
                        reads=[("hsf", g), cdk], writes=[("hsf", g)])
                    P.op("dve", lambda e, st=st, g=g: e.tensor_tensor(
                        out=hsf[:, 4 * g:4 * g + 4, :], in0=st[:, 0:256].rearrange("p (h d) -> p h d", d=64),
                        in1=hsf[:, 4 * g:4 * g + 4, :], op=ALU.add), reads=[("hsf", g), stk], writes=[("hsf", g)])
                    P.op("act", lambda e, g=g: e.copy(out=hsb[:, 4 * g:4 * g + 4, :], in_=hsf[:, 4 * g:4 * g + 4, :]),
                         reads=[("hsf", g)], writes=[("hsb", g)])
            P.dma(ygv[:, :, csl], YG[:], reads=[YGk], writes=[("yg", L, jt)])

    out_proj(P, C, L, T, ygT, 2048, W["w_out"], cur, y_out, last)


def tt(P, eng, out, in0, in1, op, r, w):
    return P.op(eng, lambda e: e.tensor_tensor(out=out, in0=in0, in1=in1, op=op), reads=r, writes=w)


def ts(P, eng, out, in0, s1, s2, op0, op1, r, w):
    if s2 is None:
        return P.op(eng, lambda e: e.tensor_scalar(out=out, in0=in0, scalar1=s1, scalar2=None, op0=op0), reads=r, writes=w)
    return P.op(eng, lambda e: e.tensor_scalar(out=out, in0=in0, scalar1=s1, scalar2=s2, op0=op0, op1=op1), reads=r, writes=w)


def stt(P, out, in0, scalar, in1, op0, op1, r, w):
    return P.op("dve", lambda e: e.scalar_tensor_tensor(out=out, in0=in0, scalar=scalar, in1=in1, op0=op0, op1=op1), reads=r, writes=w)


def actf(P, out, in_, func, r, w, bias=None, scale=None):
    kw = {}
    if bias is not None:
        kw["bias"] = bias
    if scale is not None:
        kw["scale"] = scale
    return P.op("act", lambda e: e.activation(out=out, in_=in_, func=func, **kw), reads=r, writes=w)


def mm(P, out, lhsT, rhs, start, stop, r, w):
    return P.op("pe", lambda e: e.matmul(out, lhsT=lhsT, rhs=rhs, start=start, stop=stop), reads=r, writes=w)


def trp(P, out, in_, ident, r, w):
    return P.op("pe", lambda e: e.transpose(out, in_, ident), reads=r, writes=w)


RW_GN_EPS = 64e-5
NEG_EXP_HALF = -math.exp(-0.5)
CDT = F32


def layer_rwkv(P, C, L, T, W, cur, y_out, last):
    nc = P.nc
    vres = (L == 3)
    TT = 256
    NT = T // TT
    NCK = T // 64
    ncols = 4 * D + 128 + (32 if vres else 0)
    pre = "rw%d_" % L
    names = ["At", "Bt", "Kt", "Rt", "Bg", "Kg", "V", "BON"]
    Dm = {n: nc.dram_tensor(pre + n, [D, T], (F32 if n == "BON" else BF16), kind="Internal").ap() for n in names}
    gCd = nc.dram_tensor(pre + "gC", [D, NCK], F32, kind="Internal").ap()
    gTd = nc.dram_tensor(pre + "gT", [D, T], BF16, kind="Internal").ap()
    yTd = nc.dram_tensor(pre + "yT", [D, T], F32, kind="Internal").ap()
    if L == 0:
        C.vfirst = nc.dram_tensor("rw_vfirst", [D, T], F32, kind="Internal").ap()
    vfd = C.vfirst

    def colp(name, n=8):
        t = P.sb(name, [128, n], F32)
        P.dma(t[:], W[name], writes=[name])
        return t

    with P.phase():
        stage = Ring(P, "wst", [128, 1024], F32, 2)
        wb = load_weight_bf16(P, "win", W["w_in"], D, ncols, stage)
        gcol = P.sb("gcol", [128, 8], F32)
        P.dma(gcol[:], W["norm"], writes=["gcol"])
        mu = P.sb("mu", [128, 6, 8], F32)
        P.dma(mu[:], W["mu"], writes=["mu"])
        w0 = colp("w0"); a0 = colp("a0"); k_k = colp("k_k"); k_a = colp("k_a"); gdum = None
        r_k = P.sb("r_k", [128, 8], F32)
        P.dma(r_k[:], W["r_k"], writes=["r_k"])
        omka = P.sb("omka", [128, 8], F32)
        ts(P, "dve", omka[:], k_a[:], -1.0, 1.0, ALU.mult, ALU.add, ["k_a"], ["omka"])
        def lowrank(nm, rows):
            t = P.sb(nm, [rows, D], BF16)
            st, sk = stage.next()
            P.dma(st[0:rows, :], W[nm], writes=[sk])
            P.op("pool", lambda e: e.tensor_copy(out=t[:], in_=st[0:rows, :]), reads=[sk], writes=[nm])
            return t
        wup = lowrank("w_up", 64)
        aup = lowrank("a_up", 64)
        if vres:
            v0 = colp("v0")
            vup = lowrank("v_up", 32)
        rings = in_rings(P, TT, nh=1, nx=2, nq=1, npt=1)
        carry = P.sb("carry", [128, 8, 1], F32)
        P.op("pool", lambda e: e.memset(carry[:], 0.0), writes=["carry"])
        dxT = P.sb("dxT", [128, 8, TT], F32)
        xmix = {i: P.sb("xm%d" % i, [128, 8, TT], BF16) for i in (0, 2, 3)}
        xsm = Ring(P, "xsm", [128, 8, TT], BF16, 2)
        wl = P.sb("wl", [64, TT], BF16); al = P.sb("al", [64, TT], BF16); vl = P.sb("vl", [32, TT], BF16)
        PB = [P.ps("rwa%d" % i, [128, 512], F32) for i in range(7)]
        PBk = ["rwabank%d_%d" % (L, i) for i in range(7)]

        def half(bi, hi):
            return PB[bi][:, hi * TT:(hi + 1) * TT], PBk[bi]

        class _PQ:
            i = -1

            def next(self):
                self.i = (self.i + 1) % 7
                return PB[self.i], PBk[self.i]
        pq = _PQ()
        F = lambda nm, n=2, dt=F32: Ring(P, nm, [128, TT], dt, n)
        R_ = {nm: F(nm, 2) for nm in ["sg", "lw", "cum", "gi", "ge", "gv", "gd", "aa", "kkr", "nrm", "kk", "t1", "kf", "d1", "ka", "sv", "vf"]}
        R_.update({nm: F(nm, 2) for nm in ["vv", "BON"]})
        R_.update({nm: F(nm, 2, BF16) for nm in ["At", "Bt", "Bg", "Kt", "Kg", "Rt", "vb"]})
        R_["sqb"] = F("sqb", 2, BF16); R_["rkb"] = F("rkb", 2, BF16); R_["go"] = F("go", 2, BF16)
        gCr = Ring(P, "gCt", [128, TT // 64], F32, 2)
        for j in range(NT):
            hT, hk = make_hT(P, C, j, cur, TT, rings, gcol)
            P.op("pool", lambda e, hT=hT: e.tensor_copy(out=hT[:, :, 0:1], in_=carry[:]), reads=["carry"], writes=[hk])
            P.op("pool", lambda e, hT=hT: e.tensor_copy(out=carry[:], in_=hT[:, :, TT:TT + 1]), reads=[hk], writes=["carry"])
            tt(P, "pool", dxT[:], hT[:, :, 0:TT], hT[:, :, 1:TT + 1], ALU.subtract, [hk], ["dxT"])
            tsl = slice(j * TT, (j + 1) * TT)

            def mix(i, dst, dk):
                for c in range(8):
                    stt(P, dst[:, c, :], dxT[:, c, :], mu[:, i, c:c + 1], hT[:, c, 1:TT + 1], ALU.mult, ALU.add,
                        ["dxT", "mu", hk], [dk])

            xw, xwk = xsm.next(); mix(1, xw, xwk)
            p, pk = pq.next()
            for c in range(8):
                mm(P, p[0:64, 0:TT], wb[:, c, 4096:4160], xw[:, c, :], c == 0, c == 7, ["win", xwk], [pk])
            actf(P, wl[:], p[0:64, 0:TT], AF.Tanh, [pk], ["wl"])
            xa, xak = xsm.next(); mix(4, xa, xak)
            p, pk = pq.next()
            for c in range(8):
                mm(P, p[0:64, 0:TT], wb[:, c, 4160:4224], xa[:, c, :], c == 0, c == 7, ["win", xak], [pk])
            actf(P, al[:], p[0:64, 0:TT], AF.Copy, [pk], ["al"])
            xg, xgk = xsm.next(); mix(5, xg, xgk)
            for fc in range(8):
                p, pk = pq.next()
                for c in range(8):
                    mm(P, p[:, 0:TT], wb[:, c, 3072 + fc * 128:3072 + (fc + 1) * 128], xg[:, c, :], c == 0, c == 7, ["win", xgk], [pk])
                go, gok = R_["go"].next()
                actf(P, go[:], p[:, 0:TT], AF.Silu, [pk], [gok])
                P.dma(gTd[fc * 128:(fc + 1) * 128, tsl], go[:], reads=[gok], writes=[(pre + "g", j)])
            for i in (0, 2, 3):
                mix(i, xmix[i], "xm%d" % i)
            if vres:
                p, pk = pq.next()
                for c in range(8):
                    mm(P, p[0:32, 0:TT], wb[:, c, 4224:4256], xmix[3][:, c, :], c == 0, c == 7, ["win", "xm3"], [pk])
                actf(P, vl[:], p[0:32, 0:TT], AF.Copy, [pk], ["vl"])
            def fc_body(fc, slot):
                    fs = slice(fc * 128, (fc + 1) * 128)
                    col = lambda t_: t_[:, fc:fc + 1]
                    pr, prk = half(3 * slot, 0); pkp, pkk = half(3 * slot, 1); pv, pvk = half(3 * slot + 1, 0)
                    pn0, pn0k = half(6, slot); pw2, pw2k = half(3 * slot + 2, 0); pa2, pa2k = half(3 * slot + 2, 1)
                    pv2, pv2k = half(3 * slot + 1, 1)
                    mm(P, pw2, wup[:, fs], wl[:], True, True, ["w_up", "wl"], [pw2k])
                    mm(P, pa2, aup[:, fs], al[:], True, True, ["a_up", "al"], [pa2k])
                    for c in range(8):
                        mm(P, pkp, wb[:, c, D + fc * 128:D + (fc + 1) * 128], xmix[2][:, c, :], c == 0, c == 7, ["win", "xm2"], [pkk])
                    for c in range(8):
                        mm(P, pr, wb[:, c, fc * 128:(fc + 1) * 128], xmix[0][:, c, :], c == 0, c == 7, ["win", "xm0"], [prk])
                    yield
                    for c in range(8):
                        mm(P, pv, wb[:, c, 2 * D + fc * 128:2 * D + (fc + 1) * 128], xmix[3][:, c, :], c == 0, c == 7, ["win", "xm3"], [pvk])
                    if vres:
                        mm(P, pv2, vup[:, fs], vl[:], True, True, ["v_up", "vl"], [pv2k])
                    sg, sgk = R_["sg"].next()
                    actf(P, sg[:], pw2, AF.Sigmoid, [pw2k, "w0"], [sgk], bias=col(w0))
                    lw, lwk = R_["lw"].next()
                    ts(P, "dve", lw[:], sg[:], NEG_EXP_HALF, None, ALU.mult, None, [sgk], [lwk])
                    cum, cumk = R_["cum"].next()
                    P.op("dve", lambda e, cum=cum, lw=lw: e.tensor_tensor_scan(
                        out=cum[:], data0=C.mask0[:, 0:TT], data1=lw[:], initial=0.0, op0=ALU.mult, op1=ALU.add),
                        reads=[lwk, "mask0"], writes=[cumk])
                    yield
                    gi, gik = R_["gi"].next()
                    actf(P, gi[:], cum[:], AF.Exp, [cumk], [gik])
                    ge, gek = R_["ge"].next()
                    tt(P, "pool", ge[:], cum[:], lw[:], ALU.subtract, [cumk, lwk], [gek])
                    actf(P, ge[:], ge[:], AF.Exp, [gek], [gek])
                    gv, gvk = R_["gv"].next()
                    actf(P, gv[:], cum[:], AF.Exp, [cumk], [gvk], scale=-1.0)
                    yield
                    gd, gdk = R_["gd"].next()
                    cum3 = cum[:].rearrange("p (c t) -> p c t", t=64)
                    tt(P, "pool", gd[:].rearrange("p (c t) -> p c t", t=64), cum3[:, :, 63:64].broadcast_to([128, TT // 64, 64]), cum3,
                       ALU.subtract, [cumk], [gdk])
                    actf(P, gd[:], gd[:], AF.Exp, [gdk], [gdk])
                    gCt, gCk = gCr.next()
                    actf(P, gCt[:].unsqueeze(2), cum3[:, :, 63:64], AF.Exp, [cumk], [gCk])
                    P.dma(gCd[fs, j * (TT // 64):(j + 1) * (TT // 64)], gCt[:], reads=[gCk], writes=[(pre + "gC", j)])
                    yield
                    aa, aak = R_["aa"].next()
                    actf(P, aa[:], pa2, AF.Sigmoid, [pa2k, "a0"], [aak], bias=col(a0))
                    kkr, kkrk = R_["kkr"].next()
                    ts(P, "dve", kkr[:], pkp, col(k_k), None, ALU.mult, None, [pkk, "k_k"], [kkrk])
                    sqb, sqbk = R_["sqb"].next()
                    actf(P, sqb[:], kkr[:], AF.Square, [kkrk], [sqbk])
                    mm(P, pn0, C.bonesb[:], sqb[:], True, True, ["bonesb", sqbk], [pn0k])
                    yield
                    nrm, nrmk = R_["nrm"].next()
                    actf(P, nrm[:], pn0, AF.Sqrt, [pn0k], [nrmk])
                    ts(P, "dve", nrm[:], nrm[:], 1e-12, None, ALU.max, None, [nrmk], [nrmk])
                    P.op("dve", lambda e, nrm=nrm: e.reciprocal(out=nrm[:], in_=nrm[:]), reads=[nrmk], writes=[nrmk])
                    kk, kkk = R_["kk"].next()
                    tt(P, "dve", kk[:], kkr[:], nrm[:], ALU.mult, [kkrk, nrmk], [kkk])
                    yield
                    t1, t1k = R_["t1"].next()
                    ts(P, "dve", t1[:], aa[:], col(k_a), col(omka), ALU.mult, ALU.add, [aak, "k_a", "omka"], [t1k])
                    kf, kfk = R_["kf"].next()
                    tt(P, "dve", kf[:], pkp, t1[:], ALU.mult, [pkk, t1k], [kfk])
                    yield
                    vv, vvk = R_["vv"].next()
                    if not vres:
                        actf(P, vv[:], pv, AF.Copy, [pvk], [vvk])
                        if L == 0:
                            P.dma(vfd[fs, tsl], vv[:], reads=[vvk], writes=[("vfirst", j)])
                    else:
                        vf, vfk = R_["vf"].next()
                        P.dma(vf[:], vfd[fs, tsl], reads=[("vfirst", j)], writes=[vfk])
                        sv, svk = R_["sv"].next()
                        actf(P, sv[:], pv2, AF.Sigmoid, [pv2k, "v0"], [svk], bias=col(v0))
                        d1, d1k = R_["d1"].next()
                        tt(P, "dve", d1[:], vf[:], pv, ALU.subtract, [vfk, pvk], [d1k])
                        tt(P, "pool", d1[:], d1[:], sv[:], ALU.mult, [d1k, svk], [d1k])
                        tt(P, "dve", vv[:], d1[:], pv, ALU.add, [d1k, pvk], [vvk])
                    vb, vbk = R_["vb"].next()
                    P.op("pool", lambda e, vb=vb, vv=vv: e.tensor_copy(out=vb[:], in_=vv[:]), reads=[vvk], writes=[vbk])
                    P.dma(Dm["V"][fs, tsl], vb[:], reads=[vbk], writes=[(pre + "V", j)])
                    yield
                    At, Atk = R_["At"].next()
                    stt(P, At[:], kk[:], -1.0, ge[:], ALU.mult, ALU.mult, [kkk, gek], [Atk])
                    P.dma(Dm["At"][fs, tsl], At[:], reads=[Atk], writes=[(pre + "At", j)])
                    ka, kak = R_["ka"].next()
                    tt(P, "pool", ka[:], kk[:], aa[:], ALU.mult, [kkk, aak], [kak])
                    Bt, Btk = R_["Bt"].next()
                    tt(P, "dve", Bt[:], ka[:], gv[:], ALU.mult, [kak, gvk], [Btk])
                    P.dma(Dm["Bt"][fs, tsl], Bt[:], reads=[Btk], writes=[(pre + "Bt", j)])
                    yield
                    Bg, Bgk = R_["Bg"].next()
                    tt(P, "pool", Bg[:], ka[:], gd[:], ALU.mult, [kak, gdk], [Bgk])
                    P.dma(Dm["Bg"][fs, tsl], Bg[:], reads=[Bgk], writes=[(pre + "Bg", j)])
                    Kt, Ktk = R_["Kt"].next()
                    tt(P, "pool", Kt[:], kf[:], gv[:], ALU.mult, [kfk, gvk], [Ktk])
                    P.dma(Dm["Kt"][fs, tsl], Kt[:], reads=[Ktk], writes=[(pre + "Kt", j)])
                    yield
                    Kg, Kgk = R_["Kg"].next()
                    tt(P, "pool", Kg[:], kf[:], gd[:], ALU.mult, [kfk, gdk], [Kgk])
                    P.dma(Dm["Kg"][fs, tsl], Kg[:], reads=[Kgk], writes=[(pre + "Kg", j)])
                    Rt, Rtk = R_["Rt"].next()
                    tt(P, "dve", Rt[:], pr, gi[:], ALU.mult, [prk, gik], [Rtk])
                    P.dma(Dm["Rt"][fs, tsl], Rt[:], reads=[Rtk], writes=[(pre + "Rt", j)])
                    yield
                    rkb, rkbk = R_["rkb"].next()
                    stt(P, rkb[:], pr, col(r_k), kf[:], ALU.mult, ALU.mult, [prk, "r_k", kfk], [rkbk])
                    mm(P, pn0, C.bonesb[:], rkb[:], True, True, ["bonesb", rkbk], [pn0k])
                    BON, BONk = R_["BON"].next()
                    tt(P, "dve", BON[:], pn0, vv[:], ALU.mult, [pn0k, vvk], [BONk])
                    P.dma(Dm["BON"][fs, tsl], BON[:], reads=[BONk], writes=[(pre + "BON", j)])

            for pair in range(4):
                gens = [fc_body(2 * pair, 0), fc_body(2 * pair + 1, 1)]
                while gens:
                    for g_ in list(gens):
                        try:
                            next(g_)
                        except StopIteration:
                            gens.remove(g_)

    import os
    if os.environ.get("RW_STOP") == "A":
        return
    with P.phase():
        TB = 128
        NCB = TB // 64
        opn = ["At", "Bt", "Kt", "Rt", "Bg", "Kg", "V"]
        opr = {n: Ring(P, "o" + n, [128, 8, TB], BF16, 2) for n in opn}
        gCr2 = Ring(P, "gC2", [128, 8, NCB], F32, 2)
        ST = P.sb("ST", [128, 8, 64], F32)
        STb = P.sb("STb", [128, 8, 64], BF16)
        P.op("pool", lambda e: e.memset(ST[:], 0.0), writes=["ST"])
        P.op("pool", lambda e: e.memset(STb[:], 0.0), writes=["STb"])
        YT = Ring(P, "YT", [128, 8, TB], F32, 2)
        bk = [P.ps("rwbk%d" % i, [128, 512], F32) for i in range(7)]
        bkk = ["rwbank%d_%d" % (L, i) for i in range(7)]

        class RingOf:
            def __init__(self, ids):
                self.ids = ids
                self.i = -1

            def next(self):
                self.i = (self.i + 1) % len(self.ids)
                j = self.ids[self.i]
                return bk[j], bkk[j]

        pA5 = RingOf([0, 1, 2])
        pZ = RingOf([0, 1, 2])
        pW = RingOf([2, 3])
        pU = RingOf([3]); pUO = RingOf([4]); pY2 = RingOf([5]); pS = RingOf([6])
        pT = Ring(P, "pT", [128, 1024], BF16, 1, psum=True)
        tmr = {n: Ring(P, "tm" + n, [64, D], BF16, 2) for n in ["At", "V", "Bg", "Kg"]}
        A5r = Ring(P, "A5s", [64, 192], BF16, 20)
        Zall = [P.sb("Zall%d" % i, [64, 16, 256], BF16) for i in range(2)]
        ZcAr = Ring(P, "ZcA", [64, 16, 64], BF16, 2)
        ZcPr = Ring(P, "ZcP", [64, 16, 64], F32, 2)
        ApT = Ring(P, "ApT", [128, 8, 64], BF16, 2)
        Ur = Ring(P, "U", [64, 16, 64], BF16, 2)
        Y1r = Ring(P, "Y1s", [64, 512], F32, 2)
        Ysr = Ring(P, "Ys", [64, D], BF16, 2)
        tmpS = Ring(P, "tmpS", [128, 4, 64], F32, 2)
        for jb in range(T // TB):
            tsl = slice(jb * TB, (jb + 1) * TB)
            jA = (jb * TB) // TT
            ot = {}
            for n in opn:
                t_, k_ = opr[n].next()
                P.dma(t_[:], Dm[n].rearrange("(c p) t -> p c t", p=128)[:, :, tsl], reads=[(pre + n, jA)], writes=[k_],
                      e=("act" if n in ("Bg", "Kg", "V") else "sp"))
                ot[n] = (t_, k_)
            gC2, gC2k = gCr2.next()
            P.dma(gC2[:], gCd.rearrange("(c p) k -> p c k", p=128)[:, :, jb * NCB:(jb + 1) * NCB], reads=[(pre + "gC", jA)], writes=[gC2k])
            yt, ytk = YT.next()
            for cq in range(NCB):
                cs = slice(cq * 64, (cq + 1) * 64)
                tm = {}
                for n in ["At", "V", "Bg", "Kg"]:
                    dst, dk = tmr[n].next()
                    src, sk = ot[n]
                    for hf in range(2):
                        p, pk = pT.next()
                        for q in range(4):
                            fc = hf * 4 + q
                            trp(P, p[0:64, q * 128:(q + 1) * 128], src[:, fc, cs], C.identb[:], [sk, "identb"], [pk])
                        actf(P, dst[:, hf * 512:(hf + 1) * 512], p[0:64, 0:512], AF.Copy, [pk], [(dk, hf)])
                    tm[n] = (dst, dk)
                ZcA, Zck = ZcAr.next()
                ZcP, _zp = ZcPr.next()
                apt, aptk = ApT.next()
                A5h = {}
                Z0 = Zall[0]
                zkey = lambda par, fc, part: ("Zall", L, par, fc, part)
                for h in range(16):
                    fc, e_ = h // 2, h % 2
                    rows = slice(e_ * 64, e_ * 64 + 64)
                    a5, a5k = pA5.next()
                    Kt_ = ot["Kt"][0][rows, fc, cs]; Bt_ = ot["Bt"][0][rows, fc, cs]
                    At_ = ot["At"][0][rows, fc, cs]; Rt_ = ot["Rt"][0][rows, fc, cs]
                    rk5 = [ot["Kt"][1], ot["Bt"][1], ot["At"][1], ot["Rt"][1]]
                    mm(P, a5[0:64, 0:64], Kt_, At_, True, True, rk5, [a5k])
                    mm(P, a5[0:64, 64:128], Kt_, Rt_, True, True, rk5, [a5k])
                    mm(P, a5[0:64, 128:192], Bt_, Rt_, True, True, rk5, [a5k])
                    mm(P, a5[0:64, 192:256], At_, Bt_, True, True, rk5, [a5k])
                    mm(P, a5[0:64, 256:320], Bt_, At_, True, True, rk5, [a5k])
                    a5s, a5sk = A5r.next()
                    tt(P, "dve", a5s[:], a5[0:64, 0:192], C.mask5[:, 0:192], ALU.mult, [a5k, "mask5"], [a5sk])
                    tt(P, "dve", Z0[:, h, 128:256], a5[0:64, 192:320], C.mask5[:, 192:320], ALU.mult, [a5k, "mask5"], [zkey(0, fc, "L")])
                    A5h[h] = (a5s, a5sk)
                P.op("pool", lambda e, Z0=Z0, src=tm["At"][0]: e.tensor_copy(out=Z0[:, :, 0:64], in_=src[:].rearrange("p (h d) -> p h d", d=64)),
                     reads=[(tm["At"][1], 0), (tm["At"][1], 1)], writes=[zkey(0, fc, "X0") for fc in range(8)])
                for hf in range(2):
                    pw_, pwk = pW.next()
                    for hh in range(8):
                        h = hf * 8 + hh
                        a5s, a5sk = A5h[h]
                        mm(P, pw_[0:64, hh * 64:(hh + 1) * 64], a5s[:, 0:64], tm["V"][0][:, h * 64:(h + 1) * 64], True, True,
                           [a5sk, (tm["V"][1], hf)], [pwk])
                    actf(P, Z0[:, hf * 8:(hf + 1) * 8, 64:128], pw_[0:64, :].rearrange("p (h v) -> p h v", v=64), AF.Copy, [pwk],
                         [zkey(0, fc, "X1") for fc in range(hf * 4, hf * 4 + 4)])
                for lev in range(6):
                    par = lev % 2
                    Zc_, Zn_ = Zall[par], Zall[1 - par]
                    for fc in range(8):
                        pz, pzk = pZ.next()
                        rdk = [zkey(par, fc, "X0"), zkey(par, fc, "X1"), zkey(par, fc, "L")]
                        for e_ in range(2):
                            h = 2 * fc + e_
                            lt_ap = Zc_[:, h, 192:256]
                            for c0 in range(0, 192 if lev < 5 else 128, 64):
                                mm(P, pz[0:64, e_ * 256 + c0:e_ * 256 + c0 + 64], lt_ap, Zc_[:, h, c0:c0 + 64], True, True, rdk, [pzk])
                            if lev < 5:
                                mm(P, pz[0:64, e_ * 256 + 192:e_ * 256 + 256], Zc_[:, h, 128:192], lt_ap, True, True, rdk, [pzk])
                        pz3 = pz[0:64, :].rearrange("p (e c) -> p e c", c=256)
                        hs = slice(2 * fc, 2 * fc + 2)
                        if lev < 5:
                            tt(P, "dve", Zn_[:, hs, 0:128], pz3[:, :, 0:128], Zc_[:, hs, 0:128], ALU.add, [pzk] + rdk,
                               [zkey(1 - par, fc, "X0"), zkey(1 - par, fc, "X1")])
                            P.op("dve", lambda e, Zn_=Zn_, pz3=pz3, hs=hs: e.tensor_copy(out=Zn_[:, hs, 128:256], in_=pz3[:, :, 128:256]),
                                 reads=[pzk], writes=[zkey(1 - par, fc, "L")])
                        else:
                            tt(P, "dve", ZcA[:, hs, :], pz3[:, :, 0:64], Zc_[:, hs, 0:64], ALU.add, [pzk] + rdk, [(Zck, fc)])
                            tt(P, "dve", ZcP[:, hs, :], pz3[:, :, 64:128], Zc_[:, hs, 64:128], ALU.add, [pzk] + rdk, [(Zck, fc)])
                p, pk = pT.next()
                for fc in range(8):
                    trp(P, p[:, fc * 64:(fc + 1) * 64], ZcA[:, 2 * fc:2 * fc + 2, :].rearrange("p e d -> p (e d)"), C.identb[0:64, 0:64],
                        [(Zck, fc), "identb"], [pk])
                actf(P, apt[:].rearrange("p c t -> p (c t)"), p[:, 0:512], AF.Copy, [pk], [aptk])
                ys, ysk = Ysr.next()
                U, Uk = Ur.next()
                for hf in range(2):
                    bE, bEk = pU.next()
                    bO, bOk = pUO.next()
                    banks = [(bE, bEk), (bO, bOk)]
                    for hh in range(8):
                        h = hf * 8 + hh
                        fc, e_ = h // 2, h % 2
                        q = hh // 2
                        rows = slice(e_ * 64, e_ * 64 + 64)
                        bank, bk_ = banks[e_]
                        mm(P, bank[0:64, q * 64:(q + 1) * 64], apt[rows, fc, :], STb[rows, fc, :], True, True, [aptk, "STb"], [bk_])
                    for e_ in range(2):
                        bank, bk_ = banks[e_]
                        Uv = U[:, hf * 8:(hf + 1) * 8, :].rearrange("p (q e) v -> p q e v", e=2)[:, :, e_, :]
                        Pv = ZcP[:, hf * 8:(hf + 1) * 8, :].rearrange("p (q e) v -> p q e v", e=2)[:, :, e_, :]
                        tt(P, "dve", Uv, bank[0:64, 0:256].rearrange("p (q v) -> p q v", v=64), Pv, ALU.add,
                           [bk_] + [(Zck, fc) for fc in range(hf * 4, hf * 4 + 4)], [(Uk, hf)])
                    py2, py2k = pY2.next()
                    for hh in range(8):
                        h = hf * 8 + hh
                        fc, e_ = h // 2, h % 2
                        q = hh // 2
                        rows = slice(e_ * 64, e_ * 64 + 64)
                        bank, bk_ = banks[e_]
                        mm(P, bank[0:64, 256 + q * 64:256 + (q + 1) * 64], ot["Rt"][0][rows, fc, cs], STb[rows, fc, :], True, True,
                           [ot["Rt"][1], "STb"], [bk_])
                        a5s, a5sk = A5h[h]
                        mm(P, py2[0:64, hh * 64:(hh + 1) * 64], a5s[:, 128:192], U[:, h, :], True, False, [a5sk, (Uk, hf)], [py2k])
                        mm(P, py2[0:64, hh * 64:(hh + 1) * 64], a5s[:, 64:128], tm["V"][0][:, h * 64:(h + 1) * 64], False, True,
                           [a5sk, (tm["V"][1], hf)], [py2k])
                    y1s, y1sk = Y1r.next()
                    for e_ in range(2):
                        bank, bk_ = banks[e_]
                        actf(P, y1s[:].rearrange("p (q e v) -> p q e v", e=2, v=64)[:, :, e_, :],
                             bank[0:64, 256:512].rearrange("p (q v) -> p q v", v=64), AF.Copy, [bk_], [y1sk])
                    tt(P, "dve", ys[:, hf * 512:(hf + 1) * 512], py2[0:64, :], y1s[:], ALU.add, [py2k, y1sk], [(ysk, hf)])
                    ps_, psk = pS.next()
                    for hh in range(8):
                        h = hf * 8 + hh
                        fc, e_ = h // 2, h % 2
                        orow = slice(e_ * 64, e_ * 64 + 64)
                        ocol = slice((fc % 4) * 64, (fc % 4) * 64 + 64)
                        mm(P, ps_[orow, ocol], tm["Bg"][0][:, h * 64:(h + 1) * 64], U[:, h, :], True, False,
                           [(tm["Bg"][1], hf), (Uk, hf)], [psk])
                        mm(P, ps_[orow, ocol], tm["Kg"][0][:, h * 64:(h + 1) * 64], tm["V"][0][:, h * 64:(h + 1) * 64], False, True,
                           [(tm["Kg"][1], hf), (tm["V"][1], hf)], [psk])
                    tS, tSk = tmpS.next()
                    fsl = slice(hf * 4, hf * 4 + 4)
                    tt(P, "dve", tS[:], ST[:, fsl, :], gC2[:, fsl, cq:cq + 1].broadcast_to([128, 4, 64]), ALU.mult,
                       ["ST", gC2k], [tSk])
                    tt(P, "dve", ST[:, fsl, :], ps_[:, 0:256].rearrange("p (c v) -> p c v", v=64), tS[:], ALU.add, [psk, tSk], ["ST"])
                    actf(P, STb[:, fsl, :], ST[:, fsl, :], AF.Copy, ["ST"], ["STb"])
                for hf in range(2):
                    p, pk = pT.next()
                    for q in range(4):
                        fc = hf * 4 + q
                        trp(P, p[:, q * 64:(q + 1) * 64], ys[:, fc * 128:(fc + 1) * 128], C.identb[0:64, 0:64], [(ysk, hf), "identb"], [pk])
                    P.op("dve", lambda e, yt=yt, p=p, hf=hf, cs=cs: e.tensor_copy(
                        out=yt[:, hf * 4:(hf + 1) * 4, cs], in_=p[:, 0:256].rearrange("p (c t) -> p c t", t=64)), reads=[pk], writes=[ytk])
            P.dma(yTd.rearrange("(c p) t -> p c t", p=128)[:, :, tsl], yt[:], reads=[ytk], writes=[(pre + "yT", jb)])

    if os.environ.get("RW_STOP") == "B":
        return
    with P.phase(last=last):
        TC = 512
        stage = Ring(P, "wst", [128, 2048], F32, 2)
        wob = load_weight_bf16(P, "wout", W["w_out"], D, D, stage)
        gnw = colp("gn_w"); gnb = colp("gn_b")
        yr = Ring(P, "yin", [128, 8, TC], F32, 2)
        br = Ring(P, "bin", [128, 8, TC], F32, 2)
        gr = Ring(P, "gin", [128, 8, TC], BF16, 2)
        ygr = Ring(P, "ygo", [128, 8, TC], BF16, 2)
        pm = Ring(P, "pm", [128, 512], F32, 2, psum=True)
        pv_ = Ring(P, "pv", [128, 512], F32, 2, psum=True)
        po = Ring(P, "po", [128, 512], F32, 3, psum=True)
        ycr = Ring(P, "yc", [128, TC], F32, 2)
        sqr = Ring(P, "sq2", [128, TC], BF16, 2)
        rsr = Ring(P, "rs2", [128, TC], F32, 2)
        xr = Ring(P, "xr", [128, D], F32, 3)
        gneps = P.sb("gneps", [128, 1], F32)
        P.op("pool", lambda e: e.memset(gneps[:], RW_GN_EPS), writes=["gneps"])
        for j in range(T // TC):
            tsl = slice(j * TC, (j + 1) * TC)
            yi, yik = yr.next(); bi, bik = br.next(); gi_, gik_ = gr.next()
            P.dma(yi[:], yTd.rearrange("(c p) t -> p c t", p=128)[:, :, tsl], reads=[(pre + "yT", jj) for jj in range(4 * j, 4 * j + 4)], writes=[yik])
            P.dma(bi[:], Dm["BON"].rearrange("(c p) t -> p c t", p=128)[:, :, tsl], reads=[(pre + "BON", jj) for jj in (2 * j, 2 * j + 1)], writes=[bik], e="act")
            P.dma(gi_[:], gTd.rearrange("(c p) t -> p c t", p=128)[:, :, tsl], reads=[(pre + "g", jj) for jj in (2 * j, 2 * j + 1)], writes=[gik_])
            yg, ygk = ygr.next()
            for fc in range(8):
                m, mk = pm.next()
                for c0 in range(0, TC, 128):
                    mm(P, m[:, c0:c0 + 128], C.bonesf[:], yi[:, fc, c0:c0 + 128], True, True, ["bonesf", yik], [mk])
                yc, yck = ycr.next()
                stt(P, yc[:], m[:], -1.0 / 64, yi[:, fc, :], ALU.mult, ALU.add, [mk, yik], [yck])
                sq, sqk = sqr.next()
                actf(P, sq[:], yc[:], AF.Square, [yck], [sqk])
                v_, vk = pv_.next()
                mm(P, v_[:], C.bonesb[:], sq[:], True, True, ["bonesb", sqk], [vk])
                rs, rsk = rsr.next()
                actf(P, rs[:], v_[:], AF.Sqrt, [vk, "gneps"], [rsk], bias=gneps[:, 0:1], scale=1.0 / 64)
                P.op("dve", lambda e, rs=rs: e.reciprocal(out=rs[:], in_=rs[:]), reads=[rsk], writes=[rsk])
                tt(P, "pool", yc[:], yc[:], rs[:], ALU.mult, [yck, rsk], [yck])
                ts(P, "dve", yc[:], yc[:], gnw[:, fc:fc + 1], gnb[:, fc:fc + 1], ALU.mult, ALU.add, [yck, "gn_w", "gn_b"], [yck])
                tt(P, "pool", yc[:], yc[:], bi[:, fc, :], ALU.add, [yck, bik], [yck])
                tt(P, "dve", yg[:, fc, :], yc[:], gi_[:, fc, :], ALU.mult, [yck, gik_], [(ygk, fc)])
            for s in range(TC // 128):
                r0 = j * TC + s * 128
                xt, xk = xr.next()
                P.dma(xt[:], cur[r0:r0 + 128, :], reads=[("x", r0 // 128)], writes=[xk], e="act")
                for hf in range(2):
                    o, ok = po.next()
                    for c in range(8):
                        mm(P, o[:], yg[:, c, s * 128:(s + 1) * 128], wob[:, c, hf * 512:(hf + 1) * 512], c == 0, c == 7,
                           [(ygk, c), "wout"], [ok])
                    tt(P, "dve", xt[:, hf * 512:(hf + 1) * 512], o[:], xt[:, hf * 512:(hf + 1) * 512], ALU.add, [ok, xk], [xk])
                P.dma(y_out[r0:r0 + 128, :], xt[:], reads=[xk], writes=[("x", r0 // 128)], final=last)


LAYER_PARAMS = {
    0: ["norm", "mu", "w_in", "w_up", "w0", "a_up", "a0", "k_k", "k_a", "r_k", "gn_w", "gn_b", "w_out"],
    1: ["norm", "w_in", "conv_w", "conv_b", "dt_bias", "a_log", "d_skip", "gnorm_w", "w_out"],
    2: ["norm", "w_in", "q_norm", "k_norm", "w_out"],
    3: ["norm", "mu", "w_in", "w_up", "w0", "a_up", "a0", "k_k", "k_a", "r_k", "gn_w", "gn_b", "w_out", "v_up", "v0"],
}


def host_consts():
    c = {}
    c["c_ident"] = np.eye(128, dtype=np.float32)
    i = np.arange(128)
    c["c_bones"] = (i[:, None] // 64 == i[None, :] // 64).astype(np.float32)
    tq = np.arange(256)
    c["c_mask0"] = np.tile(((tq % 64) != 0).astype(np.float32)[None, :], (128, 1))
    ss_, tt_ = np.arange(64)[:, None], np.arange(64)[None, :]
    strict = (ss_ < tt_).astype(np.float32); incl = (ss_ <= tt_).astype(np.float32)
    c["c_mask5"] = np.concatenate([strict, incl, incl, strict.T, strict], axis=1)
    c["c_triu"] = (i[:, None] <= i[None, :]).astype(np.float32)
    c["c_ntril"] = -(i[:, None] >= i[None, :]).astype(np.float32)
    t = np.arange(512)
    c["c_mask"] = np.stack([((128 * r + i)[:, None] < t[None, :]).astype(np.float32) for r in range(4)], axis=1)
    return c


def host_layout(name, arr):
    a = np.asarray(arr, dtype=np.float32)
    short = name.split("_", 1)[1]
    if short == "norm":
        return np.ascontiguousarray(a.reshape(8, 128).T)
    if short in ("q_norm", "k_norm"):
        return np.ascontiguousarray(np.tile(a, 2).reshape(128, 1))
    if short in ("w0", "a0", "k_k", "k_a", "gn_w", "gn_b", "v0", "r_k"):
        return np.ascontiguousarray(a.reshape(8, 128).T)
    if short == "mu":
        return np.ascontiguousarray(a.reshape(6, 8, 128).transpose(2, 0, 1))
    if name == "l1_conv_w":
        return np.ascontiguousarray(a.T.reshape(32, 128, 4).transpose(1, 0, 2))
    if name == "l1_conv_b":
        return np.ascontiguousarray(a.reshape(32, 128).T)
    if name in ("l1_dt_bias", "l1_a_log"):
        return np.ascontiguousarray(np.tile(a[None, :], (128, 1)))
    if name == "l1_d_skip":
        return np.ascontiguousarray(np.repeat(a, 64).reshape(16, 128).T)
    if name == "l1_gnorm_w":
        return np.ascontiguousarray(a.reshape(16, 128).T)
    return np.ascontiguousarray(a)


def build_program(T, layers):
    nc = bass.Bass("TRN2", target_bir_lowering=False)
    x_in = nc.dram_tensor("x", [T, D], F32, kind="ExternalInput").ap()
    y_out = nc.dram_tensor("y", [T, D], F32, kind="ExternalOutput").ap()
    hc = host_consts()
    cap = {k: nc.dram_tensor(k, list(v.shape), F32, kind="ExternalInput").ap() for k, v in hc.items()}
    Wd = {}
    shapes = param_shapes()
    for L in layers:
        Wd[L] = {}
        for pn in LAYER_PARAMS[L]:
            full = "l%d_%s" % (L, pn)
            Wd[L][pn] = nc.dram_tensor(full, list(shapes[full]), F32, kind="ExternalInput").ap()
    P = Prog(nc)
    C = Ctx()
    specs = [("ident", "c_ident", [128, 128], None), ("bonesb", "c_bones", [128, 128], BF16),
             ("ntril", "c_ntril", [128, 128], BF16), ("maskb", "c_mask", [128, 4, 512], BF16),
             ("triu", "c_triu", [128, 128], None), ("identb", "c_ident", [128, 128], BF16),
             ("mask0", "c_mask0", [128, 256], None), ("mask5", "c_mask5", [64, 320], None),
             ("bonesf", "c_bones", [128, 128], None)]
    tiles = {}
    for key, cn, shape, cast in specs:
        tiles[key] = P.sb(key, shape, cast or F32)
        setattr(C, key, tiles[key])
    C.onesf = P.sb("onesf", [128, 128], F32)
    C.epsc = P.sb("epsc", [128, 1], F32)
    C.onec = P.sb("onec", [128, 1], F32)
    C.onesb = P.sb("onesb", [128, 128], BF16)
    with P.phase():
        for key, cn, shape, cast in specs:
            t = tiles[key]
            if cast is None:
                P.dma(t[:], cap[cn], writes=[key])
            else:
                st = P.sb(key + "_f", shape, F32)
                P.dma(st[:], cap[cn], writes=[key + "_f"])
                P.op("pool", lambda e, t=t, st=st: e.tensor_copy(out=t[:], in_=st[:]), reads=[key + "_f"], writes=[key])
        P.op("pool", lambda e: e.memset(C.onesf[:], 1.0), writes=["onesf"])
        P.op("pool", lambda e: e.memset(C.epsc[:], NORM_EPS), writes=["epsc"])
        P.op("pool", lambda e: e.memset(C.onec[:], 1.0), writes=["onec"])
        P.op("pool", lambda e: e.memset(C.onesb[:], 1.0), writes=["onesb"])
    cur = x_in
    fns = {0: None, 1: None, 2: layer_sb, 3: None}
    for i, L in enumerate(layers):
        last = (i == len(layers) - 1)
        FNS[L](P, C, L, T, Wd[L], cur, y_out, last)
        cur = y_out
    P.close()
    return nc


def param_shapes():
    s = {}
    for L, vres in ((0, False), (3, True)):
        p = "l%d_" % L
        ncols = 4 * 1024 + 64 + 64 + (32 if vres else 0)
        s.update({p + "norm": (128, 8), p + "mu": (128, 6, 8), p + "w_in": (1024, ncols), p + "w_up": (64, 1024),
                  p + "w0": (128, 8), p + "a_up": (64, 1024), p + "a0": (128, 8), p + "k_k": (128, 8), p + "k_a": (128, 8),
                  p + "r_k": (128, 8), p + "gn_w": (128, 8), p + "gn_b": (128, 8), p + "w_out": (1024, 1024)})
        if vres:
            s.update({p + "v_up": (32, 1024), p + "v0": (128, 8)})
    s.update({"l1_norm": (128, 8), "l1_w_in": (1024, 6176), "l1_conv_w": (128, 32, 4), "l1_conv_b": (128, 32), "l1_dt_bias": (128, 32),
              "l1_a_log": (128, 32), "l1_d_skip": (128, 16), "l1_gnorm_w": (128, 16), "l1_w_out": (2048, 1024)})
    s.update({"l2_norm": (128, 8), "l2_w_in": (1024, 4096), "l2_q_norm": (128, 1), "l2_k_norm": (128, 1), "l2_w_out": (1024, 1024)})
    return s


FNS = {0: layer_rwkv, 1: layer_ssd, 2: layer_sb, 3: layer_rwkv}


def make_in_maps(inputs, layers, ncores, T):
    hc = host_consts()
    shapes = param_shapes()
    shared = dict(hc)
    for L in layers:
        for pn in LAYER_PARAMS[L]:
            full = "l%d_%s" % (L, pn)
            a = host_layout(full, inputs[full])
            assert tuple(a.shape) == tuple(shapes[full]), (full, a.shape, shapes[full])
            shared[full] = a
    x = np.asarray(inputs["x"], dtype=np.float32)
    maps = []
    for b in range(ncores):
        m = dict(shared)
        m["x"] = np.ascontiguousarray(x[b, :T])
        maps.append(m)
    return maps


ALL_INPUT_NAMES = (
    "x",
    "l0_norm",
    "l0_mu",
    "l0_w_in",
    "l0_w_up",
    "l0_w0",
    "l0_a_up",
    "l0_a0",
    "l0_k_k",
    "l0_k_a",
    "l0_r_k",
    "l0_gn_w",
    "l0_gn_b",
    "l0_w_out",
    "l1_norm",
    "l1_w_in",
    "l1_conv_w",
    "l1_conv_b",
    "l1_dt_bias",
    "l1_a_log",
    "l1_d_skip",
    "l1_gnorm_w",
    "l1_w_out",
    "l2_norm",
    "l2_w_in",
    "l2_q_norm",
    "l2_k_norm",
    "l2_w_out",
    "l3_norm",
    "l3_mu",
    "l3_w_in",
    "l3_w_up",
    "l3_w0",
    "l3_a_up",
    "l3_a0",
    "l3_k_k",
    "l3_k_a",
    "l3_r_k",
    "l3_gn_w",
    "l3_gn_b",
    "l3_w_out",
    "l3_v_up",
    "l3_v0",
)


def kernel(**inputs):
    missing = [n for n in ALL_INPUT_NAMES if n not in inputs]
    assert not missing, missing
    T = 4096
    layers = [0, 1, 2, 3]
    nc = build_program(T, layers)
    maps = make_in_maps(inputs, layers, 8, T)
    res = run_bass_kernel_spmd(nc, maps, core_ids=list(range(8)))
    return np.stack([r["y"] for r in res.results], axis=0).astype(np.float32)
```

```python
import contextlib
import math
import numpy as np
import concourse.bass as bass
import concourse.mybir as mybir
from concourse.bass_utils import run_bass_kernel_spmd

F32 = mybir.dt.float32
BF16 = mybir.dt.bfloat16
AF = mybir.ActivationFunctionType
ALU = mybir.AluOpType
AX = mybir.AxisListType

D = 1024
NORM_EPS = 1e-6
N_DMA_SEMS = 24


class Prog:
    ENGS = ("pe", "act", "dve", "pool", "sp")

    def __init__(self, nc):
        self.nc = nc
        self.root = contextlib.ExitStack()
        self.stack = self.root
        self.ops = {e: [] for e in self.ENGS}
        self.cnt = {e: 0 for e in self.ENGS}
        self.esem = {e: self.root.enter_context(nc.semaphore("s_" + e)) for e in self.ENGS}
        self.dsem = [self.root.enter_context(nc.semaphore("d%d" % i)) for i in range(N_DMA_SEMS)]
        self.dcnt = [0] * N_DMA_SEMS
        self.dlast = [None] * N_DMA_SEMS
        self.ndma = 0
        self.lw = {}
        self.rd = {}
        self.waited = {e: {} for e in self.ENGS}
        self.sem_by_id = {}
        self.final = []
        self.n_inst = 0
        self.uid = 0

    def _sid(self, sem):
        self.sem_by_id[id(sem)] = sem
        return id(sem)

    def sb(self, name, shape, dt=F32, root=False):
        self.uid += 1
        return (self.root if root else self.stack).enter_context(self.nc.sbuf_tensor("%s_%d" % (name, self.uid), list(shape), dt))

    def ps(self, name, shape, dt=F32):
        self.uid += 1
        return self.stack.enter_context(self.nc.psum_tensor("%s_%d" % (name, self.uid), list(shape), dt))

    def _deps(self, e, reads, writes):
        deps = []
        for k in reads:
            t = self.lw.get(k)
            if t is not None:
                deps.append(t)
        for k in writes:
            t = self.lw.get(k)
            if t is not None:
                deps.append(t)
            deps.extend(self.rd.get(k, ()))
        waits = {}
        for (sid, val, src) in deps:
            if src == e and e == "pe":
                continue
            if self.waited[e].get(sid, 0) >= val:
                continue
            if waits.get(sid, 0) < val:
                waits[sid] = val
        for sid, val in waits.items():
            self.waited[e][sid] = val
        return [(self.sem_by_id[sid], val) for sid, val in waits.items()]

    def _record(self, tok, reads, writes):
        for k in reads:
            lst = self.rd.setdefault(k, [])
            lst[:] = [t for t in lst if t[0] != tok[0]]
            lst.append(tok)
        for k in writes:
            self.lw[k] = tok
            self.rd[k] = []

    def op(self, e, fn, reads=(), writes=()):
        waits = self._deps(e, reads, writes)
        self.cnt[e] += 1
        sem = self.esem[e]
        tok = (self._sid(sem), self.cnt[e], e)
        self.ops[e].append((waits, fn, (sem, 1)))
        self._record(tok, reads, writes)
        self.n_inst += 1
        return tok

    def dma(self, out, in_, reads=(), writes=(), e="sp", final=False):
        i = self.ndma % N_DMA_SEMS
        self.ndma += 1
        sem = self.dsem[i]
        waits = self._deps(e, reads, writes)
        prev = self.dlast[i]
        if prev is not None and self.waited[e].get(prev[0], 0) < prev[1]:
            waits.append((sem, prev[1]))
            self.waited[e][prev[0]] = prev[1]
        self.dcnt[i] += 16
        tok = (self._sid(sem), self.dcnt[i], "dma")
        self.dlast[i] = tok
        self.ops[e].append((waits, lambda eng: eng.dma_start(out=out, in_=in_), (sem, 16)))
        self._record(tok, reads, writes)
        if final:
            self.final.append(tok)
        self.n_inst += 1
        return tok

    @contextlib.contextmanager
    def phase(self, last=False):
        outer = self.stack
        with contextlib.ExitStack() as st:
            self.stack = st
            yield
            self._emit(last)
        self.stack = outer

    def _emit(self, last):
        nc = self.nc
        fin = [(self.dsem[i], self.dcnt[i]) for i in range(N_DMA_SEMS) if self.dcnt[i] > 0]
        for i in range(N_DMA_SEMS):
            if self.dcnt[i] > 0:
                self.waited["sp"][id(self.dsem[i])] = self.dcnt[i]
        ops = self.ops
        with nc.Block() as block:
            def body(ename):
                def f(eng):
                    for waits, fn, (sem, inc) in ops[ename]:
                        for (s, v) in waits:
                            eng.wait_ge(s, v)
                        fn(eng).then_inc(sem, inc)
                    if ename == "sp":
                        for (s, v) in fin:
                            eng.wait_ge(s, v)
                return f
            block.sync(body("sp"))
            block.scalar(body("act"))
            block.vector(body("dve"))
            block.gpsimd(body("pool"))
            block.tensor(body("pe"))
        self.ops = {e: [] for e in self.ENGS}

    def close(self):
        self.root.close()


class Ring:
    def __init__(self, P, name, shape, dt=F32, n=2, psum=False):
        mk = P.ps if psum else P.sb
        self.t = [mk("%s%d" % (name, i), shape, dt) for i in range(n)]
        self.k = ["%s#%d#%d" % (name, P.uid, i) for i in range(n)]
        self.i = -1
        self.n = n

    def next(self):
        self.i = (self.i + 1) % self.n
        return self.t[self.i], self.k[self.i]


class Ctx:
    pass


def load_const(P, key, ap_dram, shape, cast=None):
    if cast is None:
        t = P.sb(key, shape, F32, root=True)
        P.dma(t[:], ap_dram, writes=[key])
        return t
    t = P.sb(key + "_f", shape, F32)
    P.dma(t[:], ap_dram, writes=[key + "_f"])
    tb = P.sb(key, shape, cast, root=True)
    P.op("pool", lambda e: e.tensor_copy(out=tb[:], in_=t[:]), reads=[key + "_f"], writes=[key])
    return tb


def load_weight_bf16(P, name, w_dram, kdim, ncols, stage):
    nck = kdim // 128
    wb = P.sb(name, [128, nck, ncols], BF16)
    wv = w_dram.rearrange("(c p) n -> p c n", p=128)
    CW = stage.t[0].shape[1]
    for c in range(nck):
        for c0 in range(0, ncols, CW):
            cw = min(CW, ncols - c0)
            st, sk = stage.next()
            P.dma(st[:, 0:cw], wv[:, c, c0:c0 + cw], writes=[sk])
            P.op("pool", lambda e, st=st, c=c, c0=c0, cw=cw: e.tensor_copy(out=wb[:, c, c0:c0 + cw], in_=st[:, 0:cw]),
                 reads=[sk], writes=[name])
    return wb


def make_hT(P, C, j, src, TT, rings, gcol, eps=NORM_EPS):
    hT, hk = rings["hT"].next()
    for s in range(TT // 128):
        r0 = j * TT + s * 128
        xt, xk = rings["xt"].next()
        P.dma(xt[:], src[r0:r0 + 128, :], reads=[("x", r0 // 128)], writes=[xk])
        sq, sqk = rings["sq"].next()
        ss, ssk = rings["ss"].next()
        P.op("act", lambda e, xt=xt, sq=sq, ss=ss: e.activation(out=sq[:], in_=xt[:], func=AF.Square, accum_out=ss[:, 0:1]),
             reads=[xk], writes=[sqk, ssk])
        P.op("act", lambda e, ss=ss: e.activation(out=ss[:, 1:2], in_=ss[:, 0:1], func=AF.Sqrt, scale=1.0 / D, bias=C.epsc[:, 0:1]),
             reads=[ssk], writes=[ssk])
        P.op("dve", lambda e, ss=ss: e.reciprocal(out=ss[:, 2:3], in_=ss[:, 1:2]), reads=[ssk], writes=[ssk])
        P.op("dve", lambda e, xt=xt, sq=sq, ss=ss: e.tensor_scalar(out=sq[:], in0=xt[:], scalar1=ss[:, 2:3], scalar2=None, op0=ALU.mult),
             reads=[xk, ssk], writes=[sqk])
        for hf in range(2):
            pT, pk = rings["pT"].next()
            for c4 in range(4):
                c = hf * 4 + c4
                P.op("pe", lambda e, pT=pT, sq=sq, c=c, c4=c4: e.transpose(pT[:, c4 * 128:(c4 + 1) * 128], sq[:, c * 128:(c + 1) * 128], C.ident[:]),
                     reads=[sqk, "ident"], writes=[pk])
            for c4 in range(4):
                c = hf * 4 + c4
                P.op("act", lambda e, pT=pT, hT=hT, c=c, c4=c4, s=s: e.activation(
                    out=hT[:, c, 1 + s * 128:1 + (s + 1) * 128], in_=pT[:, c4 * 128:(c4 + 1) * 128], func=AF.Copy, scale=gcol[:, c:c + 1]),
                    reads=[pk, "gcol"], writes=[hk])
    return hT, hk


def out_proj(P, C, L, T, ygT, kdim, w_out, cur, y_out, last):
    nck = kdim // 128
    with P.phase(last=last):
        stage = Ring(P, "wst", [128, 2048], F32, 2)
        wb = load_weight_bf16(P, "wout", w_out, kdim, D, stage)
        ygr = Ring(P, "ygr", [128, nck, 128], BF16, 3)
        xr = Ring(P, "xr", [128, D], F32, 3)
        pr = Ring(P, "po", [128, 512], F32, 4, psum=True)
        ygv = ygT.rearrange("(c p) t -> p c t", p=128)
        for s in range(T // 128):
            yg, ygk = ygr.next()
            P.dma(yg[:], ygv[:, :, s * 128:(s + 1) * 128], reads=[("yg", L, s // 4)], writes=[ygk])
            xt, xk = xr.next()
            P.dma(xt[:], cur[s * 128:(s + 1) * 128, :], reads=[("x", s)], writes=[xk], e="act")
            for hf in range(2):
                po, pk = pr.next()
                for c in range(nck):
                    P.op("pe", lambda e, po=po, yg=yg, c=c, hf=hf: e.matmul(
                        po[:], lhsT=yg[:, c, :], rhs=wb[:, c, hf * 512:(hf + 1) * 512], start=(c == 0), stop=(c == nck - 1)),
                        reads=[ygk, "wout"], writes=[pk])
                P.op("dve", lambda e, po=po, xt=xt, hf=hf: e.tensor_tensor(
                    out=xt[:, hf * 512:(hf + 1) * 512], in0=po[:], in1=xt[:, hf * 512:(hf + 1) * 512], op=ALU.add),
                    reads=[pk, xk], writes=[xk])
            P.dma(y_out[s * 128:(s + 1) * 128, :], xt[:], reads=[xk], writes=[("x", s)], final=last)


def in_rings(P, TT, nh=2, nx=3, nq=2, npt=2):
    return {
        "hT": Ring(P, "hT", [128, 8, 1 + TT], F32, nh),
        "xt": Ring(P, "xt", [128, D], F32, nx),
        "sq": Ring(P, "sq", [128, D], F32, nq),
        "ss": Ring(P, "ss", [128, 4], F32, 4),
        "pT": Ring(P, "pT", [128, 512], F32, npt, psum=True),
    }


def layer_sb(P, C, L, T, W, cur, y_out, last):
    nc = P.nc
    TT = 512
    NT = T // TT
    H = 16
    qT = nc.dram_tensor("sb_qT", [D, T], BF16, kind="Internal").ap()
    kT = nc.dram_tensor("sb_kT", [D, T], BF16, kind="Internal").ap()
    gT = nc.dram_tensor("sb_gT", [D, T], BF16, kind="Internal").ap()
    vTM = nc.dram_tensor("sb_v", [T, D], BF16, kind="Internal").ap()
    ygT = nc.dram_tensor("sb_ygT", [D, T], BF16, kind="Internal").ap()

    with P.phase():
        stage = Ring(P, "wst", [128, 2048], F32, 2)
        wb = load_weight_bf16(P, "win", W["w_in"], D, 4 * D, stage)
        gcol = P.sb("gcol", [128, 8], F32)
        P.dma(gcol[:], W["norm"], writes=["gcol"])
        qn = P.sb("qn", [128, 2], F32)
        P.dma(qn[:, 0:1], W["q_norm"], writes=["qn"])
        P.dma(qn[:, 1:2], W["k_norm"], writes=["qn"])
        P.op("dve", lambda e: e.tensor_scalar(out=qn[:, 0:1], in0=qn[:, 0:1], scalar1=0.125, scalar2=None, op0=ALU.mult),
             reads=["qn"], writes=["qn"])
        rings = in_rings(P, TT)
        hbr = Ring(P, "hb", [128, 8, TT], BF16, 2)
        pp = Ring(P, "pp", [128, 512], F32, 3, psum=True)
        pm = Ring(P, "pm", [128, 512], F32, 2, psum=True)
        sqb = Ring(P, "sqb", [128, 512], BF16, 2)
        rs = Ring(P, "rs", [128, 512], F32, 2)
        ob = Ring(P, "ob", [128, 512], BF16, 4)
        for j in range(NT):
            hT, hk = make_hT(P, C, j, cur, TT, rings, gcol)
            hb, hbk = hbr.next()
            for c in range(8):
                P.op("pool", lambda e, hb=hb, hT=hT, c=c: e.tensor_copy(out=hb[:, c, :], in_=hT[:, c, 1:1 + TT]),
                     reads=[hk], writes=[hbk])
            tsl = slice(j * TT, (j + 1) * TT)
            for which in range(3):
                cbase = {0: 0, 1: D, 2: 3 * D}[which]
                for fc in range(8):
                    p, pk = pp.next()
                    for c in range(8):
                        P.op("pe", lambda e, p=p, c=c, hb=hb, col=cbase + fc * 128: e.matmul(
                            p[:], lhsT=wb[:, c, col:col + 128], rhs=hb[:, c, :], start=(c == 0), stop=(c == 7)),
                            reads=["win", hbk], writes=[pk])
                    o, ok = ob.next()
                    if which == 2:
                        P.op("act", lambda e, p=p, o=o: e.activation(out=o[:], in_=p[:], func=AF.Silu), reads=[pk], writes=[ok])
                        P.dma(gT[fc * 128:(fc + 1) * 128, tsl], o[:], reads=[ok], writes=[("sbg", j)])
                    else:
                        s2, s2k = sqb.next()
                        P.op("act", lambda e, p=p, s2=s2: e.activation(out=s2[:], in_=p[:], func=AF.Square), reads=[pk], writes=[s2k])
                        m, mk = pm.next()
                        P.op("pe", lambda e, m=m, s2=s2: e.matmul(m[:], lhsT=C.bonesb[:], rhs=s2[:], start=True, stop=True),
                             reads=["bonesb", s2k], writes=[mk])
                        r, rk = rs.next()
                        P.op("act", lambda e, m=m, r=r: e.activation(out=r[:], in_=m[:], func=AF.Sqrt, scale=1.0 / 64, bias=C.epsc[:, 0:1]),
                             reads=[mk], writes=[rk])
                        P.op("dve", lambda e, r=r: e.reciprocal(out=r[:], in_=r[:]), reads=[rk], writes=[rk])
                        P.op("dve", lambda e, p=p, r=r, o=o, which=which: e.scalar_tensor_tensor(
                            out=o[:], in0=p[:], scalar=qn[:, which:which + 1], in1=r[:], op0=ALU.mult, op1=ALU.mult),
                            reads=[pk, rk, "qn"], writes=[ok])
                        dst = qT if which == 0 else kT
                        P.dma(dst[fc * 128:(fc + 1) * 128, tsl], o[:], reads=[ok], writes=[("sbq" if which == 0 else "sbk", j)])
            for s in range(TT // 128):
                for cb in range(2):
                    p, pk = pp.next()
                    for c in range(8):
                        P.op("pe", lambda e, p=p, c=c, hb=hb, s=s, cb=cb: e.matmul(
                            p[:], lhsT=hb[:, c, s * 128:(s + 1) * 128], rhs=wb[:, c, 2 * D + cb * 512:2 * D + (cb + 1) * 512],
                            start=(c == 0), stop=(c == 7)), reads=["win", hbk], writes=[pk])
                    o, ok = ob.next()
                    P.op("act", lambda e, p=p, o=o: e.copy(out=o[:], in_=p[:]), reads=[pk], writes=[ok])
                    r0 = j * TT + s * 128
                    P.dma(vTM[r0:r0 + 128, cb * 512:(cb + 1) * 512], o[:], reads=[ok], writes=[("sbv", j)])

    with P.phase():
        NB = T // 128
        kh_r = Ring(P, "kh", [64, T], BF16, 2)
        vh_r = Ring(P, "vh", [128, NB, 64], BF16, 2)
        qh_r = Ring(P, "qh", [64, TT], BF16, 2)
        gh_r = Ring(P, "gh", [64, TT], BF16, 2)
        pz = Ring(P, "pz", [128, 512], F32, 2, psum=True)
        pb = Ring(P, "pb", [128, 512], F32, 2, psum=True)
        pc = Ring(P, "pc", [64, 512], F32, 2, psum=True)
        po = Ring(P, "pov", [64, 512], F32, 2, psum=True)
        Er = Ring(P, "E", [128, 512], F32, 2)
        spr = Ring(P, "sp", [128, 512], BF16, 3)
        Pr = Ring(P, "Pm", [128, 512], BF16, 3)
        fr = Ring(P, "f", [64, 512], F32, 2)
        accr = Ring(P, "acc", [64, 512], F32, 2)
        ygr = Ring(P, "yg", [64, 512], BF16, 2)
        vview = vTM.rearrange("(b p) d -> p b d", p=128)
        qh_r = Ring(P, "qh3", [64, TT], BF16, 3)
        gh_r = Ring(P, "gh3", [64, TT], BF16, 3)
        spr = Ring(P, "sp4", [128, 512], BF16, 4)
        Pr = Ring(P, "Pm4", [128, 512], BF16, 4)
        units = [(h, tq) for h in range(H) for tq in range(NT)]
        ures = {}

        def load_unit(u):
            h, tq = units[u]
            r = {}
            if tq == 0:
                kh, khk = kh_r.next()
                P.dma(kh[:], kT[h * 64:(h + 1) * 64, :], reads=[("sbk", j) for j in range(NT)], writes=[khk])
                vh, vhk = vh_r.next()
                P.dma(vh[:], vview[:, :, h * 64:(h + 1) * 64], reads=[("sbv", j) for j in range(NT)], writes=[vhk], e="act")
                ures[("kv", h)] = (kh, khk, vh, vhk)
            tsl = slice(tq * TT, (tq + 1) * TT)
            r["qh"], r["qhk"] = qh_r.next()
            P.dma(r["qh"][:], qT[h * 64:(h + 1) * 64, tsl], reads=[("sbq", tq)], writes=[r["qhk"]])
            r["gh"], r["ghk"] = gh_r.next()
            P.dma(r["gh"][:], gT[h * 64:(h + 1) * 64, tsl], reads=[("sbg", tq)], writes=[r["ghk"]])
            r["acc"], r["acck"] = accr.next()
            ures[u] = r

        blocks = []
        for u, (h, tq) in enumerate(units):
            nkb = 4 * tq + 4
            for b_ in range(nkb):
                blocks.append((u, h, tq, b_, b_ - 4 * tq, b_ == nkb - 1))
        NBK = len(blocks)
        bres = {}
        load_unit(0)

        def S12(i):
            u, h, tq, b_, r, lastb = blocks[i]
            if b_ == 0 and u + 1 < len(units):
                load_unit(u + 1)
            U_ = ures[u]
            kh, khk, vh, vhk = ures[("kv", h)]
            R = {}
            z, zk = pz.next()
            mm(P, z[:], kh[:, b_ * 128:(b_ + 1) * 128], U_["qh"][:], True, True, [khk, U_["qhk"]], [zk])
            E, Ek = Er.next()
            actf(P, E[:], z[:], AF.Exp, [zk], [Ek])
            sp, spk = spr.next()
            actf(P, sp[:], E[:], AF.Ln, [Ek, "onec"], [spk], bias=C.onec[:, 0:1])
            if r >= 0:
                tt(P, "pool", sp[:], sp[:], C.maskb[:, r, :], ALU.mult, [spk, "maskb"], [spk])
            R["sp"], R["spk"] = sp, spk
            bres[i] = R

        def S34(i):
            u, h, tq, b_, r, lastb = blocks[i]
            U_ = ures[u]
            kh, khk, vh, vhk = ures[("kv", h)]
            R = bres[i]
            sp, spk = R["sp"], R["spk"]
            bb, bk = pb.next()
            mm(P, bb[:], C.ntril[:], sp[:], True, False, ["ntril", spk], [bk])
            mm(P, bb[:], kh[:, b_ * 128:(b_ + 1) * 128], U_["qh"][:], False, True, [khk, U_["qhk"]], [bk])
            cc, ck = pc.next()
            if b_ > 0:
                mm(P, cc[:], C.onesb[:, 0:64], sp[:], True, True, ["onesb", spk], [ck])
            Pm, Pk = Pr.next()
            actf(P, Pm[:], bb[:], AF.Exp, [bk], [Pk])
            if r >= 0:
                tt(P, "pool", Pm[:], Pm[:], C.maskb[:, r, :], ALU.mult, [Pk, "maskb"], [Pk])
            R["Pm"], R["Pk"] = Pm, Pk
            if b_ > 0:
                f, fk = fr.next()
                actf(P, f[:], cc[:], AF.Exp, [ck], [fk], scale=-1.0)
                R["f"], R["fk"] = f, fk

        def S56(i):
            u, h, tq, b_, r, lastb = blocks[i]
            U_ = ures[u]
            kh, khk, vh, vhk = ures[("kv", h)]
            R = bres.pop(i)
            acc, acck = U_["acc"], U_["acck"]
            ov, ovk = po.next()
            mm(P, ov[:], vh[:, b_, :], R["Pm"][:], True, True, [vhk, R["Pk"]], [ovk])
            if b_ == 0:
                P.op("dve", lambda e, acc=acc, ov=ov: e.tensor_copy(out=acc[:], in_=ov[:]), reads=[ovk], writes=[acck])
            else:
                tt(P, "dve", acc[:], acc[:], R["f"][:], ALU.mult, [acck, R["fk"]], [acck])
                tt(P, "dve", acc[:], ov[:], acc[:], ALU.add, [acck, ovk], [acck])
            if lastb:
                tsl = slice(tq * TT, (tq + 1) * TT)
                yg, ygk = ygr.next()
                tt(P, "dve", yg[:], acc[:], U_["gh"][:], ALU.mult, [acck, U_["ghk"]], [ygk])
                P.dma(ygT[h * 64:(h + 1) * 64, tsl], yg[:], reads=[ygk], writes=[("yg", L, tq)])
                del ures[u]

        for i in range(NBK + 2):
            if i < NBK:
                S12(i)
            if 0 <= i - 1 < NBK:
                S34(i - 1)
            if 0 <= i - 2 < NBK:
                S56(i - 2)

    out_proj(P, C, L, T, ygT, D, W["w_out"], cur, y_out, last)


def layer_ssd(P, C, L, T, W, cur, y_out, last):
    nc = P.nc
    TT = 512
    NT = T // TT
    NCH = T // 128
    xbcT = nc.dram_tensor("ssd_xbcT", [4096, T], BF16, kind="Internal").ap()
    zT = nc.dram_tensor("ssd_zT", [2048, T], BF16, kind="Internal").ap()
    dtD = nc.dram_tensor("ssd_dt", [T, 64], F32, kind="Internal").ap()
    ygT = nc.dram_tensor("ssd_ygT", [2048, T], BF16, kind="Internal").ap()

    with P.phase():
        stage = Ring(P, "wst", [128, 2048], F32, 2)
        wb = load_weight_bf16(P, "win", W["w_in"], D, 6176, stage)
        gcol = P.sb("gcol", [128, 8], F32)
        P.dma(gcol[:], W["norm"], writes=["gcol"])
        cw = P.sb("cw", [128, 32, 4], F32)
        P.dma(cw[:], W["conv_w"], writes=["cw"])
        cbias = P.sb("cbias", [128, 32], F32)
        P.dma(cbias[:], W["conv_b"], writes=["cbias"])
        dtb = P.sb("dtb", [128, 32], F32)
        P.dma(dtb[:], W["dt_bias"], writes=["dtb"])
        Arep = P.sb("Arep", [128, 32], F32)
        P.dma(Arep[:], W["a_log"], writes=["Arep"])
        P.op("act", lambda e: e.activation(out=Arep[:], in_=Arep[:], func=AF.Exp), reads=["Arep"], writes=["Arep"])
        P.op("dve", lambda e: e.tensor_scalar(out=Arep[:], in0=Arep[:], scalar1=-1.0, scalar2=None, op0=ALU.mult),
             reads=["Arep"], writes=["Arep"])
        hist = P.sb("hist", [128, 32, 3], F32)
        P.op("pool", lambda e: e.memset(hist[:], 0.0), writes=["hist"])
        rings = in_rings(P, TT, nh=1, nx=2)
        hbr = Ring(P, "hb", [128, 8, TT], BF16, 2)
        pp = Ring(P, "pp", [128, 512], F32, 3, psum=True)
        pd = Ring(P, "pd", [128, 64], F32, 2, psum=True)
        xcr = Ring(P, "xc", [128, 3 + TT], F32, 2)
        cvr = Ring(P, "cv", [128, TT], F32, 2)
        ob = Ring(P, "ob", [128, 512], BF16, 4)
        dtr = Ring(P, "dtr", [128, 64], F32, 2)
        for j in range(NT):
            hT, hk = make_hT(P, C, j, cur, TT, rings, gcol)
            hb, hbk = hbr.next()
            for c in range(8):
                P.op("pool", lambda e, hb=hb, hT=hT, c=c: e.tensor_copy(out=hb[:, c, :], in_=hT[:, c, 1:1 + TT]),
                     reads=[hk], writes=[hbk])
            tsl = slice(j * TT, (j + 1) * TT)
            for fc in range(48):
                p, pk = pp.next()
                for c in range(8):
                    P.op("pe", lambda e, p=p, c=c, hb=hb, col=fc * 128: e.matmul(
                        p[:], lhsT=wb[:, c, col:col + 128], rhs=hb[:, c, :], start=(c == 0), stop=(c == 7)),
                        reads=["win", hbk], writes=[pk])
                o, ok = ob.next()
                if fc < 16:
                    P.op("act", lambda e, p=p, o=o: e.activation(out=o[:], in_=p[:], func=AF.Silu), reads=[pk], writes=[ok])
                    P.dma(zT[fc * 128:(fc + 1) * 128, tsl], o[:], reads=[ok], writes=[("ssdz", j)])
                else:
                    cc = fc - 16
                    xc, xck = xcr.next()
                    P.op("pool", lambda e, xc=xc, cc=cc: e.tensor_copy(out=xc[:, 0:3], in_=hist[:, cc, :]), reads=["hist"], writes=[xck])
                    P.op("act", lambda e, xc=xc, p=p: e.copy(out=xc[:, 3:3 + TT], in_=p[:]), reads=[pk], writes=[xck])
                    P.op("pool", lambda e, xc=xc, cc=cc: e.tensor_copy(out=hist[:, cc, :], in_=xc[:, TT:TT + 3]), reads=[xck], writes=["hist"])
                    cv, cvk = cvr.next()
                    P.op("dve", lambda e, cv=cv, xc=xc, cc=cc: e.tensor_scalar(
                        out=cv[:], in0=xc[:, 0:TT], scalar1=cw[:, cc, 0:1], scalar2=None, op0=ALU.mult), reads=[xck, "cw"], writes=[cvk])
                    for kk in range(1, 4):
                        P.op("dve", lambda e, cv=cv, xc=xc, cc=cc, kk=kk: e.scalar_tensor_tensor(
                            out=cv[:], in0=xc[:, kk:kk + TT], scalar=cw[:, cc, kk:kk + 1], in1=cv[:], op0=ALU.mult, op1=ALU.add),
                            reads=[xck, "cw", cvk], writes=[cvk])
                    P.op("act", lambda e, cv=cv, o=o, cc=cc: e.activation(out=o[:], in_=cv[:], func=AF.Silu, bias=cbias[:, cc:cc + 1]),
                         reads=[cvk, "cbias"], writes=[ok])
                    P.dma(xbcT[cc * 128:(cc + 1) * 128, tsl], o[:], reads=[ok], writes=[("ssdx", j)])
            for s in range(TT // 128):
                p, pk = pd.next()
                for c in range(8):
                    P.op("pe", lambda e, p=p, c=c, hb=hb, s=s: e.matmul(
                        p[:, 0:32], lhsT=hb[:, c, s * 128:(s + 1) * 128], rhs=wb[:, c, 6144:6176], start=(c == 0), stop=(c == 7)),
                        reads=["win", hbk], writes=[pk])
                d, dk = dtr.next()
                P.op("dve", lambda e, d=d, p=p: e.tensor_tensor(out=d[:, 0:32], in0=p[:, 0:32], in1=dtb[:], op=ALU.add),
                     reads=[pk, "dtb"], writes=[dk])
                P.op("act", lambda e, d=d: e.activation(out=d[:, 0:32], in_=d[:, 0:32], func=AF.Exp), reads=[dk], writes=[dk])
                P.op("act", lambda e, d=d: e.activation(out=d[:, 0:32], in_=d[:, 0:32], func=AF.Ln, bias=C.onec[:, 0:1]),
                     reads=[dk], writes=[dk])
                P.op("dve", lambda e, d=d: e.tensor_tensor(out=d[:, 32:64], in0=d[:, 0:32], in1=Arep[:], op=ALU.mult),
                     reads=[dk, "Arep"], writes=[dk])
                r0 = j * TT + s * 128
                P.dma(dtD[r0:r0 + 128, :], d[:], reads=[dk], writes=[("ssdd", j)])

    with P.phase():
        dsk = P.sb("dsk", [128, 16], F32)
        P.dma(dsk[:], W["d_skip"], writes=["dsk"])
        gnw = P.sb("gnw", [128, 16], F32)
        P.dma(gnw[:], W["gnorm_w"], writes=["gnw"])
        hsf = P.sb("hsf", [128, 32, 64], F32)
        hsb = P.sb("hsb", [128, 32, 64], BF16)
        for g in range(8):
            P.op("pool", lambda e, g=g: e.memset(hsf[:, 4 * g:4 * g + 4, :], 0.0), writes=[("hsf", g)])
            P.op("pool", lambda e, g=g: e.memset(hsb[:, 4 * g:4 * g + 4, :], 0.0), writes=[("hsb", g)])
        xbr = Ring(P, "xb", [128, 32, 128], BF16, 2)
        zr = Ring(P, "zs", [128, 16, 128], BF16, 2)
        dr = Ring(P, "dta", [128, 64], F32, 2)
        p_cb = Ring(P, "pcb", [128, 512], F32, 2, psum=True)
        p_rb = Ring(P, "prb", [128, 512], F32, 2, psum=True)
        p_y = Ring(P, "py", [128, 512], F32, 1, psum=True)
        p_ms = Ring(P, "pms", [128, 512], F32, 1, psum=True)
        p_tr = Ring(P, "ptr", [128, 1024], BF16, 1, psum=True)
        p_ac = Ring(P, "pac", [128, 512], F32, 1, psum=True)
        nacr = Ring(P, "nac", [128, 32], F32, 2)
        cdr = Ring(P, "cd", [128, 32], F32, 2)
        dsr = Ring(P, "ds", [128, 32], F32, 2)
        xdtr = Ring(P, "xdt", [128, 32, 64], BF16, 2)
        xsr = Ring(P, "xs", [128, 32, 64], BF16, 2)
        bmr = Ring(P, "bm", [128, 1024], BF16, 2)
        cbmr = Ring(P, "cbm", [128, 128], F32, 3)
        E4r = Ring(P, "E4", [128, 4, 128], F32, 3)
        EL4r = Ring(P, "EL4", [128, 4, 128], F32, 3)
        MT4r = Ring(P, "MT4", [128, 4, 128], BF16, 3)
        CS4r = Ring(P, "CS4", [128, 4, 128], BF16, 3)
        y1r = Ring(P, "y1", [128, 2, 128], F32, 2)
        Yr = Ring(P, "Y", [128, 16, 128], F32, 2)
        sqr = Ring(P, "sqy", [128, 2, 128], BF16, 2)
        rrr = Ring(P, "rr", [128, 128], F32, 2)
        YGr = Ring(P, "YG", [128, 16, 128], BF16, 2)
        xv = xbcT.rearrange("(c p) t -> p c t", p=128)
        zv = zT.rearrange("(c p) t -> p c t", p=128)
        ygv = ygT.rearrange("(c p) t -> p c t", p=128)
        cres = {}

        def prologue(ch):
            csl = slice(ch * 128, (ch + 1) * 128)
            jt = ch // 4
            R = {}
            xb, xbk = xbr.next()
            P.dma(xb[:], xv[:, :, csl], reads=[("ssdx", jt)], writes=[xbk])
            zs, zsk = zr.next()
            P.dma(zs[:], zv[:, :, csl], reads=[("ssdz", jt)], writes=[zsk], e="act")
            da, dak = dr.next()
            P.dma(da[:], dtD[csl, :], reads=[("ssdd", jt)], writes=[dak])
            ac, ack = p_ac.next()
            P.op("pe", lambda e: e.matmul(ac[:, 0:32], lhsT=C.triu[:], rhs=da[:, 32:64], start=True, stop=True),
                 reads=["triu", dak], writes=[ack])
            P.op("pe", lambda e: e.matmul(ac[:, 32:64], lhsT=C.onesf[:], rhs=da[:, 32:64], start=True, stop=True),
                 reads=["onesf", dak], writes=[ack])
            nac, nack = nacr.next()
            ts(P, "dve", nac[:], ac[:, 0:32], -1.0, None, ALU.mult, None, [ack], [nack])
            cd, cdk = cdr.next()
            actf(P, cd[:], ac[:, 32:64], AF.Exp, [ack], [cdk])
            ds, dsk_ = dsr.next()
            tt(P, "dve", ds[:], ac[:, 32:64], nac[:], ALU.add, [ack, nack], [dsk_])
            actf(P, ds[:], ds[:], AF.Exp, [dsk_], [dsk_])
            xdt, xdtk = xdtr.next()
            xs, xsk = xsr.next()
            for hf in range(2):
                tr, trk = p_tr.next()
                for q in range(8):
                    trp(P, tr[:, q * 128:(q + 1) * 128], xb[:, hf * 8 + q, :], C.identb[:], [xbk, "identb"], [trk])
                tt(P, "dve", xdt[:, hf * 16:(hf + 1) * 16, :], tr[:].rearrange("p (h d) -> p h d", d=64),
                   da[:, hf * 16:(hf + 1) * 16].unsqueeze(2).broadcast_to([128, 16, 64]), ALU.mult, [trk, dak], [xdtk])
            tt(P, "pool", xs[:], xdt[:], ds[:].unsqueeze(2).broadcast_to([128, 32, 64]), ALU.mult, [xdtk, dsk_], [xsk])
            tr, trk = p_tr.next()
            for q in range(8):
                trp(P, tr[:, q * 128:(q + 1) * 128], xb[:, 16 + q, :], C.identb[:], [xbk, "identb"], [trk])
            bm, bmk = bmr.next()
            actf(P, bm[:], tr[:], AF.Copy, [trk], [bmk])
            Y, Yk = Yr.next()
            YG, YGk = YGr.next()
            R.update(xb=xb, xbk=xbk, zs=zs, zsk=zsk, da=da, dak=dak, nac=nac, nack=nack, cd=cd, cdk=cdk, xdt=xdt, xdtk=xdtk,
                     xs=xs, xsk=xsk, bm=bm, bmk=bmk, Y=Y, Yk=Yk, YG=YG, YGk=YGk)
            cres[ch] = R

        gres = {}

        def G12(i):
            ch, g = divmod(i, 8)
            if g == 0:
                prologue(ch)
            R = cres[ch]
            xb, xbk, da, dak, nac, nack = R["xb"], R["xbk"], R["da"], R["dak"], R["nac"], R["nack"]
            G = {}
            cb, cbk = p_cb.next()
            mm(P, cb[:, 0:128], xb[:, 16 + g, :], xb[:, 24 + g, :], True, True, [xbk], [cbk])
            rb, rbk = p_rb.next()
            for r in range(4):
                h = 4 * g + r
                mm(P, rb[:, r * 128:(r + 1) * 128], da[:, 32 + h:33 + h].broadcast_to([128, 128]), C.triu[:], True, True, [dak, "triu"], [rbk])
            cbm, cbmk = cbmr.next()
            tt(P, "dve", cbm[:], cb[:, 0:128], C.triu[:], ALU.mult, [cbk, "triu"], [cbmk])
            E4, E4k = E4r.next()
            for r in range(4):
                h = 4 * g + r
                ts(P, "dve", E4[:, r, :], rb[:, r * 128:(r + 1) * 128], nac[:, h:h + 1], 0.0, ALU.add, ALU.min, [rbk, nack], [E4k])
            actf(P, E4[:].rearrange("p r l -> p (r l)"), E4[:].rearrange("p r l -> p (r l)"), AF.Exp, [E4k], [E4k])
            MT4, MT4k = MT4r.next()
            tt(P, "pool", MT4[:], E4[:], cbm[:].unsqueeze(1).broadcast_to([128, 4, 128]), ALU.mult, [E4k, cbmk], [MT4k])
            G.update(MT4=MT4, MT4k=MT4k)
            if ch > 0:
                EL4, EL4k = EL4r.next()
                actf(P, EL4[:].rearrange("p r l -> p (r l)"), rb[:], AF.Exp, [rbk, E4k], [EL4k])
                CS4, CS4k = CS4r.next()
                tt(P, "pool", CS4[:], EL4[:], xb[:, 24 + g, :].unsqueeze(1).broadcast_to([128, 4, 128]), ALU.mult, [EL4k, xbk], [CS4k])
                G.update(CS4=CS4, CS4k=CS4k)
            gres[i] = G

        def G34(i):
            ch, g = divmod(i, 8)
            R = cres[ch]
            G = gres.pop(i)
            xb, xbk, zs, zsk = R["xb"], R["xbk"], R["zs"], R["zsk"]
            xdt, xdtk, xs, xsk, bm, bmk = R["xdt"], R["xdtk"], R["xs"], R["xsk"], R["bm"], R["bmk"]
            Y, Yk, YG, YGk, cd, cdk = R["Y"], R["Yk"], R["YG"], R["YGk"], R["cd"], R["cdk"]
            py, pyk = p_y.next()
            for r in range(4):
                h = 4 * g + r
                osl = slice((r % 2) * 64, (r % 2) * 64 + 64)
                pc = slice((r // 2) * 128, (r // 2) * 128 + 128)
                mm(P, py[osl, pc], xdt[:, h, :], G["MT4"][:, r, :], True, ch == 0, [xdtk, G["MT4k"]], [pyk])
                if ch > 0:
                    mm(P, py[osl, pc], hsb[:, h, :], G["CS4"][:, r, :], False, True, [("hsb", g), G["CS4k"]], [pyk])
            y1, y1k = y1r.next()
            for pp_ in range(2):
                pq_ = 2 * g + pp_
                stt(P, y1[:, pp_, :], xb[:, pq_, :], dsk[:, pq_:pq_ + 1], py[:, pp_ * 128:(pp_ + 1) * 128], ALU.mult, ALU.add,
                    [xbk, "dsk", pyk], [y1k])
            tt(P, "pool", Y[:, 2 * g:2 * g + 2, :], y1[:], zs[:, 2 * g:2 * g + 2, :], ALU.mult, [y1k, zsk], [(Yk, g)])
            sq, sqk = sqr.next()
            actf(P, sq[:], Y[:, 2 * g:2 * g + 2, :], AF.Square, [(Yk, g)], [sqk])
            ms, msk = p_ms.next()
            for pp_ in range(2):
                mm(P, ms[:, 256:384], C.onesb[:], sq[:, pp_, :], pp_ == 0, pp_ == 1, ["onesb", sqk], [msk])
            rr, rrk = rrr.next()
            actf(P, rr[:], ms[:, 256:384], AF.Sqrt, [msk], [rrk], scale=1.0 / 256, bias=C.epsc[:, 0:1])
            P.op("dve", lambda e, rr=rr: e.reciprocal(out=rr[:], in_=rr[:]), reads=[rrk], writes=[rrk])
            for pq_ in (2 * g, 2 * g + 1):
                stt(P, YG[:, pq_, :], Y[:, pq_, :], gnw[:, pq_:pq_ + 1], rr[:], ALU.mult, ALU.mult, [(Yk, g), rrk, "gnw"], [YGk])
            if ch < NCH - 1:
                mm(P, ms[:, 0:256], bm[:, g * 128:(g + 1) * 128], xs[:, 4 * g:4 * g + 4, :].rearrange("p h d -> p (h d)"), True, True,
                   [bmk, xsk], [msk])
                tt(P, "dve", hsf[:, 4 * g:4 * g + 4, :], hsf[:, 4 * g:4 * g + 4, :],
                   cd[:, 4 * g:4 * g + 4].unsqueeze(2).broadcast_to([128, 4, 64]), ALU.mult, [("hsf", g), cdk], [("hsf", g)])
                tt(P, "dve", hsf[:, 4 * g:4 * g + 4, :], ms[:, 0:256].rearrange("p (h d) -> p h d", d=64), hsf[:, 4 * g:4 * g + 4, :],
                   ALU.add, [("hsf", g), msk], [("hsf", g)])
                actf(P, hsb[:, 4 * g:4 * g + 4, :], hsf[:, 4 * g:4 * g + 4, :], AF.Copy, [("hsf", g)], [("hsb", g)])
            if g == 7:
                csl = slice(ch * 128, (ch + 1) * 128)
                P.dma(ygv[:, :, csl], YG[:], reads=[YGk], writes=[("yg", L, ch // 4)])
                del cres[ch]

        NG = NCH * 8
        G12(0)
        for i in range(NG):
            if i + 1 < NG:
                G12(i + 1)
            G34(i)

    out_proj(P, C, L, T, ygT, 2048, W["w_out"], cur, y_out, last)


def tt(P, eng, out, in0, in1, op, r, w):
    return P.op(eng, lambda e: e.tensor_tensor(out=out, in0=in0, in1=in1, op=op), reads=r, writes=w)


def ts(P, eng, out, in0, s1, s2, op0, op1, r, w):
    if s2 is None:
        return P.op(eng, lambda e: e.tensor_scalar(out=out, in0=in0, scalar1=s1, scalar2=None, op0=op0), reads=r, writes=w)
    return P.op(eng, lambda e: e.tensor_scalar(out=out, in0=in0, scalar1=s1, scalar2=s2, op0=op0, op1=op1), reads=r, writes=w)


def stt(P, out, in0, scalar, in1, op0, op1, r, w):
    return P.op("dve", lambda e: e.scalar_tensor_tensor(out=out, in0=in0, scalar=scalar, in1=in1, op0=op0, op1=op1), reads=r, writes=w)


def actf(P, out, in_, func, r, w, bias=None, scale=None):
    kw = {}
    if bias is not None:
        kw["bias"] = bias
    if scale is not None:
        kw["scale"] = scale
    return P.op("act", lambda e: e.activation(out=out, in_=in_, func=func, **kw), reads=r, writes=w)


def mm(P, out, lhsT, rhs, start, stop, r, w):
    return P.op("pe", lambda e: e.matmul(out, lhsT=lhsT, rhs=rhs, start=start, stop=stop), reads=r, writes=w)


def trp(P, out, in_, ident, r, w):
    return P.op("pe", lambda e: e.transpose(out, in_, ident), reads=r, writes=w)


RW_GN_EPS = 64e-5
NEG_EXP_HALF = -math.exp(-0.5)
CDT = F32


def layer_rwkv(P, C, L, T, W, cur, y_out, last):
    nc = P.nc
    vres = (L == 3)
    TT = 256
    NT = T // TT
    NCK = T // 64
    ncols = 4 * D + 128 + (32 if vres else 0)
    pre = "rw%d_" % L
    names = ["At", "Bt", "Kt", "Rt", "Bg", "Kg", "V", "BON"]
    Dm = {n: nc.dram_tensor(pre + n, [D, T], (F32 if n == "BON" else BF16), kind="Internal").ap() for n in names}
    gCd = nc.dram_tensor(pre + "gC", [D, NCK], F32, kind="Internal").ap()
    gTd = nc.dram_tensor(pre + "gT", [D, T], BF16, kind="Internal").ap()
    yTd = nc.dram_tensor(pre + "yT", [D, T], F32, kind="Internal").ap()
    if L == 0:
        C.vfirst = nc.dram_tensor("rw_vfirst", [D, T], F32, kind="Internal").ap()
    vfd = C.vfirst

    def colp(name, n=8):
        t = P.sb(name, [128, n], F32)
        P.dma(t[:], W[name], writes=[name])
        return t

    with P.phase():
        stage = Ring(P, "wst", [128, 1024], F32, 2)
        wb = load_weight_bf16(P, "win", W["w_in"], D, ncols, stage)
        gcol = P.sb("gcol", [128, 8], F32)
        P.dma(gcol[:], W["norm"], writes=["gcol"])
        mu = P.sb("mu", [128, 6, 8], F32)
        P.dma(mu[:], W["mu"], writes=["mu"])
        w0 = colp("w0"); a0 = colp("a0"); k_k = colp("k_k"); k_a = colp("k_a"); gdum = None
        r_k = P.sb("r_k", [128, 8], F32)
        P.dma(r_k[:], W["r_k"], writes=["r_k"])
        omka = P.sb("omka", [128, 8], F32)
        ts(P, "dve", omka[:], k_a[:], -1.0, 1.0, ALU.mult, ALU.add, ["k_a"], ["omka"])
        def lowrank(nm, rows):
            t = P.sb(nm, [rows, D], BF16)
            st, sk = stage.next()
            P.dma(st[0:rows, :], W[nm], writes=[sk])
            P.op("pool", lambda e: e.tensor_copy(out=t[:], in_=st[0:rows, :]), reads=[sk], writes=[nm])
            return t
        wup = lowrank("w_up", 64)
        aup = lowrank("a_up", 64)
        if vres:
            v0 = colp("v0")
            vup = lowrank("v_up", 32)
        rings = in_rings(P, TT, nh=1, nx=2, nq=1, npt=1)
        carry = P.sb("carry", [128, 8, 1], F32)
        P.op("pool", lambda e: e.memset(carry[:], 0.0), writes=["carry"])
        dxT = P.sb("dxT", [128, 8, TT], F32)
        xmix = {i: P.sb("xm%d" % i, [128, 8, TT], BF16) for i in (0, 2, 3)}
        xsm = Ring(P, "xsm", [128, 8, TT], BF16, 2)
        wl = P.sb("wl", [64, TT], BF16); al = P.sb("al", [64, TT], BF16); vl = P.sb("vl", [32, TT], BF16)
        PB = [P.ps("rwa%d" % i, [128, 512], F32) for i in range(7)]
        PBk = ["rwabank%d_%d" % (L, i) for i in range(7)]

        def half(bi, hi):
            return PB[bi][:, hi * TT:(hi + 1) * TT], PBk[bi]

        class _PQ:
            i = -1

            def next(self):
                self.i = (self.i + 1) % 7
                return PB[self.i], PBk[self.i]
        pq = _PQ()
        F = lambda nm, n=2, dt=F32: Ring(P, nm, [128, TT], dt, n)
        R_ = {nm: F(nm, 2) for nm in ["sg", "lw", "cum", "gi", "ge", "gv", "gd", "aa", "kkr", "nrm", "kk", "t1", "kf", "d1", "ka", "sv", "vf"]}
        R_.update({nm: F(nm, 2) for nm in ["vv", "BON"]})
        R_.update({nm: F(nm, 2, BF16) for nm in ["At", "Bt", "Bg", "Kt", "Kg", "Rt", "vb"]})
        R_["sqb"] = F("sqb", 2, BF16); R_["rkb"] = F("rkb", 2, BF16); R_["go"] = F("go", 2, BF16)
        gCr = Ring(P, "gCt", [128, TT // 64], F32, 2)
        for j in range(NT):
            hT, hk = make_hT(P, C, j, cur, TT, rings, gcol)
            P.op("pool", lambda e, hT=hT: e.tensor_copy(out=hT[:, :, 0:1], in_=carry[:]), reads=["carry"], writes=[hk])
            P.op("pool", lambda e, hT=hT: e.tensor_copy(out=carry[:], in_=hT[:, :, TT:TT + 1]), reads=[hk], writes=["carry"])
            tt(P, "pool", dxT[:], hT[:, :, 0:TT], hT[:, :, 1:TT + 1], ALU.subtract, [hk], ["dxT"])
            tsl = slice(j * TT, (j + 1) * TT)

            def mix(i, dst, dk):
                for c in range(8):
                    stt(P, dst[:, c, :], dxT[:, c, :], mu[:, i, c:c + 1], hT[:, c, 1:TT + 1], ALU.mult, ALU.add,
                        ["dxT", "mu", hk], [dk])

            xw, xwk = xsm.next(); mix(1, xw, xwk)
            p, pk = pq.next()
            for c in range(8):
                mm(P, p[0:64, 0:TT], wb[:, c, 4096:4160], xw[:, c, :], c == 0, c == 7, ["win", xwk], [pk])
            actf(P, wl[:], p[0:64, 0:TT], AF.Tanh, [pk], ["wl"])
            xa, xak = xsm.next(); mix(4, xa, xak)
            p, pk = pq.next()
            for c in range(8):
                mm(P, p[0:64, 0:TT], wb[:, c, 4160:4224], xa[:, c, :], c == 0, c == 7, ["win", xak], [pk])
            actf(P, al[:], p[0:64, 0:TT], AF.Copy, [pk], ["al"])
            xg, xgk = xsm.next(); mix(5, xg, xgk)
            for fc in range(8):
                p, pk = pq.next()
                for c in range(8):
                    mm(P, p[:, 0:TT], wb[:, c, 3072 + fc * 128:3072 + (fc + 1) * 128], xg[:, c, :], c == 0, c == 7, ["win", xgk], [pk])
                go, gok = R_["go"].next()
                actf(P, go[:], p[:, 0:TT], AF.Silu, [pk], [gok])
                P.dma(gTd[fc * 128:(fc + 1) * 128, tsl], go[:], reads=[gok], writes=[(pre + "g", j)])
            for i in (0, 2, 3):
                mix(i, xmix[i], "xm%d" % i)
            if vres:
                p, pk = pq.next()
                for c in range(8):
                    mm(P, p[0:32, 0:TT], wb[:, c, 4224:4256], xmix[3][:, c, :], c == 0, c == 7, ["win", "xm3"], [pk])
                actf(P, vl[:], p[0:32, 0:TT], AF.Copy, [pk], ["vl"])
            def fc_body(fc, slot):
                    fs = slice(fc * 128, (fc + 1) * 128)
                    col = lambda t_: t_[:, fc:fc + 1]
                    pr, prk = half(3 * slot, 0); pkp, pkk = half(3 * slot, 1); pv, pvk = half(3 * slot + 1, 0)
                    pn0, pn0k = half(6, slot); pw2, pw2k = half(3 * slot + 2, 0); pa2, pa2k = half(3 * slot + 2, 1)
                    pv2, pv2k = half(3 * slot + 1, 1)
                    mm(P, pw2, wup[:, fs], wl[:], True, True, ["w_up", "wl"], [pw2k])
                    mm(P, pa2, aup[:, fs], al[:], True, True, ["a_up", "al"], [pa2k])
                    for c in range(8):
                        mm(P, pkp, wb[:, c, D + fc * 128:D + (fc + 1) * 128], xmix[2][:, c, :], c == 0, c == 7, ["win", "xm2"], [pkk])
                    for c in range(8):
                        mm(P, pr, wb[:, c, fc * 128:(fc + 1) * 128], xmix[0][:, c, :], c == 0, c == 7, ["win", "xm0"], [prk])
                    yield
                    for c in range(8):
                        mm(P, pv, wb[:, c, 2 * D + fc * 128:2 * D + (fc + 1) * 128], xmix[3][:, c, :], c == 0, c == 7, ["win", "xm3"], [pvk])
                    if vres:
                        mm(P, pv2, vup[:, fs], vl[:], True, True, ["v_up", "vl"], [pv2k])
                    sg, sgk = R_["sg"].next()
                    actf(P, sg[:], pw2, AF.Sigmoid, [pw2k, "w0"], [sgk], bias=col(w0))
                    lw, lwk = R_["lw"].next()
                    ts(P, "dve", lw[:], sg[:], NEG_EXP_HALF, None, ALU.mult, None, [sgk], [lwk])
                    cum, cumk = R_["cum"].next()
                    P.op("dve", lambda e, cum=cum, lw=lw: e.tensor_tensor_scan(
                        out=cum[:], data0=C.mask0[:, 0:TT], data1=lw[:], initial=0.0, op0=ALU.mult, op1=ALU.add),
                        reads=[lwk, "mask0"], writes=[cumk])
                    yield
                    gi, gik = R_["gi"].next()
                    actf(P, gi[:], cum[:], AF.Exp, [cumk], [gik])
                    ge, gek = R_["ge"].next()
                    tt(P, "pool", ge[:], cum[:], lw[:], ALU.subtract, [cumk, lwk], [gek])
                    actf(P, ge[:], ge[:], AF.Exp, [gek], [gek])
                    gv, gvk = R_["gv"].next()
                    actf(P, gv[:], cum[:], AF.Exp, [cumk], [gvk], scale=-1.0)
                    yield
                    gd, gdk = R_["gd"].next()
                    cum3 = cum[:].rearrange("p (c t) -> p c t", t=64)
                    tt(P, "pool", gd[:].rearrange("p (c t) -> p c t", t=64), cum3[:, :, 63:64].broadcast_to([128, TT // 64, 64]), cum3,
                       ALU.subtract, [cumk], [gdk])
                    actf(P, gd[:], gd[:], AF.Exp, [gdk], [gdk])
                    gCt, gCk = gCr.next()
                    actf(P, gCt[:].unsqueeze(2), cum3[:, :, 63:64], AF.Exp, [cumk], [gCk])
                    P.dma(gCd[fs, j * (TT // 64):(j + 1) * (TT // 64)], gCt[:], reads=[gCk], writes=[(pre + "gC", j)])
                    yield
                    aa, aak = R_["aa"].next()
                    actf(P, aa[:], pa2, AF.Sigmoid, [pa2k, "a0"], [aak], bias=col(a0))
                    kkr, kkrk = R_["kkr"].next()
                    ts(P, "dve", kkr[:], pkp, col(k_k), None, ALU.mult, None, [pkk, "k_k"], [kkrk])
                    sqb, sqbk = R_["sqb"].next()
                    actf(P, sqb[:], kkr[:], AF.Square, [kkrk], [sqbk])
                    mm(P, pn0, C.bonesb[:], sqb[:], True, True, ["bonesb", sqbk], [pn0k])
                    yield
                    nrm, nrmk = R_["nrm"].next()
                    actf(P, nrm[:], pn0, AF.Sqrt, [pn0k], [nrmk])
                    ts(P, "dve", nrm[:], nrm[:], 1e-12, None, ALU.max, None, [nrmk], [nrmk])
                    P.op("dve", lambda e, nrm=nrm: e.reciprocal(out=nrm[:], in_=nrm[:]), reads=[nrmk], writes=[nrmk])
                    kk, kkk = R_["kk"].next()
                    tt(P, "dve", kk[:], kkr[:], nrm[:], ALU.mult, [kkrk, nrmk], [kkk])
                    yield
                    t1, t1k = R_["t1"].next()
                    ts(P, "dve", t1[:], aa[:], col(k_a), col(omka), ALU.mult, ALU.add, [aak, "k_a", "omka"], [t1k])
                    kf, kfk = R_["kf"].next()
                    tt(P, "dve", kf[:], pkp, t1[:], ALU.mult, [pkk, t1k], [kfk])
                    yield
                    vv, vvk = R_["vv"].next()
                    if not vres:
                        actf(P, vv[:], pv, AF.Copy, [pvk], [vvk])
                        if L == 0:
                            P.dma(vfd[fs, tsl], vv[:], reads=[vvk], writes=[("vfirst", j)])
                    else:
                        vf, vfk = R_["vf"].next()
                        P.dma(vf[:], vfd[fs, tsl], reads=[("vfirst", j)], writes=[vfk])
                        sv, svk = R_["sv"].next()
                        actf(P, sv[:], pv2, AF.Sigmoid, [pv2k, "v0"], [svk], bias=col(v0))
                        d1, d1k = R_["d1"].next()
                        tt(P, "dve", d1[:], vf[:], pv, ALU.subtract, [vfk, pvk], [d1k])
                        tt(P, "pool", d1[:], d1[:], sv[:], ALU.mult, [d1k, svk], [d1k])
                        tt(P, "dve", vv[:], d1[:], pv, ALU.add, [d1k, pvk], [vvk])
                    vb, vbk = R_["vb"].next()
                    P.op("pool", lambda e, vb=vb, vv=vv: e.tensor_copy(out=vb[:], in_=vv[:]), reads=[vvk], writes=[vbk])
                    P.dma(Dm["V"][fs, tsl], vb[:], reads=[vbk], writes=[(pre + "V", j)])
                    yield
                    At, Atk = R_["At"].next()
                    stt(P, At[:], kk[:], -1.0, ge[:], ALU.mult, ALU.mult, [kkk, gek], [Atk])
                    P.dma(Dm["At"][fs, tsl], At[:], reads=[Atk], writes=[(pre + "At", j)])
                    ka, kak = R_["ka"].next()
                    tt(P, "pool", ka[:], kk[:], aa[:], ALU.mult, [kkk, aak], [kak])
                    Bt, Btk = R_["Bt"].next()
                    tt(P, "dve", Bt[:], ka[:], gv[:], ALU.mult, [kak, gvk], [Btk])
                    P.dma(Dm["Bt"][fs, tsl], Bt[:], reads=[Btk], writes=[(pre + "Bt", j)])
                    yield
                    Bg, Bgk = R_["Bg"].next()
                    tt(P, "pool", Bg[:], ka[:], gd[:], ALU.mult, [kak, gdk], [Bgk])
                    P.dma(Dm["Bg"][fs, tsl], Bg[:], reads=[Bgk], writes=[(pre + "Bg", j)])
                    Kt, Ktk = R_["Kt"].next()
                    tt(P, "pool", Kt[:], kf[:], gv[:], ALU.mult, [kfk, gvk], [Ktk])
                    P.dma(Dm["Kt"][fs, tsl], Kt[:], reads=[Ktk], writes=[(pre + "Kt", j)])
                    yield
                    Kg, Kgk = R_["Kg"].next()
                    tt(P, "pool", Kg[:], kf[:], gd[:], ALU.mult, [kfk, gdk], [Kgk])
                    P.dma(Dm["Kg"][fs, tsl], Kg[:], reads=[Kgk], writes=[(pre + "Kg", j)])
                    Rt, Rtk = R_["Rt"].next()
                    tt(P, "dve", Rt[:], pr, gi[:], ALU.mult, [prk, gik], [Rtk])
                    P.dma(Dm["Rt"][fs, tsl], Rt[:], reads=[Rtk], writes=[(pre + "Rt", j)])
                    yield
                    rkb, rkbk = R_["rkb"].next()
                    stt(P, rkb[:], pr, col(r_k), kf[:], ALU.mult, ALU.mult, [prk, "r_k", kfk], [rkbk])
                    mm(P, pn0, C.bonesb[:], rkb[:], True, True, ["bonesb", rkbk], [pn0k])
                    BON, BONk = R_["BON"].next()
                    tt(P, "dve", BON[:], pn0, vv[:], ALU.mult, [pn0k, vvk], [BONk])
                    P.dma(Dm["BON"][fs, tsl], BON[:], reads=[BONk], writes=[(pre + "BON", j)])

            for pair in range(4):
                gens = [fc_body(2 * pair, 0), fc_body(2 * pair + 1, 1)]
                while gens:
                    for g_ in list(gens):
                        try:
                            next(g_)
                        except StopIteration:
                            gens.remove(g_)

    import os
    if os.environ.get("RW_STOP") == "A":
        return
    with P.phase():
        TB = 128
        NCB = TB // 64
        opn = ["At", "Bt", "Kt", "Rt", "Bg", "Kg", "V"]
        opr = {n: Ring(P, "o" + n, [128, 8, TB], BF16, 2) for n in opn}
        gCr2 = Ring(P, "gC2", [128, 8, NCB], F32, 2)
        ST = P.sb("ST", [128, 8, 64], F32)
        STb = P.sb("STb", [128, 8, 64], BF16)
        P.op("pool", lambda e: e.memset(ST[:], 0.0), writes=["ST"])
        P.op("pool", lambda e: e.memset(STb[:], 0.0), writes=["STb"])
        YT = Ring(P, "YT", [128, 8, TB], F32, 2)
        bk = [P.ps("rwbk%d" % i, [128, 512], F32) for i in range(7)]
        bkk = ["rwbank%d_%d" % (L, i) for i in range(7)]

        class RingOf:
            def __init__(self, ids):
                self.ids = ids
                self.i = -1

            def next(self):
                self.i = (self.i + 1) % len(self.ids)
                j = self.ids[self.i]
                return bk[j], bkk[j]

        pA5 = RingOf([0, 1, 2])
        pZ = RingOf([0, 1, 2])
        pW = RingOf([2, 3])
        pU = RingOf([3]); pUO = RingOf([4]); pY2 = RingOf([5]); pS = RingOf([6])
        pT = Ring(P, "pT", [128, 1024], BF16, 1, psum=True)
        tmr = {n: Ring(P, "tm" + n, [64, D], BF16, 2) for n in ["At", "V", "Bg", "Kg"]}
        A5r = Ring(P, "A5s", [64, 192], BF16, 20)
        Zall = [P.sb("Zall%d" % i, [64, 16, 256], BF16) for i in range(2)]
        ZcAr = Ring(P, "ZcA", [64, 16, 64], BF16, 2)
        ZcPr = Ring(P, "ZcP", [64, 16, 64], F32, 2)
        ApT = Ring(P, "ApT", [128, 8, 64], BF16, 2)
        Ur = Ring(P, "U", [64, 16, 64], BF16, 2)
        Y1r = Ring(P, "Y1s", [64, 512], F32, 2)
        Ysr = Ring(P, "Ys", [64, D], BF16, 2)
        tmpS = Ring(P, "tmpS", [128, 4, 64], F32, 2)
        for jb in range(T // TB):
            tsl = slice(jb * TB, (jb + 1) * TB)
            jA = (jb * TB) // TT
            ot = {}
            for n in opn:
                t_, k_ = opr[n].next()
                P.dma(t_[:], Dm[n].rearrange("(c p) t -> p c t", p=128)[:, :, tsl], reads=[(pre + n, jA)], writes=[k_],
                      e=("act" if n in ("Bg", "Kg", "V") else "sp"))
                ot[n] = (t_, k_)
            gC2, gC2k = gCr2.next()
            P.dma(gC2[:], gCd.rearrange("(c p) k -> p c k", p=128)[:, :, jb * NCB:(jb + 1) * NCB], reads=[(pre + "gC", jA)], writes=[gC2k])
            yt, ytk = YT.next()
            for cq in range(NCB):
                cs = slice(cq * 64, (cq + 1) * 64)
                tm = {}
                for n in ["At", "V", "Bg", "Kg"]:
                    dst, dk = tmr[n].next()
                    src, sk = ot[n]
                    for hf in range(2):
                        p, pk = pT.next()
                        for q in range(4):
                            fc = hf * 4 + q
                            trp(P, p[0:64, q * 128:(q + 1) * 128], src[:, fc, cs], C.identb[:], [sk, "identb"], [pk])
                        actf(P, dst[:, hf * 512:(hf + 1) * 512], p[0:64, 0:512], AF.Copy, [pk], [(dk, hf)])
                    tm[n] = (dst, dk)
                ZcA, Zck = ZcAr.next()
                ZcP, _zp = ZcPr.next()
                apt, aptk = ApT.next()
                A5h = {}
                Z0 = Zall[0]
                zkey = lambda par, fc, part: ("Zall", L, par, fc, part)
                for h in range(16):
                    fc, e_ = h // 2, h % 2
                    rows = slice(e_ * 64, e_ * 64 + 64)
                    a5, a5k = pA5.next()
                    Kt_ = ot["Kt"][0][rows, fc, cs]; Bt_ = ot["Bt"][0][rows, fc, cs]
                    At_ = ot["At"][0][rows, fc, cs]; Rt_ = ot["Rt"][0][rows, fc, cs]
                    rk5 = [ot["Kt"][1], ot["Bt"][1], ot["At"][1], ot["Rt"][1]]
                    mm(P, a5[0:64, 0:64], Kt_, At_, True, True, rk5, [a5k])
                    mm(P, a5[0:64, 64:128], Kt_, Rt_, True, True, rk5, [a5k])
                    mm(P, a5[0:64, 128:192], Bt_, Rt_, True, True, rk5, [a5k])
                    mm(P, a5[0:64, 192:256], At_, Bt_, True, True, rk5, [a5k])
                    mm(P, a5[0:64, 256:320], Bt_, At_, True, True, rk5, [a5k])
                    a5s, a5sk = A5r.next()
                    tt(P, "dve", a5s[:], a5[0:64, 0:192], C.mask5[:, 0:192], ALU.mult, [a5k, "mask5"], [a5sk])
                    tt(P, "dve", Z0[:, h, 128:256], a5[0:64, 192:320], C.mask5[:, 192:320], ALU.mult, [a5k, "mask5"], [zkey(0, fc, "L")])
                    A5h[h] = (a5s, a5sk)
                P.op("pool", lambda e, Z0=Z0, src=tm["At"][0]: e.tensor_copy(out=Z0[:, :, 0:64], in_=src[:].rearrange("p (h d) -> p h d", d=64)),
                     reads=[(tm["At"][1], 0), (tm["At"][1], 1)], writes=[zkey(0, fc, "X0") for fc in range(8)])
                for hf in range(2):
                    pw_, pwk = pW.next()
                    for hh in range(8):
                        h = hf * 8 + hh
                        a5s, a5sk = A5h[h]
                        mm(P, pw_[0:64, hh * 64:(hh + 1) * 64], a5s[:, 0:64], tm["V"][0][:, h * 64:(h + 1) * 64], True, True,
                           [a5sk, (tm["V"][1], hf)], [pwk])
                    actf(P, Z0[:, hf * 8:(hf + 1) * 8, 64:128], pw_[0:64, :].rearrange("p (h v) -> p h v", v=64), AF.Copy, [pwk],
                         [zkey(0, fc, "X1") for fc in range(hf * 4, hf * 4 + 4)])
                for lev in range(6):
                    par = lev % 2
                    Zc_, Zn_ = Zall[par], Zall[1 - par]
                    for fc in range(8):
                        pz, pzk = pZ.next()
                        rdk = [zkey(par, fc, "X0"), zkey(par, fc, "X1"), zkey(par, fc, "L")]
                        for e_ in range(2):
                            h = 2 * fc + e_
                            lt_ap = Zc_[:, h, 192:256]
                            ncol = 192 if lev < 5 else 128
                            mm(P, pz[0:64, e_ * 256:e_ * 256 + ncol], lt_ap, Zc_[:, h, 0:ncol], True, True, rdk, [pzk])
                            if lev < 5:
                                mm(P, pz[0:64, e_ * 256 + 192:e_ * 256 + 256], Zc_[:, h, 128:192], lt_ap, True, True, rdk, [pzk])
                        pz3 = pz[0:64, :].rearrange("p (e c) -> p e c", c=256)
                        hs = slice(2 * fc, 2 * fc + 2)
                        if lev < 5:
                            tt(P, "dve", Zn_[:, hs, 0:128], pz3[:, :, 0:128], Zc_[:, hs, 0:128], ALU.add, [pzk] + rdk,
                               [zkey(1 - par, fc, "X0"), zkey(1 - par, fc, "X1")])
                            P.op("dve", lambda e, Zn_=Zn_, pz3=pz3, hs=hs: e.tensor_copy(out=Zn_[:, hs, 128:256], in_=pz3[:, :, 128:256]),
                                 reads=[pzk], writes=[zkey(1 - par, fc, "L")])
                        else:
                            tt(P, "dve", ZcA[:, hs, :], pz3[:, :, 0:64], Zc_[:, hs, 0:64], ALU.add, [pzk] + rdk, [(Zck, fc)])
                            tt(P, "dve", ZcP[:, hs, :], pz3[:, :, 64:128], Zc_[:, hs, 64:128], ALU.add, [pzk] + rdk, [(Zck, fc)])
                p, pk = pT.next()
                for fc in range(8):
                    trp(P, p[:, fc * 64:(fc + 1) * 64], ZcA[:, 2 * fc:2 * fc + 2, :].rearrange("p e d -> p (e d)"), C.identb[0:64, 0:64],
                        [(Zck, fc), "identb"], [pk])
                actf(P, apt[:].rearrange("p c t -> p (c t)"), p[:, 0:512], AF.Copy, [pk], [aptk])
                ys, ysk = Ysr.next()
                U, Uk = Ur.next()
                for hf in range(2):
                    bE, bEk = pU.next()
                    bO, bOk = pUO.next()
                    banks = [(bE, bEk), (bO, bOk)]
                    for hh in range(8):
                        h = hf * 8 + hh
                        fc, e_ = h // 2, h % 2
                        q = hh // 2
                        rows = slice(e_ * 64, e_ * 64 + 64)
                        bank, bk_ = banks[e_]
                        mm(P, bank[0:64, q * 64:(q + 1) * 64], apt[rows, fc, :], STb[rows, fc, :], True, True, [aptk, "STb"], [bk_])
                    for e_ in range(2):
                        bank, bk_ = banks[e_]
                        Uv = U[:, hf * 8:(hf + 1) * 8, :].rearrange("p (q e) v -> p q e v", e=2)[:, :, e_, :]
                        Pv = ZcP[:, hf * 8:(hf + 1) * 8, :].rearrange("p (q e) v -> p q e v", e=2)[:, :, e_, :]
                        tt(P, "dve", Uv, bank[0:64, 0:256].rearrange("p (q v) -> p q v", v=64), Pv, ALU.add,
                           [bk_] + [(Zck, fc) for fc in range(hf * 4, hf * 4 + 4)], [(Uk, hf)])
                    py2, py2k = pY2.next()
                    for hh in range(8):
                        h = hf * 8 + hh
                        fc, e_ = h // 2, h % 2
                        q = hh // 2
                        rows = slice(e_ * 64, e_ * 64 + 64)
                        bank, bk_ = banks[e_]
                        mm(P, bank[0:64, 256 + q * 64:256 + (q + 1) * 64], ot["Rt"][0][rows, fc, cs], STb[rows, fc, :], True, True,
                           [ot["Rt"][1], "STb"], [bk_])
                        a5s, a5sk = A5h[h]
                        mm(P, py2[0:64, hh * 64:(hh + 1) * 64], a5s[:, 128:192], U[:, h, :], True, False, [a5sk, (Uk, hf)], [py2k])
                        mm(P, py2[0:64, hh * 64:(hh + 1) * 64], a5s[:, 64:128], tm["V"][0][:, h * 64:(h + 1) * 64], False, True,
                           [a5sk, (tm["V"][1], hf)], [py2k])
                    y1s, y1sk = Y1r.next()
                    for e_ in range(2):
                        bank, bk_ = banks[e_]
                        actf(P, y1s[:].rearrange("p (q e v) -> p q e v", e=2, v=64)[:, :, e_, :],
                             bank[0:64, 256:512].rearrange("p (q v) -> p q v", v=64), AF.Copy, [bk_], [y1sk])
                    tt(P, "dve", ys[:, hf * 512:(hf + 1) * 512], py2[0:64, :], y1s[:], ALU.add, [py2k, y1sk], [(ysk, hf)])
                    ps_, psk = pS.next()
                    for hh in range(8):
                        h = hf * 8 + hh
                        fc, e_ = h // 2, h % 2
                        orow = slice(e_ * 64, e_ * 64 + 64)
                        ocol = slice((fc % 4) * 64, (fc % 4) * 64 + 64)
                        mm(P, ps_[orow, ocol], tm["Bg"][0][:, h * 64:(h + 1) * 64], U[:, h, :], True, False,
                           [(tm["Bg"][1], hf), (Uk, hf)], [psk])
                        mm(P, ps_[orow, ocol], tm["Kg"][0][:, h * 64:(h + 1) * 64], tm["V"][0][:, h * 64:(h + 1) * 64], False, True,
                           [(tm["Kg"][1], hf), (tm["V"][1], hf)], [psk])
                    tS, tSk = tmpS.next()
                    fsl = slice(hf * 4, hf * 4 + 4)
                    tt(P, "dve", tS[:], ST[:, fsl, :], gC2[:, fsl, cq:cq + 1].broadcast_to([128, 4, 64]), ALU.mult,
                       ["ST", gC2k], [tSk])
                    tt(P, "dve", ST[:, fsl, :], ps_[:, 0:256].rearrange("p (c v) -> p c v", v=64), tS[:], ALU.add, [psk, tSk], ["ST"])
                    actf(P, STb[:, fsl, :], ST[:, fsl, :], AF.Copy, ["ST"], ["STb"])
                for hf in range(2):
                    p, pk = pT.next()
                    for q in range(4):
                        fc = hf * 4 + q
                        trp(P, p[:, q * 64:(q + 1) * 64], ys[:, fc * 128:(fc + 1) * 128], C.identb[0:64, 0:64], [(ysk, hf), "identb"], [pk])
                    P.op("dve", lambda e, yt=yt, p=p, hf=hf, cs=cs: e.tensor_copy(
                        out=yt[:, hf * 4:(hf + 1) * 4, cs], in_=p[:, 0:256].rearrange("p (c t) -> p c t", t=64)), reads=[pk], writes=[ytk])
            P.dma(yTd.rearrange("(c p) t -> p c t", p=128)[:, :, tsl], yt[:], reads=[ytk], writes=[(pre + "yT", jb)])

    if os.environ.get("RW_STOP") == "B":
        return
    with P.phase(last=last):
        TC = 512
        stage = Ring(P, "wst", [128, 2048], F32, 2)
        wob = load_weight_bf16(P, "wout", W["w_out"], D, D, stage)
        gnw = colp("gn_w"); gnb = colp("gn_b")
        yr = Ring(P, "yin", [128, 8, TC], F32, 2)
        br = Ring(P, "bin", [128, 8, TC], F32, 2)
        gr = Ring(P, "gin", [128, 8, TC], BF16, 2)
        ygr = Ring(P, "ygo", [128, 8, TC], BF16, 2)
        pm = Ring(P, "pm", [128, 512], F32, 2, psum=True)
        pv_ = Ring(P, "pv", [128, 512], F32, 2, psum=True)
        po = Ring(P, "po", [128, 512], F32, 3, psum=True)
        ycr = Ring(P, "yc", [128, TC], F32, 2)
        sqr = Ring(P, "sq2", [128, TC], BF16, 2)
        rsr = Ring(P, "rs2", [128, TC], F32, 2)
        xr = Ring(P, "xr", [128, D], F32, 3)
        gneps = P.sb("gneps", [128, 1], F32)
        P.op("pool", lambda e: e.memset(gneps[:], RW_GN_EPS), writes=["gneps"])
        for j in range(T // TC):
            tsl = slice(j * TC, (j + 1) * TC)
            yi, yik = yr.next(); bi, bik = br.next(); gi_, gik_ = gr.next()
            P.dma(yi[:], yTd.rearrange("(c p) t -> p c t", p=128)[:, :, tsl], reads=[(pre + "yT", jj) for jj in range(4 * j, 4 * j + 4)], writes=[yik])
            P.dma(bi[:], Dm["BON"].rearrange("(c p) t -> p c t", p=128)[:, :, tsl], reads=[(pre + "BON", jj) for jj in (2 * j, 2 * j + 1)], writes=[bik], e="act")
            P.dma(gi_[:], gTd.rearrange("(c p) t -> p c t", p=128)[:, :, tsl], reads=[(pre + "g", jj) for jj in (2 * j, 2 * j + 1)], writes=[gik_])
            yg, ygk = ygr.next()
            for fc in range(8):
                m, mk = pm.next()
                for c0 in range(0, TC, 128):
                    mm(P, m[:, c0:c0 + 128], C.bonesf[:], yi[:, fc, c0:c0 + 128], True, True, ["bonesf", yik], [mk])
                yc, yck = ycr.next()
                stt(P, yc[:], m[:], -1.0 / 64, yi[:, fc, :], ALU.mult, ALU.add, [mk, yik], [yck])
                sq, sqk = sqr.next()
                actf(P, sq[:], yc[:], AF.Square, [yck], [sqk])
                v_, vk = pv_.next()
                mm(P, v_[:], C.bonesb[:], sq[:], True, True, ["bonesb", sqk], [vk])
                rs, rsk = rsr.next()
                actf(P, rs[:], v_[:], AF.Sqrt, [vk, "gneps"], [rsk], bias=gneps[:, 0:1], scale=1.0 / 64)
                P.op("dve", lambda e, rs=rs: e.reciprocal(out=rs[:], in_=rs[:]), reads=[rsk], writes=[rsk])
                tt(P, "pool", yc[:], yc[:], rs[:], ALU.mult, [yck, rsk], [yck])
                ts(P, "dve", yc[:], yc[:], gnw[:, fc:fc + 1], gnb[:, fc:fc + 1], ALU.mult, ALU.add, [yck, "gn_w", "gn_b"], [yck])
                tt(P, "pool", yc[:], yc[:], bi[:, fc, :], ALU.add, [yck, bik], [yck])
                tt(P, "dve", yg[:, fc, :], yc[:], gi_[:, fc, :], ALU.mult, [yck, gik_], [(ygk, fc)])
            for s in range(TC // 128):
                r0 = j * TC + s * 128
                xt, xk = xr.next()
                P.dma(xt[:], cur[r0:r0 + 128, :], reads=[("x", r0 // 128)], writes=[xk], e="act")
                for hf in range(2):
                    o, ok = po.next()
                    for c in range(8):
                        mm(P, o[:], yg[:, c, s * 128:(s + 1) * 128], wob[:, c, hf * 512:(hf + 1) * 512], c == 0, c == 7,
                           [(ygk, c), "wout"], [ok])
                    tt(P, "dve", xt[:, hf * 512:(hf + 1) * 512], o[:], xt[:, hf * 512:(hf + 1) * 512], ALU.add, [ok, xk], [xk])
                P.dma(y_out[r0:r0 + 128, :], xt[:], reads=[xk], writes=[("x", r0 // 128)], final=last)


LAYER_PARAMS = {
    0: ["norm", "mu", "w_in", "w_up", "w0", "a_up", "a0", "k_k", "k_a", "r_k", "gn_w", "gn_b", "w_out"],
    1: ["norm", "w_in", "conv_w", "conv_b", "dt_bias", "a_log", "d_skip", "gnorm_w", "w_out"],
    2: ["norm", "w_in", "q_norm", "k_norm", "w_out"],
    3: ["norm", "mu", "w_in", "w_up", "w0", "a_up", "a0", "k_k", "k_a", "r_k", "gn_w", "gn_b", "w_out", "v_up", "v0"],
}


def host_consts():
    c = {}
    c["c_ident"] = np.eye(128, dtype=np.float32)
    i = np.arange(128)
    c["c_bones"] = (i[:, None] // 64 == i[None, :] // 64).astype(np.float32)
    tq = np.arange(256)
    c["c_mask0"] = np.tile(((tq % 64) != 0).astype(np.float32)[None, :], (128, 1))
    ss_, tt_ = np.arange(64)[:, None], np.arange(64)[None, :]
    strict = (ss_ < tt_).astype(np.float32); incl = (ss_ <= tt_).astype(np.float32)
    c["c_mask5"] = np.concatenate([strict, incl, incl, strict.T, strict], axis=1)
    c["c_triu"] = (i[:, None] <= i[None, :]).astype(np.float32)
    c["c_ntril"] = -(i[:, None] >= i[None, :]).astype(np.float32)
    t = np.arange(512)
    c["c_mask"] = np.stack([((128 * r + i)[:, None] < t[None, :]).astype(np.float32) for r in range(4)], axis=1)
    return c


def host_layout(name, arr):
    a = np.asarray(arr, dtype=np.float32)
    short = name.split("_", 1)[1]
    if short == "norm":
        return np.ascontiguousarray(a.reshape(8, 128).T)
    if short in ("q_norm", "k_norm"):
        return np.ascontiguousarray(np.tile(a, 2).reshape(128, 1))
    if short in ("w0", "a0", "k_k", "k_a", "gn_w", "gn_b", "v0", "r_k"):
        return np.ascontiguousarray(a.reshape(8, 128).T)
    if short == "mu":
        return np.ascontiguousarray(a.reshape(6, 8, 128).transpose(2, 0, 1))
    if name == "l1_conv_w":
        return np.ascontiguousarray(a.T.reshape(32, 128, 4).transpose(1, 0, 2))
    if name == "l1_conv_b":
        return np.ascontiguousarray(a.reshape(32, 128).T)
    if name in ("l1_dt_bias", "l1_a_log"):
        return np.ascontiguousarray(np.tile(a[None, :], (128, 1)))
    if name == "l1_d_skip":
        return np.ascontiguousarray(np.repeat(a, 64).reshape(16, 128).T)
    if name == "l1_gnorm_w":
        return np.ascontiguousarray(a.reshape(16, 128).T)
    return np.ascontiguousarray(a)


def build_program(T, layers):
    nc = bass.Bass("TRN2", target_bir_lowering=False)
    x_in = nc.dram_tensor("x", [T, D], F32, kind="ExternalInput").ap()
    y_out = nc.dram_tensor("y", [T, D], F32, kind="ExternalOutput").ap()
    hc = host_consts()
    cap = {k: nc.dram_tensor(k, list(v.shape), F32, kind="ExternalInput").ap() for k, v in hc.items()}
    Wd = {}
    shapes = param_shapes()
    for L in layers:
        Wd[L] = {}
        for pn in LAYER_PARAMS[L]:
            full = "l%d_%s" % (L, pn)
            Wd[L][pn] = nc.dram_tensor(full, list(shapes[full]), F32, kind="ExternalInput").ap()
    P = Prog(nc)
    C = Ctx()
    specs = [("ident", "c_ident", [128, 128], None), ("bonesb", "c_bones", [128, 128], BF16),
             ("ntril", "c_ntril", [128, 128], BF16), ("maskb", "c_mask", [128, 4, 512], BF16),
             ("triu", "c_triu", [128, 128], None), ("identb", "c_ident", [128, 128], BF16),
             ("mask0", "c_mask0", [128, 256], None), ("mask5", "c_mask5", [64, 320], None),
             ("bonesf", "c_bones", [128, 128], None)]
    tiles = {}
    for key, cn, shape, cast in specs:
        tiles[key] = P.sb(key, shape, cast or F32)
        setattr(C, key, tiles[key])
    C.onesf = P.sb("onesf", [128, 128], F32)
    C.epsc = P.sb("epsc", [128, 1], F32)
    C.onec = P.sb("onec", [128, 1], F32)
    C.onesb = P.sb("onesb", [128, 128], BF16)
    with P.phase():
        for key, cn, shape, cast in specs:
            t = tiles[key]
            if cast is None:
                P.dma(t[:], cap[cn], writes=[key])
            else:
                st = P.sb(key + "_f", shape, F32)
                P.dma(st[:], cap[cn], writes=[key + "_f"])
                P.op("pool", lambda e, t=t, st=st: e.tensor_copy(out=t[:], in_=st[:]), reads=[key + "_f"], writes=[key])
        P.op("pool", lambda e: e.memset(C.onesf[:], 1.0), writes=["onesf"])
        P.op("pool", lambda e: e.memset(C.epsc[:], NORM_EPS), writes=["epsc"])
        P.op("pool", lambda e: e.memset(C.onec[:], 1.0), writes=["onec"])
        P.op("pool", lambda e: e.memset(C.onesb[:], 1.0), writes=["onesb"])
    cur = x_in
    fns = {0: None, 1: None, 2: layer_sb, 3: None}
    for i, L in enumerate(layers):
        last = (i == len(layers) - 1)
        FNS[L](P, C, L, T, Wd[L], cur, y_out, last)
        cur = y_out
    P.close()
    return nc


def param_shapes():
    s = {}
    for L, vres in ((0, False), (3, True)):
        p = "l%d_" % L
        ncols = 4 * 1024 + 64 + 64 + (32 if vres else 0)
        s.update({p + "norm": (128, 8), p + "mu": (128, 6, 8), p + "w_in": (1024, ncols), p + "w_up": (64, 1024),
                  p + "w0": (128, 8), p + "a_up": (64, 1024), p + "a0": (128, 8), p + "k_k": (128, 8), p + "k_a": (128, 8),
                  p + "r_k": (128, 8), p + "gn_w": (128, 8), p + "gn_b": (128, 8), p + "w_out": (1024, 1024)})
        if vres:
            s.update({p + "v_up": (32, 1024), p + "v0": (128, 8)})
    s.update({"l1_norm": (128, 8), "l1_w_in": (1024, 6176), "l1_conv_w": (128, 32, 4), "l1_conv_b": (128, 32), "l1_dt_bias": (128, 32),
              "l1_a_log": (128, 32), "l1_d_skip": (128, 16), "l1_gnorm_w": (128, 16), "l1_w_out": (2048, 1024)})
    s.update({"l2_norm": (128, 8), "l2_w_in": (1024, 4096), "l2_q_norm": (128, 1), "l2_k_norm": (128, 1), "l2_w_out": (1024, 1024)})
    return s


FNS = {0: layer_rwkv, 1: layer_ssd, 2: layer_sb, 3: layer_rwkv}


def make_in_maps(inputs, layers, ncores, T):
    hc = host_consts()
    shapes = param_shapes()
    shared = dict(hc)
    for L in layers:
        for pn in LAYER_PARAMS[L]:
            full = "l%d_%s" % (L, pn)
            a = host_layout(full, inputs[full])
            assert tuple(a.shape) == tuple(shapes[full]), (full, a.shape, shapes[full])
            shared[full] = a
    x = np.asarray(inputs["x"], dtype=np.float32)
    maps = []
    for b in range(ncores):
        m = dict(shared)
        m["x"] = np.ascontiguousarray(x[b, :T])
        maps.append(m)
    return maps


ALL_INPUT_NAMES = (
    "x",
    "l0_norm",
    "l0_mu",
    "l0_w_in",
    "l0_w_up",
    "l0_w0",
    "l0_a_up",
    "l0_a0",
    "l0_k_k",
    "l0_k_a",
    "l0_r_k",
    "l0_gn_w",
    "l0_gn_b",
    "l0_w_out",
    "l1_norm",
    "l1_w_in",
    "l1_conv_w",
    "l1_conv_b",
    "l1_dt_bias",
    "l1_a_log",
    "l1_d_skip",
    "l1_gnorm_w",
    "l1_w_out",
    "l2_norm",
    "l2_w_in",
    "l2_q_norm",
    "l2_k_norm",
    "l2_w_out",
    "l3_norm",
    "l3_mu",
    "l3_w_in",
    "l3_w_up",
    "l3_w0",
    "l3_a_up",
    "l3_a0",
    "l3_k_k",
    "l3_k_a",
    "l3_r_k",
    "l3_gn_w",
    "l3_gn_b",
    "l3_w_out",
    "l3_v_up",
    "l3_v0",
)


def kernel(**inputs):
    missing = [n for n in ALL_INPUT_NAMES if n not in inputs]
    assert not missing, missing
    T = 4096
    layers = [0, 1, 2, 3]
    nc = build_program(T, layers)
    maps = make_in_maps(inputs, layers, 8, T)
    res = run_bass_kernel_spmd(nc, maps, core_ids=list(range(8)))
    return np.stack([r["y"] for r in res.results], axis=0).astype(np.float32)
```

```python
import contextlib
import math
import numpy as np
import concourse.bass as bass
import concourse.mybir as mybir
from concourse.bass_utils import run_bass_kernel_spmd

F32 = mybir.dt.float32
BF16 = mybir.dt.bfloat16
AF = mybir.ActivationFunctionType
ALU = mybir.AluOpType
AX = mybir.AxisListType

D = 1024
NORM_EPS = 1e-6
N_DMA_SEMS = 24


class Prog:
    ENGS = ("pe", "act", "dve", "pool", "sp")

    def __init__(self, nc):
        self.nc = nc
        self.root = contextlib.ExitStack()
        self.stack = self.root
        self.ops = {e: [] for e in self.ENGS}
        self.cnt = {e: 0 for e in self.ENGS}
        self.esem = {e: self.root.enter_context(nc.semaphore("s_" + e)) for e in self.ENGS}
        self.dsem = [self.root.enter_context(nc.semaphore("d%d" % i)) for i in range(N_DMA_SEMS)]
        self.dcnt = [0] * N_DMA_SEMS
        self.dlast = [None] * N_DMA_SEMS
        self.ndma = 0
        self.lw = {}
        self.rd = {}
        self.waited = {e: {} for e in self.ENGS}
        self.sem_by_id = {}
        self.final = []
        self.n_inst = 0
        self.uid = 0

    def _sid(self, sem):
        self.sem_by_id[id(sem)] = sem
        return id(sem)

    def sb(self, name, shape, dt=F32, root=False):
        self.uid += 1
        return (self.root if root else self.stack).enter_context(self.nc.sbuf_tensor("%s_%d" % (name, self.uid), list(shape), dt))

    def ps(self, name, shape, dt=F32):
        self.uid += 1
        return self.stack.enter_context(self.nc.psum_tensor("%s_%d" % (name, self.uid), list(shape), dt))

    def _deps(self, e, reads, writes):
        deps = []
        for k in reads:
            t = self.lw.get(k)
            if t is not None:
                deps.append(t)
        for k in writes:
            t = self.lw.get(k)
            if t is not None:
                deps.append(t)
            deps.extend(self.rd.get(k, ()))
        waits = {}
        for (sid, val, src) in deps:
            if src == e and e == "pe":
                continue
            if self.waited[e].get(sid, 0) >= val:
                continue
            if waits.get(sid, 0) < val:
                waits[sid] = val
        for sid, val in waits.items():
            self.waited[e][sid] = val
        return [(self.sem_by_id[sid], val) for sid, val in waits.items()]

    def _record(self, tok, reads, writes):
        for k in reads:
            lst = self.rd.setdefault(k, [])
            lst[:] = [t for t in lst if t[0] != tok[0]]
            lst.append(tok)
        for k in writes:
            self.lw[k] = tok
            self.rd[k] = []

    def op(self, e, fn, reads=(), writes=()):
        waits = self._deps(e, reads, writes)
        self.cnt[e] += 1
        sem = self.esem[e]
        tok = (self._sid(sem), self.cnt[e], e)
        self.ops[e].append((waits, fn, (sem, 1)))
        self._record(tok, reads, writes)
        self.n_inst += 1
        return tok

    def dma(self, out, in_, reads=(), writes=(), e="sp", final=False):
        i = self.ndma % N_DMA_SEMS
        self.ndma += 1
        sem = self.dsem[i]
        waits = self._deps(e, reads, writes)
        prev = self.dlast[i]
        if prev is not None and self.waited[e].get(prev[0], 0) < prev[1]:
            waits.append((sem, prev[1]))
            self.waited[e][prev[0]] = prev[1]
        self.dcnt[i] += 16
        tok = (self._sid(sem), self.dcnt[i], "dma")
        self.dlast[i] = tok
        self.ops[e].append((waits, lambda eng: eng.dma_start(out=out, in_=in_), (sem, 16)))
        self._record(tok, reads, writes)
        if final:
            self.final.append(tok)
        self.n_inst += 1
        return tok

    @contextlib.contextmanager
    def phase(self, last=False):
        outer = self.stack
        with contextlib.ExitStack() as st:
            self.stack = st
            yield
            self._emit(last)
        self.stack = outer

    def _emit(self, last):
        nc = self.nc
        fin = [(self.dsem[i], self.dcnt[i]) for i in range(N_DMA_SEMS) if self.dcnt[i] > 0]
        for i in range(N_DMA_SEMS):
            if self.dcnt[i] > 0:
                self.waited["sp"][id(self.dsem[i])] = self.dcnt[i]
        ops = self.ops
        with nc.Block() as block:
            def body(ename):
                def f(eng):
                    for waits, fn, (sem, inc) in ops[ename]:
                        for (s, v) in waits:
                            eng.wait_ge(s, v)
                        fn(eng).then_inc(sem, inc)
                    if ename == "sp":
                        for (s, v) in fin:
                            eng.wait_ge(s, v)
                return f
            block.sync(body("sp"))
            block.scalar(body("act"))
            block.vector(body("dve"))
            block.gpsimd(body("pool"))
            block.tensor(body("pe"))
        self.ops = {e: [] for e in self.ENGS}

    def close(self):
        self.root.close()


class Ring:
    def __init__(self, P, name, shape, dt=F32, n=2, psum=False):
        mk = P.ps if psum else P.sb
        self.t = [mk("%s%d" % (name, i), shape, dt) for i in range(n)]
        self.k = ["%s#%d#%d" % (name, P.uid, i) for i in range(n)]
        self.i = -1
        self.n = n

    def next(self):
        self.i = (self.i + 1) % self.n
        return self.t[self.i], self.k[self.i]


class Ctx:
    pass


def load_const(P, key, ap_dram, shape, cast=None):
    if cast is None:
        t = P.sb(key, shape, F32, root=True)
        P.dma(t[:], ap_dram, writes=[key])
        return t
    t = P.sb(key + "_f", shape, F32)
    P.dma(t[:], ap_dram, writes=[key + "_f"])
    tb = P.sb(key, shape, cast, root=True)
    P.op("pool", lambda e: e.tensor_copy(out=tb[:], in_=t[:]), reads=[key + "_f"], writes=[key])
    return tb


def load_weight_bf16(P, name, w_dram, kdim, ncols, stage):
    nck = kdim // 128
    wb = P.sb(name, [128, nck, ncols], BF16)
    wv = w_dram.rearrange("(c p) n -> p c n", p=128)
    CW = stage.t[0].shape[1]
    for c in range(nck):
        for c0 in range(0, ncols, CW):
            cw = min(CW, ncols - c0)
            st, sk = stage.next()
            P.dma(st[:, 0:cw], wv[:, c, c0:c0 + cw], writes=[sk])
            ce = ("act", "dve", "pool", "act", "dve")[(c * 7 + c0 // CW) % 5]
            if ce == "act":
                P.op("act", lambda e, st=st, c=c, c0=c0, cw=cw: e.copy(out=wb[:, c, c0:c0 + cw], in_=st[:, 0:cw]), reads=[sk], writes=[name])
            else:
                P.op(ce, lambda e, st=st, c=c, c0=c0, cw=cw: e.tensor_copy(out=wb[:, c, c0:c0 + cw], in_=st[:, 0:cw]),
                     reads=[sk], writes=[name])
    return wb


def make_hT(P, C, j, src, TT, rings, gcol, eps=NORM_EPS):
    hT, hk = rings["hT"].next()
    for s in range(TT // 128):
        r0 = j * TT + s * 128
        xt, xk = rings["xt"].next()
        P.dma(xt[:], src[r0:r0 + 128, :], reads=[("x", r0 // 128)], writes=[xk])
        sq, sqk = rings["sq"].next()
        ss, ssk = rings["ss"].next()
        P.op("act", lambda e, xt=xt, sq=sq, ss=ss: e.activation(out=sq[:], in_=xt[:], func=AF.Square, accum_out=ss[:, 0:1]),
             reads=[xk], writes=[sqk, ssk])
        P.op("act", lambda e, ss=ss: e.activation(out=ss[:, 1:2], in_=ss[:, 0:1], func=AF.Sqrt, scale=1.0 / D, bias=C.epsc[:, 0:1]),
             reads=[ssk], writes=[ssk])
        P.op("dve", lambda e, ss=ss: e.reciprocal(out=ss[:, 2:3], in_=ss[:, 1:2]), reads=[ssk], writes=[ssk])
        P.op("dve", lambda e, xt=xt, sq=sq, ss=ss: e.tensor_scalar(out=sq[:], in0=xt[:], scalar1=ss[:, 2:3], scalar2=None, op0=ALU.mult),
             reads=[xk, ssk], writes=[sqk])
        for hf in range(2):
            pT, pk = rings["pT"].next()
            for c4 in range(4):
                c = hf * 4 + c4
                P.op("pe", lambda e, pT=pT, sq=sq, c=c, c4=c4: e.transpose(pT[:, c4 * 128:(c4 + 1) * 128], sq[:, c * 128:(c + 1) * 128], C.ident[:]),
                     reads=[sqk, "ident"], writes=[pk])
            for c4 in range(4):
                c = hf * 4 + c4
                P.op("act", lambda e, pT=pT, hT=hT, c=c, c4=c4, s=s: e.activation(
                    out=hT[:, c, 1 + s * 128:1 + (s + 1) * 128], in_=pT[:, c4 * 128:(c4 + 1) * 128], func=AF.Copy, scale=gcol[:, c:c + 1]),
                    reads=[pk, "gcol"], writes=[hk])
    return hT, hk


def out_proj(P, C, L, T, ygT, kdim, w_out, cur, y_out, last):
    nck = kdim // 128
    with P.phase(last=last):
        stage = Ring(P, "wst", [128, 2048], F32, 2)
        wb = load_weight_bf16(P, "wout", w_out, kdim, D, stage)
        ygr = Ring(P, "ygr", [128, nck, 128], BF16, 3)
        xr = Ring(P, "xr", [128, D], F32, 3)
        pr = Ring(P, "po", [128, 512], F32, 4, psum=True)
        ygv = ygT.rearrange("(c p) t -> p c t", p=128)
        for s in range(T // 128):
            yg, ygk = ygr.next()
            P.dma(yg[:], ygv[:, :, s * 128:(s + 1) * 128], reads=[("yg", L, s // 4)], writes=[ygk])
            xt, xk = xr.next()
            P.dma(xt[:], cur[s * 128:(s + 1) * 128, :], reads=[("x", s)], writes=[xk], e="act")
            for hf in range(2):
                po, pk = pr.next()
                for c in range(nck):
                    P.op("pe", lambda e, po=po, yg=yg, c=c, hf=hf: e.matmul(
                        po[:], lhsT=yg[:, c, :], rhs=wb[:, c, hf * 512:(hf + 1) * 512], start=(c == 0), stop=(c == nck - 1)),
                        reads=[ygk, "wout"], writes=[pk])
                P.op("dve", lambda e, po=po, xt=xt, hf=hf: e.tensor_tensor(
                    out=xt[:, hf * 512:(hf + 1) * 512], in0=po[:], in1=xt[:, hf * 512:(hf + 1) * 512], op=ALU.add),
                    reads=[pk, xk], writes=[xk])
            P.dma(y_out[s * 128:(s + 1) * 128, :], xt[:], reads=[xk], writes=[("x", s)], final=last)


def in_rings(P, TT, nh=2, nx=3, nq=2, npt=2):
    return {
        "hT": Ring(P, "hT", [128, 8, 1 + TT], F32, nh),
        "xt": Ring(P, "xt", [128, D], F32, nx),
        "sq": Ring(P, "sq", [128, D], F32, nq),
        "ss": Ring(P, "ss", [128, 4], F32, 4),
        "pT": Ring(P, "pT", [128, 512], F32, npt, psum=True),
    }


def layer_sb(P, C, L, T, W, cur, y_out, last):
    nc = P.nc
    TT = 512
    NT = T // TT
    H = 16
    qT = nc.dram_tensor("sb_qT", [D, T], BF16, kind="Internal").ap()
    kT = nc.dram_tensor("sb_kT", [D, T], BF16, kind="Internal").ap()
    gT = nc.dram_tensor("sb_gT", [D, T], BF16, kind="Internal").ap()
    vTM = nc.dram_tensor("sb_v", [T, D], BF16, kind="Internal").ap()
    ygT = nc.dram_tensor("sb_ygT", [D, T], BF16, kind="Internal").ap()

    with P.phase():
        stage = Ring(P, "wst", [128, 2048], F32, 2)
        wb = load_weight_bf16(P, "win", W["w_in"], D, 4 * D, stage)
        gcol = P.sb("gcol", [128, 8], F32)
        P.dma(gcol[:], W["norm"], writes=["gcol"])
        qn = P.sb("qn", [128, 2], F32)
        P.dma(qn[:, 0:1], W["q_norm"], writes=["qn"])
        P.dma(qn[:, 1:2], W["k_norm"], writes=["qn"])
        P.op("dve", lambda e: e.tensor_scalar(out=qn[:, 0:1], in0=qn[:, 0:1], scalar1=0.125, scalar2=None, op0=ALU.mult),
             reads=["qn"], writes=["qn"])
        rings = in_rings(P, TT)
        hbr = Ring(P, "hb", [128, 8, TT], BF16, 2)
        pp = Ring(P, "pp", [128, 512], F32, 3, psum=True)
        pm = Ring(P, "pm", [128, 512], F32, 2, psum=True)
        sqb = Ring(P, "sqb", [128, 512], BF16, 2)
        rs = Ring(P, "rs", [128, 512], F32, 2)
        ob = Ring(P, "ob", [128, 512], BF16, 4)
        for j in range(NT):
            hT, hk = make_hT(P, C, j, cur, TT, rings, gcol)
            hb, hbk = hbr.next()
            for c in range(8):
                P.op("pool", lambda e, hb=hb, hT=hT, c=c: e.tensor_copy(out=hb[:, c, :], in_=hT[:, c, 1:1 + TT]),
                     reads=[hk], writes=[hbk])
            tsl = slice(j * TT, (j + 1) * TT)
            for which in range(3):
                cbase = {0: 0, 1: D, 2: 3 * D}[which]
                for fc in range(8):
                    p, pk = pp.next()
                    for c in range(8):
                        P.op("pe", lambda e, p=p, c=c, hb=hb, col=cbase + fc * 128: e.matmul(
                            p[:], lhsT=wb[:, c, col:col + 128], rhs=hb[:, c, :], start=(c == 0), stop=(c == 7)),
                            reads=["win", hbk], writes=[pk])
                    o, ok = ob.next()
                    if which == 2:
                        P.op("act", lambda e, p=p, o=o: e.activation(out=o[:], in_=p[:], func=AF.Silu), reads=[pk], writes=[ok])
                        P.dma(gT[fc * 128:(fc + 1) * 128, tsl], o[:], reads=[ok], writes=[("sbg", j)])
                    else:
                        s2, s2k = sqb.next()
                        P.op("act", lambda e, p=p, s2=s2: e.activation(out=s2[:], in_=p[:], func=AF.Square), reads=[pk], writes=[s2k])
                        m, mk = pm.next()
                        P.op("pe", lambda e, m=m, s2=s2: e.matmul(m[:], lhsT=C.bonesb[:], rhs=s2[:], start=True, stop=True),
                             reads=["bonesb", s2k], writes=[mk])
                        r, rk = rs.next()
                        P.op("act", lambda e, m=m, r=r: e.activation(out=r[:], in_=m[:], func=AF.Sqrt, scale=1.0 / 64, bias=C.epsc[:, 0:1]),
                             reads=[mk], writes=[rk])
                        P.op("dve", lambda e, r=r: e.reciprocal(out=r[:], in_=r[:]), reads=[rk], writes=[rk])
                        P.op("dve", lambda e, p=p, r=r, o=o, which=which: e.scalar_tensor_tensor(
                            out=o[:], in0=p[:], scalar=qn[:, which:which + 1], in1=r[:], op0=ALU.mult, op1=ALU.mult),
                            reads=[pk, rk, "qn"], writes=[ok])
                        dst = qT if which == 0 else kT
                        P.dma(dst[fc * 128:(fc + 1) * 128, tsl], o[:], reads=[ok], writes=[("sbq" if which == 0 else "sbk", j)])
            for s in range(TT // 128):
                for cb in range(2):
                    p, pk = pp.next()
                    for c in range(8):
                        P.op("pe", lambda e, p=p, c=c, hb=hb, s=s, cb=cb: e.matmul(
                            p[:], lhsT=hb[:, c, s * 128:(s + 1) * 128], rhs=wb[:, c, 2 * D + cb * 512:2 * D + (cb + 1) * 512],
                            start=(c == 0), stop=(c == 7)), reads=["win", hbk], writes=[pk])
                    o, ok = ob.next()
                    P.op("act", lambda e, p=p, o=o: e.copy(out=o[:], in_=p[:]), reads=[pk], writes=[ok])
                    r0 = j * TT + s * 128
                    P.dma(vTM[r0:r0 + 128, cb * 512:(cb + 1) * 512], o[:], reads=[ok], writes=[("sbv", j)])

    with P.phase():
        NB = T // 128
        kh_r = Ring(P, "kh", [64, T], BF16, 2)
        vh_r = Ring(P, "vh", [128, NB, 64], BF16, 2)
        qh_r = Ring(P, "qh", [64, TT], BF16, 2)
        gh_r = Ring(P, "gh", [64, TT], BF16, 2)
        pz = Ring(P, "pz", [128, 512], F32, 2, psum=True)
        pb = Ring(P, "pb", [128, 512], F32, 2, psum=True)
        pc = Ring(P, "pc", [64, 512], F32, 2, psum=True)
        po = Ring(P, "pov", [64, 512], F32, 2, psum=True)
        Er = Ring(P, "E", [128, 512], F32, 2)
        spr = Ring(P, "sp", [128, 512], BF16, 3)
        Pr = Ring(P, "Pm", [128, 512], BF16, 3)
        fr = Ring(P, "f", [64, 512], F32, 2)
        accr = Ring(P, "acc", [64, 512], F32, 2)
        ygr = Ring(P, "yg", [64, 512], BF16, 2)
        vview = vTM.rearrange("(b p) d -> p b d", p=128)
        qh_r = Ring(P, "qh3", [64, TT], BF16, 3)
        gh_r = Ring(P, "gh3", [64, TT], BF16, 3)
        spr = Ring(P, "sp4", [128, 512], BF16, 4)
        Pr = Ring(P, "Pm4", [128, 512], BF16, 4)
        units = [(h, tq) for h in range(H) for tq in range(NT)]
        ures = {}

        def load_unit(u):
            h, tq = units[u]
            r = {}
            if tq == 0:
                kh, khk = kh_r.next()
                P.dma(kh[:], kT[h * 64:(h + 1) * 64, :], reads=[("sbk", j) for j in range(NT)], writes=[khk])
                vh, vhk = vh_r.next()
                P.dma(vh[:], vview[:, :, h * 64:(h + 1) * 64], reads=[("sbv", j) for j in range(NT)], writes=[vhk], e="act")
                ures[("kv", h)] = (kh, khk, vh, vhk)
            tsl = slice(tq * TT, (tq + 1) * TT)
            r["qh"], r["qhk"] = qh_r.next()
            P.dma(r["qh"][:], qT[h * 64:(h + 1) * 64, tsl], reads=[("sbq", tq)], writes=[r["qhk"]])
            r["gh"], r["ghk"] = gh_r.next()
            P.dma(r["gh"][:], gT[h * 64:(h + 1) * 64, tsl], reads=[("sbg", tq)], writes=[r["ghk"]])
            r["acc"], r["acck"] = accr.next()
            ures[u] = r

        blocks = []
        for u, (h, tq) in enumerate(units):
            nkb = 4 * tq + 4
            for b_ in range(nkb):
                blocks.append((u, h, tq, b_, b_ - 4 * tq, b_ == nkb - 1))
        NBK = len(blocks)
        bres = {}
        load_unit(0)

        def S12(i):
            u, h, tq, b_, r, lastb = blocks[i]
            if b_ == 0 and u + 1 < len(units):
                load_unit(u + 1)
            U_ = ures[u]
            kh, khk, vh, vhk = ures[("kv", h)]
            R = {}
            z, zk = pz.next()
            mm(P, z[:], kh[:, b_ * 128:(b_ + 1) * 128], U_["qh"][:], True, True, [khk, U_["qhk"]], [zk])
            E, Ek = Er.next()
            actf(P, E[:], z[:], AF.Exp, [zk], [Ek])
            sp, spk = spr.next()
            actf(P, sp[:], E[:], AF.Ln, [Ek, "onec"], [spk], bias=C.onec[:, 0:1])
            if r >= 0:
                tt(P, "pool", sp[:], sp[:], C.maskb[:, r, :], ALU.mult, [spk, "maskb"], [spk])
            R["sp"], R["spk"] = sp, spk
            bres[i] = R

        def S34(i):
            u, h, tq, b_, r, lastb = blocks[i]
            U_ = ures[u]
            kh, khk, vh, vhk = ures[("kv", h)]
            R = bres[i]
            sp, spk = R["sp"], R["spk"]
            bb, bk = pb.next()
            mm(P, bb[:], C.ntril[:], sp[:], True, False, ["ntril", spk], [bk])
            mm(P, bb[:], kh[:, b_ * 128:(b_ + 1) * 128], U_["qh"][:], False, True, [khk, U_["qhk"]], [bk])
            cc, ck = pc.next()
            if b_ > 0:
                mm(P, cc[:], C.onesb[:, 0:64], sp[:], True, True, ["onesb", spk], [ck])
            Pm, Pk = Pr.next()
            actf(P, Pm[:], bb[:], AF.Exp, [bk], [Pk])
            if r >= 0:
                tt(P, "pool", Pm[:], Pm[:], C.maskb[:, r, :], ALU.mult, [Pk, "maskb"], [Pk])
            R["Pm"], R["Pk"] = Pm, Pk
            if b_ > 0:
                f, fk = fr.next()
                actf(P, f[:], cc[:], AF.Exp, [ck], [fk], scale=-1.0)
                R["f"], R["fk"] = f, fk

        def S56(i):
            u, h, tq, b_, r, lastb = blocks[i]
            U_ = ures[u]
            kh, khk, vh, vhk = ures[("kv", h)]
            R = bres.pop(i)
            acc, acck = U_["acc"], U_["acck"]
            ov, ovk = po.next()
            mm(P, ov[:], vh[:, b_, :], R["Pm"][:], True, True, [vhk, R["Pk"]], [ovk])
            if b_ == 0:
                P.op("dve", lambda e, acc=acc, ov=ov: e.tensor_copy(out=acc[:], in_=ov[:]), reads=[ovk], writes=[acck])
            else:
                tt(P, "dve", acc[:], acc[:], R["f"][:], ALU.mult, [acck, R["fk"]], [acck])
                tt(P, "dve", acc[:], ov[:], acc[:], ALU.add, [acck, ovk], [acck])
            if lastb:
                tsl = slice(tq * TT, (tq + 1) * TT)
                yg, ygk = ygr.next()
                tt(P, "dve", yg[:], acc[:], U_["gh"][:], ALU.mult, [acck, U_["ghk"]], [ygk])
                P.dma(ygT[h * 64:(h + 1) * 64, tsl], yg[:], reads=[ygk], writes=[("yg", L, tq)])
                del ures[u]

        for i in range(NBK + 2):
            if i < NBK:
                S12(i)
            if 0 <= i - 1 < NBK:
                S34(i - 1)
            if 0 <= i - 2 < NBK:
                S56(i - 2)

    out_proj(P, C, L, T, ygT, D, W["w_out"], cur, y_out, last)


def layer_ssd(P, C, L, T, W, cur, y_out, last):
    nc = P.nc
    TT = 512
    NT = T // TT
    NCH = T // 128
    xbcT = nc.dram_tensor("ssd_xbcT", [4096, T], BF16, kind="Internal").ap()
    zT = nc.dram_tensor("ssd_zT", [2048, T], BF16, kind="Internal").ap()
    dtD = nc.dram_tensor("ssd_dt", [T, 64], F32, kind="Internal").ap()
    ygT = nc.dram_tensor("ssd_ygT", [2048, T], BF16, kind="Internal").ap()

    with P.phase():
        stage = Ring(P, "wst", [128, 2048], F32, 2)
        wb = load_weight_bf16(P, "win", W["w_in"], D, 6176, stage)
        gcol = P.sb("gcol", [128, 8], F32)
        P.dma(gcol[:], W["norm"], writes=["gcol"])
        cw = P.sb("cw", [128, 32, 4], F32)
        P.dma(cw[:], W["conv_w"], writes=["cw"])
        cbias = P.sb("cbias", [128, 32], F32)
        P.dma(cbias[:], W["conv_b"], writes=["cbias"])
        dtb = P.sb("dtb", [128, 32], F32)
        P.dma(dtb[:], W["dt_bias"], writes=["dtb"])
        Arep = P.sb("Arep", [128, 32], F32)
        P.dma(Arep[:], W["a_log"], writes=["Arep"])
        P.op("act", lambda e: e.activation(out=Arep[:], in_=Arep[:], func=AF.Exp), reads=["Arep"], writes=["Arep"])
        P.op("dve", lambda e: e.tensor_scalar(out=Arep[:], in0=Arep[:], scalar1=-1.0, scalar2=None, op0=ALU.mult),
             reads=["Arep"], writes=["Arep"])
        hist = P.sb("hist", [128, 32, 3], F32)
        P.op("pool", lambda e: e.memset(hist[:], 0.0), writes=["hist"])
        rings = in_rings(P, TT, nh=1, nx=2)
        hbr = Ring(P, "hb", [128, 8, TT], BF16, 2)
        pp = Ring(P, "pp", [128, 512], F32, 3, psum=True)
        pd = Ring(P, "pd", [128, 64], F32, 2, psum=True)
        xcr = Ring(P, "xc", [128, 3 + TT], F32, 2)
        cvr = Ring(P, "cv", [128, TT], F32, 2)
        ob = Ring(P, "ob", [128, 512], BF16, 4)
        dtr = Ring(P, "dtr", [128, 64], F32, 2)
        for j in range(NT):
            hT, hk = make_hT(P, C, j, cur, TT, rings, gcol)
            hb, hbk = hbr.next()
            for c in range(8):
                P.op("pool", lambda e, hb=hb, hT=hT, c=c: e.tensor_copy(out=hb[:, c, :], in_=hT[:, c, 1:1 + TT]),
                     reads=[hk], writes=[hbk])
            tsl = slice(j * TT, (j + 1) * TT)
            for fc in range(48):
                p, pk = pp.next()
                for c in range(8):
                    P.op("pe", lambda e, p=p, c=c, hb=hb, col=fc * 128: e.matmul(
                        p[:], lhsT=wb[:, c, col:col + 128], rhs=hb[:, c, :], start=(c == 0), stop=(c == 7)),
                        reads=["win", hbk], writes=[pk])
                o, ok = ob.next()
                if fc < 16:
                    P.op("act", lambda e, p=p, o=o: e.activation(out=o[:], in_=p[:], func=AF.Silu), reads=[pk], writes=[ok])
                    P.dma(zT[fc * 128:(fc + 1) * 128, tsl], o[:], reads=[ok], writes=[("ssdz", j)])
                else:
                    cc = fc - 16
                    xc, xck = xcr.next()
                    P.op("pool", lambda e, xc=xc, cc=cc: e.tensor_copy(out=xc[:, 0:3], in_=hist[:, cc, :]), reads=["hist"], writes=[xck])
                    P.op("act", lambda e, xc=xc, p=p: e.copy(out=xc[:, 3:3 + TT], in_=p[:]), reads=[pk], writes=[xck])
                    P.op("pool", lambda e, xc=xc, cc=cc: e.tensor_copy(out=hist[:, cc, :], in_=xc[:, TT:TT + 3]), reads=[xck], writes=["hist"])
                    cv, cvk = cvr.next()
                    P.op("dve", lambda e, cv=cv, xc=xc, cc=cc: e.tensor_scalar(
                        out=cv[:], in0=xc[:, 0:TT], scalar1=cw[:, cc, 0:1], scalar2=None, op0=ALU.mult), reads=[xck, "cw"], writes=[cvk])
                    for kk in range(1, 4):
                        P.op("dve", lambda e, cv=cv, xc=xc, cc=cc, kk=kk: e.scalar_tensor_tensor(
                            out=cv[:], in0=xc[:, kk:kk + TT], scalar=cw[:, cc, kk:kk + 1], in1=cv[:], op0=ALU.mult, op1=ALU.add),
                            reads=[xck, "cw", cvk], writes=[cvk])
                    P.op("act", lambda e, cv=cv, o=o, cc=cc: e.activation(out=o[:], in_=cv[:], func=AF.Silu, bias=cbias[:, cc:cc + 1]),
                         reads=[cvk, "cbias"], writes=[ok])
                    P.dma(xbcT[cc * 128:(cc + 1) * 128, tsl], o[:], reads=[ok], writes=[("ssdx", j)])
            for s in range(TT // 128):
                p, pk = pd.next()
                for c in range(8):
                    P.op("pe", lambda e, p=p, c=c, hb=hb, s=s: e.matmul(
                        p[:, 0:32], lhsT=hb[:, c, s * 128:(s + 1) * 128], rhs=wb[:, c, 6144:6176], start=(c == 0), stop=(c == 7)),
                        reads=["win", hbk], writes=[pk])
                d, dk = dtr.next()
                P.op("dve", lambda e, d=d, p=p: e.tensor_tensor(out=d[:, 0:32], in0=p[:, 0:32], in1=dtb[:], op=ALU.add),
                     reads=[pk, "dtb"], writes=[dk])
                P.op("act", lambda e, d=d: e.activation(out=d[:, 0:32], in_=d[:, 0:32], func=AF.Exp), reads=[dk], writes=[dk])
                P.op("act", lambda e, d=d: e.activation(out=d[:, 0:32], in_=d[:, 0:32], func=AF.Ln, bias=C.onec[:, 0:1]),
                     reads=[dk], writes=[dk])
                P.op("dve", lambda e, d=d: e.tensor_tensor(out=d[:, 32:64], in0=d[:, 0:32], in1=Arep[:], op=ALU.mult),
                     reads=[dk, "Arep"], writes=[dk])
                r0 = j * TT + s * 128
                P.dma(dtD[r0:r0 + 128, :], d[:], reads=[dk], writes=[("ssdd", j)])

    with P.phase():
        dsk = P.sb("dsk", [128, 16], F32)
        P.dma(dsk[:], W["d_skip"], writes=["dsk"])
        gnw = P.sb("gnw", [128, 16], F32)
        P.dma(gnw[:], W["gnorm_w"], writes=["gnw"])
        hsf = P.sb("hsf", [128, 32, 64], F32)
        hsb = P.sb("hsb", [128, 32, 64], BF16)
        for g in range(8):
            P.op("pool", lambda e, g=g: e.memset(hsf[:, 4 * g:4 * g + 4, :], 0.0), writes=[("hsf", g)])
            P.op("pool", lambda e, g=g: e.memset(hsb[:, 4 * g:4 * g + 4, :], 0.0), writes=[("hsb", g)])
        xbr = Ring(P, "xb", [128, 32, 128], BF16, 2)
        zr = Ring(P, "zs", [128, 16, 128], BF16, 2)
        dr = Ring(P, "dta", [128, 64], F32, 2)
        p_rb = Ring(P, "prb", [128, 512], F32, 2, psum=True)
        p_y = Ring(P, "py", [128, 512], F32, 2, psum=True)
        p_ms = Ring(P, "pms", [128, 512], F32, 1, psum=True)
        p_st = Ring(P, "pst", [128, 512], F32, 1, psum=True)
        p_tr = Ring(P, "ptr", [128, 1024], BF16, 1, psum=True)
        p_ac = Ring(P, "pac", [128, 512], F32, 1, psum=True)
        nacr = Ring(P, "nac", [128, 32], F32, 2)
        cdr = Ring(P, "cd", [128, 32], F32, 2)
        dsr = Ring(P, "ds", [128, 32], F32, 2)
        xdtr = Ring(P, "xdt", [128, 32, 64], BF16, 2)
        xsr = Ring(P, "xs", [128, 32, 64], BF16, 2)
        bmr = Ring(P, "bm", [128, 1024], BF16, 2)
        cbmr = Ring(P, "cbm", [128, 128], F32, 4)
        E4r = Ring(P, "E4", [128, 4, 128], F32, 4)
        EL4r = Ring(P, "EL4", [128, 4, 128], F32, 4)
        MT4r = Ring(P, "MT4", [128, 4, 128], BF16, 4)
        CS4r = Ring(P, "CS4", [128, 4, 128], BF16, 4)
        y1r = Ring(P, "y1", [128, 2, 128], F32, 2)
        Yr = Ring(P, "Y", [128, 16, 128], F32, 2)
        sqr = Ring(P, "sqy", [128, 2, 128], BF16, 2)
        rrr = Ring(P, "rr", [128, 128], F32, 2)
        YGr = Ring(P, "YG", [128, 16, 128], BF16, 2)
        xv = xbcT.rearrange("(c p) t -> p c t", p=128)
        zv = zT.rearrange("(c p) t -> p c t", p=128)
        ygv = ygT.rearrange("(c p) t -> p c t", p=128)
        cres = {}

        def prologue(ch):
            csl = slice(ch * 128, (ch + 1) * 128)
            jt = ch // 4
            R = {}
            xb, xbk = xbr.next()
            P.dma(xb[:], xv[:, :, csl], reads=[("ssdx", jt)], writes=[xbk])
            zs, zsk = zr.next()
            P.dma(zs[:], zv[:, :, csl], reads=[("ssdz", jt)], writes=[zsk], e="act")
            da, dak = dr.next()
            P.dma(da[:], dtD[csl, :], reads=[("ssdd", jt)], writes=[dak])
            ac, ack = p_ac.next()
            P.op("pe", lambda e: e.matmul(ac[:, 0:32], lhsT=C.triu[:], rhs=da[:, 32:64], start=True, stop=True),
                 reads=["triu", dak], writes=[ack])
            P.op("pe", lambda e: e.matmul(ac[:, 32:64], lhsT=C.onesf[:], rhs=da[:, 32:64], start=True, stop=True),
                 reads=["onesf", dak], writes=[ack])
            nac, nack = nacr.next()
            ts(P, "dve", nac[:], ac[:, 0:32], -1.0, None, ALU.mult, None, [ack], [nack])
            cd, cdk = cdr.next()
            actf(P, cd[:], ac[:, 32:64], AF.Exp, [ack], [cdk])
            ds, dsk_ = dsr.next()
            tt(P, "dve", ds[:], ac[:, 32:64], nac[:], ALU.add, [ack, nack], [dsk_])
            actf(P, ds[:], ds[:], AF.Exp, [dsk_], [dsk_])
            xdt, xdtk = xdtr.next()
            xs, xsk = xsr.next()
            for hf in range(2):
                tr, trk = p_tr.next()
                for q in range(8):
                    trp(P, tr[:, q * 128:(q + 1) * 128], xb[:, hf * 8 + q, :], C.identb[:], [xbk, "identb"], [trk])
                tt(P, "dve", xdt[:, hf * 16:(hf + 1) * 16, :], tr[:].rearrange("p (h d) -> p h d", d=64),
                   da[:, hf * 16:(hf + 1) * 16].unsqueeze(2).broadcast_to([128, 16, 64]), ALU.mult, [trk, dak], [xdtk])
            tt(P, "pool", xs[:], xdt[:], ds[:].unsqueeze(2).broadcast_to([128, 32, 64]), ALU.mult, [xdtk, dsk_], [xsk])
            tr, trk = p_tr.next()
            for q in range(8):
                trp(P, tr[:, q * 128:(q + 1) * 128], xb[:, 16 + q, :], C.identb[:], [xbk, "identb"], [trk])
            bm, bmk = bmr.next()
            actf(P, bm[:], tr[:], AF.Copy, [trk], [bmk])
            Y, Yk = Yr.next()
            YG, YGk = YGr.next()
            R.update(xb=xb, xbk=xbk, zs=zs, zsk=zsk, da=da, dak=dak, nac=nac, nack=nack, cd=cd, cdk=cdk, xdt=xdt, xdtk=xdtk,
                     xs=xs, xsk=xsk, bm=bm, bmk=bmk, Y=Y, Yk=Yk, YG=YG, YGk=YGk)
            cres[ch] = R

        gres = {}

        def G12(i):
            ch, g = divmod(i, 8)
            if g == 0:
                prologue(ch)
            R = cres[ch]
            xb, xbk, da, dak, nac, nack = R["xb"], R["xbk"], R["da"], R["dak"], R["nac"], R["nack"]
            G = {}
            cb_, cbk = p_ac.t[0], p_ac.k[0]
            cb = cb_[:, 128:256]
            mm(P, cb, xb[:, 16 + g, :], xb[:, 24 + g, :], True, True, [xbk], [cbk])
            rb, rbk = p_rb.next()
            for r in range(4):
                h = 4 * g + r
                mm(P, rb[:, r * 128:(r + 1) * 128], da[:, 32 + h:33 + h].broadcast_to([128, 128]), C.triu[:], True, True, [dak, "triu"], [rbk])
            cbm, cbmk = cbmr.next()
            tt(P, "dve", cbm[:], cb, C.triu[:], ALU.mult, [cbk, "triu"], [cbmk])
            E4, E4k = E4r.next()
            for r in range(4):
                h = 4 * g + r
                ts(P, "dve", E4[:, r, :], rb[:, r * 128:(r + 1) * 128], nac[:, h:h + 1], 0.0, ALU.add, ALU.min, [rbk, nack], [E4k])
            actf(P, E4[:].rearrange("p r l -> p (r l)"), E4[:].rearrange("p r l -> p (r l)"), AF.Exp, [E4k], [E4k])
            MT4, MT4k = MT4r.next()
            tt(P, "pool", MT4[:], E4[:], cbm[:].unsqueeze(1).broadcast_to([128, 4, 128]), ALU.mult, [E4k, cbmk], [MT4k])
            G.update(MT4=MT4, MT4k=MT4k)
            if ch > 0:
                EL4, EL4k = EL4r.next()
                actf(P, EL4[:].rearrange("p r l -> p (r l)"), rb[:], AF.Exp, [rbk, E4k], [EL4k])
                CS4, CS4k = CS4r.next()
                tt(P, "pool", CS4[:], EL4[:], xb[:, 24 + g, :].unsqueeze(1).broadcast_to([128, 4, 128]), ALU.mult, [EL4k, xbk], [CS4k])
                G.update(CS4=CS4, CS4k=CS4k)
            gres[i] = G

        def G34(i):
            ch, g = divmod(i, 8)
            R = cres[ch]
            G = gres.pop(i)
            xb, xbk, zs, zsk = R["xb"], R["xbk"], R["zs"], R["zsk"]
            xdt, xdtk, xs, xsk, bm, bmk = R["xdt"], R["xdtk"], R["xs"], R["xsk"], R["bm"], R["bmk"]
            Y, Yk, YG, YGk, cd, cdk = R["Y"], R["Yk"], R["YG"], R["YGk"], R["cd"], R["cdk"]
            py, pyk = p_y.next()
            for r in range(4):
                h = 4 * g + r
                osl = slice((r % 2) * 64, (r % 2) * 64 + 64)
                pc = slice((r // 2) * 128, (r // 2) * 128 + 128)
                mm(P, py[osl, pc], xdt[:, h, :], G["MT4"][:, r, :], True, ch == 0, [xdtk, G["MT4k"]], [pyk])
                if ch > 0:
                    mm(P, py[osl, pc], hsb[:, h, :], G["CS4"][:, r, :], False, True, [("hsb", g), G["CS4k"]], [pyk])
            y1, y1k = y1r.next()
            for pp_ in range(2):
                pq_ = 2 * g + pp_
                stt(P, y1[:, pp_, :], xb[:, pq_, :], dsk[:, pq_:pq_ + 1], py[:, pp_ * 128:(pp_ + 1) * 128], ALU.mult, ALU.add,
                    [xbk, "dsk", pyk], [y1k])
            tt(P, "pool", Y[:, 2 * g:2 * g + 2, :], y1[:], zs[:, 2 * g:2 * g + 2, :], ALU.mult, [y1k, zsk], [(Yk, g)])
            sq, sqk = sqr.next()
            actf(P, sq[:], Y[:, 2 * g:2 * g + 2, :], AF.Square, [(Yk, g)], [sqk])
            ms, msk = p_ms.next()
            for pp_ in range(2):
                mm(P, ms[:, 256:384], C.onesb[:], sq[:, pp_, :], pp_ == 0, pp_ == 1, ["onesb", sqk], [msk])
            rr, rrk = rrr.next()
            actf(P, rr[:], ms[:, 256:384], AF.Sqrt, [msk], [rrk], scale=1.0 / 256, bias=C.epsc[:, 0:1])
            P.op("dve", lambda e, rr=rr: e.reciprocal(out=rr[:], in_=rr[:]), reads=[rrk], writes=[rrk])
            for pq_ in (2 * g, 2 * g + 1):
                stt(P, YG[:, pq_, :], Y[:, pq_, :], gnw[:, pq_:pq_ + 1], rr[:], ALU.mult, ALU.mult, [(Yk, g), rrk, "gnw"], [YGk])
            if ch < NCH - 1:
                st_, stk = p_st.next()
                mm(P, st_[:, 0:256], bm[:, g * 128:(g + 1) * 128], xs[:, 4 * g:4 * g + 4, :].rearrange("p h d -> p (h d)"), True, True,
                   [bmk, xsk], [stk])
                tt(P, "dve", hsf[:, 4 * g:4 * g + 4, :], hsf[:, 4 * g:4 * g + 4, :],
                   cd[:, 4 * g:4 * g + 4].unsqueeze(2).broadcast_to([128, 4, 64]), ALU.mult, [("hsf", g), cdk], [("hsf", g)])
                tt(P, "dve", hsf[:, 4 * g:4 * g + 4, :], st_[:, 0:256].rearrange("p (h d) -> p h d", d=64), hsf[:, 4 * g:4 * g + 4, :],
                   ALU.add, [("hsf", g), stk], [("hsf", g)])
                actf(P, hsb[:, 4 * g:4 * g + 4, :], hsf[:, 4 * g:4 * g + 4, :], AF.Copy, [("hsf", g)], [("hsb", g)])
            if g == 7:
                csl = slice(ch * 128, (ch + 1) * 128)
                P.dma(ygv[:, :, csl], YG[:], reads=[YGk], writes=[("yg", L, ch // 4)])
                del cres[ch]

        NG = NCH * 8
        G12(0)
        G12(1)
        for i in range(NG):
            if i + 2 < NG:
                G12(i + 2)
            G34(i)

    out_proj(P, C, L, T, ygT, 2048, W["w_out"], cur, y_out, last)


def tt(P, eng, out, in0, in1, op, r, w):
    return P.op(eng, lambda e: e.tensor_tensor(out=out, in0=in0, in1=in1, op=op), reads=r, writes=w)


def ts(P, eng, out, in0, s1, s2, op0, op1, r, w):
    if s2 is None:
        return P.op(eng, lambda e: e.tensor_scalar(out=out, in0=in0, scalar1=s1, scalar2=None, op0=op0), reads=r, writes=w)
    return P.op(eng, lambda e: e.tensor_scalar(out=out, in0=in0, scalar1=s1, scalar2=s2, op0=op0, op1=op1), reads=r, writes=w)


def stt(P, out, in0, scalar, in1, op0, op1, r, w):
    return P.op("dve", lambda e: e.scalar_tensor_tensor(out=out, in0=in0, scalar=scalar, in1=in1, op0=op0, op1=op1), reads=r, writes=w)


def actf(P, out, in_, func, r, w, bias=None, scale=None):
    kw = {}
    if bias is not None:
        kw["bias"] = bias
    if scale is not None:
        kw["scale"] = scale
    return P.op("act", lambda e: e.activation(out=out, in_=in_, func=func, **kw), reads=r, writes=w)


def mm(P, out, lhsT, rhs, start, stop, r, w):
    return P.op("pe", lambda e: e.matmul(out, lhsT=lhsT, rhs=rhs, start=start, stop=stop), reads=r, writes=w)


def trp(P, out, in_, ident, r, w):
    return P.op("pe", lambda e: e.transpose(out, in_, ident), reads=r, writes=w)


RW_GN_EPS = 64e-5
NEG_EXP_HALF = -math.exp(-0.5)
CDT = F32


def layer_rwkv(P, C, L, T, W, cur, y_out, last):
    nc = P.nc
    vres = (L == 3)
    TT = 256
    NT = T // TT
    NCK = T // 64
    ncols = 4 * D + 128 + (32 if vres else 0)
    pre = "rw%d_" % L
    names = ["At", "Bt", "Kt", "Rt", "Bg", "Kg", "V", "BON"]
    Dm = {n: nc.dram_tensor(pre + n, [D, T], (F32 if n == "BON" else BF16), kind="Internal").ap() for n in names}
    gCd = nc.dram_tensor(pre + "gC", [D, NCK], F32, kind="Internal").ap()
    gTd = nc.dram_tensor(pre + "gT", [D, T], BF16, kind="Internal").ap()
    yTd = nc.dram_tensor(pre + "yT", [D, T], F32, kind="Internal").ap()
    if L == 0:
        C.vfirst = nc.dram_tensor("rw_vfirst", [D, T], F32, kind="Internal").ap()
    vfd = C.vfirst

    def colp(name, n=8):
        t = P.sb(name, [128, n], F32)
        P.dma(t[:], W[name], writes=[name])
        return t

    with P.phase():
        stage = Ring(P, "wst", [128, 1024], F32, 2)
        wb = load_weight_bf16(P, "win", W["w_in"], D, ncols, stage)
        gcol = P.sb("gcol", [128, 8], F32)
        P.dma(gcol[:], W["norm"], writes=["gcol"])
        mu = P.sb("mu", [128, 6, 8], F32)
        P.dma(mu[:], W["mu"], writes=["mu"])
        w0 = colp("w0"); a0 = colp("a0"); k_k = colp("k_k"); k_a = colp("k_a"); gdum = None
        r_k = P.sb("r_k", [128, 8], F32)
        P.dma(r_k[:], W["r_k"], writes=["r_k"])
        omka = P.sb("omka", [128, 8], F32)
        ts(P, "dve", omka[:], k_a[:], -1.0, 1.0, ALU.mult, ALU.add, ["k_a"], ["omka"])
        def lowrank(nm, rows):
            t = P.sb(nm, [rows, D], BF16)
            st, sk = stage.next()
            P.dma(st[0:rows, :], W[nm], writes=[sk])
            P.op("pool", lambda e: e.tensor_copy(out=t[:], in_=st[0:rows, :]), reads=[sk], writes=[nm])
            return t
        wup = lowrank("w_up", 64)
        aup = lowrank("a_up", 64)
        if vres:
            v0 = colp("v0")
            vup = lowrank("v_up", 32)
        rings = in_rings(P, TT, nh=1, nx=2, nq=1, npt=1)
        carry = P.sb("carry", [128, 8, 1], F32)
        P.op("pool", lambda e: e.memset(carry[:], 0.0), writes=["carry"])
        dxT = P.sb("dxT", [128, 8, TT], F32)
        xmix = {i: P.sb("xm%d" % i, [128, 8, TT], BF16) for i in (0, 2, 3)}
        xsm = Ring(P, "xsm", [128, 8, TT], BF16, 2)
        wl = P.sb("wl", [64, TT], BF16); al = P.sb("al", [64, TT], BF16); vl = P.sb("vl", [32, TT], BF16)
        PB = [P.ps("rwa%d" % i, [128, 512], F32) for i in range(7)]
        PBk = ["rwabank%d_%d" % (L, i) for i in range(7)]

        def half(bi, hi):
            return PB[bi][:, hi * TT:(hi + 1) * TT], PBk[bi]

        class _PQ:
            i = -1

            def next(self):
                self.i = (self.i + 1) % 7
                return PB[self.i], PBk[self.i]
        pq = _PQ()
        F = lambda nm, n=2, dt=F32: Ring(P, nm, [128, TT], dt, n)
        R_ = {nm: F(nm, 2) for nm in ["sg", "lw", "cum", "gi", "ge", "gv", "gd", "aa", "kkr", "nrm", "kk", "t1", "kf", "d1", "ka", "sv", "vf"]}
        R_.update({nm: F(nm, 2) for nm in ["vv", "BON"]})
        R_.update({nm: F(nm, 2, BF16) for nm in ["At", "Bt", "Bg", "Kt", "Kg", "Rt", "vb"]})
        R_["sqb"] = F("sqb", 2, BF16); R_["rkb"] = F("rkb", 2, BF16); R_["go"] = F("go", 2, BF16)
        gCr = Ring(P, "gCt", [128, TT // 64], F32, 2)
        for j in range(NT):
            hT, hk = make_hT(P, C, j, cur, TT, rings, gcol)
            P.op("pool", lambda e, hT=hT: e.tensor_copy(out=hT[:, :, 0:1], in_=carry[:]), reads=["carry"], writes=[hk])
            P.op("pool", lambda e, hT=hT: e.tensor_copy(out=carry[:], in_=hT[:, :, TT:TT + 1]), reads=[hk], writes=["carry"])
            tt(P, "pool", dxT[:], hT[:, :, 0:TT], hT[:, :, 1:TT + 1], ALU.subtract, [hk], ["dxT"])
            tsl = slice(j * TT, (j + 1) * TT)

            def mix(i, dst, dk):
                for c in range(8):
                    stt(P, dst[:, c, :], dxT[:, c, :], mu[:, i, c:c + 1], hT[:, c, 1:TT + 1], ALU.mult, ALU.add,
                        ["dxT", "mu", hk], [dk])

            xw, xwk = xsm.next(); mix(1, xw, xwk)
            p, pk = pq.next()
            for c in range(8):
                mm(P, p[0:64, 0:TT], wb[:, c, 4096:4160], xw[:, c, :], c == 0, c == 7, ["win", xwk], [pk])
            actf(P, wl[:], p[0:64, 0:TT], AF.Tanh, [pk], ["wl"])
            xa, xak = xsm.next(); mix(4, xa, xak)
            p, pk = pq.next()
            for c in range(8):
                mm(P, p[0:64, 0:TT], wb[:, c, 4160:4224], xa[:, c, :], c == 0, c == 7, ["win", xak], [pk])
            actf(P, al[:], p[0:64, 0:TT], AF.Copy, [pk], ["al"])
            xg, xgk = xsm.next(); mix(5, xg, xgk)
            for fc in range(8):
                p, pk = pq.next()
                for c in range(8):
                    mm(P, p[:, 0:TT], wb[:, c, 3072 + fc * 128:3072 + (fc + 1) * 128], xg[:, c, :], c == 0, c == 7, ["win", xgk], [pk])
                go, gok = R_["go"].next()
                actf(P, go[:], p[:, 0:TT], AF.Silu, [pk], [gok])
                P.dma(gTd[fc * 128:(fc + 1) * 128, tsl], go[:], reads=[gok], writes=[(pre + "g", j)])
            for i in (0, 2, 3):
                mix(i, xmix[i], "xm%d" % i)
            if vres:
                p, pk = pq.next()
                for c in range(8):
                    mm(P, p[0:32, 0:TT], wb[:, c, 4224:4256], xmix[3][:, c, :], c == 0, c == 7, ["win", "xm3"], [pk])
                actf(P, vl[:], p[0:32, 0:TT], AF.Copy, [pk], ["vl"])
            def fc_body(fc, slot):
                    fs = slice(fc * 128, (fc + 1) * 128)
                    col = lambda t_: t_[:, fc:fc + 1]
                    pr, prk = half(3 * slot, 0); pkp, pkk = half(3 * slot, 1); pv, pvk = half(3 * slot + 1, 0)
                    pn0, pn0k = half(6, slot); pw2, pw2k = half(3 * slot + 2, 0); pa2, pa2k = half(3 * slot + 2, 1)
                    pv2, pv2k = half(3 * slot + 1, 1)
                    mm(P, pw2, wup[:, fs], wl[:], True, True, ["w_up", "wl"], [pw2k])
                    mm(P, pa2, aup[:, fs], al[:], True, True, ["a_up", "al"], [pa2k])
                    for c in range(8):
                        mm(P, pkp, wb[:, c, D + fc * 128:D + (fc + 1) * 128], xmix[2][:, c, :], c == 0, c == 7, ["win", "xm2"], [pkk])
                    for c in range(8):
                        mm(P, pr, wb[:, c, fc * 128:(fc + 1) * 128], xmix[0][:, c, :], c == 0, c == 7, ["win", "xm0"], [prk])
                    yield
                    for c in range(8):
                        mm(P, pv, wb[:, c, 2 * D + fc * 128:2 * D + (fc + 1) * 128], xmix[3][:, c, :], c == 0, c == 7, ["win", "xm3"], [pvk])
                    if vres:
                        mm(P, pv2, vup[:, fs], vl[:], True, True, ["v_up", "vl"], [pv2k])
                    sg, sgk = R_["sg"].next()
                    actf(P, sg[:], pw2, AF.Sigmoid, [pw2k, "w0"], [sgk], bias=col(w0))
                    lw, lwk = R_["lw"].next()
                    ts(P, "dve", lw[:], sg[:], NEG_EXP_HALF, None, ALU.mult, None, [sgk], [lwk])
                    cum, cumk = R_["cum"].next()
                    P.op("dve", lambda e, cum=cum, lw=lw: e.tensor_tensor_scan(
                        out=cum[:], data0=C.mask0[:, 0:TT], data1=lw[:], initial=0.0, op0=ALU.mult, op1=ALU.add),
                        reads=[lwk, "mask0"], writes=[cumk])
                    yield
                    gi, gik = R_["gi"].next()
                    actf(P, gi[:], cum[:], AF.Exp, [cumk], [gik])
                    ge, gek = R_["ge"].next()
                    tt(P, "pool", ge[:], cum[:], lw[:], ALU.subtract, [cumk, lwk], [gek])
                    actf(P, ge[:], ge[:], AF.Exp, [gek], [gek])
                    gv, gvk = R_["gv"].next()
                    actf(P, gv[:], cum[:], AF.Exp, [cumk], [gvk], scale=-1.0)
                    yield
                    gd, gdk = R_["gd"].next()
                    cum3 = cum[:].rearrange("p (c t) -> p c t", t=64)
                    tt(P, "pool", gd[:].rearrange("p (c t) -> p c t", t=64), cum3[:, :, 63:64].broadcast_to([128, TT // 64, 64]), cum3,
                       ALU.subtract, [cumk], [gdk])
                    actf(P, gd[:], gd[:], AF.Exp, [gdk], [gdk])
                    gCt, gCk = gCr.next()
                    actf(P, gCt[:].unsqueeze(2), cum3[:, :, 63:64], AF.Exp, [cumk], [gCk])
                    P.dma(gCd[fs, j * (TT // 64):(j + 1) * (TT // 64)], gCt[:], reads=[gCk], writes=[(pre + "gC", j)])
                    yield
                    aa, aak = R_["aa"].next()
                    actf(P, aa[:], pa2, AF.Sigmoid, [pa2k, "a0"], [aak], bias=col(a0))
                    kkr, kkrk = R_["kkr"].next()
                    ts(P, "dve", kkr[:], pkp, col(k_k), None, ALU.mult, None, [pkk, "k_k"], [kkrk])
                    sqb, sqbk = R_["sqb"].next()
                    actf(P, sqb[:], kkr[:], AF.Square, [kkrk], [sqbk])
                    mm(P, pn0, C.bonesb[:], sqb[:], True, True, ["bonesb", sqbk], [pn0k])
                    yield
                    nrm, nrmk = R_["nrm"].next()
                    actf(P, nrm[:], pn0, AF.Sqrt, [pn0k], [nrmk])
                    ts(P, "dve", nrm[:], nrm[:], 1e-12, None, ALU.max, None, [nrmk], [nrmk])
                    P.op("dve", lambda e, nrm=nrm: e.reciprocal(out=nrm[:], in_=nrm[:]), reads=[nrmk], writes=[nrmk])
                    kk, kkk = R_["kk"].next()
                    tt(P, "dve", kk[:], kkr[:], nrm[:], ALU.mult, [kkrk, nrmk], [kkk])
                    yield
                    t1, t1k = R_["t1"].next()
                    ts(P, "dve", t1[:], aa[:], col(k_a), col(omka), ALU.mult, ALU.add, [aak, "k_a", "omka"], [t1k])
                    kf, kfk = R_["kf"].next()
                    tt(P, "dve", kf[:], pkp, t1[:], ALU.mult, [pkk, t1k], [kfk])
                    yield
                    vv, vvk = R_["vv"].next()
                    if not vres:
                        actf(P, vv[:], pv, AF.Copy, [pvk], [vvk])
                        if L == 0:
                            P.dma(vfd[fs, tsl], vv[:], reads=[vvk], writes=[("vfirst", j)])
                    else:
                        vf, vfk = R_["vf"].next()
                        P.dma(vf[:], vfd[fs, tsl], reads=[("vfirst", j)], writes=[vfk])
                        sv, svk = R_["sv"].next()
                        actf(P, sv[:], pv2, AF.Sigmoid, [pv2k, "v0"], [svk], bias=col(v0))
                        d1, d1k = R_["d1"].next()
                        tt(P, "dve", d1[:], vf[:], pv, ALU.subtract, [vfk, pvk], [d1k])
                        tt(P, "pool", d1[:], d1[:], sv[:], ALU.mult, [d1k, svk], [d1k])
                        tt(P, "dve", vv[:], d1[:], pv, ALU.add, [d1k, pvk], [vvk])
                    vb, vbk = R_["vb"].next()
                    P.op("pool", lambda e, vb=vb, vv=vv: e.tensor_copy(out=vb[:], in_=vv[:]), reads=[vvk], writes=[vbk])
                    P.dma(Dm["V"][fs, tsl], vb[:], reads=[vbk], writes=[(pre + "V", j)])
                    yield
                    At, Atk = R_["At"].next()
                    stt(P, At[:], kk[:], -1.0, ge[:], ALU.mult, ALU.mult, [kkk, gek], [Atk])
                    P.dma(Dm["At"][fs, tsl], At[:], reads=[Atk], writes=[(pre + "At", j)])
                    ka, kak = R_["ka"].next()
                    tt(P, "pool", ka[:], kk[:], aa[:], ALU.mult, [kkk, aak], [kak])
                    Bt, Btk = R_["Bt"].next()
                    tt(P, "dve", Bt[:], ka[:], gv[:], ALU.mult, [kak, gvk], [Btk])
                    P.dma(Dm["Bt"][fs, tsl], Bt[:], reads=[Btk], writes=[(pre + "Bt", j)])
                    yield
                    Bg, Bgk = R_["Bg"].next()
                    tt(P, "pool", Bg[:], ka[:], gd[:], ALU.mult, [kak, gdk], [Bgk])
                    P.dma(Dm["Bg"][fs, tsl], Bg[:], reads=[Bgk], writes=[(pre + "Bg", j)])
                    Kt, Ktk = R_["Kt"].next()
                    tt(P, "pool", Kt[:], kf[:], gv[:], ALU.mult, [kfk, gvk], [Ktk])
                    P.dma(Dm["Kt"][fs, tsl], Kt[:], reads=[Ktk], writes=[(pre + "Kt", j)])
                    yield
                    Kg, Kgk = R_["Kg"].next()
                    tt(P, "pool", Kg[:], kf[:], gd[:], ALU.mult, [kfk, gdk], [Kgk])
                    P.dma(Dm["Kg"][fs, tsl], Kg[:], reads=[Kgk], writes=[(pre + "Kg", j)])
                    Rt, Rtk = R_["Rt"].next()
                    tt(P, "dve", Rt[:], pr, gi[:], ALU.mult, [prk, gik], [Rtk])
                    P.dma(Dm["Rt"][fs, tsl], Rt[:], reads=[Rtk], writes=[(pre + "Rt", j)])
                    yield
                    rkb, rkbk = R_["rkb"].next()
                    stt(P, rkb[:], pr, col(r_k), kf[:], ALU.mult, ALU.mult, [prk, "r_k", kfk], [rkbk])
                    mm(P, pn0, C.bonesb[:], rkb[:], True, True, ["bonesb", rkbk], [pn0k])
                    BON, BONk = R_["BON"].next()
                    tt(P, "dve", BON[:], pn0, vv[:], ALU.mult, [pn0k, vvk], [BONk])
                    P.dma(Dm["BON"][fs, tsl], BON[:], reads=[BONk], writes=[(pre + "BON", j)])

            for pair in range(4):
                gens = [fc_body(2 * pair, 0), fc_body(2 * pair + 1, 1)]
                while gens:
                    for g_ in list(gens):
                        try:
                            next(g_)
                        except StopIteration:
                            gens.remove(g_)

    import os
    if os.environ.get("RW_STOP") == "A":
        return
    with P.phase():
        TB = 128
        NCB = TB // 64
        opn = ["At", "Bt", "Kt", "Rt", "Bg", "Kg", "V"]
        opr = {n: Ring(P, "o" + n, [128, 8, TB], BF16, 2) for n in opn}
        gCr2 = Ring(P, "gC2", [128, 8, NCB], F32, 2)
        ST = P.sb("ST", [128, 8, 64], F32)
        STb = P.sb("STb", [128, 8, 64], BF16)
        P.op("pool", lambda e: e.memset(ST[:], 0.0), writes=["ST"])
        P.op("pool", lambda e: e.memset(STb[:], 0.0), writes=["STb"])
        YT = Ring(P, "YT", [128, 8, TB], F32, 2)
        bk = [P.ps("rwbk%d" % i, [128, 512], F32) for i in range(7)]
        bkk = ["rwbank%d_%d" % (L, i) for i in range(7)]

        class RingOf:
            def __init__(self, ids):
                self.ids = ids
                self.i = -1

            def next(self):
                self.i = (self.i + 1) % len(self.ids)
                j = self.ids[self.i]
                return bk[j], bkk[j]

        pA5 = RingOf([0, 1, 2])
        pZ = RingOf([0, 1, 2, 3])
        pW = RingOf([2, 3])
        pU = RingOf([3]); pUO = RingOf([4]); pY2 = RingOf([5]); pS = RingOf([6])
        pT = Ring(P, "pT", [128, 1024], BF16, 1, psum=True)
        tmr = {n: Ring(P, "tm" + n, [64, D], BF16, 2) for n in ["At", "V", "Bg", "Kg"]}
        A5allr = Ring(P, "A5all", [64, 16, 320], BF16, 2)
        Zall = [P.sb("Zall%d" % i, [64, 16, 256], BF16) for i in range(2)]
        ZcAr = Ring(P, "ZcA", [64, 16, 64], BF16, 2)
        ZcPr = Ring(P, "ZcP", [64, 16, 64], F32, 2)
        ApT = Ring(P, "ApT", [128, 8, 64], BF16, 2)
        Ur = Ring(P, "U", [64, 16, 64], BF16, 2)
        Y1r = Ring(P, "Y1s", [64, 512], F32, 2)
        Ysr = Ring(P, "Ys", [64, D], BF16, 2)
        tmpS = Ring(P, "tmpS", [128, 4, 64], F32, 2)
        for jb in range(T // TB):
            tsl = slice(jb * TB, (jb + 1) * TB)
            jA = (jb * TB) // TT
            ot = {}
            for n in opn:
                t_, k_ = opr[n].next()
                P.dma(t_[:], Dm[n].rearrange("(c p) t -> p c t", p=128)[:, :, tsl], reads=[(pre + n, jA)], writes=[k_],
                      e=("act" if n in ("Bg", "Kg", "V") else "sp"))
                ot[n] = (t_, k_)
            gC2, gC2k = gCr2.next()
            P.dma(gC2[:], gCd.rearrange("(c p) k -> p c k", p=128)[:, :, jb * NCB:(jb + 1) * NCB], reads=[(pre + "gC", jA)], writes=[gC2k])
            yt, ytk = YT.next()
            for cq in range(NCB):
                cs = slice(cq * 64, (cq + 1) * 64)
                tm = {}
                for n in ["At", "V", "Bg", "Kg"]:
                    dst, dk = tmr[n].next()
                    src, sk = ot[n]
                    for hf in range(2):
                        p, pk = pT.next()
                        for q in range(4):
                            fc = hf * 4 + q
                            trp(P, p[0:64, q * 128:(q + 1) * 128], src[:, fc, cs], C.identb[:], [sk, "identb"], [pk])
                        actf(P, dst[:, hf * 512:(hf + 1) * 512], p[0:64, 0:512], AF.Copy, [pk], [(dk, hf)])
                    tm[n] = (dst, dk)
                ZcA, Zck = ZcAr.next()
                ZcP, _zp = ZcPr.next()
                apt, aptk = ApT.next()
                A5h = {}
                Z0 = Zall[0]
                zkey = lambda par, fc, part: ("Zall", L, par, fc, part)
                A5a, A5ak = A5allr.next()
                for h in range(16):
                    fc, e_ = h // 2, h % 2
                    rows = slice(e_ * 64, e_ * 64 + 64)
                    a5, a5k = pA5.next()
                    Kt_ = ot["Kt"][0][rows, fc, cs]; Bt_ = ot["Bt"][0][rows, fc, cs]
                    At_ = ot["At"][0][rows, fc, cs]; Rt_ = ot["Rt"][0][rows, fc, cs]
                    rk5 = [ot["Kt"][1], ot["Bt"][1], ot["At"][1], ot["Rt"][1]]
                    mm(P, a5[0:64, 0:64], Kt_, At_, True, True, rk5, [a5k])
                    mm(P, a5[0:64, 64:128], Kt_, Rt_, True, True, rk5, [a5k])
                    mm(P, a5[0:64, 128:192], Bt_, Rt_, True, True, rk5, [a5k])
                    mm(P, a5[0:64, 192:256], At_, Bt_, True, True, rk5, [a5k])
                    mm(P, a5[0:64, 256:320], Bt_, At_, True, True, rk5, [a5k])
                    tt(P, "dve", A5a[:, h, :], a5[0:64, 0:320], C.mask5[:], ALU.mult, [a5k, "mask5"], [(A5ak, h)])
                    A5h[h] = (A5a[:, h, :], (A5ak, h))
                P.op("pool", lambda e, Z0=Z0, src=tm["At"][0]: e.tensor_copy(out=Z0[:, :, 0:64], in_=src[:].rearrange("p (h d) -> p h d", d=64)),
                     reads=[(tm["At"][1], 0), (tm["At"][1], 1)], writes=[zkey(0, fc, "X0") for fc in range(8)])
                for hf in range(2):
                    pw_, pwk = pW.next()
                    for hh in range(8):
                        h = hf * 8 + hh
                        a5s, a5sk = A5h[h]
                        mm(P, pw_[0:64, hh * 64:(hh + 1) * 64], a5s[:, 0:64], tm["V"][0][:, h * 64:(h + 1) * 64], True, True,
                           [a5sk, (tm["V"][1], hf)], [pwk])
                    actf(P, Z0[:, hf * 8:(hf + 1) * 8, 64:128], pw_[0:64, :].rearrange("p (h v) -> p h v", v=64), AF.Copy, [pwk],
                         [zkey(0, fc, "X1") for fc in range(hf * 4, hf * 4 + 4)])
                for lev in range(6):
                    par = lev % 2
                    Zc_, Zn_ = Zall[par], Zall[1 - par]
                    for fp in range(4):
                        pzx, pzxk = pZ.next()
                        pzl, pzlk = pZ.next()
                        fcs = (2 * fp, 2 * fp + 1)
                        rdk = []
                        for fc in fcs:
                            rdk += [zkey(par, fc, "X0"), zkey(par, fc, "X1")]
                            rdk += ([zkey(par, fc, "L")] if lev > 0 else [(A5ak, 2 * fc), (A5ak, 2 * fc + 1)])
                        for j in range(4):
                            h = 4 * fp + j
                            if lev == 0:
                                l_ap, lt_ap = A5a[:, h, 192:256], A5a[:, h, 256:320]
                            else:
                                l_ap, lt_ap = Zc_[:, h, 128:192], Zc_[:, h, 192:256]
                            mm(P, pzx[0:64, j * 128:(j + 1) * 128], lt_ap, Zc_[:, h, 0:128], True, True, rdk, [pzxk])
                            if lev < 5:
                                mm(P, pzl[0:64, j * 128:j * 128 + 64], lt_ap, l_ap, True, True, rdk, [pzlk])
                                mm(P, pzl[0:64, j * 128 + 64:(j + 1) * 128], l_ap, lt_ap, True, True, rdk, [pzlk])
                        hs = slice(4 * fp, 4 * fp + 4)
                        px3 = pzx[0:64, :].rearrange("p (h c) -> p h c", c=128)
                        if lev < 5:
                            tt(P, "dve", Zn_[:, hs, 0:128], px3, Zc_[:, hs, 0:128], ALU.add, [pzxk] + rdk,
                               [zkey(1 - par, fc, x) for fc in fcs for x in ("X0", "X1")])
                            actf(P, Zn_[:, hs, 128:256], pzl[0:64, :].rearrange("p (h c) -> p h c", c=128), AF.Copy, [pzlk],
                                 [zkey(1 - par, fc, "L") for fc in fcs])
                        else:
                            tt(P, "dve", ZcA[:, hs, :], px3[:, :, 0:64], Zc_[:, hs, 0:64], ALU.add, [pzxk] + rdk, [(Zck, fc) for fc in fcs])
                            tt(P, "dve", ZcP[:, hs, :], px3[:, :, 64:128], Zc_[:, hs, 64:128], ALU.add, [pzxk] + rdk, [(Zck, fc) for fc in fcs])
                p, pk = pT.next()
                for fc in range(8):
                    trp(P, p[:, fc * 64:(fc + 1) * 64], ZcA[:, 2 * fc:2 * fc + 2, :].rearrange("p e d -> p (e d)"), C.identb[0:64, 0:64],
                        [(Zck, fc), "identb"], [pk])
                actf(P, apt[:].rearrange("p c t -> p (c t)"), p[:, 0:512], AF.Copy, [pk], [aptk])
                ys, ysk = Ysr.next()
                U, Uk = Ur.next()
                for hf in range(2):
                    bE, bEk = pU.next()
                    bO, bOk = pUO.next()
                    banks = [(bE, bEk), (bO, bOk)]
                    for hh in range(8):
                        h = hf * 8 + hh
                        fc, e_ = h // 2, h % 2
                        q = hh // 2
                        rows = slice(e_ * 64, e_ * 64 + 64)
                        bank, bk_ = banks[e_]
                        mm(P, bank[0:64, q * 64:(q + 1) * 64], apt[rows, fc, :], STb[rows, fc, :], True, True, [aptk, "STb"], [bk_])
                    for e_ in range(2):
                        bank, bk_ = banks[e_]
                        Uv = U[:, hf * 8:(hf + 1) * 8, :].rearrange("p (q e) v -> p q e v", e=2)[:, :, e_, :]
                        Pv = ZcP[:, hf * 8:(hf + 1) * 8, :].rearrange("p (q e) v -> p q e v", e=2)[:, :, e_, :]
                        tt(P, "dve", Uv, bank[0:64, 0:256].rearrange("p (q v) -> p q v", v=64), Pv, ALU.add,
                           [bk_] + [(Zck, fc) for fc in range(hf * 4, hf * 4 + 4)], [(Uk, hf)])
                    py2, py2k = pY2.next()
                    for hh in range(8):
                        h = hf * 8 + hh
                        fc, e_ = h // 2, h % 2
                        q = hh // 2
                        rows = slice(e_ * 64, e_ * 64 + 64)
                        bank, bk_ = banks[e_]
                        mm(P, bank[0:64, 256 + q * 64:256 + (q + 1) * 64], ot["Rt"][0][rows, fc, cs], STb[rows, fc, :], True, True,
                           [ot["Rt"][1], "STb"], [bk_])
                        a5s, a5sk = A5h[h]
                        mm(P, py2[0:64, hh * 64:(hh + 1) * 64], a5s[:, 128:192], U[:, h, :], True, False, [a5sk, (Uk, hf)], [py2k])
                        mm(P, py2[0:64, hh * 64:(hh + 1) * 64], a5s[:, 64:128], tm["V"][0][:, h * 64:(h + 1) * 64], False, True,
                           [a5sk, (tm["V"][1], hf)], [py2k])
                    y1s, y1sk = Y1r.next()
                    for e_ in range(2):
                        bank, bk_ = banks[e_]
                        actf(P, y1s[:].rearrange("p (q e v) -> p q e v", e=2, v=64)[:, :, e_, :],
                             bank[0:64, 256:512].rearrange("p (q v) -> p q v", v=64), AF.Copy, [bk_], [y1sk])
                    tt(P, "dve", ys[:, hf * 512:(hf + 1) * 512], py2[0:64, :], y1s[:], ALU.add, [py2k, y1sk], [(ysk, hf)])
                    ps_, psk = pS.next()
                    for hh in range(8):
                        h = hf * 8 + hh
                        fc, e_ = h // 2, h % 2
                        orow = slice(e_ * 64, e_ * 64 + 64)
                        ocol = slice((fc % 4) * 64, (fc % 4) * 64 + 64)
                        mm(P, ps_[orow, ocol], tm["Bg"][0][:, h * 64:(h + 1) * 64], U[:, h, :], True, False,
                           [(tm["Bg"][1], hf), (Uk, hf)], [psk])
                        mm(P, ps_[orow, ocol], tm["Kg"][0][:, h * 64:(h + 1) * 64], tm["V"][0][:, h * 64:(h + 1) * 64], False, True,
                           [(tm["Kg"][1], hf), (tm["V"][1], hf)], [psk])
                    tS, tSk = tmpS.next()
                    fsl = slice(hf * 4, hf * 4 + 4)
                    tt(P, "dve", tS[:], ST[:, fsl, :], gC2[:, fsl, cq:cq + 1].broadcast_to([128, 4, 64]), ALU.mult,
                       ["ST", gC2k], [tSk])
                    tt(P, "dve", ST[:, fsl, :], ps_[:, 0:256].rearrange("p (c v) -> p c v", v=64), tS[:], ALU.add, [psk, tSk], ["ST"])
                    actf(P, STb[:, fsl, :], ST[:, fsl, :], AF.Copy, ["ST"], ["STb"])
                for hf in range(2):
                    p, pk = pT.next()
                    for q in range(4):
                        fc = hf * 4 + q
                        trp(P, p[:, q * 64:(q + 1) * 64], ys[:, fc * 128:(fc + 1) * 128], C.identb[0:64, 0:64], [(ysk, hf), "identb"], [pk])
                    P.op("dve", lambda e, yt=yt, p=p, hf=hf, cs=cs: e.tensor_copy(
                        out=yt[:, hf * 4:(hf + 1) * 4, cs], in_=p[:, 0:256].rearrange("p (c t) -> p c t", t=64)), reads=[pk], writes=[ytk])
            P.dma(yTd.rearrange("(c p) t -> p c t", p=128)[:, :, tsl], yt[:], reads=[ytk], writes=[(pre + "yT", jb)])

    if os.environ.get("RW_STOP") == "B":
        return
    with P.phase(last=last):
        TC = 512
        stage = Ring(P, "wst", [128, 2048], F32, 2)
        wob = load_weight_bf16(P, "wout", W["w_out"], D, D, stage)
        gnw = colp("gn_w"); gnb = colp("gn_b")
        yr = Ring(P, "yin", [128, 8, TC], F32, 2)
        br = Ring(P, "bin", [128, 8, TC], F32, 2)
        gr = Ring(P, "gin", [128, 8, TC], BF16, 2)
        ygr = Ring(P, "ygo", [128, 8, TC], BF16, 2)
        pm = Ring(P, "pm", [128, 512], F32, 2, psum=True)
        pv_ = Ring(P, "pv", [128, 512], F32, 2, psum=True)
        po = Ring(P, "po", [128, 512], F32, 3, psum=True)
        ycr = Ring(P, "yc", [128, TC], F32, 2)
        sqr = Ring(P, "sq2", [128, TC], BF16, 2)
        rsr = Ring(P, "rs2", [128, TC], F32, 2)
        xr = Ring(P, "xr", [128, D], F32, 3)
        gneps = P.sb("gneps", [128, 1], F32)
        P.op("pool", lambda e: e.memset(gneps[:], RW_GN_EPS), writes=["gneps"])
        for j in range(T // TC):
            tsl = slice(j * TC, (j + 1) * TC)
            yi, yik = yr.next(); bi, bik = br.next(); gi_, gik_ = gr.next()
            P.dma(yi[:], yTd.rearrange("(c p) t -> p c t", p=128)[:, :, tsl], reads=[(pre + "yT", jj) for jj in range(4 * j, 4 * j + 4)], writes=[yik])
            P.dma(bi[:], Dm["BON"].rearrange("(c p) t -> p c t", p=128)[:, :, tsl], reads=[(pre + "BON", jj) for jj in (2 * j, 2 * j + 1)], writes=[bik], e="act")
            P.dma(gi_[:], gTd.rearrange("(c p) t -> p c t", p=128)[:, :, tsl], reads=[(pre + "g", jj) for jj in (2 * j, 2 * j + 1)], writes=[gik_])
            yg, ygk = ygr.next()
            for fc in range(8):
                m, mk = pm.next()
                for c0 in range(0, TC, 128):
                    mm(P, m[:, c0:c0 + 128], C.bonesf[:], yi[:, fc, c0:c0 + 128], True, True, ["bonesf", yik], [mk])
                yc, yck = ycr.next()
                stt(P, yc[:], m[:], -1.0 / 64, yi[:, fc, :], ALU.mult, ALU.add, [mk, yik], [yck])
                sq, sqk = sqr.next()
                actf(P, sq[:], yc[:], AF.Square, [yck], [sqk])
                v_, vk = pv_.next()
                mm(P, v_[:], C.bonesb[:], sq[:], True, True, ["bonesb", sqk], [vk])
                rs, rsk = rsr.next()
                actf(P, rs[:], v_[:], AF.Sqrt, [vk, "gneps"], [rsk], bias=gneps[:, 0:1], scale=1.0 / 64)
                P.op("dve", lambda e, rs=rs: e.reciprocal(out=rs[:], in_=rs[:]), reads=[rsk], writes=[rsk])
                tt(P, "pool", yc[:], yc[:], rs[:], ALU.mult, [yck, rsk], [yck])
                ts(P, "dve", yc[:], yc[:], gnw[:, fc:fc + 1], gnb[:, fc:fc + 1], ALU.mult, ALU.add, [yck, "gn_w", "gn_b"], [yck])
                tt(P, "pool", yc[:], yc[:], bi[:, fc, :], ALU.add, [yck, bik], [yck])
                tt(P, "dve", yg[:, fc, :], yc[:], gi_[:, fc, :], ALU.mult, [yck, gik_], [(ygk, fc)])
            for s in range(TC // 128):
                r0 = j * TC + s * 128
                xt, xk = xr.next()
                P.dma(xt[:], cur[r0:r0 + 128, :], reads=[("x", r0 // 128)], writes=[xk], e="act")
                for hf in range(2):
                    o, ok = po.next()
                    for c in range(8):
                        mm(P, o[:], yg[:, c, s * 128:(s + 1) * 128], wob[:, c, hf * 512:(hf + 1) * 512], c == 0, c == 7,
                           [(ygk, c), "wout"], [ok])
                    tt(P, "dve", xt[:, hf * 512:(hf + 1) * 512], o[:], xt[:, hf * 512:(hf + 1) * 512], ALU.add, [ok, xk], [xk])
                P.dma(y_out[r0:r0 + 128, :], xt[:], reads=[xk], writes=[("x", r0 // 128)], final=last)


LAYER_PARAMS = {
    0: ["norm", "mu", "w_in", "w_up", "w0", "a_up", "a0", "k_k", "k_a", "r_k", "gn_w", "gn_b", "w_out"],
    1: ["norm", "w_in", "conv_w", "conv_b", "dt_bias", "a_log", "d_skip", "gnorm_w", "w_out"],
    2: ["norm", "w_in", "q_norm", "k_norm", "w_out"],
    3: ["norm", "mu", "w_in", "w_up", "w0", "a_up", "a0", "k_k", "k_a", "r_k", "gn_w", "gn_b", "w_out", "v_up", "v0"],
}


def host_consts():
    c = {}
    c["c_ident"] = np.eye(128, dtype=np.float32)
    i = np.arange(128)
    c["c_bones"] = (i[:, None] // 64 == i[None, :] // 64).astype(np.float32)
    tq = np.arange(256)
    c["c_mask0"] = np.tile(((tq % 64) != 0).astype(np.float32)[None, :], (128, 1))
    ss_, tt_ = np.arange(64)[:, None], np.arange(64)[None, :]
    strict = (ss_ < tt_).astype(np.float32); incl = (ss_ <= tt_).astype(np.float32)
    c["c_mask5"] = np.concatenate([strict, incl, incl, strict.T, strict], axis=1)
    c["c_triu"] = (i[:, None] <= i[None, :]).astype(np.float32)
    c["c_ntril"] = -(i[:, None] >= i[None, :]).astype(np.float32)
    t = np.arange(512)
    c["c_mask"] = np.stack([((128 * r + i)[:, None] < t[None, :]).astype(np.float32) for r in range(4)], axis=1)
    return c


def host_layout(name, arr):
    a = np.asarray(arr, dtype=np.float32)
    short = name.split("_", 1)[1]
    if short == "norm":
        return np.ascontiguousarray(a.reshape(8, 128).T)
    if short in ("q_norm", "k_norm"):
        return np.ascontiguousarray(np.tile(a, 2).reshape(128, 1))
    if short in ("w0", "a0", "k_k", "k_a", "gn_w", "gn_b", "v0", "r_k"):
        return np.ascontiguousarray(a.reshape(8, 128).T)
    if short == "mu":
        return np.ascontiguousarray(a.reshape(6, 8, 128).transpose(2, 0, 1))
    if name == "l1_conv_w":
        return np.ascontiguousarray(a.T.reshape(32, 128, 4).transpose(1, 0, 2))
    if name == "l1_conv_b":
        return np.ascontiguousarray(a.reshape(32, 128).T)
    if name in ("l1_dt_bias", "l1_a_log"):
        return np.ascontiguousarray(np.tile(a[None, :], (128, 1)))
    if name == "l1_d_skip":
        return np.ascontiguousarray(np.repeat(a, 64).reshape(16, 128).T)
    if name == "l1_gnorm_w":
        return np.ascontiguousarray(a.reshape(16, 128).T)
    return np.ascontiguousarray(a)


def build_program(T, layers):
    nc = bass.Bass("TRN2", target_bir_lowering=False)
    x_in = nc.dram_tensor("x", [T, D], F32, kind="ExternalInput").ap()
    y_out = nc.dram_tensor("y", [T, D], F32, kind="ExternalOutput").ap()
    hc = host_consts()
    cap = {k: nc.dram_tensor(k, list(v.shape), F32, kind="ExternalInput").ap() for k, v in hc.items()}
    Wd = {}
    shapes = param_shapes()
    for L in layers:
        Wd[L] = {}
        for pn in LAYER_PARAMS[L]:
            full = "l%d_%s" % (L, pn)
            Wd[L][pn] = nc.dram_tensor(full, list(shapes[full]), F32, kind="ExternalInput").ap()
    P = Prog(nc)
    C = Ctx()
    specs = [("ident", "c_ident", [128, 128], None), ("bonesb", "c_bones", [128, 128], BF16),
             ("ntril", "c_ntril", [128, 128], BF16), ("maskb", "c_mask", [128, 4, 512], BF16),
             ("triu", "c_triu", [128, 128], None), ("identb", "c_ident", [128, 128], BF16),
             ("mask0", "c_mask0", [128, 256], None), ("mask5", "c_mask5", [64, 320], None),
             ("bonesf", "c_bones", [128, 128], None)]
    tiles = {}
    for key, cn, shape, cast in specs:
        tiles[key] = P.sb(key, shape, cast or F32)
        setattr(C, key, tiles[key])
    C.onesf = P.sb("onesf", [128, 128], F32)
    C.epsc = P.sb("epsc", [128, 1], F32)
    C.onec = P.sb("onec", [128, 1], F32)
    C.onesb = P.sb("onesb", [128, 128], BF16)
    with P.phase():
        for key, cn, shape, cast in specs:
            t = tiles[key]
            if cast is None:
                P.dma(t[:], cap[cn], writes=[key])
            else:
                st = P.sb(key + "_f", shape, F32)
                P.dma(st[:], cap[cn], writes=[key + "_f"])
                P.op("pool", lambda e, t=t, st=st: e.tensor_copy(out=t[:], in_=st[:]), reads=[key + "_f"], writes=[key])
        P.op("pool", lambda e: e.memset(C.onesf[:], 1.0), writes=["onesf"])
        P.op("pool", lambda e: e.memset(C.epsc[:], NORM_EPS), writes=["epsc"])
        P.op("pool", lambda e: e.memset(C.onec[:], 1.0), writes=["onec"])
        P.op("pool", lambda e: e.memset(C.onesb[:], 1.0), writes=["onesb"])
    cur = x_in
    fns = {0: None, 1: None, 2: layer_sb, 3: None}
    for i, L in enumerate(layers):
        last = (i == len(layers) - 1)
        FNS[L](P, C, L, T, Wd[L], cur, y_out, last)
        cur = y_out
    P.close()
    return nc


def param_shapes():
    s = {}
    for L, vres in ((0, False), (3, True)):
        p = "l%d_" % L
        ncols = 4 * 1024 + 64 + 64 + (32 if vres else 0)
        s.update({p + "norm": (128, 8), p + "mu": (128, 6, 8), p + "w_in": (1024, ncols), p + "w_up": (64, 1024),
                  p + "w0": (128, 8), p + "a_up": (64, 1024), p + "a0": (128, 8), p + "k_k": (128, 8), p + "k_a": (128, 8),
                  p + "r_k": (128, 8), p + "gn_w": (128, 8), p + "gn_b": (128, 8), p + "w_out": (1024, 1024)})
        if vres:
            s.update({p + "v_up": (32, 1024), p + "v0": (128, 8)})
    s.update({"l1_norm": (128, 8), "l1_w_in": (1024, 6176), "l1_conv_w": (128, 32, 4), "l1_conv_b": (128, 32), "l1_dt_bias": (128, 32),
              "l1_a_log": (128, 32), "l1_d_skip": (128, 16), "l1_gnorm_w": (128, 16), "l1_w_out": (2048, 1024)})
    s.update({"l2_norm": (128, 8), "l2_w_in": (1024, 4096), "l2_q_norm": (128, 1), "l2_k_norm": (128, 1), "l2_w_out": (1024, 1024)})
    return s


FNS = {0: layer_rwkv, 1: layer_ssd, 2: layer_sb, 3: layer_rwkv}


def make_in_maps(inputs, layers, ncores, T):
    hc = host_consts()
    shapes = param_shapes()
    shared = dict(hc)
    for L in layers:
        for pn in LAYER_PARAMS[L]:
            full = "l%d_%s" % (L, pn)
            a = host_layout(full, inputs[full])
            assert tuple(a.shape) == tuple(shapes[full]), (full, a.shape, shapes[full])
            shared[full] = a
    x = np.asarray(inputs["x"], dtype=np.float32)
    maps = []
    for b in range(ncores):
        m = dict(shared)
        m["x"] = np.ascontiguousarray(x[b, :T])
        maps.append(m)
    return maps


ALL_INPUT_NAMES = (
    "x",
    "l0_norm",
    "l0_mu",
    "l0_w_in",
    "l0_w_up",
    "l0_w0",
    "l0_a_up",
    "l0_a0",
    "l0_k_k",
    "l0_k_a",
    "l0_r_k",
    "l0_gn_w",
    "l0_gn_b",
    "l0_w_out",
    "l1_norm",
    "l1_w_in",
    "l1_conv_w",
    "l1_conv_b",
    "l1_dt_bias",
    "l1_a_log",
    "l1_d_skip",
    "l1_gnorm_w",
    "l1_w_out",
    "l2_norm",
    "l2_w_in",
    "l2_q_norm",
    "l2_k_norm",
    "l2_w_out",
    "l3_norm",
    "l3_mu",
    "l3_w_in",
    "l3_w_up",
    "l3_w0",
    "l3_a_up",
    "l3_a0",
    "l3_k_k",
    "l3_k_a",
    "l3_r_k",
    "l3_gn_w",
    "l3_gn_b",
    "l3_w_out",
    "l3_v_up",
    "l3_v0",
)


def kernel(**inputs):
    missing = [n for n in ALL_INPUT_NAMES if n not in inputs]
    assert not missing, missing
    T = 4096
    layers = [0, 1, 2, 3]
    nc = build_program(T, layers)
    maps = make_in_maps(inputs, layers, 8, T)
    res = run_bass_kernel_spmd(nc, maps, core_ids=list(range(8)))
    return np.stack([r["y"] for r in res.results], axis=0).astype(np.float32)
```

```python
import contextlib
import math
import numpy as np
import concourse.bass as bass
import concourse.mybir as mybir
from concourse.bass_utils import run_bass_kernel_spmd

F32 = mybir.dt.float32
BF16 = mybir.dt.bfloat16
AF = mybir.ActivationFunctionType
ALU = mybir.AluOpType
AX = mybir.AxisListType

D = 1024
NORM_EPS = 1e-6
N_DMA_SEMS = 24


class Prog:
    ENGS = ("pe", "act", "dve", "pool", "sp")

    def __init__(self, nc):
        self.nc = nc
        self.root = contextlib.ExitStack()
        self.stack = self.root
        self.ops = {e: [] for e in self.ENGS}
        self.cnt = {e: 0 for e in self.ENGS}
        self.esem = {e: self.root.enter_context(nc.semaphore("s_" + e)) for e in self.ENGS}
        self.dsem = [self.root.enter_context(nc.semaphore("d%d" % i)) for i in range(N_DMA_SEMS)]
        self.dcnt = [0] * N_DMA_SEMS
        self.dlast = [None] * N_DMA_SEMS
        self.ndma = 0
        self.lw = {}
        self.rd = {}
        self.waited = {e: {} for e in self.ENGS}
        self.sem_by_id = {}
        self.final = []
        self.n_inst = 0
        self.uid = 0

    def _sid(self, sem):
        self.sem_by_id[id(sem)] = sem
        return id(sem)

    def sb(self, name, shape, dt=F32, root=False):
        self.uid += 1
        return (self.root if root else self.stack).enter_context(self.nc.sbuf_tensor("%s_%d" % (name, self.uid), list(shape), dt))

    def ps(self, name, shape, dt=F32):
        self.uid += 1
        return self.stack.enter_context(self.nc.psum_tensor("%s_%d" % (name, self.uid), list(shape), dt))

    def _deps(self, e, reads, writes):
        deps = []
        for k in reads:
            t = self.lw.get(k)
            if t is not None:
                deps.append(t)
        for k in writes:
            t = self.lw.get(k)
            if t is not None:
                deps.append(t)
            deps.extend(self.rd.get(k, ()))
        waits = {}
        for (sid, val, src) in deps:
            if src == e and e == "pe":
                continue
            if self.waited[e].get(sid, 0) >= val:
                continue
            if waits.get(sid, 0) < val:
                waits[sid] = val
        for sid, val in waits.items():
            self.waited[e][sid] = val
        return [(self.sem_by_id[sid], val) for sid, val in waits.items()]

    def _record(self, tok, reads, writes):
        for k in reads:
            lst = self.rd.setdefault(k, [])
            lst[:] = [t for t in lst if t[0] != tok[0]]
            lst.append(tok)
        for k in writes:
            self.lw[k] = tok
            self.rd[k] = []

    def op(self, e, fn, reads=(), writes=()):
        waits = self._deps(e, reads, writes)
        self.cnt[e] += 1
        sem = self.esem[e]
        tok = (self._sid(sem), self.cnt[e], e)
        self.ops[e].append((waits, fn, (sem, 1)))
        self._record(tok, reads, writes)
        self.n_inst += 1
        return tok

    def dma(self, out, in_, reads=(), writes=(), e="sp", final=False):
        i = self.ndma % N_DMA_SEMS
        self.ndma += 1
        sem = self.dsem[i]
        waits = self._deps(e, reads, writes)
        prev = self.dlast[i]
        if prev is not None and self.waited[e].get(prev[0], 0) < prev[1]:
            waits.append((sem, prev[1]))
            self.waited[e][prev[0]] = prev[1]
        self.dcnt[i] += 16
        tok = (self._sid(sem), self.dcnt[i], "dma")
        self.dlast[i] = tok
        self.ops[e].append((waits, lambda eng: eng.dma_start(out=out, in_=in_), (sem, 16)))
        self._record(tok, reads, writes)
        if final:
            self.final.append(tok)
        self.n_inst += 1
        return tok

    @contextlib.contextmanager
    def phase(self, last=False):
        outer = self.stack
        with contextlib.ExitStack() as st:
            self.stack = st
            yield
            self._emit(last)
        self.stack = outer

    def _emit(self, last):
        nc = self.nc
        fin = [(self.dsem[i], self.dcnt[i]) for i in range(N_DMA_SEMS) if self.dcnt[i] > 0]
        for i in range(N_DMA_SEMS):
            if self.dcnt[i] > 0:
                self.waited["sp"][id(self.dsem[i])] = self.dcnt[i]
        ops = self.ops
        with nc.Block() as block:
            def body(ename):
                def f(eng):
                    for waits, fn, (sem, inc) in ops[ename]:
                        for (s, v) in waits:
                            eng.wait_ge(s, v)
                        fn(eng).then_inc(sem, inc)
                    if ename == "sp":
                        for (s, v) in fin:
                            eng.wait_ge(s, v)
                return f
            block.sync(body("sp"))
            block.scalar(body("act"))
            block.vector(body("dve"))
            block.gpsimd(body("pool"))
            block.tensor(body("pe"))
        self.ops = {e: [] for e in self.ENGS}

    def close(self):
        self.root.close()


class Ring:
    def __init__(self, P, name, shape, dt=F32, n=2, psum=False):
        mk = P.ps if psum else P.sb
        self.t = [mk("%s%d" % (name, i), shape, dt) for i in range(n)]
        self.k = ["%s#%d#%d" % (name, P.uid, i) for i in range(n)]
        self.i = -1
        self.n = n

    def next(self):
        self.i = (self.i + 1) % self.n
        return self.t[self.i], self.k[self.i]


class Ctx:
    pass


def load_const(P, key, ap_dram, shape, cast=None):
    if cast is None:
        t = P.sb(key, shape, F32, root=True)
        P.dma(t[:], ap_dram, writes=[key])
        return t
    t = P.sb(key + "_f", shape, F32)
    P.dma(t[:], ap_dram, writes=[key + "_f"])
    tb = P.sb(key, shape, cast, root=True)
    P.op("pool", lambda e: e.tensor_copy(out=tb[:], in_=t[:]), reads=[key + "_f"], writes=[key])
    return tb


def load_weight_bf16(P, name, w_dram, kdim, ncols, stage):
    nck = kdim // 128
    wb = P.sb(name, [128, nck, ncols], BF16)
    wv = w_dram.rearrange("(c p) n -> p c n", p=128)
    CW = stage.t[0].shape[1]
    for c in range(nck):
        for c0 in range(0, ncols, CW):
            cw = min(CW, ncols - c0)
            st, sk = stage.next()
            P.dma(st[:, 0:cw], wv[:, c, c0:c0 + cw], writes=[sk])
            ce = ("act", "dve", "pool", "act", "dve")[(c * 7 + c0 // CW) % 5]
            if ce == "act":
                P.op("act", lambda e, st=st, c=c, c0=c0, cw=cw: e.copy(out=wb[:, c, c0:c0 + cw], in_=st[:, 0:cw]), reads=[sk], writes=[name])
            else:
                P.op(ce, lambda e, st=st, c=c, c0=c0, cw=cw: e.tensor_copy(out=wb[:, c, c0:c0 + cw], in_=st[:, 0:cw]),
                     reads=[sk], writes=[name])
    return wb


def make_hT(P, C, j, src, TT, rings, gcol, eps=NORM_EPS):
    hT, hk = rings["hT"].next()
    for s in range(TT // 128):
        r0 = j * TT + s * 128
        xt, xk = rings["xt"].next()
        P.dma(xt[:], src[r0:r0 + 128, :], reads=[("x", r0 // 128)], writes=[xk])
        sq, sqk = rings["sq"].next()
        ss, ssk = rings["ss"].next()
        P.op("act", lambda e, xt=xt, sq=sq, ss=ss: e.activation(out=sq[:], in_=xt[:], func=AF.Square, accum_out=ss[:, 0:1]),
             reads=[xk], writes=[sqk, ssk])
        P.op("act", lambda e, ss=ss: e.activation(out=ss[:, 1:2], in_=ss[:, 0:1], func=AF.Sqrt, scale=1.0 / D, bias=C.epsc[:, 0:1]),
             reads=[ssk], writes=[ssk])
        P.op("dve", lambda e, ss=ss: e.reciprocal(out=ss[:, 2:3], in_=ss[:, 1:2]), reads=[ssk], writes=[ssk])
        P.op("dve", lambda e, xt=xt, sq=sq, ss=ss: e.tensor_scalar(out=sq[:], in0=xt[:], scalar1=ss[:, 2:3], scalar2=None, op0=ALU.mult),
             reads=[xk, ssk], writes=[sqk])
        for hf in range(2):
            pT, pk = rings["pT"].next()
            for c4 in range(4):
                c = hf * 4 + c4
                P.op("pe", lambda e, pT=pT, sq=sq, c=c, c4=c4: e.transpose(pT[:, c4 * 128:(c4 + 1) * 128], sq[:, c * 128:(c + 1) * 128], C.ident[:]),
                     reads=[sqk, "ident"], writes=[pk])
            for c4 in range(4):
                c = hf * 4 + c4
                P.op("act", lambda e, pT=pT, hT=hT, c=c, c4=c4, s=s: e.activation(
                    out=hT[:, c, 1 + s * 128:1 + (s + 1) * 128], in_=pT[:, c4 * 128:(c4 + 1) * 128], func=AF.Copy, scale=gcol[:, c:c + 1]),
                    reads=[pk, "gcol"], writes=[hk])
    return hT, hk


def out_proj(P, C, L, T, ygT, kdim, w_out, cur, y_out, last):
    nck = kdim // 128
    with P.phase(last=last):
        stage = Ring(P, "wst", [128, 2048], F32, 2)
        wb = load_weight_bf16(P, "wout", w_out, kdim, D, stage)
        ygr = Ring(P, "ygr", [128, nck, 128], BF16, 3)
        xr = Ring(P, "xr", [128, D], F32, 3)
        pr = Ring(P, "po", [128, 512], F32, 4, psum=True)
        ygv = ygT.rearrange("(c p) t -> p c t", p=128)
        for s in range(T // 128):
            yg, ygk = ygr.next()
            P.dma(yg[:], ygv[:, :, s * 128:(s + 1) * 128], reads=[("yg", L, s // 4)], writes=[ygk])
            xt, xk = xr.next()
            P.dma(xt[:], cur[s * 128:(s + 1) * 128, :], reads=[("x", s)], writes=[xk], e="act")
            for hf in range(2):
                po, pk = pr.next()
                for c in range(nck):
                    P.op("pe", lambda e, po=po, yg=yg, c=c, hf=hf: e.matmul(
                        po[:], lhsT=yg[:, c, :], rhs=wb[:, c, hf * 512:(hf + 1) * 512], start=(c == 0), stop=(c == nck - 1)),
                        reads=[ygk, "wout"], writes=[pk])
                P.op("dve", lambda e, po=po, xt=xt, hf=hf: e.tensor_tensor(
                    out=xt[:, hf * 512:(hf + 1) * 512], in0=po[:], in1=xt[:, hf * 512:(hf + 1) * 512], op=ALU.add),
                    reads=[pk, xk], writes=[xk])
            P.dma(y_out[s * 128:(s + 1) * 128, :], xt[:], reads=[xk], writes=[("x", s)], final=last)


def in_rings(P, TT, nh=2, nx=3, nq=2, npt=2):
    return {
        "hT": Ring(P, "hT", [128, 8, 1 + TT], F32, nh),
        "xt": Ring(P, "xt", [128, D], F32, nx),
        "sq": Ring(P, "sq", [128, D], F32, nq),
        "ss": Ring(P, "ss", [128, 4], F32, 4),
        "pT": Ring(P, "pT", [128, 512], F32, npt, psum=True),
    }


def layer_sb(P, C, L, T, W, cur, y_out, last):
    nc = P.nc
    TT = 512
    NT = T // TT
    H = 16
    qT = nc.dram_tensor("sb_qT", [D, T], BF16, kind="Internal").ap()
    kT = nc.dram_tensor("sb_kT", [D, T], BF16, kind="Internal").ap()
    gT = nc.dram_tensor("sb_gT", [D, T], BF16, kind="Internal").ap()
    vTM = nc.dram_tensor("sb_v", [T, D], BF16, kind="Internal").ap()
    ygT = nc.dram_tensor("sb_ygT", [D, T], BF16, kind="Internal").ap()

    with P.phase():
        stage = Ring(P, "wst", [128, 2048], F32, 2)
        wb = load_weight_bf16(P, "win", W["w_in"], D, 4 * D, stage)
        gcol = P.sb("gcol", [128, 8], F32)
        P.dma(gcol[:], W["norm"], writes=["gcol"])
        qn = P.sb("qn", [128, 2], F32)
        P.dma(qn[:, 0:1], W["q_norm"], writes=["qn"])
        P.dma(qn[:, 1:2], W["k_norm"], writes=["qn"])
        P.op("dve", lambda e: e.tensor_scalar(out=qn[:, 0:1], in0=qn[:, 0:1], scalar1=0.125, scalar2=None, op0=ALU.mult),
             reads=["qn"], writes=["qn"])
        rings = in_rings(P, TT)
        hbr = Ring(P, "hb", [128, 8, TT], BF16, 2)
        pp = Ring(P, "pp", [128, 512], F32, 3, psum=True)
        pm = Ring(P, "pm", [128, 512], F32, 2, psum=True)
        sqb = Ring(P, "sqb", [128, 512], BF16, 2)
        rs = Ring(P, "rs", [128, 512], F32, 2)
        ob = Ring(P, "ob", [128, 512], BF16, 4)
        for j in range(NT):
            hT, hk = make_hT(P, C, j, cur, TT, rings, gcol)
            hb, hbk = hbr.next()
            for c in range(8):
                P.op("pool", lambda e, hb=hb, hT=hT, c=c: e.tensor_copy(out=hb[:, c, :], in_=hT[:, c, 1:1 + TT]),
                     reads=[hk], writes=[hbk])
            tsl = slice(j * TT, (j + 1) * TT)
            for which in range(3):
                cbase = {0: 0, 1: D, 2: 3 * D}[which]
                for fc in range(8):
                    p, pk = pp.next()
                    for c in range(8):
                        P.op("pe", lambda e, p=p, c=c, hb=hb, col=cbase + fc * 128: e.matmul(
                            p[:], lhsT=wb[:, c, col:col + 128], rhs=hb[:, c, :], start=(c == 0), stop=(c == 7)),
                            reads=["win", hbk], writes=[pk])
                    o, ok = ob.next()
                    if which == 2:
                        P.op("act", lambda e, p=p, o=o: e.activation(out=o[:], in_=p[:], func=AF.Silu), reads=[pk], writes=[ok])
                        P.dma(gT[fc * 128:(fc + 1) * 128, tsl], o[:], reads=[ok], writes=[("sbg", j)])
                    else:
                        s2, s2k = sqb.next()
                        P.op("act", lambda e, p=p, s2=s2: e.activation(out=s2[:], in_=p[:], func=AF.Square), reads=[pk], writes=[s2k])
                        m, mk = pm.next()
                        P.op("pe", lambda e, m=m, s2=s2: e.matmul(m[:], lhsT=C.bonesb[:], rhs=s2[:], start=True, stop=True),
                             reads=["bonesb", s2k], writes=[mk])
                        r, rk = rs.next()
                        P.op("act", lambda e, m=m, r=r: e.activation(out=r[:], in_=m[:], func=AF.Sqrt, scale=1.0 / 64, bias=C.epsc[:, 0:1]),
                             reads=[mk], writes=[rk])
                        P.op("dve", lambda e, r=r: e.reciprocal(out=r[:], in_=r[:]), reads=[rk], writes=[rk])
                        P.op("dve", lambda e, p=p, r=r, o=o, which=which: e.scalar_tensor_tensor(
                            out=o[:], in0=p[:], scalar=qn[:, which:which + 1], in1=r[:], op0=ALU.mult, op1=ALU.mult),
                            reads=[pk, rk, "qn"], writes=[ok])
                        dst = qT if which == 0 else kT
                        P.dma(dst[fc * 128:(fc + 1) * 128, tsl], o[:], reads=[ok], writes=[("sbq" if which == 0 else "sbk", j)])
            for s in range(TT // 128):
                for cb in range(2):
                    p, pk = pp.next()
                    for c in range(8):
                        P.op("pe", lambda e, p=p, c=c, hb=hb, s=s, cb=cb: e.matmul(
                            p[:], lhsT=hb[:, c, s * 128:(s + 1) * 128], rhs=wb[:, c, 2 * D + cb * 512:2 * D + (cb + 1) * 512],
                            start=(c == 0), stop=(c == 7)), reads=["win", hbk], writes=[pk])
                    o, ok = ob.next()
                    P.op("act", lambda e, p=p, o=o: e.copy(out=o[:], in_=p[:]), reads=[pk], writes=[ok])
                    r0 = j * TT + s * 128
                    P.dma(vTM[r0:r0 + 128, cb * 512:(cb + 1) * 512], o[:], reads=[ok], writes=[("sbv", j)])

    with P.phase():
        NB = T // 128
        kh_r = Ring(P, "kh", [64, T], BF16, 2)
        vh_r = Ring(P, "vh", [128, NB, 64], BF16, 2)
        qh_r = Ring(P, "qh", [64, TT], BF16, 2)
        gh_r = Ring(P, "gh", [64, TT], BF16, 2)
        pz = Ring(P, "pz", [128, 512], F32, 2, psum=True)
        pb = Ring(P, "pb", [128, 512], F32, 2, psum=True)
        pc = Ring(P, "pc", [64, 512], F32, 2, psum=True)
        po = Ring(P, "pov", [64, 512], F32, 2, psum=True)
        Er = Ring(P, "E", [128, 512], F32, 2)
        spr = Ring(P, "sp", [128, 512], BF16, 3)
        Pr = Ring(P, "Pm", [128, 512], BF16, 3)
        fr = Ring(P, "f", [64, 512], F32, 2)
        accr = Ring(P, "acc", [64, 512], F32, 2)
        ygr = Ring(P, "yg", [64, 512], BF16, 2)
        vview = vTM.rearrange("(b p) d -> p b d", p=128)
        qh_r = Ring(P, "qh3", [64, TT], BF16, 3)
        gh_r = Ring(P, "gh3", [64, TT], BF16, 3)
        spr = Ring(P, "sp4", [128, 512], BF16, 4)
        Pr = Ring(P, "Pm4", [128, 512], BF16, 4)
        units = [(h, tq) for h in range(H) for tq in range(NT)]
        ures = {}

        def load_unit(u):
            h, tq = units[u]
            r = {}
            if tq == 0:
                kh, khk = kh_r.next()
                P.dma(kh[:], kT[h * 64:(h + 1) * 64, :], reads=[("sbk", j) for j in range(NT)], writes=[khk])
                vh, vhk = vh_r.next()
                P.dma(vh[:], vview[:, :, h * 64:(h + 1) * 64], reads=[("sbv", j) for j in range(NT)], writes=[vhk], e="act")
                ures[("kv", h)] = (kh, khk, vh, vhk)
            tsl = slice(tq * TT, (tq + 1) * TT)
            r["qh"], r["qhk"] = qh_r.next()
            P.dma(r["qh"][:], qT[h * 64:(h + 1) * 64, tsl], reads=[("sbq", tq)], writes=[r["qhk"]])
            r["gh"], r["ghk"] = gh_r.next()
            P.dma(r["gh"][:], gT[h * 64:(h + 1) * 64, tsl], reads=[("sbg", tq)], writes=[r["ghk"]])
            r["acc"], r["acck"] = accr.next()
            ures[u] = r

        blocks = []
        for u, (h, tq) in enumerate(units):
            nkb = 4 * tq + 4
            for b_ in range(nkb):
                blocks.append((u, h, tq, b_, b_ - 4 * tq, b_ == nkb - 1))
        NBK = len(blocks)
        bres = {}
        load_unit(0)

        def S12(i):
            u, h, tq, b_, r, lastb = blocks[i]
            if b_ == 0 and u + 1 < len(units):
                load_unit(u + 1)
            U_ = ures[u]
            kh, khk, vh, vhk = ures[("kv", h)]
            R = {}
            cl = slice(128 * max(r, 0), 512)
            R["cl"] = cl
            z, zk = pz.next()
            mm(P, z[:, cl], kh[:, b_ * 128:(b_ + 1) * 128], U_["qh"][:, cl], True, True, [khk, U_["qhk"]], [zk])
            E, Ek = Er.next()
            actf(P, E[:, cl], z[:, cl], AF.Exp, [zk], [Ek])
            sp, spk = spr.next()
            actf(P, sp[:, cl], E[:, cl], AF.Ln, [Ek, "onec"], [spk], bias=C.onec[:, 0:1])
            if r >= 0:
                tt(P, "pool", sp[:, cl], sp[:, cl], C.maskb[:, r, cl], ALU.mult, [spk, "maskb"], [spk])
            R["sp"], R["spk"] = sp, spk
            bres[i] = R

        def S34(i):
            u, h, tq, b_, r, lastb = blocks[i]
            U_ = ures[u]
            kh, khk, vh, vhk = ures[("kv", h)]
            R = bres[i]
            cl = R["cl"]
            sp, spk = R["sp"], R["spk"]
            bb, bk = pb.next()
            mm(P, bb[:, cl], C.ntril[:], sp[:, cl], True, False, ["ntril", spk], [bk])
            mm(P, bb[:, cl], kh[:, b_ * 128:(b_ + 1) * 128], U_["qh"][:, cl], False, True, [khk, U_["qhk"]], [bk])
            cc, ck = pc.next()
            if b_ > 0:
                mm(P, cc[:, cl], C.onesb[:, 0:64], sp[:, cl], True, True, ["onesb", spk], [ck])
            Pm, Pk = Pr.next()
            actf(P, Pm[:, cl], bb[:, cl], AF.Exp, [bk], [Pk])
            if r >= 0:
                tt(P, "pool", Pm[:, cl], Pm[:, cl], C.maskb[:, r, cl], ALU.mult, [Pk, "maskb"], [Pk])
            R["Pm"], R["Pk"] = Pm, Pk
            if b_ > 0:
                f, fk = fr.next()
                actf(P, f[:, cl], cc[:, cl], AF.Exp, [ck], [fk], scale=-1.0)
                R["f"], R["fk"] = f, fk

        def S56(i):
            u, h, tq, b_, r, lastb = blocks[i]
            U_ = ures[u]
            kh, khk, vh, vhk = ures[("kv", h)]
            R = bres.pop(i)
            cl = R["cl"]
            acc, acck = U_["acc"], U_["acck"]
            ov, ovk = po.next()
            mm(P, ov[:, cl], vh[:, b_, :], R["Pm"][:, cl], True, True, [vhk, R["Pk"]], [ovk])
            if b_ == 0:
                P.op("dve", lambda e, acc=acc, ov=ov: e.tensor_copy(out=acc[:], in_=ov[:]), reads=[ovk], writes=[acck])
            else:
                tt(P, "dve", acc[:, cl], acc[:, cl], R["f"][:, cl], ALU.mult, [acck, R["fk"]], [acck])
                tt(P, "dve", acc[:, cl], ov[:, cl], acc[:, cl], ALU.add, [acck, ovk], [acck])
            if lastb:
                tsl = slice(tq * TT, (tq + 1) * TT)
                yg, ygk = ygr.next()
                tt(P, "dve", yg[:], acc[:], U_["gh"][:], ALU.mult, [acck, U_["ghk"]], [ygk])
                P.dma(ygT[h * 64:(h + 1) * 64, tsl], yg[:], reads=[ygk], writes=[("yg", L, tq)])
                del ures[u]

        for i in range(NBK + 2):
            if i < NBK:
                S12(i)
            if 0 <= i - 1 < NBK:
                S34(i - 1)
            if 0 <= i - 2 < NBK:
                S56(i - 2)

    out_proj(P, C, L, T, ygT, D, W["w_out"], cur, y_out, last)


def layer_ssd(P, C, L, T, W, cur, y_out, last):
    nc = P.nc
    TT = 512
    NT = T // TT
    NCH = T // 128
    xbcT = nc.dram_tensor("ssd_xbcT", [4096, T], BF16, kind="Internal").ap()
    zT = nc.dram_tensor("ssd_zT", [2048, T], BF16, kind="Internal").ap()
    dtD = nc.dram_tensor("ssd_dt", [T, 64], F32, kind="Internal").ap()
    ygT = nc.dram_tensor("ssd_ygT", [2048, T], BF16, kind="Internal").ap()

    with P.phase():
        stage = Ring(P, "wst", [128, 2048], F32, 2)
        wb = load_weight_bf16(P, "win", W["w_in"], D, 6176, stage)
        gcol = P.sb("gcol", [128, 8], F32)
        P.dma(gcol[:], W["norm"], writes=["gcol"])
        cw = P.sb("cw", [128, 32, 4], F32)
        P.dma(cw[:], W["conv_w"], writes=["cw"])
        cbias = P.sb("cbias", [128, 32], F32)
        P.dma(cbias[:], W["conv_b"], writes=["cbias"])
        dtb = P.sb("dtb", [128, 32], F32)
        P.dma(dtb[:], W["dt_bias"], writes=["dtb"])
        Arep = P.sb("Arep", [128, 32], F32)
        P.dma(Arep[:], W["a_log"], writes=["Arep"])
        P.op("act", lambda e: e.activation(out=Arep[:], in_=Arep[:], func=AF.Exp), reads=["Arep"], writes=["Arep"])
        P.op("dve", lambda e: e.tensor_scalar(out=Arep[:], in0=Arep[:], scalar1=-1.0, scalar2=None, op0=ALU.mult),
             reads=["Arep"], writes=["Arep"])
        hist = P.sb("hist", [128, 32, 3], F32)
        P.op("pool", lambda e: e.memset(hist[:], 0.0), writes=["hist"])
        rings = in_rings(P, TT, nh=1, nx=2)
        hbr = Ring(P, "hb", [128, 8, TT], BF16, 2)
        pp = Ring(P, "pp", [128, 512], F32, 3, psum=True)
        pd = Ring(P, "pd", [128, 64], F32, 2, psum=True)
        xcr = Ring(P, "xc", [128, 3 + TT], F32, 2)
        cvr = Ring(P, "cv", [128, TT], F32, 2)
        ob = Ring(P, "ob", [128, 512], BF16, 4)
        dtr = Ring(P, "dtr", [128, 64], F32, 2)
        for j in range(NT):
            hT, hk = make_hT(P, C, j, cur, TT, rings, gcol)
            hb, hbk = hbr.next()
            for c in range(8):
                P.op("pool", lambda e, hb=hb, hT=hT, c=c: e.tensor_copy(out=hb[:, c, :], in_=hT[:, c, 1:1 + TT]),
                     reads=[hk], writes=[hbk])
            tsl = slice(j * TT, (j + 1) * TT)
            for fc in range(48):
                p, pk = pp.next()
                for c in range(8):
                    P.op("pe", lambda e, p=p, c=c, hb=hb, col=fc * 128: e.matmul(
                        p[:], lhsT=wb[:, c, col:col + 128], rhs=hb[:, c, :], start=(c == 0), stop=(c == 7)),
                        reads=["win", hbk], writes=[pk])
                o, ok = ob.next()
                if fc < 16:
                    P.op("act", lambda e, p=p, o=o: e.activation(out=o[:], in_=p[:], func=AF.Silu), reads=[pk], writes=[ok])
                    P.dma(zT[fc * 128:(fc + 1) * 128, tsl], o[:], reads=[ok], writes=[("ssdz", j)])
                else:
                    cc = fc - 16
                    xc, xck = xcr.next()
                    P.op("pool", lambda e, xc=xc, cc=cc: e.tensor_copy(out=xc[:, 0:3], in_=hist[:, cc, :]), reads=["hist"], writes=[xck])
                    P.op("act", lambda e, xc=xc, p=p: e.copy(out=xc[:, 3:3 + TT], in_=p[:]), reads=[pk], writes=[xck])
                    P.op("pool", lambda e, xc=xc, cc=cc: e.tensor_copy(out=hist[:, cc, :], in_=xc[:, TT:TT + 3]), reads=[xck], writes=["hist"])
                    cv, cvk = cvr.next()
                    P.op("dve", lambda e, cv=cv, xc=xc, cc=cc: e.tensor_scalar(
                        out=cv[:], in0=xc[:, 0:TT], scalar1=cw[:, cc, 0:1], scalar2=None, op0=ALU.mult), reads=[xck, "cw"], writes=[cvk])
                    for kk in range(1, 4):
                        P.op("dve", lambda e, cv=cv, xc=xc, cc=cc, kk=kk: e.scalar_tensor_tensor(
                            out=cv[:], in0=xc[:, kk:kk + TT], scalar=cw[:, cc, kk:kk + 1], in1=cv[:], op0=ALU.mult, op1=ALU.add),
                            reads=[xck, "cw", cvk], writes=[cvk])
                    P.op("act", lambda e, cv=cv, o=o, cc=cc: e.activation(out=o[:], in_=cv[:], func=AF.Silu, bias=cbias[:, cc:cc + 1]),
                         reads=[cvk, "cbias"], writes=[ok])
                    P.dma(xbcT[cc * 128:(cc + 1) * 128, tsl], o[:], reads=[ok], writes=[("ssdx", j)])
            for s in range(TT // 128):
                p, pk = pd.next()
                for c in range(8):
                    P.op("pe", lambda e, p=p, c=c, hb=hb, s=s: e.matmul(
                        p[:, 0:32], lhsT=hb[:, c, s * 128:(s + 1) * 128], rhs=wb[:, c, 6144:6176], start=(c == 0), stop=(c == 7)),
                        reads=["win", hbk], writes=[pk])
                d, dk = dtr.next()
                P.op("dve", lambda e, d=d, p=p: e.tensor_tensor(out=d[:, 0:32], in0=p[:, 0:32], in1=dtb[:], op=ALU.add),
                     reads=[pk, "dtb"], writes=[dk])
                P.op("act", lambda e, d=d: e.activation(out=d[:, 0:32], in_=d[:, 0:32], func=AF.Exp), reads=[dk], writes=[dk])
                P.op("act", lambda e, d=d: e.activation(out=d[:, 0:32], in_=d[:, 0:32], func=AF.Ln, bias=C.onec[:, 0:1]),
                     reads=[dk], writes=[dk])
                P.op("dve", lambda e, d=d: e.tensor_tensor(out=d[:, 32:64], in0=d[:, 0:32], in1=Arep[:], op=ALU.mult),
                     reads=[dk, "Arep"], writes=[dk])
                r0 = j * TT + s * 128
                P.dma(dtD[r0:r0 + 128, :], d[:], reads=[dk], writes=[("ssdd", j)])

    with P.phase():
        dsk = P.sb("dsk", [128, 16], F32)
        P.dma(dsk[:], W["d_skip"], writes=["dsk"])
        gnw = P.sb("gnw", [128, 16], F32)
        P.dma(gnw[:], W["gnorm_w"], writes=["gnw"])
        hsf = P.sb("hsf", [128, 32, 64], F32)
        hsb = P.sb("hsb", [128, 32, 64], BF16)
        for g in range(8):
            P.op("pool", lambda e, g=g: e.memset(hsf[:, 4 * g:4 * g + 4, :], 0.0), writes=[("hsf", g)])
            P.op("pool", lambda e, g=g: e.memset(hsb[:, 4 * g:4 * g + 4, :], 0.0), writes=[("hsb", g)])
        xbr = Ring(P, "xb", [128, 32, 128], BF16, 2)
        zr = Ring(P, "zs", [128, 16, 128], BF16, 2)
        dr = Ring(P, "dta", [128, 64], F32, 2)
        p_rb = Ring(P, "prb", [128, 512], F32, 2, psum=True)
        p_y = Ring(P, "py", [128, 512], F32, 2, psum=True)
        p_ms = Ring(P, "pms", [128, 512], F32, 1, psum=True)
        p_st = Ring(P, "pst", [128, 512], F32, 1, psum=True)
        p_tr = Ring(P, "ptr", [128, 1024], BF16, 1, psum=True)
        p_ac = Ring(P, "pac", [128, 512], F32, 1, psum=True)
        nacr = Ring(P, "nac", [128, 32], F32, 2)
        cdr = Ring(P, "cd", [128, 32], F32, 2)
        dsr = Ring(P, "ds", [128, 32], F32, 2)
        xdtr = Ring(P, "xdt", [128, 32, 64], BF16, 2)
        xsr = Ring(P, "xs", [128, 32, 64], BF16, 2)
        bmr = Ring(P, "bm", [128, 1024], BF16, 2)
        cbmr = Ring(P, "cbm", [128, 128], F32, 4)
        E4r = Ring(P, "E4", [128, 4, 128], F32, 4)
        EL4r = Ring(P, "EL4", [128, 4, 128], F32, 4)
        MT4r = Ring(P, "MT4", [128, 4, 128], BF16, 4)
        CS4r = Ring(P, "CS4", [128, 4, 128], BF16, 4)
        y1r = Ring(P, "y1", [128, 2, 128], F32, 2)
        Yr = Ring(P, "Y", [128, 16, 128], F32, 2)
        sqr = Ring(P, "sqy", [128, 2, 128], BF16, 2)
        rrr = Ring(P, "rr", [128, 128], F32, 2)
        YGr = Ring(P, "YG", [128, 16, 128], BF16, 2)
        xv = xbcT.rearrange("(c p) t -> p c t", p=128)
        zv = zT.rearrange("(c p) t -> p c t", p=128)
        ygv = ygT.rearrange("(c p) t -> p c t", p=128)
        cres = {}

        def prologue(ch):
            csl = slice(ch * 128, (ch + 1) * 128)
            jt = ch // 4
            R = {}
            xb, xbk = xbr.next()
            P.dma(xb[:], xv[:, :, csl], reads=[("ssdx", jt)], writes=[xbk])
            zs, zsk = zr.next()
            P.dma(zs[:], zv[:, :, csl], reads=[("ssdz", jt)], writes=[zsk], e="act")
            da, dak = dr.next()
            P.dma(da[:], dtD[csl, :], reads=[("ssdd", jt)], writes=[dak])
            ac, ack = p_ac.next()
            P.op("pe", lambda e: e.matmul(ac[:, 0:32], lhsT=C.triu[:], rhs=da[:, 32:64], start=True, stop=True),
                 reads=["triu", dak], writes=[ack])
            P.op("pe", lambda e: e.matmul(ac[:, 32:64], lhsT=C.onesf[:], rhs=da[:, 32:64], start=True, stop=True),
                 reads=["onesf", dak], writes=[ack])
            nac, nack = nacr.next()
            ts(P, "dve", nac[:], ac[:, 0:32], -1.0, None, ALU.mult, None, [ack], [nack])
            cd, cdk = cdr.next()
            actf(P, cd[:], ac[:, 32:64], AF.Exp, [ack], [cdk])
            ds, dsk_ = dsr.next()
            tt(P, "dve", ds[:], ac[:, 32:64], nac[:], ALU.add, [ack, nack], [dsk_])
            actf(P, ds[:], ds[:], AF.Exp, [dsk_], [dsk_])
            xdt, xdtk = xdtr.next()
            xs, xsk = xsr.next()
            for hf in range(2):
                tr, trk = p_tr.next()
                for q in range(8):
                    trp(P, tr[:, q * 128:(q + 1) * 128], xb[:, hf * 8 + q, :], C.identb[:], [xbk, "identb"], [trk])
                tt(P, "dve", xdt[:, hf * 16:(hf + 1) * 16, :], tr[:].rearrange("p (h d) -> p h d", d=64),
                   da[:, hf * 16:(hf + 1) * 16].unsqueeze(2).broadcast_to([128, 16, 64]), ALU.mult, [trk, dak], [xdtk])
            tt(P, "pool", xs[:], xdt[:], ds[:].unsqueeze(2).broadcast_to([128, 32, 64]), ALU.mult, [xdtk, dsk_], [xsk])
            tr, trk = p_tr.next()
            for q in range(8):
                trp(P, tr[:, q * 128:(q + 1) * 128], xb[:, 16 + q, :], C.identb[:], [xbk, "identb"], [trk])
            bm, bmk = bmr.next()
            actf(P, bm[:], tr[:], AF.Copy, [trk], [bmk])
            Y, Yk = Yr.next()
            YG, YGk = YGr.next()
            R.update(xb=xb, xbk=xbk, zs=zs, zsk=zsk, da=da, dak=dak, nac=nac, nack=nack, cd=cd, cdk=cdk, xdt=xdt, xdtk=xdtk,
                     xs=xs, xsk=xsk, bm=bm, bmk=bmk, Y=Y, Yk=Yk, YG=YG, YGk=YGk)
            cres[ch] = R

        gres = {}

        def G12(i):
            ch, g = divmod(i, 8)
            if g == 0:
                prologue(ch)
            R = cres[ch]
            xb, xbk, da, dak, nac, nack = R["xb"], R["xbk"], R["da"], R["dak"], R["nac"], R["nack"]
            G = {}
            cb_, cbk = p_ac.t[0], p_ac.k[0]
            cb = cb_[:, 128:256]
            mm(P, cb, xb[:, 16 + g, :], xb[:, 24 + g, :], True, True, [xbk], [cbk])
            rb, rbk = p_rb.next()
            for r in range(4):
                h = 4 * g + r
                mm(P, rb[:, r * 128:(r + 1) * 128], da[:, 32 + h:33 + h].broadcast_to([128, 128]), C.triu[:], True, True, [dak, "triu"], [rbk])
            cbm, cbmk = cbmr.next()
            tt(P, "dve", cbm[:], cb, C.triu[:], ALU.mult, [cbk, "triu"], [cbmk])
            E4, E4k = E4r.next()
            for r in range(4):
                h = 4 * g + r
                ts(P, "dve", E4[:, r, :], rb[:, r * 128:(r + 1) * 128], nac[:, h:h + 1], 0.0, ALU.add, ALU.min, [rbk, nack], [E4k])
            actf(P, E4[:].rearrange("p r l -> p (r l)"), E4[:].rearrange("p r l -> p (r l)"), AF.Exp, [E4k], [E4k])
            MT4, MT4k = MT4r.next()
            tt(P, "pool", MT4[:], E4[:], cbm[:].unsqueeze(1).broadcast_to([128, 4, 128]), ALU.mult, [E4k, cbmk], [MT4k])
            G.update(MT4=MT4, MT4k=MT4k)
            if ch > 0:
                EL4, EL4k = EL4r.next()
                actf(P, EL4[:].rearrange("p r l -> p (r l)"), rb[:], AF.Exp, [rbk, E4k], [EL4k])
                CS4, CS4k = CS4r.next()
                tt(P, "pool", CS4[:], EL4[:], xb[:, 24 + g, :].unsqueeze(1).broadcast_to([128, 4, 128]), ALU.mult, [EL4k, xbk], [CS4k])
                G.update(CS4=CS4, CS4k=CS4k)
            gres[i] = G

        def G34(i):
            ch, g = divmod(i, 8)
            R = cres[ch]
            G = gres.pop(i)
            xb, xbk, zs, zsk = R["xb"], R["xbk"], R["zs"], R["zsk"]
            xdt, xdtk, xs, xsk, bm, bmk = R["xdt"], R["xdtk"], R["xs"], R["xsk"], R["bm"], R["bmk"]
            Y, Yk, YG, YGk, cd, cdk = R["Y"], R["Yk"], R["YG"], R["YGk"], R["cd"], R["cdk"]
            py, pyk = p_y.next()
            for r in range(4):
                h = 4 * g + r
                osl = slice((r % 2) * 64, (r % 2) * 64 + 64)
                pc = slice((r // 2) * 128, (r // 2) * 128 + 128)
                mm(P, py[osl, pc], xdt[:, h, :], G["MT4"][:, r, :], True, ch == 0, [xdtk, G["MT4k"]], [pyk])
                if ch > 0:
                    mm(P, py[osl, pc], hsb[:, h, :], G["CS4"][:, r, :], False, True, [("hsb", g), G["CS4k"]], [pyk])
            y1, y1k = y1r.next()
            for pp_ in range(2):
                pq_ = 2 * g + pp_
                stt(P, y1[:, pp_, :], xb[:, pq_, :], dsk[:, pq_:pq_ + 1], py[:, pp_ * 128:(pp_ + 1) * 128], ALU.mult, ALU.add,
                    [xbk, "dsk", pyk], [y1k])
            tt(P, "pool", Y[:, 2 * g:2 * g + 2, :], y1[:], zs[:, 2 * g:2 * g + 2, :], ALU.mult, [y1k, zsk], [(Yk, g)])
            sq, sqk = sqr.next()
            actf(P, sq[:], Y[:, 2 * g:2 * g + 2, :], AF.Square, [(Yk, g)], [sqk])
            ms, msk = p_ms.next()
            for pp_ in range(2):
                mm(P, ms[:, 256:384], C.onesb[:], sq[:, pp_, :], pp_ == 0, pp_ == 1, ["onesb", sqk], [msk])
            rr, rrk = rrr.next()
            actf(P, rr[:], ms[:, 256:384], AF.Sqrt, [msk], [rrk], scale=1.0 / 256, bias=C.epsc[:, 0:1])
            P.op("dve", lambda e, rr=rr: e.reciprocal(out=rr[:], in_=rr[:]), reads=[rrk], writes=[rrk])
            for pq_ in (2 * g, 2 * g + 1):
                stt(P, YG[:, pq_, :], Y[:, pq_, :], gnw[:, pq_:pq_ + 1], rr[:], ALU.mult, ALU.mult, [(Yk, g), rrk, "gnw"], [YGk])
            if ch < NCH - 1:
                st_, stk = p_st.next()
                mm(P, st_[:, 0:256], bm[:, g * 128:(g + 1) * 128], xs[:, 4 * g:4 * g + 4, :].rearrange("p h d -> p (h d)"), True, True,
                   [bmk, xsk], [stk])
                tt(P, "dve", hsf[:, 4 * g:4 * g + 4, :], hsf[:, 4 * g:4 * g + 4, :],
                   cd[:, 4 * g:4 * g + 4].unsqueeze(2).broadcast_to([128, 4, 64]), ALU.mult, [("hsf", g), cdk], [("hsf", g)])
                tt(P, "dve", hsf[:, 4 * g:4 * g + 4, :], st_[:, 0:256].rearrange("p (h d) -> p h d", d=64), hsf[:, 4 * g:4 * g + 4, :],
                   ALU.add, [("hsf", g), stk], [("hsf", g)])
                actf(P, hsb[:, 4 * g:4 * g + 4, :], hsf[:, 4 * g:4 * g + 4, :], AF.Copy, [("hsf", g)], [("hsb", g)])
            if g == 7:
                csl = slice(ch * 128, (ch + 1) * 128)
                P.dma(ygv[:, :, csl], YG[:], reads=[YGk], writes=[("yg", L, ch // 4)])
                del cres[ch]

        NG = NCH * 8
        G12(0)
        G12(1)
        for i in range(NG):
            if i + 2 < NG:
                G12(i + 2)
            G34(i)

    out_proj(P, C, L, T, ygT, 2048, W["w_out"], cur, y_out, last)


def tt(P, eng, out, in0, in1, op, r, w):
    return P.op(eng, lambda e: e.tensor_tensor(out=out, in0=in0, in1=in1, op=op), reads=r, writes=w)


def ts(P, eng, out, in0, s1, s2, op0, op1, r, w):
    if s2 is None:
        return P.op(eng, lambda e: e.tensor_scalar(out=out, in0=in0, scalar1=s1, scalar2=None, op0=op0), reads=r, writes=w)
    return P.op(eng, lambda e: e.tensor_scalar(out=out, in0=in0, scalar1=s1, scalar2=s2, op0=op0, op1=op1), reads=r, writes=w)


def stt(P, out, in0, scalar, in1, op0, op1, r, w):
    return P.op("dve", lambda e: e.scalar_tensor_tensor(out=out, in0=in0, scalar=scalar, in1=in1, op0=op0, op1=op1), reads=r, writes=w)


def actf(P, out, in_, func, r, w, bias=None, scale=None):
    kw = {}
    if bias is not None:
        kw["bias"] = bias
    if scale is not None:
        kw["scale"] = scale
    return P.op("act", lambda e: e.activation(out=out, in_=in_, func=func, **kw), reads=r, writes=w)


def mm(P, out, lhsT, rhs, start, stop, r, w):
    return P.op("pe", lambda e: e.matmul(out, lhsT=lhsT, rhs=rhs, start=start, stop=stop), reads=r, writes=w)


def trp(P, out, in_, ident, r, w):
    return P.op("pe", lambda e: e.transpose(out, in_, ident), reads=r, writes=w)


RW_GN_EPS = 64e-5
NEG_EXP_HALF = -math.exp(-0.5)
CDT = F32


def layer_rwkv(P, C, L, T, W, cur, y_out, last):
    nc = P.nc
    vres = (L == 3)
    TT = 256
    NT = T // TT
    NCK = T // 64
    ncols = 4 * D + 128 + (32 if vres else 0)
    pre = "rw%d_" % L
    names = ["At", "Bt", "Kt", "Rt", "Bg", "Kg", "V", "BON"]
    Dm = {n: nc.dram_tensor(pre + n, [D, T], (F32 if n == "BON" else BF16), kind="Internal").ap() for n in names}
    gCd = nc.dram_tensor(pre + "gC", [D, NCK], F32, kind="Internal").ap()
    gTd = nc.dram_tensor(pre + "gT", [D, T], BF16, kind="Internal").ap()
    yTd = nc.dram_tensor(pre + "yT", [D, T], F32, kind="Internal").ap()
    if L == 0:
        C.vfirst = nc.dram_tensor("rw_vfirst", [D, T], F32, kind="Internal").ap()
    vfd = C.vfirst

    def colp(name, n=8):
        t = P.sb(name, [128, n], F32)
        P.dma(t[:], W[name], writes=[name])
        return t

    with P.phase():
        stage = Ring(P, "wst", [128, 1024], F32, 2)
        wb = load_weight_bf16(P, "win", W["w_in"], D, ncols, stage)
        gcol = P.sb("gcol", [128, 8], F32)
        P.dma(gcol[:], W["norm"], writes=["gcol"])
        mu = P.sb("mu", [128, 6, 8], F32)
        P.dma(mu[:], W["mu"], writes=["mu"])
        w0 = colp("w0"); a0 = colp("a0"); k_k = colp("k_k"); k_a = colp("k_a"); gdum = None
        r_k = P.sb("r_k", [128, 8], F32)
        P.dma(r_k[:], W["r_k"], writes=["r_k"])
        omka = P.sb("omka", [128, 8], F32)
        ts(P, "dve", omka[:], k_a[:], -1.0, 1.0, ALU.mult, ALU.add, ["k_a"], ["omka"])
        def lowrank(nm, rows):
            t = P.sb(nm, [rows, D], BF16)
            st, sk = stage.next()
            P.dma(st[0:rows, :], W[nm], writes=[sk])
            P.op("pool", lambda e: e.tensor_copy(out=t[:], in_=st[0:rows, :]), reads=[sk], writes=[nm])
            return t
        wup = lowrank("w_up", 64)
        aup = lowrank("a_up", 64)
        if vres:
            v0 = colp("v0")
            vup = lowrank("v_up", 32)
        rings = in_rings(P, TT, nh=1, nx=2, nq=1, npt=1)
        carry = P.sb("carry", [128, 8, 1], F32)
        P.op("pool", lambda e: e.memset(carry[:], 0.0), writes=["carry"])
        dxT = P.sb("dxT", [128, 8, TT], F32)
        xmix = {i: P.sb("xm%d" % i, [128, 8, TT], BF16) for i in (0, 2, 3)}
        xsm = Ring(P, "xsm", [128, 8, TT], BF16, 2)
        wl = P.sb("wl", [64, TT], BF16); al = P.sb("al", [64, TT], BF16); vl = P.sb("vl", [32, TT], BF16)
        PB = [P.ps("rwa%d" % i, [128, 512], F32) for i in range(7)]
        PBk = ["rwabank%d_%d" % (L, i) for i in range(7)]

        def half(bi, hi):
            return PB[bi][:, hi * TT:(hi + 1) * TT], PBk[bi]

        class _PQ:
            i = -1

            def next(self):
                self.i = (self.i + 1) % 7
                return PB[self.i], PBk[self.i]
        pq = _PQ()
        F = lambda nm, n=2, dt=F32: Ring(P, nm, [128, TT], dt, n)
        R_ = {nm: F(nm, 2) for nm in ["sg", "lw", "cum", "gi", "ge", "gv", "gd", "aa", "kkr", "nrm", "kk", "t1", "kf", "d1", "ka", "sv", "vf"]}
        R_.update({nm: F(nm, 2) for nm in ["vv", "BON"]})
        R_.update({nm: F(nm, 2, BF16) for nm in ["At", "Bt", "Bg", "Kt", "Kg", "Rt", "vb"]})
        R_["sqb"] = F("sqb", 2, BF16); R_["rkb"] = F("rkb", 2, BF16); R_["go"] = F("go", 2, BF16)
        gCr = Ring(P, "gCt", [128, TT // 64], F32, 2)
        for j in range(NT):
            hT, hk = make_hT(P, C, j, cur, TT, rings, gcol)
            P.op("pool", lambda e, hT=hT: e.tensor_copy(out=hT[:, :, 0:1], in_=carry[:]), reads=["carry"], writes=[hk])
            P.op("pool", lambda e, hT=hT: e.tensor_copy(out=carry[:], in_=hT[:, :, TT:TT + 1]), reads=[hk], writes=["carry"])
            tt(P, "pool", dxT[:], hT[:, :, 0:TT], hT[:, :, 1:TT + 1], ALU.subtract, [hk], ["dxT"])
            tsl = slice(j * TT, (j + 1) * TT)

            def mix(i, dst, dk):
                for c in range(8):
                    stt(P, dst[:, c, :], dxT[:, c, :], mu[:, i, c:c + 1], hT[:, c, 1:TT + 1], ALU.mult, ALU.add,
                        ["dxT", "mu", hk], [dk])

            xw, xwk = xsm.next(); mix(1, xw, xwk)
            p, pk = pq.next()
            for c in range(8):
                mm(P, p[0:64, 0:TT], wb[:, c, 4096:4160], xw[:, c, :], c == 0, c == 7, ["win", xwk], [pk])
            actf(P, wl[:], p[0:64, 0:TT], AF.Tanh, [pk], ["wl"])
            xa, xak = xsm.next(); mix(4, xa, xak)
            p, pk = pq.next()
            for c in range(8):
                mm(P, p[0:64, 0:TT], wb[:, c, 4160:4224], xa[:, c, :], c == 0, c == 7, ["win", xak], [pk])
            actf(P, al[:], p[0:64, 0:TT], AF.Copy, [pk], ["al"])
            xg, xgk = xsm.next(); mix(5, xg, xgk)
            for fc in range(8):
                p, pk = pq.next()
                for c in range(8):
                    mm(P, p[:, 0:TT], wb[:, c, 3072 + fc * 128:3072 + (fc + 1) * 128], xg[:, c, :], c == 0, c == 7, ["win", xgk], [pk])
                go, gok = R_["go"].next()
                actf(P, go[:], p[:, 0:TT], AF.Silu, [pk], [gok])
                P.dma(gTd[fc * 128:(fc + 1) * 128, tsl], go[:], reads=[gok], writes=[(pre + "g", j)])
            for i in (0, 2, 3):
                mix(i, xmix[i], "xm%d" % i)
            if vres:
                p, pk = pq.next()
                for c in range(8):
                    mm(P, p[0:32, 0:TT], wb[:, c, 4224:4256], xmix[3][:, c, :], c == 0, c == 7, ["win", "xm3"], [pk])
                actf(P, vl[:], p[0:32, 0:TT], AF.Copy, [pk], ["vl"])
            def fc_body(fc, slot):
                    fs = slice(fc * 128, (fc + 1) * 128)
                    col = lambda t_: t_[:, fc:fc + 1]
                    pr, prk = half(3 * slot, 0); pkp, pkk = half(3 * slot, 1); pv, pvk = half(3 * slot + 1, 0)
                    pn0, pn0k = half(6, slot); pw2, pw2k = half(3 * slot + 2, 0); pa2, pa2k = half(3 * slot + 2, 1)
                    pv2, pv2k = half(3 * slot + 1, 1)
                    mm(P, pw2, wup[:, fs], wl[:], True, True, ["w_up", "wl"], [pw2k])
                    mm(P, pa2, aup[:, fs], al[:], True, True, ["a_up", "al"], [pa2k])
                    for c in range(8):
                        mm(P, pkp, wb[:, c, D + fc * 128:D + (fc + 1) * 128], xmix[2][:, c, :], c == 0, c == 7, ["win", "xm2"], [pkk])
                    for c in range(8):
                        mm(P, pr, wb[:, c, fc * 128:(fc + 1) * 128], xmix[0][:, c, :], c == 0, c == 7, ["win", "xm0"], [prk])
                    yield
                    for c in range(8):
                        mm(P, pv, wb[:, c, 2 * D + fc * 128:2 * D + (fc + 1) * 128], xmix[3][:, c, :], c == 0, c == 7, ["win", "xm3"], [pvk])
                    if vres:
                        mm(P, pv2, vup[:, fs], vl[:], True, True, ["v_up", "vl"], [pv2k])
                    sg, sgk = R_["sg"].next()
                    actf(P, sg[:], pw2, AF.Sigmoid, [pw2k, "w0"], [sgk], bias=col(w0))
                    lw, lwk = R_["lw"].next()
                    ts(P, "dve", lw[:], sg[:], NEG_EXP_HALF, None, ALU.mult, None, [sgk], [lwk])
                    cum, cumk = R_["cum"].next()
                    P.op("dve", lambda e, cum=cum, lw=lw: e.tensor_tensor_scan(
                        out=cum[:], data0=C.mask0[:, 0:TT], data1=lw[:], initial=0.0, op0=ALU.mult, op1=ALU.add),
                        reads=[lwk, "mask0"], writes=[cumk])
                    yield
                    gi, gik = R_["gi"].next()
                    actf(P, gi[:], cum[:], AF.Exp, [cumk], [gik])
                    ge, gek = R_["ge"].next()
                    tt(P, "pool", ge[:], cum[:], lw[:], ALU.subtract, [cumk, lwk], [gek])
                    actf(P, ge[:], ge[:], AF.Exp, [gek], [gek])
                    gv, gvk = R_["gv"].next()
                    actf(P, gv[:], cum[:], AF.Exp, [cumk], [gvk], scale=-1.0)
                    yield
                    gd, gdk = R_["gd"].next()
                    cum3 = cum[:].rearrange("p (c t) -> p c t", t=64)
                    tt(P, "pool", gd[:].rearrange("p (c t) -> p c t", t=64), cum3[:, :, 63:64].broadcast_to([128, TT // 64, 64]), cum3,
                       ALU.subtract, [cumk], [gdk])
                    actf(P, gd[:], gd[:], AF.Exp, [gdk], [gdk])
                    gCt, gCk = gCr.next()
                    actf(P, gCt[:].unsqueeze(2), cum3[:, :, 63:64], AF.Exp, [cumk], [gCk])
                    P.dma(gCd[fs, j * (TT // 64):(j + 1) * (TT // 64)], gCt[:], reads=[gCk], writes=[(pre + "gC", j)])
                    yield
                    aa, aak = R_["aa"].next()
                    actf(P, aa[:], pa2, AF.Sigmoid, [pa2k, "a0"], [aak], bias=col(a0))
                    kkr, kkrk = R_["kkr"].next()
                    ts(P, "dve", kkr[:], pkp, col(k_k), None, ALU.mult, None, [pkk, "k_k"], [kkrk])
                    sqb, sqbk = R_["sqb"].next()
                    actf(P, sqb[:], kkr[:], AF.Square, [kkrk], [sqbk])
                    mm(P, pn0, C.bonesb[:], sqb[:], True, True, ["bonesb", sqbk], [pn0k])
                    yield
                    nrm, nrmk = R_["nrm"].next()
                    actf(P, nrm[:], pn0, AF.Sqrt, [pn0k], [nrmk])
                    ts(P, "dve", nrm[:], nrm[:], 1e-12, None, ALU.max, None, [nrmk], [nrmk])
                    P.op("dve", lambda e, nrm=nrm: e.reciprocal(out=nrm[:], in_=nrm[:]), reads=[nrmk], writes=[nrmk])
                    kk, kkk = R_["kk"].next()
                    tt(P, "dve", kk[:], kkr[:], nrm[:], ALU.mult, [kkrk, nrmk], [kkk])
                    yield
                    t1, t1k = R_["t1"].next()
                    ts(P, "dve", t1[:], aa[:], col(k_a), col(omka), ALU.mult, ALU.add, [aak, "k_a", "omka"], [t1k])
                    kf, kfk = R_["kf"].next()
                    tt(P, "dve", kf[:], pkp, t1[:], ALU.mult, [pkk, t1k], [kfk])
                    yield
                    vv, vvk = R_["vv"].next()
                    if not vres:
                        actf(P, vv[:], pv, AF.Copy, [pvk], [vvk])
                        if L == 0:
                            P.dma(vfd[fs, tsl], vv[:], reads=[vvk], writes=[("vfirst", j)])
                    else:
                        vf, vfk = R_["vf"].next()
                        P.dma(vf[:], vfd[fs, tsl], reads=[("vfirst", j)], writes=[vfk])
                        sv, svk = R_["sv"].next()
                        actf(P, sv[:], pv2, AF.Sigmoid, [pv2k, "v0"], [svk], bias=col(v0))
                        d1, d1k = R_["d1"].next()
                        tt(P, "dve", d1[:], vf[:], pv, ALU.subtract, [vfk, pvk], [d1k])
                        tt(P, "pool", d1[:], d1[:], sv[:], ALU.mult, [d1k, svk], [d1k])
                        tt(P, "dve", vv[:], d1[:], pv, ALU.add, [d1k, pvk], [vvk])
                    vb, vbk = R_["vb"].next()
                    P.op("pool", lambda e, vb=vb, vv=vv: e.tensor_copy(out=vb[:], in_=vv[:]), reads=[vvk], writes=[vbk])
                    P.dma(Dm["V"][fs, tsl], vb[:], reads=[vbk], writes=[(pre + "V", j)])
                    yield
                    At, Atk = R_["At"].next()
                    stt(P, At[:], kk[:], -1.0, ge[:], ALU.mult, ALU.mult, [kkk, gek], [Atk])
                    P.dma(Dm["At"][fs, tsl], At[:], reads=[Atk], writes=[(pre + "At", j)])
                    ka, kak = R_["ka"].next()
                    tt(P, "pool", ka[:], kk[:], aa[:], ALU.mult, [kkk, aak], [kak])
                    Bt, Btk = R_["Bt"].next()
                    tt(P, "dve", Bt[:], ka[:], gv[:], ALU.mult, [kak, gvk], [Btk])
                    P.dma(Dm["Bt"][fs, tsl], Bt[:], reads=[Btk], writes=[(pre + "Bt", j)])
                    yield
                    Bg, Bgk = R_["Bg"].next()
                    tt(P, "pool", Bg[:], ka[:], gd[:], ALU.mult, [kak, gdk], [Bgk])
                    P.dma(Dm["Bg"][fs, tsl], Bg[:], reads=[Bgk], writes=[(pre + "Bg", j)])
                    Kt, Ktk = R_["Kt"].next()
                    tt(P, "pool", Kt[:], kf[:], gv[:], ALU.mult, [kfk, gvk], [Ktk])
                    P.dma(Dm["Kt"][fs, tsl], Kt[:], reads=[Ktk], writes=[(pre + "Kt", j)])
                    yield
                    Kg, Kgk = R_["Kg"].next()
                    tt(P, "pool", Kg[:], kf[:], gd[:], ALU.mult, [kfk, gdk], [Kgk])
                    P.dma(Dm["Kg"][fs, tsl], Kg[:], reads=[Kgk], writes=[(pre + "Kg", j)])
                    Rt, Rtk = R_["Rt"].next()
                    tt(P, "dve", Rt[:], pr, gi[:], ALU.mult, [prk, gik], [Rtk])
                    P.dma(Dm["Rt"][fs, tsl], Rt[:], reads=[Rtk], writes=[(pre + "Rt", j)])
                    yield
                    rkb, rkbk = R_["rkb"].next()
                    stt(P, rkb[:], pr, col(r_k), kf[:], ALU.mult, ALU.mult, [prk, "r_k", kfk], [rkbk])
                    mm(P, pn0, C.bonesb[:], rkb[:], True, True, ["bonesb", rkbk], [pn0k])
                    BON, BONk = R_["BON"].next()
                    tt(P, "dve", BON[:], pn0, vv[:], ALU.mult, [pn0k, vvk], [BONk])
                    P.dma(Dm["BON"][fs, tsl], BON[:], reads=[BONk], writes=[(pre + "BON", j)])

            for pair in range(4):
                gens = [fc_body(2 * pair, 0), fc_body(2 * pair + 1, 1)]
                while gens:
                    for g_ in list(gens):
                        try:
                            next(g_)
                        except StopIteration:
                            gens.remove(g_)

    import os
    if os.environ.get("RW_STOP") == "A":
        return
    with P.phase():
        TB = 128
        NCB = TB // 64
        opn = ["At", "Bt", "Kt", "Rt", "Bg", "Kg", "V"]
        opr = {n: Ring(P, "o" + n, [128, 8, TB], BF16, 2) for n in opn}
        gCr2 = Ring(P, "gC2", [128, 8, NCB], F32, 2)
        ST = P.sb("ST", [128, 8, 64], F32)
        STb = P.sb("STb", [128, 8, 64], BF16)
        P.op("pool", lambda e: e.memset(ST[:], 0.0), writes=["ST"])
        P.op("pool", lambda e: e.memset(STb[:], 0.0), writes=["STb"])
        YT = Ring(P, "YT", [128, 8, TB], F32, 2)
        bk = [P.ps("rwbk%d" % i, [128, 512], F32) for i in range(7)]
        bkk = ["rwbank%d_%d" % (L, i) for i in range(7)]

        class RingOf:
            def __init__(self, ids):
                self.ids = ids
                self.i = -1

            def next(self):
                self.i = (self.i + 1) % len(self.ids)
                j = self.ids[self.i]
                return bk[j], bkk[j]

        pA5 = RingOf([0, 1, 2])
        pZ = RingOf([0, 1, 2, 3])
        pW = RingOf([2, 3])
        pU = RingOf([3]); pUO = RingOf([4]); pY2 = RingOf([5]); pS = RingOf([6])
        pT = Ring(P, "pT", [128, 1024], BF16, 1, psum=True)
        tmr = {n: Ring(P, "tm" + n, [64, D], BF16, 2) for n in ["At", "V", "Bg", "Kg"]}
        A5allr = Ring(P, "A5all", [64, 16, 320], BF16, 2)
        Zall = [P.sb("Zall%d" % i, [64, 16, 256], BF16) for i in range(2)]
        ZcAr = Ring(P, "ZcA", [64, 16, 64], BF16, 2)
        ZcPr = Ring(P, "ZcP", [64, 16, 64], F32, 2)
        ApT = Ring(P, "ApT", [128, 8, 64], BF16, 2)
        Ur = Ring(P, "U", [64, 16, 64], BF16, 2)
        Y1r = Ring(P, "Y1s", [64, 512], F32, 2)
        Ysr = Ring(P, "Ys", [64, D], BF16, 2)
        tmpS = Ring(P, "tmpS", [128, 4, 64], F32, 2)
        for jb in range(T // TB):
            tsl = slice(jb * TB, (jb + 1) * TB)
            jA = (jb * TB) // TT
            ot = {}
            for n in opn:
                t_, k_ = opr[n].next()
                P.dma(t_[:], Dm[n].rearrange("(c p) t -> p c t", p=128)[:, :, tsl], reads=[(pre + n, jA)], writes=[k_],
                      e=("act" if n in ("Bg", "Kg", "V") else "sp"))
                ot[n] = (t_, k_)
            gC2, gC2k = gCr2.next()
            P.dma(gC2[:], gCd.rearrange("(c p) k -> p c k", p=128)[:, :, jb * NCB:(jb + 1) * NCB], reads=[(pre + "gC", jA)], writes=[gC2k])
            yt, ytk = YT.next()
            for cq in range(NCB):
                cs = slice(cq * 64, (cq + 1) * 64)
                tm = {}
                for n in ["At", "V", "Bg", "Kg"]:
                    dst, dk = tmr[n].next()
                    src, sk = ot[n]
                    for hf in range(2):
                        p, pk = pT.next()
                        for q in range(4):
                            fc = hf * 4 + q
                            trp(P, p[0:64, q * 128:(q + 1) * 128], src[:, fc, cs], C.identb[:], [sk, "identb"], [pk])
                        actf(P, dst[:, hf * 512:(hf + 1) * 512], p[0:64, 0:512], AF.Copy, [pk], [(dk, hf)])
                    tm[n] = (dst, dk)
                ZcA, Zck = ZcAr.next()
                ZcP, _zp = ZcPr.next()
                apt, aptk = ApT.next()
                A5h = {}
                Z0 = Zall[0]
                zkey = lambda par, fc, part: ("Zall", L, par, fc, part)
                A5a, A5ak = A5allr.next()
                for h in range(16):
                    fc, e_ = h // 2, h % 2
                    rows = slice(e_ * 64, e_ * 64 + 64)
                    a5, a5k = pA5.next()
                    Kt_ = ot["Kt"][0][rows, fc, cs]; Bt_ = ot["Bt"][0][rows, fc, cs]
                    At_ = ot["At"][0][rows, fc, cs]; Rt_ = ot["Rt"][0][rows, fc, cs]
                    rk5 = [ot["Kt"][1], ot["Bt"][1], ot["At"][1], ot["Rt"][1]]
                    mm(P, a5[0:64, 0:64], Kt_, At_, True, True, rk5, [a5k])
                    mm(P, a5[0:64, 64:128], Kt_, Rt_, True, True, rk5, [a5k])
                    mm(P, a5[0:64, 128:192], Bt_, Rt_, True, True, rk5, [a5k])
                    mm(P, a5[0:64, 192:256], At_, Bt_, True, True, rk5, [a5k])
                    mm(P, a5[0:64, 256:320], Bt_, At_, True, True, rk5, [a5k])
                    tt(P, "dve", A5a[:, h, :], a5[0:64, 0:320], C.mask5[:], ALU.mult, [a5k, "mask5"], [(A5ak, h)])
                    A5h[h] = (A5a[:, h, :], (A5ak, h))
                P.op("pool", lambda e, Z0=Z0, src=tm["At"][0]: e.tensor_copy(out=Z0[:, :, 0:64], in_=src[:].rearrange("p (h d) -> p h d", d=64)),
                     reads=[(tm["At"][1], 0), (tm["At"][1], 1)], writes=[zkey(0, fc, "X0") for fc in range(8)])
                for hf in range(2):
                    pw_, pwk = pW.next()
                    for hh in range(8):
                        h = hf * 8 + hh
                        a5s, a5sk = A5h[h]
                        mm(P, pw_[0:64, hh * 64:(hh + 1) * 64], a5s[:, 0:64], tm["V"][0][:, h * 64:(h + 1) * 64], True, True,
                           [a5sk, (tm["V"][1], hf)], [pwk])
                    actf(P, Z0[:, hf * 8:(hf + 1) * 8, 64:128], pw_[0:64, :].rearrange("p (h v) -> p h v", v=64), AF.Copy, [pwk],
                         [zkey(0, fc, "X1") for fc in range(hf * 4, hf * 4 + 4)])
                for lev in range(6):
                    par = lev % 2
                    Zc_, Zn_ = Zall[par], Zall[1 - par]
                    for fp in range(4):
                        pzx, pzxk = pZ.next()
                        pzl, pzlk = pZ.next()
                        fcs = (2 * fp, 2 * fp + 1)
                        rdk = []
                        for fc in fcs:
                            rdk += [zkey(par, fc, "X0"), zkey(par, fc, "X1")]
                            rdk += ([zkey(par, fc, "L")] if lev > 0 else [(A5ak, 2 * fc), (A5ak, 2 * fc + 1)])
                        for j in range(4):
                            h = 4 * fp + j
                            if lev == 0:
                                l_ap, lt_ap = A5a[:, h, 192:256], A5a[:, h, 256:320]
                            else:
                                l_ap, lt_ap = Zc_[:, h, 128:192], Zc_[:, h, 192:256]
                            mm(P, pzx[0:64, j * 128:(j + 1) * 128], lt_ap, Zc_[:, h, 0:128], True, True, rdk, [pzxk])
                            if lev < 5:
                                mm(P, pzl[0:64, j * 128:j * 128 + 64], lt_ap, l_ap, True, True, rdk, [pzlk])
                                mm(P, pzl[0:64, j * 128 + 64:(j + 1) * 128], l_ap, lt_ap, True, True, rdk, [pzlk])
                        hs = slice(4 * fp, 4 * fp + 4)
                        px3 = pzx[0:64, :].rearrange("p (h c) -> p h c", c=128)
                        if lev < 5:
                            tt(P, "dve", Zn_[:, hs, 0:128], px3, Zc_[:, hs, 0:128], ALU.add, [pzxk] + rdk,
                               [zkey(1 - par, fc, x) for fc in fcs for x in ("X0", "X1")])
                            actf(P, Zn_[:, hs, 128:256], pzl[0:64, :].rearrange("p (h c) -> p h c", c=128), AF.Copy, [pzlk],
                                 [zkey(1 - par, fc, "L") for fc in fcs])
                        else:
                            tt(P, "dve", ZcA[:, hs, :], px3[:, :, 0:64], Zc_[:, hs, 0:64], ALU.add, [pzxk] + rdk, [(Zck, fc) for fc in fcs])
                            tt(P, "dve", ZcP[:, hs, :], px3[:, :, 64:128], Zc_[:, hs, 64:128], ALU.add, [pzxk] + rdk, [(Zck, fc) for fc in fcs])
                p, pk = pT.next()
                for fc in range(8):
                    trp(P, p[:, fc * 64:(fc + 1) * 64], ZcA[:, 2 * fc:2 * fc + 2, :].rearrange("p e d -> p (e d)"), C.identb[0:64, 0:64],
                        [(Zck, fc), "identb"], [pk])
                actf(P, apt[:].rearrange("p c t -> p (c t)"), p[:, 0:512], AF.Copy, [pk], [aptk])
                ys, ysk = Ysr.next()
                U, Uk = Ur.next()
                for hf in range(2):
                    bE, bEk = pU.next()
                    bO, bOk = pUO.next()
                    banks = [(bE, bEk), (bO, bOk)]
                    for hh in range(8):
                        h = hf * 8 + hh
                        fc, e_ = h // 2, h % 2
                        q = hh // 2
                        rows = slice(e_ * 64, e_ * 64 + 64)
                        bank, bk_ = banks[e_]
                        mm(P, bank[0:64, q * 64:(q + 1) * 64], apt[rows, fc, :], STb[rows, fc, :], True, True, [aptk, "STb"], [bk_])
                    for e_ in range(2):
                        bank, bk_ = banks[e_]
                        Uv = U[:, hf * 8:(hf + 1) * 8, :].rearrange("p (q e) v -> p q e v", e=2)[:, :, e_, :]
                        Pv = ZcP[:, hf * 8:(hf + 1) * 8, :].rearrange("p (q e) v -> p q e v", e=2)[:, :, e_, :]
                        tt(P, "dve", Uv, bank[0:64, 0:256].rearrange("p (q v) -> p q v", v=64), Pv, ALU.add,
                           [bk_] + [(Zck, fc) for fc in range(hf * 4, hf * 4 + 4)], [(Uk, hf)])
                    py2, py2k = pY2.next()
                    for hh in range(8):
                        h = hf * 8 + hh
                        fc, e_ = h // 2, h % 2
                        q = hh // 2
                        rows = slice(e_ * 64, e_ * 64 + 64)
                        bank, bk_ = banks[e_]
                        mm(P, bank[0:64, 256 + q * 64:256 + (q + 1) * 64], ot["Rt"][0][rows, fc, cs], STb[rows, fc, :], True, True,
                           [ot["Rt"][1], "STb"], [bk_])
                        a5s, a5sk = A5h[h]
                        mm(P, py2[0:64, hh * 64:(hh + 1) * 64], a5s[:, 128:192], U[:, h, :], True, False, [a5sk, (Uk, hf)], [py2k])
                        mm(P, py2[0:64, hh * 64:(hh + 1) * 64], a5s[:, 64:128], tm["V"][0][:, h * 64:(h + 1) * 64], False, True,
                           [a5sk, (tm["V"][1], hf)], [py2k])
                    y1s, y1sk = Y1r.next()
                    for e_ in range(2):
                        bank, bk_ = banks[e_]
                        actf(P, y1s[:].rearrange("p (q e v) -> p q e v", e=2, v=64)[:, :, e_, :],
                             bank[0:64, 256:512].rearrange("p (q v) -> p q v", v=64), AF.Copy, [bk_], [y1sk])
                    tt(P, "dve", ys[:, hf * 512:(hf + 1) * 512], py2[0:64, :], y1s[:], ALU.add, [py2k, y1sk], [(ysk, hf)])
                    ps_, psk = pS.next()
                    for hh in range(8):
                        h = hf * 8 + hh
                        fc, e_ = h // 2, h % 2
                        orow = slice(e_ * 64, e_ * 64 + 64)
                        ocol = slice((fc % 4) * 64, (fc % 4) * 64 + 64)
                        mm(P, ps_[orow, ocol], tm["Bg"][0][:, h * 64:(h + 1) * 64], U[:, h, :], True, False,
                           [(tm["Bg"][1], hf), (Uk, hf)], [psk])
                        mm(P, ps_[orow, ocol], tm["Kg"][0][:, h * 64:(h + 1) * 64], tm["V"][0][:, h * 64:(h + 1) * 64], False, True,
                           [(tm["Kg"][1], hf), (tm["V"][1], hf)], [psk])
                    tS, tSk = tmpS.next()
                    fsl = slice(hf * 4, hf * 4 + 4)
                    tt(P, "dve", tS[:], ST[:, fsl, :], gC2[:, fsl, cq:cq + 1].broadcast_to([128, 4, 64]), ALU.mult,
                       ["ST", gC2k], [tSk])
                    tt(P, "dve", ST[:, fsl, :], ps_[:, 0:256].rearrange("p (c v) -> p c v", v=64), tS[:], ALU.add, [psk, tSk], ["ST"])
                    actf(P, STb[:, fsl, :], ST[:, fsl, :], AF.Copy, ["ST"], ["STb"])
                for hf in range(2):
                    p, pk = pT.next()
                    for q in range(4):
                        fc = hf * 4 + q
                        trp(P, p[:, q * 64:(q + 1) * 64], ys[:, fc * 128:(fc + 1) * 128], C.identb[0:64, 0:64], [(ysk, hf), "identb"], [pk])
                    P.op("dve", lambda e, yt=yt, p=p, hf=hf, cs=cs: e.tensor_copy(
                        out=yt[:, hf * 4:(hf + 1) * 4, cs], in_=p[:, 0:256].rearrange("p (c t) -> p c t", t=64)), reads=[pk], writes=[ytk])
            P.dma(yTd.rearrange("(c p) t -> p c t", p=128)[:, :, tsl], yt[:], reads=[ytk], writes=[(pre + "yT", jb)])

    if os.environ.get("RW_STOP") == "B":
        return
    with P.phase(last=last):
        TC = 512
        stage = Ring(P, "wst", [128, 2048], F32, 2)
        wob = load_weight_bf16(P, "wout", W["w_out"], D, D, stage)
        gnw = colp("gn_w"); gnb = colp("gn_b")
        yr = Ring(P, "yin", [128, 8, TC], F32, 2)
        br = Ring(P, "bin", [128, 8, TC], F32, 2)
        gr = Ring(P, "gin", [128, 8, TC], BF16, 2)
        ygr = Ring(P, "ygo", [128, 8, TC], BF16, 2)
        pm = Ring(P, "pm", [128, 512], F32, 2, psum=True)
        pv_ = Ring(P, "pv", [128, 512], F32, 2, psum=True)
        po = Ring(P, "po", [128, 512], F32, 3, psum=True)
        ycr = Ring(P, "yc", [128, TC], F32, 2)
        sqr = Ring(P, "sq2", [128, TC], BF16, 2)
        rsr = Ring(P, "rs2", [128, TC], F32, 2)
        xr = Ring(P, "xr", [128, D], F32, 3)
        gneps = P.sb("gneps", [128, 1], F32)
        P.op("pool", lambda e: e.memset(gneps[:], RW_GN_EPS), writes=["gneps"])
        for j in range(T // TC):
            tsl = slice(j * TC, (j + 1) * TC)
            yi, yik = yr.next(); bi, bik = br.next(); gi_, gik_ = gr.next()
            P.dma(yi[:], yTd.rearrange("(c p) t -> p c t", p=128)[:, :, tsl], reads=[(pre + "yT", jj) for jj in range(4 * j, 4 * j + 4)], writes=[yik])
            P.dma(bi[:], Dm["BON"].rearrange("(c p) t -> p c t", p=128)[:, :, tsl], reads=[(pre + "BON", jj) for jj in (2 * j, 2 * j + 1)], writes=[bik], e="act")
            P.dma(gi_[:], gTd.rearrange("(c p) t -> p c t", p=128)[:, :, tsl], reads=[(pre + "g", jj) for jj in (2 * j, 2 * j + 1)], writes=[gik_])
            yg, ygk = ygr.next()
            for fc in range(8):
                m, mk = pm.next()
                for c0 in range(0, TC, 128):
                    mm(P, m[:, c0:c0 + 128], C.bonesf[:], yi[:, fc, c0:c0 + 128], True, True, ["bonesf", yik], [mk])
                yc, yck = ycr.next()
                stt(P, yc[:], m[:], -1.0 / 64, yi[:, fc, :], ALU.mult, ALU.add, [mk, yik], [yck])
                sq, sqk = sqr.next()
                actf(P, sq[:], yc[:], AF.Square, [yck], [sqk])
                v_, vk = pv_.next()
                mm(P, v_[:], C.bonesb[:], sq[:], True, True, ["bonesb", sqk], [vk])
                rs, rsk = rsr.next()
                actf(P, rs[:], v_[:], AF.Sqrt, [vk, "gneps"], [rsk], bias=gneps[:, 0:1], scale=1.0 / 64)
                P.op("dve", lambda e, rs=rs: e.reciprocal(out=rs[:], in_=rs[:]), reads=[rsk], writes=[rsk])
                tt(P, "pool", yc[:], yc[:], rs[:], ALU.mult, [yck, rsk], [yck])
                ts(P, "dve", yc[:], yc[:], gnw[:, fc:fc + 1], gnb[:, fc:fc + 1], ALU.mult, ALU.add, [yck, "gn_w", "gn_b"], [yck])
                tt(P, "pool", yc[:], yc[:], bi[:, fc, :], ALU.add, [yck, bik], [yck])
                tt(P, "dve", yg[:, fc, :], yc[:], gi_[:, fc, :], ALU.mult, [yck, gik_], [(ygk, fc)])
            for s in range(TC // 128):
                r0 = j * TC + s * 128
                xt, xk = xr.next()
                P.dma(xt[:], cur[r0:r0 + 128, :], reads=[("x", r0 // 128)], writes=[xk], e="act")
                for hf in range(2):
                    o, ok = po.next()
                    for c in range(8):
                        mm(P, o[:], yg[:, c, s * 128:(s + 1) * 128], wob[:, c, hf * 512:(hf + 1) * 512], c == 0, c == 7,
                           [(ygk, c), "wout"], [ok])
                    tt(P, "dve", xt[:, hf * 512:(hf + 1) * 512], o[:], xt[:, hf * 512:(hf + 1) * 512], ALU.add, [ok, xk], [xk])
                P.dma(y_out[r0:r0 + 128, :], xt[:], reads=[xk], writes=[("x", r0 // 128)], final=last)


LAYER_PARAMS = {
    0: ["norm", "mu", "w_in", "w_up", "w0", "a_up", "a0", "k_k", "k_a", "r_k", "gn_w", "gn_b", "w_out"],
    1: ["norm", "w_in", "conv_w", "conv_b", "dt_bias", "a_log", "d_skip", "gnorm_w", "w_out"],
    2: ["norm", "w_in", "q_norm", "k_norm", "w_out"],
    3: ["norm", "mu", "w_in", "w_up", "w0", "a_up", "a0", "k_k", "k_a", "r_k", "gn_w", "gn_b", "w_out", "v_up", "v0"],
}


def host_consts():
    c = {}
    c["c_ident"] = np.eye(128, dtype=np.float32)
    i = np.arange(128)
    c["c_bones"] = (i[:, None] // 64 == i[None, :] // 64).astype(np.float32)
    tq = np.arange(256)
    c["c_mask0"] = np.tile(((tq % 64) != 0).astype(np.float32)[None, :], (128, 1))
    ss_, tt_ = np.arange(64)[:, None], np.arange(64)[None, :]
    strict = (ss_ < tt_).astype(np.float32); incl = (ss_ <= tt_).astype(np.float32)
    c["c_mask5"] = np.concatenate([strict, incl, incl, strict.T, strict], axis=1)
    c["c_triu"] = (i[:, None] <= i[None, :]).astype(np.float32)
    c["c_ntril"] = -(i[:, None] >= i[None, :]).astype(np.float32)
    t = np.arange(512)
    c["c_mask"] = np.stack([((128 * r + i)[:, None] < t[None, :]).astype(np.float32) for r in range(4)], axis=1)
    return c


def host_layout(name, arr):
    a = np.asarray(arr, dtype=np.float32)
    short = name.split("_", 1)[1]
    if short == "norm":
        return np.ascontiguousarray(a.reshape(8, 128).T)
    if short in ("q_norm", "k_norm"):
        return np.ascontiguousarray(np.tile(a, 2).reshape(128, 1))
    if short in ("w0", "a0", "k_k", "k_a", "gn_w", "gn_b", "v0", "r_k"):
        return np.ascontiguousarray(a.reshape(8, 128).T)
    if short == "mu":
        return np.ascontiguousarray(a.reshape(6, 8, 128).transpose(2, 0, 1))
    if name == "l1_conv_w":
        return np.ascontiguousarray(a.T.reshape(32, 128, 4).transpose(1, 0, 2))
    if name == "l1_conv_b":
        return np.ascontiguousarray(a.reshape(32, 128).T)
    if name in ("l1_dt_bias", "l1_a_log"):
        return np.ascontiguousarray(np.tile(a[None, :], (128, 1)))
    if name == "l1_d_skip":
        return np.ascontiguousarray(np.repeat(a, 64).reshape(16, 128).T)
    if name == "l1_gnorm_w":
        return np.ascontiguousarray(a.reshape(16, 128).T)
    return np.ascontiguousarray(a)


def build_program(T, layers):
    nc = bass.Bass("TRN2", target_bir_lowering=False)
    x_in = nc.dram_tensor("x", [T, D], F32, kind="ExternalInput").ap()
    y_out = nc.dram_tensor("y", [T, D], F32, kind="ExternalOutput").ap()
    hc = host_consts()
    cap = {k: nc.dram_tensor(k, list(v.shape), F32, kind="ExternalInput").ap() for k, v in hc.items()}
    Wd = {}
    shapes = param_shapes()
    for L in layers:
        Wd[L] = {}
        for pn in LAYER_PARAMS[L]:
            full = "l%d_%s" % (L, pn)
            Wd[L][pn] = nc.dram_tensor(full, list(shapes[full]), F32, kind="ExternalInput").ap()
    P = Prog(nc)
    C = Ctx()
    specs = [("ident", "c_ident", [128, 128], None), ("bonesb", "c_bones", [128, 128], BF16),
             ("ntril", "c_ntril", [128, 128], BF16), ("maskb", "c_mask", [128, 4, 512], BF16),
             ("triu", "c_triu", [128, 128], None), ("identb", "c_ident", [128, 128], BF16),
             ("mask0", "c_mask0", [128, 256], None), ("mask5", "c_mask5", [64, 320], None),
             ("bonesf", "c_bones", [128, 128], None)]
    tiles = {}
    for key, cn, shape, cast in specs:
        tiles[key] = P.sb(key, shape, cast or F32)
        setattr(C, key, tiles[key])
    C.onesf = P.sb("onesf", [128, 128], F32)
    C.epsc = P.sb("epsc", [128, 1], F32)
    C.onec = P.sb("onec", [128, 1], F32)
    C.onesb = P.sb("onesb", [128, 128], BF16)
    with P.phase():
        for key, cn, shape, cast in specs:
            t = tiles[key]
            if cast is None:
                P.dma(t[:], cap[cn], writes=[key])
            else:
                st = P.sb(key + "_f", shape, F32)
                P.dma(st[:], cap[cn], writes=[key + "_f"])
                P.op("pool", lambda e, t=t, st=st: e.tensor_copy(out=t[:], in_=st[:]), reads=[key + "_f"], writes=[key])
        P.op("pool", lambda e: e.memset(C.onesf[:], 1.0), writes=["onesf"])
        P.op("pool", lambda e: e.memset(C.epsc[:], NORM_EPS), writes=["epsc"])
        P.op("pool", lambda e: e.memset(C.onec[:], 1.0), writes=["onec"])
        P.op("pool", lambda e: e.memset(C.onesb[:], 1.0), writes=["onesb"])
    cur = x_in
    fns = {0: None, 1: None, 2: layer_sb, 3: None}
    for i, L in enumerate(layers):
        last = (i == len(layers) - 1)
        FNS[L](P, C, L, T, Wd[L], cur, y_out, last)
        cur = y_out
    P.close()
    return nc


def param_shapes():
    s = {}
    for L, vres in ((0, False), (3, True)):
        p = "l%d_" % L
        ncols = 4 * 1024 + 64 + 64 + (32 if vres else 0)
        s.update({p + "norm": (128, 8), p + "mu": (128, 6, 8), p + "w_in": (1024, ncols), p + "w_up": (64, 1024),
                  p + "w0": (128, 8), p + "a_up": (64, 1024), p + "a0": (128, 8), p + "k_k": (128, 8), p + "k_a": (128, 8),
                  p + "r_k": (128, 8), p + "gn_w": (128, 8), p + "gn_b": (128, 8), p + "w_out": (1024, 1024)})
        if vres:
            s.update({p + "v_up": (32, 1024), p + "v0": (128, 8)})
    s.update({"l1_norm": (128, 8), "l1_w_in": (1024, 6176), "l1_conv_w": (128, 32, 4), "l1_conv_b": (128, 32), "l1_dt_bias": (128, 32),
              "l1_a_log": (128, 32), "l1_d_skip": (128, 16), "l1_gnorm_w": (128, 16), "l1_w_out": (2048, 1024)})
    s.update({"l2_norm": (128, 8), "l2_w_in": (1024, 4096), "l2_q_norm": (128, 1), "l2_k_norm": (128, 1), "l2_w_out": (1024, 1024)})
    return s


FNS = {0: layer_rwkv, 1: layer_ssd, 2: layer_sb, 3: layer_rwkv}


def make_in_maps(inputs, layers, ncores, T):
    hc = host_consts()
    shapes = param_shapes()
    shared = dict(hc)
    for L in layers:
        for pn in LAYER_PARAMS[L]:
            full = "l%d_%s" % (L, pn)
            a = host_layout(full, inputs[full])
            assert tuple(a.shape) == tuple(shapes[full]), (full, a.shape, shapes[full])
            shared[full] = a
    x = np.asarray(inputs["x"], dtype=np.float32)
    maps = []
    for b in range(ncores):
        m = dict(shared)
        m["x"] = np.ascontiguousarray(x[b, :T])
        maps.append(m)
    return maps


ALL_INPUT_NAMES = (
    "x",
    "l0_norm",
    "l0_mu",
    "l0_w_in",
    "l0_w_up",
    "l0_w0",
    "l0_a_up",
    "l0_a0",
    "l0_k_k",
    "l0_k_a",
    "l0_r_k",
    "l0_gn_w",
    "l0_gn_b",
    "l0_w_out",
    "l1_norm",
    "l1_w_in",
    "l1_conv_w",
    "l1_conv_b",
    "l1_dt_bias",
    "l1_a_log",
    "l1_d_skip",
    "l1_gnorm_w",
    "l1_w_out",
    "l2_norm",
    "l2_w_in",
    "l2_q_norm",
    "l2_k_norm",
    "l2_w_out",
    "l3_norm",
    "l3_mu",
    "l3_w_in",
    "l3_w_up",
    "l3_w0",
    "l3_a_up",
    "l3_a0",
    "l3_k_k",
    "l3_k_a",
    "l3_r_k",
    "l3_gn_w",
    "l3_gn_b",
    "l3_w_out",
    "l3_v_up",
    "l3_v0",
)


def kernel(**inputs):
    missing = [n for n in ALL_INPUT_NAMES if n not in inputs]
    assert not missing, missing
    T = 4096
    layers = [0, 1, 2, 3]
    nc = build_program(T, layers)
    maps = make_in_maps(inputs, layers, 8, T)
    res = run_bass_kernel_spmd(nc, maps, core_ids=list(range(8)))
    return np.stack([r["y"] for r in res.results], axis=0).astype(np.float32)
```
